# Optimizing a Trainium2 kernel written in Bass

```python
import math
import jax, jax.numpy as jnp
from jax import lax
import numpy as np

D_MODEL = 1024
BATCH = 8
SEQ = 2048
DEPTH = 2

SSD_INNER = 1024
SSD_HEADS = 16
SSD_HEAD_DIM = 64
SSD_GROUPS = 2
SSD_STATE = 128
SSD_CONV = 4
SSD_CHUNK = 128
SSD_CONV_DIM = SSD_INNER + 2 * SSD_GROUPS * SSD_STATE

CONF_WIDTH = 512
CONF_KERNEL = 31

ATTN_HEADS = 8
ATTN_KV_HEADS = 2
ATTN_HEAD_DIM = 64
ATTN_WIDTH = ATTN_HEADS * ATTN_HEAD_DIM
ATTN_KV_WIDTH = ATTN_KV_HEADS * ATTN_HEAD_DIM
WINDOW = 128
ATTN_BLOCK = 128

N_BUCKETS = 32
MAX_DISTANCE = 128

PLE_DIM = 256
MIX_WIDTH = SSD_INNER + CONF_WIDTH + ATTN_WIDTH
EPS = 1e-6

SPLIT_SIZES = (
    SSD_INNER,
    SSD_CONV_DIM,
    SSD_HEADS,
    2 * CONF_WIDTH,
    CONF_WIDTH,
    ATTN_WIDTH,
    ATTN_KV_WIDTH,
    ATTN_KV_WIDTH,
    ATTN_WIDTH,
)
IN_WIDTH = sum(SPLIT_SIZES)

kernel_name = "hybrid_ssd_conformer_swa_parallel_block"


def rms_norm(x, w):
    xf = x.astype(jnp.float32)
    y = xf * lax.rsqrt(jnp.mean(xf * xf, axis=-1, keepdims=True) + EPS)
    return (y * w.astype(jnp.float32)).astype(x.dtype)


def layer_norm(x, w, b):
    xf = x.astype(jnp.float32)
    mu = jnp.mean(xf, axis=-1, keepdims=True)
    xc = xf - mu
    y = xc * lax.rsqrt(jnp.mean(xc * xc, axis=-1, keepdims=True) + EPS)
    return (y * w.astype(jnp.float32) + b.astype(jnp.float32)).astype(x.dtype)


def causal_depthwise_conv(u, w, b):
    k, c = w.shape
    out = lax.conv_general_dilated(
        u, w[:, None, :].astype(u.dtype), window_strides=(1,), padding=[(k - 1, 0)],
        dimension_numbers=('NWC', 'WIO', 'NWC'), feature_group_count=c)
    return out + b.astype(u.dtype)


def split_columns(u):
    parts, start = [], 0
    for size in SPLIT_SIZES:
        parts.append(u[..., start:start + size])
        start += size
    return parts


def ssd_mixer(z, xbc, dt_raw, conv_w, conv_b, dt_bias, a_log, d_skip, norm_w):
    bsz, seq, _ = xbc.shape
    nc = seq // SSD_CHUNK
    hpg = SSD_HEADS // SSD_GROUPS
    xbc = jax.nn.silu(causal_depthwise_conv(xbc, conv_w, conv_b))
    xs = xbc[..., :SSD_INNER]
    bm = xbc[..., SSD_INNER:SSD_INNER + SSD_GROUPS * SSD_STATE]
    cm = xbc[..., SSD_INNER + SSD_GROUPS * SSD_STATE:]
    x = xs.reshape(bsz, nc, SSD_CHUNK, SSD_GROUPS, hpg, SSD_HEAD_DIM)
    bm = bm.reshape(bsz, nc, SSD_CHUNK, SSD_GROUPS, SSD_STATE)
    cm = cm.reshape(bsz, nc, SSD_CHUNK, SSD_GROUPS, SSD_STATE)
    dt = jax.nn.softplus(dt_raw.astype(jnp.float32) + dt_bias.astype(jnp.float32))
    a = -jnp.exp(a_log.astype(jnp.float32))
    dt = dt.reshape(bsz, nc, SSD_CHUNK, SSD_GROUPS, hpg)
    da = dt * a.reshape(SSD_GROUPS, hpg)
    a_cs = jnp.cumsum(jnp.transpose(da, (0, 1, 3, 4, 2)), axis=-1)
    xdt = x * dt[..., None]
    seg = a_cs[..., :, None] - a_cs[..., None, :]
    causal = jnp.tril(jnp.ones((SSD_CHUNK, SSD_CHUNK), dtype=bool))
    lmat = jnp.exp(jnp.where(causal, seg, -jnp.inf))
    cb = jnp.einsum('bclgn,bcsgn->bcgls', cm, bm)
    y_diag = jnp.einsum('bcgls,bcgrls,bcsgrp->bclgrp', cb, lmat, xdt)
    decay_states = jnp.exp(a_cs[..., -1:] - a_cs)
    states = jnp.einsum('bclgn,bcgrl,bclgrp->bcgrpn', bm, decay_states, xdt)
    a_tot = a_cs[..., -1]

    def step(h, inp):
        st, dec = inp
        return h * jnp.exp(dec)[..., None, None] + st, h

    h0 = jnp.zeros((bsz, SSD_GROUPS, hpg, SSD_HEAD_DIM, SSD_STATE), states.dtype)
    _, h_in = lax.scan(step, h0, (jnp.moveaxis(states, 1, 0), jnp.moveaxis(a_tot, 1, 0)))
    h_in = jnp.moveaxis(h_in, 0, 1)
    y_off = jnp.einsum('bclgn,bcgrpn,bcgrl->bclgrp', cm, h_in, jnp.exp(a_cs))
    y = y_diag + y_off + x * d_skip.reshape(SSD_GROUPS, hpg)[:, :, None]
    y = y.reshape(bsz, seq, SSD_INNER)
    yg = (y * jax.nn.silu(z.astype(y.dtype))).reshape(bsz, seq, SSD_GROUPS, SSD_INNER // SSD_GROUPS)
    yg = yg.astype(jnp.float32)
    yg = yg * lax.rsqrt(jnp.mean(yg * yg, axis=-1, keepdims=True) + EPS)
    return yg.reshape(bsz, seq, SSD_INNER) * norm_w.astype(jnp.float32)


def conformer_conv(u, dw_w, dw_b, ln_w, ln_b):
    glu = u[..., :CONF_WIDTH] * jax.nn.sigmoid(u[..., CONF_WIDTH:])
    h = causal_depthwise_conv(glu, dw_w, dw_b)
    return jax.nn.silu(layer_norm(h, ln_w, ln_b))


def t5_bucket(dist):
    max_exact = N_BUCKETS // 2
    d = jnp.maximum(dist, 0)
    large = max_exact + (jnp.log(jnp.maximum(d, 1).astype(jnp.float32) / max_exact)
                         / math.log(MAX_DISTANCE / max_exact) * (N_BUCKETS - max_exact)).astype(jnp.int32)
    large = jnp.minimum(large, N_BUCKETS - 1)
    return jnp.where(d < max_exact, d, large)


def sliding_window_attention(q, k, v, sinks, rel_bias):
    bsz, seq, _ = q.shape
    nb = seq // ATTN_BLOCK
    grp = ATTN_HEADS // ATTN_KV_HEADS
    q = q.reshape(bsz, nb, ATTN_BLOCK, ATTN_KV_HEADS, grp, ATTN_HEAD_DIM)
    k = k.reshape(bsz, seq, ATTN_KV_HEADS, ATTN_HEAD_DIM)
    v = v.reshape(bsz, seq, ATTN_KV_HEADS, ATTN_HEAD_DIM)
    pad = ((0, 0), (ATTN_BLOCK, 0), (0, 0), (0, 0))

    def band(t):
        prev = jnp.pad(t, pad)[:, :seq].reshape(bsz, nb, ATTN_BLOCK, ATTN_KV_HEADS, ATTN_HEAD_DIM)
        cur = t.reshape(bsz, nb, ATTN_BLOCK, ATTN_KV_HEADS, ATTN_HEAD_DIM)
        return jnp.concatenate([prev, cur], axis=2)

    kb, vb = band(k), band(v)
    s = jnp.einsum('bnqkgd,bnskd->bnkgqs', q, kb).astype(jnp.float32) * (ATTN_HEAD_DIM ** -0.5)
    q_idx = jnp.arange(ATTN_BLOCK)[:, None]
    s_idx = jnp.arange(2 * ATTN_BLOCK)[None, :]
    dist = q_idx + ATTN_BLOCK - s_idx
    bias = rel_bias.astype(jnp.float32)[t5_bucket(dist)]
    bias = jnp.transpose(bias, (2, 0, 1)).reshape(ATTN_KV_HEADS, grp, ATTN_BLOCK, 2 * ATTN_BLOCK)
    blk = jnp.arange(nb)[:, None, None]
    valid = (dist >= 0) & (dist < WINDOW) & ((blk > 0) | (s_idx >= ATTN_BLOCK))
    s = jnp.where(valid[None, :, None, None], s + bias, -jnp.inf)
    sink = sinks.astype(jnp.float32).reshape(1, 1, ATTN_KV_HEADS, grp, 1, 1)
    m = jnp.maximum(jnp.max(s, axis=-1, keepdims=True), sink)
    e = jnp.exp(s - m)
    probs = e / (jnp.sum(e, axis=-1, keepdims=True) + jnp.exp(sink - m))
    o = jnp.einsum('bnkgqs,bnskd->bnqkgd', probs.astype(vb.dtype), vb)
    return o.reshape(bsz, seq, ATTN_WIDTH)


def hybrid_layer(x, p_i, pre_norm_w, w_in, ssd_conv_w, ssd_conv_b, ssd_dt_bias, ssd_a_log, ssd_d,
                 ssd_norm_w, conf_dw_w, conf_dw_b, conf_ln_w, conf_ln_b, attn_sinks, rel_bias,
                 w_out, post_norm_w, ple_proj, ple_gate):
    hn = rms_norm(x, pre_norm_w)
    u = jnp.einsum('bld,de->ble', hn, w_in)
    z, xbc, dt_raw, conf_in, conf_gate, q, k, v, attn_gate = split_columns(u)
    y_ssd = ssd_mixer(z, xbc, dt_raw, ssd_conv_w, ssd_conv_b, ssd_dt_bias, ssd_a_log, ssd_d, ssd_norm_w)
    y_conf = conformer_conv(conf_in, conf_dw_w, conf_dw_b, conf_ln_w, conf_ln_b) * jax.nn.silu(conf_gate)
    y_attn = sliding_window_attention(q, k, v, attn_sinks, rel_bias) * jax.nn.silu(attn_gate)
    y = jnp.concatenate([y_ssd.astype(x.dtype), y_conf.astype(x.dtype), y_attn.astype(x.dtype)], axis=-1)
    h = x + rms_norm(jnp.einsum('ble,ed->bld', y, w_out), post_norm_w)
    gate = jax.nn.sigmoid(jnp.einsum('bld,de->ble', h, ple_gate))
    return h + jnp.einsum('blp,pd->bld', p_i, ple_proj) * gate


def setup_inputs(seed: int = 0) -> dict:
    key = jax.random.key(seed)
    ks = jax.random.split(key, 24)
    f32 = jnp.float32
    nrm = lambda k, shape, scale: jax.random.normal(k, shape, f32) * scale
    dt = jnp.exp(jax.random.uniform(ks[6], (DEPTH, SSD_HEADS), f32) * (math.log(0.1) - math.log(0.001)) + math.log(0.001))
    dt_bias = dt + jnp.log(-jnp.expm1(-dt))
    return {
        "x": nrm(ks[0], (BATCH, SEQ, D_MODEL), 1.0),
        "p": nrm(ks[1], (DEPTH, BATCH, SEQ, PLE_DIM), 1.0),
        "pre_norm_w": 1.0 + nrm(ks[2], (DEPTH, D_MODEL), 0.05),
        "w_in": nrm(ks[3], (DEPTH, D_MODEL, IN_WIDTH), D_MODEL ** -0.5),
        "ssd_conv_w": nrm(ks[4], (DEPTH, SSD_CONV, SSD_CONV_DIM), SSD_CONV ** -0.5),
        "ssd_conv_b": nrm(ks[5], (DEPTH, SSD_CONV_DIM), 0.02),
        "ssd_dt_bias": dt_bias,
        "ssd_a_log": jnp.log(jax.random.uniform(ks[7], (DEPTH, SSD_HEADS), f32, 1.0, 16.0)),
        "ssd_d": 1.0 + nrm(ks[8], (DEPTH, SSD_HEADS), 0.1),
        "ssd_norm_w": 1.0 + nrm(ks[9], (DEPTH, SSD_INNER), 0.05),
        "conf_dw_w": nrm(ks[10], (DEPTH, CONF_KERNEL, CONF_WIDTH), CONF_KERNEL ** -0.5),
        "conf_dw_b": nrm(ks[11], (DEPTH, CONF_WIDTH), 0.02),
        "conf_ln_w": 1.0 + nrm(ks[12], (DEPTH, CONF_WIDTH), 0.05),
        "conf_ln_b": nrm(ks[13], (DEPTH, CONF_WIDTH), 0.02),
        "attn_sinks": nrm(ks[14], (DEPTH, ATTN_HEADS), 0.5),
        "rel_bias": nrm(ks[15], (N_BUCKETS, ATTN_HEADS), 0.1),
        "w_out": nrm(ks[16], (DEPTH, MIX_WIDTH, D_MODEL), MIX_WIDTH ** -0.5),
        "post_norm_w": 1.0 + nrm(ks[17], (DEPTH, D_MODEL), 0.05),
        "ple_proj": nrm(ks[18], (DEPTH, PLE_DIM, D_MODEL), PLE_DIM ** -0.5),
        "ple_gate": nrm(ks[19], (DEPTH, D_MODEL, D_MODEL), D_MODEL ** -0.5),
    }


def reference(x, p, pre_norm_w, w_in, ssd_conv_w, ssd_conv_b, ssd_dt_bias, ssd_a_log, ssd_d,
              ssd_norm_w, conf_dw_w, conf_dw_b, conf_ln_w, conf_ln_b, attn_sinks, rel_bias,
              w_out, post_norm_w, ple_proj, ple_gate):
    h = x
    for i in range(DEPTH):
        h = hybrid_layer(h, p[i], pre_norm_w[i], w_in[i], ssd_conv_w[i], ssd_conv_b[i], ssd_dt_bias[i],
                         ssd_a_log[i], ssd_d[i], ssd_norm_w[i], conf_dw_w[i], conf_dw_b[i], conf_ln_w[i],
                         conf_ln_b[i], attn_sinks[i], rel_bias, w_out[i], post_norm_w[i], ple_proj[i],
                         ple_gate[i]).astype(x.dtype)
    return h
```

```python
from contextlib import ExitStack
import math
import numpy as np
import concourse.bass as bass
import concourse.mybir as mybir
from concourse.bass_utils import run_bass_kernel_spmd

F32 = mybir.dt.float32
BF16 = mybir.dt.bfloat16
AF = mybir.ActivationFunctionType
ALU = mybir.AluOpType

L_SEQ = 2048
D = 1024
DEPTH = 2
TB = 512
NT = 4
NBLK = L_SEQ // TB
EPS = 1e-6
NEG = -30000.0

O_PREW, O_SSDNW, O_POSTW = 0, 1024, 2048
O_DTB, O_ALOG, O_D16 = 3072, 3088, 3104
O_CW, O_CB = 3120, 3168
O_DWW, O_DWB, O_LNW, O_LNB, O_SINK = 3180, 3304, 3308, 3312, 3316
NP = 3328

DEBUG = False


class Buf:
    __slots__ = ("w", "r")

    def __init__(self):
        self.w = {}
        self.r = {}


class Sched:
    ENGS = ("pe", "act", "dve", "pool", "sp")
    NSLOT = {"sp": 12, "pool": 44, "act": 8}

    def __init__(self, nc, stack):
        self.nc = nc
        self.eng = {"pe": nc.tensor, "act": nc.scalar, "dve": nc.vector,
                    "pool": nc.gpsimd, "sp": nc.sync}
        self.sems = {}
        self.cnt = {}
        self.known = {e: {} for e in self.ENGS}
        for e in self.ENGS:
            self.sems[e] = stack.enter_context(nc.semaphore("s_" + e))
            self.cnt[e] = 0
        self.slot_i = {}
        self.fence = {}
        for q, n in self.NSLOT.items():
            self.slot_i[q] = 0
            for k in range(n):
                key = "d_%s%d" % (q, k)
                self.sems[key] = stack.enter_context(nc.semaphore(key))
                self.cnt[key] = 0

    LOG = None

    def _push(self, engname, waits, fn, key, inc):
        if Sched.LOG is not None:
            Sched.LOG.append((engname, list(waits), key if fn is not None else None, inc))
        e = self.eng[engname]
        for k, v in waits:
            e.wait_ge(self.sems[k], v)
        if fn is not None:
            fn(e).then_inc(self.sems[key], inc)

    def barrier(self):
        engs = ("pe", "act", "dve", "pool")
        for e in engs:
            waits = []
            tgt = {k: self.cnt[k] for k in engs if k != e}
            tgt.update(self.fence)
            for k, v in tgt.items():
                if v > 0 and self.known[e].get(k, 0) < v:
                    self.known[e][k] = v
                    waits.append((k, v))
            self._push(e, waits, None, None, 0)

    def _deps(self, eng, reads, writes):
        need = {}

        def add(d):
            for k, v in d.items():
                if need.get(k, 0) < v:
                    need[k] = v
        for b in reads:
            add(b.w)
        for b in writes:
            add(b.w)
            add(b.r)
        out = []
        kn = self.known[eng]
        for k, v in need.items():
            if k == eng and eng == "pe":
                continue
            if kn.get(k, 0) >= v:
                continue
            kn[k] = v
            out.append((k, v))
        return out

    def op(self, eng, fn, reads=(), writes=()):
        waits = self._deps(eng, reads, writes)
        self.cnt[eng] += 1
        t = self.cnt[eng]
        for b in reads:
            if b.r.get(eng, 0) < t:
                b.r[eng] = t
        for b in writes:
            b.w = {eng: t}
            b.r = {}
        self._push(eng, waits, fn, eng, 1)

    def dma(self, q, out, in_, reads=(), writes=(), fence=None):
        if fence is None:
            fence = (q == "pool")
        waits = self._deps(q, reads, writes)
        i = self.slot_i[q]
        self.slot_i[q] += 1
        key = "d_%s%d" % (q, i % self.NSLOT[q])
        prev = self.cnt[key]
        if prev > 0 and self.known[q].get(key, 0) < prev:
            self.known[q][key] = prev
            waits.append((key, prev))
        self.cnt[key] += 16
        t = self.cnt[key]
        if fence:
            self.fence[key] = t
        for b in reads:
            if b.r.get(key, 0) < t:
                b.r[key] = t
        for b in writes:
            b.w = {key: t}
            b.r = {}
        self._push(q, waits, lambda e: e.dma_start(out=out, in_=in_), key, 16)

    def wait_all(self, eng, bufs):
        waits = self._deps(eng, bufs, ())
        self._push(eng, waits, None, None, 0)


class Tl:
    def __init__(self, t, nb=1):
        self.t = t
        self.bs = [Buf() for _ in range(nb)]

    @property
    def b(self):
        return self.bs[0]

    def __getitem__(self, k):
        return self.t[k]


class Prog:
    def __init__(self, nc, st, layers, first_in, last_out):
        self.nc = nc
        self.st = st
        self.S = Sched(nc, st)
        self.layers = layers

    _uid = [0]

    def sb(self, stack, name, shape, dt, nb=1):
        self._uid[0] += 1
        name = "%s_%d" % (name, self._uid[0])
        return Tl(stack.enter_context(self.nc.sbuf_tensor(name, shape, dt)), nb)

    def ps(self, stack, name, shape, dt):
        return Tl(stack.enter_context(self.nc.psum_tensor(name, shape, dt)))

    def mm(self, out, pairs, reads, writes, start=True, stop=True, sgc=False):
        def fn(e):
            n = len(pairs)
            ins = None
            for i, (l, r) in enumerate(pairs):
                kw = {}
                if sgc:
                    kw["skip_group_check"] = True
                ins = e.matmul(out, lhsT=l, rhs=r, start=(start and i == 0),
                               stop=(stop and i == n - 1), **kw)
            return ins
        self.S.op("pe", fn, reads, writes)

    def mms(self, items, reads, writes):
        def fn(e):
            ins = None
            for (o, l, r, s0, s1) in items:
                ins = e.matmul(o, lhsT=l, rhs=r, start=s0, stop=s1, skip_group_check=True)
            return ins
        self.S.op("pe", fn, reads, writes)

    def trs(self, items, ident, reads, writes):
        def fn(e):
            ins = None
            for (o, i) in items:
                ins = e.transpose(out=o, in_=i, identity=ident)
            return ins
        self.S.op("pe", fn, reads, writes)

    def act(self, out, in_, func, reads, writes, bias=None, scale=None, accum=None):
        kw = {}
        if bias is not None:
            kw["bias"] = bias
        if scale is not None:
            kw["scale"] = scale
        if accum is not None:
            kw["accum_out"] = accum
        self.S.op("act", lambda e: e.activation(out=out, in_=in_, func=func, **kw), reads, writes)

    def tt(self, out, in0, in1, op, reads, writes, eng="dve"):
        self.S.op(eng, lambda e: e.tensor_tensor(out=out, in0=in0, in1=in1, op=op), reads, writes)

    def ts(self, out, in0, s1, s2, op0, op1, reads, writes):
        if s2 is None:
            self.S.op("dve", lambda e: e.tensor_scalar(out=out, in0=in0, scalar1=s1, scalar2=None, op0=op0),
                      reads, writes)
        else:
            self.S.op("dve", lambda e: e.tensor_scalar(out=out, in0=in0, scalar1=s1, scalar2=s2,
                                                       op0=op0, op1=op1), reads, writes)

    def stt(self, out, in0, scalar, in1, op0, op1, reads, writes):
        self.S.op("dve", lambda e: e.scalar_tensor_tensor(out=out, in0=in0, scalar=scalar, in1=in1,
                                                          op0=op0, op1=op1), reads, writes)

    def cp(self, eng, out, in_, reads, writes):
        if eng == "act":
            self.S.op("act", lambda e: e.copy(out=out, in_=in_), reads, writes)
        else:
            self.S.op(eng, lambda e: e.tensor_copy(out=out, in_=in_), reads, writes)

    def memset(self, ap, val, writes, eng="dve"):
        self.S.op(eng, lambda e: e.memset(ap, val), (), writes)

    def rstd(self, out, ssq, inv_n, tmp, reads, writes):
        self.ts(tmp, ssq, inv_n, EPS, ALU.mult, ALU.add, reads, writes)
        self.act(tmp, tmp, AF.Ln, writes, writes)
        self.act(out, tmp, AF.Exp, writes, writes, scale=-0.5)

    def build(self, dram):
        nc, S, st = self.nc, self.S, self.st
        sb, ps = self.sb, self.ps
        mm, mms, trs, act, tt, ts, stt, cp, memset = (self.mm, self.mms, self.trs, self.act, self.tt,
                                                      self.ts, self.stt, self.cp, self.memset)
        b_scr = {}
        for l in self.layers:
            for name, ng in (("w_proj", 1), ("w_in", 11), ("w_out", 4), ("w_gate", 2)):
                for g in range(ng):
                    b_scr[(l, name, g)] = Buf()

        cast_q = []
        for l in self.layers:
            cast_q.append((l, "w_proj", 0))
            for g in range(11):
                cast_q.append((l, "w_in", g))
            for g in range(4):
                cast_q.append((l, "w_out", g))
            for g in range(2):
                cast_q.append((l, "w_gate", g))
        cast_pos = [0]

        def cast_some(n):
            for _ in range(n):
                if cast_pos[0] < len(cast_q):
                    l_, name, g = cast_q[cast_pos[0]]
                    cast_pos[0] += 1
                    S.dma("pool", dram[name + "_s"][l_, g], dram[name][l_, g], writes=[b_scr[(l_, name, g)]],
                          fence=False)

        IDF = sb(st, "IDF", [128, 128], F32)
        IDB = sb(st, "IDB", [128, 128], BF16)
        U = sb(st, "U", [128, 128], F32)
        ONEF = sb(st, "ONEF", [128, 128], F32)
        ONE512 = sb(st, "ONE512", [128, 128], F32)
        ONEB = sb(st, "ONEB", [128, 256], BF16)
        AMASK = sb(st, "AMASK", [128, 2, 128], F32)
        PRM = sb(st, "PRM", [128, NP], F32)
        DER = sb(st, "DER", [128, 64], F32)
        BT = sb(st, "BT", [128, 2, 8, 128], F32)
        WP = sb(st, "WP", [128, 2, 1024], BF16)
        WB = [sb(st, "WB%d" % i, [128, 8, 512], BF16) for i in range(3)]
        XNT = sb(st, "XNT", [128, 8, 512], BF16)
        YT = sb(st, "YT", [128, 16, 512], BF16)
        HS = sb(st, "HS", [128, 1024], F32)
        HSB = sb(st, "HSB", [128, 1024], BF16)
        TAIL = sb(st, "TAIL", [128, 12, 3], F32)
        GLU = sb(st, "GLU", [128, 4, 30 + TB], BF16)
        KT = sb(st, "KT", [128, 128 + TB], BF16)
        VT = sb(st, "VT", [128, 5, 128], BF16)
        JUNK = sb(st, "JUNK", [128, 1024], BF16)
        SM = sb(st, "SM", [128, 64], F32)
        PA = ps(st, "PA", [128, 512], F32)
        PB = ps(st, "PB", [128, 512], F32)
        PT = ps(st, "PT", [128, 1024], BF16)
        PS_ = ps(st, "PS", [128, 512], F32)
        PX0 = ps(st, "PX0", [128, 512], F32)
        PX1 = ps(st, "PX1", [128, 512], F32)
        PY = ps(st, "PY", [128, 512], F32)
        PO = ps(st, "PO", [128, 512], F32)

        def view(ap, bufs):
            v = Tl.__new__(Tl)
            v.t = ap
            v.bs = list(bufs)
            return v

        XR = [sb(st, "XR%d" % i, [128, 1024], F32) for i in range(2)]
        YN = sb(st, "YN", [128, 1024], BF16)
        XN2 = sb(st, "XN2", [128, 1024], BF16)
        XNs = [YN, XN2]
        PG = sb(st, "PG", [128, 4, 8 + TB], F32, nb=4)
        PRE = [view(PG[:, i, 0:3 + TB], [PG.bs[i]]) for i in range(2)]
        PREB = [view(PG[:, i, 0:264].bitcast(BF16)[:, 0:3 + TB], [PG.bs[i]]) for i in range(2)]
        DGS = [view(PG[:, i, 264:520].bitcast(BF16).rearrange("p (a b) -> p a b", a=4), [PG.bs[i]]) for i in range(2)]
        TAILB = sb(st, "TAILB", [128, 12, 4], BF16)
        ACC = [view(PG[:, 2 + i, 0:TB], [PG.bs[2 + i]]) for i in range(2)]
        TH = [sb(st, "TH%d" % i, [128, TB], F32) for i in range(2)]
        XBC = sb(st, "XBC", [128, 12, TB], BF16, nb=12)
        ZS = sb(st, "ZS", [128, NT, 1024], BF16, nb=NT)
        DTR = sb(st, "DTR", [128, NT, 16], F32, nb=NT)
        SD2 = sb(st, "SD2", [128, 9, 64], F32, nb=9)
        XG = sb(st, "XG", [128, 6, 1024], BF16, nb=6)
        XDT = [view(XG[:, i, :], [XG.bs[i]]) for i in range(2)]
        XD = view(XG[:, 2, :], [XG.bs[2]])
        XDS = view(XG[:, 3, :], [XG.bs[3]])
        MT4 = [view(XG[:, 4 + i // 2, (i % 2) * 512:(i % 2 + 1) * 512].rearrange("p (a b) -> p a b", a=4),
                    [Buf()]) for i in range(4)]
        DG = view(PG[:].rearrange("p a b -> p (a b)").bitcast(BF16)[:, 0:31 * 128].rearrange("p (a b) -> p a b", a=31),
                  PG.bs)
        BMT = sb(st, "BMT", [128, 2, 128], BF16)
        CBM = [sb(st, "CBM%d" % i, [128, 2, 128], F32) for i in range(2)]
        MNEG = sb(st, "MNEG", [128, 4, 128], BF16)
        T1 = sb(st, "T1", [128, 512], F32)
        D1 = T1
        Y = sb(st, "Y", [128, 1024], F32)
        EB = [sb(st, "EB%d" % i, [128, 4, 128], F32) for i in range(2)]
        MU = view(XR[0][:, 0:512], [XR[0].b])
        RS = view(XR[0][:, 512:1024], [XR[0].b])
        U1 = sb(st, "U1", [128, TB], F32)
        QT = sb(st, "QT", [128, 4, TB], BF16, nb=4)
        AG = sb(st, "AG", [128, 4, TB], BF16, nb=4)
        RB = [sb(st, "RB%d" % i, [128, 4, 128], F32) for i in range(2)]
        SC = [view(XR[1][:, i * 512:(i + 1) * 512], [XR[1].b]) for i in range(2)]
        ET = [sb(st, "ET%d" % i, [128, 512], BF16) for i in range(4)]
        RD = sb(st, "RD", [128, 256], F32)
        OT = sb(st, "OT", [128, 256], F32)
        O = sb(st, "O", [128, NT, 1024], F32, nb=NT)
        HH = view(O[:, 0:2, :].rearrange("p a (b c) -> p (a b) c", b=2), [O.bs[0], O.bs[0], O.bs[1], O.bs[1]])
        HQ = [view(O[:, 2, i * TB:(i + 1) * TB], [O.bs[2]]) for i in range(2)]
        CG = view(O[:, 3, :].bitcast(BF16).rearrange("p (a b) -> p a b", a=4), [O.bs[3]] * 4)
        PR = [sb(st, "PR%d" % i, [128, 256], F32) for i in range(2)]
        SM1s = [sb(st, "SM1_%d" % i, [128, 8], F32) for i in range(2)]
        SM5s = [sb(st, "SM5_%d" % i, [128, 8], F32) for i in range(2)]
        SMEs = [sb(st, "SME_%d" % i, [128, 8], F32) for i in range(2)]
        HT = view(YT[:, 0:8, :], [Buf()])
        HBs = [view(YT[:, 8 + 6 * i:10 + 6 * i, :].rearrange("p a b -> p (a b)"), [Buf()]) for i in range(2)]
        PTT = view(YT[:, 10:12, :], [Buf()])
        TG = [view(YT[:, 12:14, :].rearrange("p a b -> p (a b)").bitcast(F32), [Buf()])] * 2
        PB16s = [sb(st, "PB16a", [128, 256], BF16), sb(st, "PB16b", [128, 256], BF16)]
        p5views = [HT, HBs[0], HBs[1], PTT, TG[0]]
        PSB = view(PS_[:, 384:512].bitcast(BF16), [PS_.b])
        mmb = [PA, PB]
        mmi = [0]

        def nextmm():
            p = mmb[mmi[0] % 2]
            mmi[0] += 1
            return p

        S.dma("sp", IDF[:], dram["c_ident"], writes=[IDF.b])
        S.dma("sp", U[:], dram["c_tri"], writes=[U.b])
        S.dma("sp", AMASK[:], dram["c_amask"], writes=[AMASK.b])
        cp("dve", IDB[:], IDF[:], [IDF.b], [IDB.b])
        S.dma("sp", EB[0][:, 0, :], dram["c_mneg"], writes=[EB[0].b])
        cp("dve", MNEG[:], EB[0][:, 0, :].unsqueeze(1).to_broadcast([128, 4, 128]), [EB[0].b], [MNEG.b])
        memset(ONEF[:], 1.0, [ONEF.b])
        memset(ONE512[:], 1.0 / 512.0, [ONE512.b])
        memset(ONEB[:], 1.0, [ONEB.b])

        wplan = []
        wloaded = []
        wstate = ["free", "free", "free"]
        cast_idx = {k: i for i, k in enumerate(cast_q)}

        def w_pump():
            while wplan and len(wloaded) < 2 and "free" in wstate:
                key = wplan.pop(0)
                bi = wstate.index("free")
                wstate[bi] = "loaded"
                l_, name, g = key
                assert cast_idx[key] < cast_pos[0], ("cast not issued", key)
                S.dma("sp", WB[bi][:].rearrange("p a b -> p (a b)"), dram[name + "_s"][l_, g],
                      reads=[b_scr[key]], writes=[WB[bi].b])
                wloaded.append((key, bi))

        def w_acquire(key):
            cast_some(1)
            if not wloaded:
                w_pump()
            k2, bi = wloaded.pop(0)
            assert k2 == key, (k2, key)
            wstate[bi] = "held"
            w_pump()
            return WB[bi]

        def w_release(buf):
            bi = WB.index(buf)
            assert wstate[bi] == "held"
            wstate[bi] = "free"
            w_pump()

        def inherit(dst, srcs):
            for sb_ in srcs:
                for d in (sb_.r, sb_.w):
                    for k, v in d.items():
                        if dst.r.get(k, 0) < v:
                            dst.r[k] = v

        def interleave(A, B):
            out = []
            na, nb = len(A), len(B)
            ia = ib = 0
            while ia < na or ib < nb:
                if ib >= nb or (ia < na and ia * nb <= ib * na):
                    out.append(A[ia]); ia += 1
                else:
                    out.append(B[ib]); ib += 1
            return out

        b_hscr = [Buf() for _ in range(16)]
        b_out = [Buf() for _ in range(16)]

        for li, l in enumerate(self.layers):
            src = dram["x"] if li == 0 else dram["hscr"]
            dst = dram["out"] if li == len(self.layers) - 1 else dram["hscr"]
            b_src = None if li == 0 else b_hscr
            b_dst = b_out if li == len(self.layers) - 1 else b_hscr
            S.dma("sp", PRM[:], dram["prm"][l], writes=[PRM.b])
            S.dma("sp", BT[:].rearrange("p a h q -> p (a h q)"), dram["c_abias"], writes=[BT.b])
            preloaded = set()
            if li == 0:
                for t in range(2):
                    S.dma("sp", XR[t][:], src[t * 128:(t + 1) * 128, :], writes=[XR[t].b])
                    preloaded.add(t)
                S.wait_all("pool", [PRM.b, BT.b, IDF.b, U.b, AMASK.b, EB[0].b, XR[0].b, XR[1].b])
                cast_some(4)
            S.dma("sp", WP[:].rearrange("p a b -> p (a b)"), dram["w_proj_s"][l, 0],
                  reads=[b_scr[(l, "w_proj", 0)]], writes=[WP.b])
            ts(PRM[:, O_CW:O_CB + 12], PRM[:, O_CW:O_CB + 12], 0.5, None, ALU.mult, None, [PRM.b], [PRM.b])
            ts(PRM[:, O_DWW:O_DWW + 124], PRM[:, O_DWW:O_DWW + 124], 0.5, None, ALU.mult, None, [PRM.b], [PRM.b])
            act(DER[:, 0:16], PRM[:, O_ALOG:O_ALOG + 16], AF.Exp, [PRM.b], [DER.b])
            ts(DER[:, 0:16], DER[:, 0:16], -1.0, None, ALU.mult, None, [DER.b], [DER.b])
            ts(DER[:, 16:24], PRM[:, O_LNW:O_LNW + 8], 0.5, None, ALU.mult, None, [PRM.b], [DER.b])
            tt(BT[:], BT[:], AMASK[:].unsqueeze(2).to_broadcast([128, 2, 8, 128]), ALU.add, [BT.b, AMASK.b], [BT.b])
            tt(BT[:], BT[:], PRM[:, O_SINK:O_SINK + 8].unsqueeze(1).unsqueeze(3).to_broadcast([128, 2, 8, 128]),
               ALU.subtract, [BT.b, PRM.b], [BT.b])
            memset(HS[:], 0.0, [HS.b])
            memset(HSB[:], 0.0, [HSB.b])
            memset(TAIL[:], 0.0, [TAIL.b])
            memset(TAILB[:], 0.0, [TAILB.b])
            memset(GLU[:], 0.0, [GLU.b])
            memset(KT[:], 0.0, [KT.b])
            memset(VT[:], 0.0, [VT.b])

            def make_p1(tb):
                tok0 = tb * TB
                ths = []

                def tileA(t):
                    xr = XR[t % 2]
                    xn = XNs[t % 2]
                    r0 = tok0 + t * 128
                    if not (tb == 0 and t in preloaded):
                        S.dma("act", xr[:], src[r0:r0 + 128, :],
                              reads=([] if b_src is None else [b_src[tb * NT + t]]), writes=[xr.b])
                    SM1 = SM1s[t % 2]
                    memset(SM1[:, 0:1], 0.0, [SM1.b])
                    act(JUNK[:], xr[:], AF.Square, [xr.b, SM1.b], [JUNK.b, SM1.b], accum=SM1[:, 0:1])
                    self.rstd(SM1[:, 2:3], SM1[:, 0:1], 1.0 / D, SM1[:, 1:2], [SM1.b], [SM1.b])
                    stt(xn[:], xr[:], SM1[:, 2:3], PRM[:, O_PREW:O_PREW + 1024], ALU.mult, ALU.mult,
                        [xr.b, SM1.b, PRM.b], [xn.b])

                def tileB(t):
                    xn = XNs[t % 2]
                    trs([(PT[:, j * 128:(j + 1) * 128], xn[:, j * 128:(j + 1) * 128]) for j in range(8)],
                        IDB[:], [xn.b, IDB.b], [PT.b])
                    cp("act", XNT[:, :, t * 128:(t + 1) * 128], PT[:].rearrange("p (a b) -> p a b", a=8),
                       [PT.b], [XNT.b])
                order = [("A", 0), ("A", 1), ("B", 0), ("A", 2), ("B", 1), ("A", 3), ("B", 2), ("B", 3)]
                for kind, t in order:
                    ths.append(((lambda t=t: tileA(t)) if kind == "A" else (lambda t=t: tileB(t)), []))
                return ths

            def make_p2a(tb):
                ths = []
                hold = {}

                def xbcA(c):
                    if c % 4 == 0:
                        hold["w"] = w_acquire((l, "w_in", c // 4))
                    wb = hold["w"]
                    c4 = c % 4
                    pm = nextmm()
                    mm(pm[:], [(wb[:, kc, c4 * 128:(c4 + 1) * 128], XNT[:, kc, :]) for kc in range(8)],
                       [wb.b, XNT.b], [pm.b])
                    if c % 4 == 3:
                        w_release(wb)
                    pre, dgs = PREB[c % 2], DGS[c % 2]
                    tt(dgs[:], IDF[:].unsqueeze(1).to_broadcast([128, 4, 128]),
                       PRM[:, O_CW + c * 4:O_CW + c * 4 + 4].unsqueeze(2).to_broadcast([128, 4, 128]), ALU.mult,
                       [IDF.b, PRM.b], [dgs.b])
                    cp("dve", pre[:, 0:3], TAILB[:, c, 0:3], [TAILB.b], [pre.b])
                    cp("act", pre[:, 3:3 + TB], pm[:], [pm.b], [pre.b])
                    cp("dve", TAILB[:, c, 0:3], pre[:, TB:TB + 3], [pre.b], [TAILB.b])

                def xbcB(c):
                    pre, acc, th, dgs = PREB[c % 2], ACC[c % 2], TH[c % 2], DGS[c % 2]
                    pc = nextmm()
                    mm(pc[:], [(dgs[:, k, :], pre[:, k:k + TB]) for k in range(4)], [dgs.b, pre.b], [pc.b])
                    act(acc[:], pc[:], AF.Identity, [pc.b, PRM.b], [acc.b], bias=PRM[:, O_CB + c:O_CB + c + 1])
                    act(th[:], acc[:], AF.Tanh, [acc.b], [th.b])
                    stt(XBC[:, c, :], th[:], 1.0, acc[:], ALU.add, ALU.mult, [th.b, acc.b], [XBC.bs[c]])

                def z_tile(half, t):
                    if t == 0:
                        hold["w"] = w_acquire((l, "w_in", 3 + half))
                    wb = hold["w"]
                    pm = nextmm()
                    mm(pm[:], [(XNT[:, kc, t * 128:(t + 1) * 128], wb[:, kc, :]) for kc in range(8)],
                       [wb.b, XNT.b], [pm.b])
                    if t == NT - 1:
                        w_release(wb)
                    th = TH[t % 2]
                    act(th[:], pm[:], AF.Tanh, [pm.b], [th.b], scale=0.5)
                    stt(ZS[:, t, half * 512:(half + 1) * 512], th[:], 1.0, pm[:], ALU.add, ALU.mult,
                        [th.b, pm.b], [ZS.bs[t]])

                def kv():
                    wb = w_acquire((l, "w_in", 5))
                    cp("dve", KT[:, 0:128], KT[:, TB:TB + 128], [KT.b], [KT.b])
                    cp("dve", VT[:, 0, :], VT[:, 4, :], [VT.b], [VT.b])
                    pm = nextmm()
                    mm(pm[:], [(wb[:, kc, 0:128], XNT[:, kc, :]) for kc in range(8)], [wb.b, XNT.b], [pm.b])
                    cp("act", KT[:, 128:128 + TB], pm[:], [pm.b], [KT.b])
                    for t in range(NT):
                        pm = nextmm()
                        mm(pm[:, 0:144], [(XNT[:, kc, t * 128:(t + 1) * 128], wb[:, kc, 128:272]) for kc in range(8)],
                           [wb.b, XNT.b], [pm.b])
                        cp("dve", DTR[:, t, :], pm[:, 0:16], [pm.b], [DTR.bs[t]])
                        cp("act", VT[:, 1 + t, :], pm[:, 16:144], [pm.b], [VT.b])
                    w_release(wb)
                for c in range(13):
                    if c < 12:
                        ths.append((lambda c=c: xbcA(c), [(l, "w_in", c // 4)] if c % 4 == 0 else []))
                    if c >= 1:
                        ths.append((lambda c=c: xbcB(c - 1), []))
                for half in range(2):
                    for t in range(NT):
                        ths.append((lambda half=half, t=t: z_tile(half, t), [(l, "w_in", 3 + half)] if t == 0 else []))
                ths.append((kv, [(l, "w_in", 5)]))
                return ths

            def make_p5b(tb):
                tok0 = tb * TB
                ths = []
                hold = {}

                def post(t):
                    r0 = tok0 + t * 128
                    xr = XR[t % 2]
                    S.dma("act", xr[:], src[r0:r0 + 128, :],
                          reads=([] if b_src is None else [b_src[tb * NT + t]]), writes=[xr.b])
                    pr = PR[t % 2]
                    S.dma("act", pr[:], dram["p"][l, r0:r0 + 128, :], writes=[pr.b])
                    SM5 = SM5s[t % 2]
                    memset(SM5[:, 0:1], 0.0, [SM5.b])
                    act(JUNK[:], O[:, t, :], AF.Square, [O.bs[t], SM5.b], [JUNK.b, SM5.b], accum=SM5[:, 0:1])
                    self.rstd(SM5[:, 2:3], SM5[:, 0:1], 1.0 / D, SM5[:, 1:2], [SM5.b], [SM5.b])
                    stt(O[:, t, :], O[:, t, :], SM5[:, 2:3], PRM[:, O_POSTW:O_POSTW + 1024], ALU.mult, ALU.mult,
                        [O.bs[t], SM5.b, PRM.b], [O.bs[t]])
                    tt(O[:, t, :], O[:, t, :], xr[:], ALU.add, [O.bs[t], xr.b], [O.bs[t]])
                    HB, PB16 = HBs[t % 2], PB16s[t % 2]
                    cp("act", HB[:], O[:, t, :], [O.bs[t]], [HB.b])
                    cp("dve", PB16[:], pr[:], [pr.b], [PB16.b])

                def postB(t):
                    HB, PB16 = HBs[t % 2], PB16s[t % 2]
                    trs([(PT[:, j * 128:(j + 1) * 128], HB[:, j * 128:(j + 1) * 128]) for j in range(8)], IDB[:],
                        [HB.b, IDB.b], [PT.b])
                    cp("act", HT[:, :, t * 128:(t + 1) * 128], PT[:].rearrange("p (a b) -> p a b", a=8),
                       [PT.b], [HT.b])
                    trs([(PSB[:, j * 128:(j + 1) * 128], PB16[:, j * 128:(j + 1) * 128]) for j in range(2)], IDB[:],
                        [PB16.b, IDB.b], [PSB.b])
                    cp("act", PTT[:, :, t * 128:(t + 1) * 128], PSB[:].rearrange("p (a b) -> p a b", a=2),
                       [PSB.b], [PTT.b])

                def ple(ch, t):
                    if t == 0:
                        hold["w"] = w_acquire((l, "w_gate", ch))
                    wg = hold["w"]
                    tcols = slice(t * 128, (t + 1) * 128)
                    pg = nextmm()
                    mm(pg[:], [(HT[:, kc, tcols], wg[:, kc, :]) for kc in range(8)], [HT.b, wg.b], [pg.b])
                    if t == NT - 1:
                        w_release(wg)
                    pp = nextmm()
                    mm(pp[:], [(PTT[:, kc, tcols], WP[:, kc, ch * 512:(ch + 1) * 512]) for kc in range(2)],
                       [PTT.b, WP.b], [pp.b])
                    tg = TG[t % 2]
                    act(tg[:], pg[:], AF.Tanh, [pg.b], [tg.b], scale=0.5)
                    stt(tg[:], tg[:], 1.0, pp[:], ALU.add, ALU.mult, [tg.b, pp.b], [tg.b])
                    stt(O[:, t, ch * 512:(ch + 1) * 512], tg[:], 0.5, O[:, t, ch * 512:(ch + 1) * 512],
                        ALU.mult, ALU.add, [tg.b, O.bs[t]], [O.bs[t]])

                def store(t):
                    r0 = tok0 + t * 128
                    S.dma("pool", dst[r0:r0 + 128, :], O[:, t, :], reads=[O.bs[t]], writes=[b_dst[tb * NT + t]])
                ths.append((lambda: post(0), []))
                ths.append((lambda: post(1), []))
                ths.append((lambda: postB(0), []))
                ths.append((lambda: post(2), []))
                ths.append((lambda: postB(1), []))
                ths.append((lambda: ple(0, 0), [(l, "w_gate", 0)]))
                ths.append((lambda: post(3), []))
                ths.append((lambda: postB(2), []))
                ths.append((lambda: ple(0, 1), []))
                ths.append((lambda: postB(3), []))
                ths.append((lambda: ple(0, 2), []))
                ths.append((lambda: ple(0, 3), []))
                for t in range(NT):
                    ths.append((lambda t=t: ple(1, t), [(l, "w_gate", 1)] if t == 0 else []))
                for t in range(NT):
                    ths.append((lambda t=t: store(t), []))
                return ths

            def run(ths):
                for fn, _ in ths:
                    fn()

            def keys_of(ths):
                return [k for _, ws in ths for k in ws]

            k_ssdx = [(l, "w_in", g) for g in range(6, 11)]
            k_p5a = [(l, "w_out", g) for g in range(4)]
            first = make_p1(0) + make_p2a(0)
            wplan.extend(keys_of(first) + k_ssdx + k_p5a)
            w_pump()
            run(first)

            for tb in range(NBLK):
                tok0 = tb * TB
                if True:
                    row = lambda i: SD2[:, i, :]
                    r3 = lambda i: SD2[:, i, :].rearrange("p (t h) -> p t h", t=4)
                    sbf = SD2.bs
                    tt(r3(0), DTR[:], PRM[:, O_DTB:O_DTB + 16].unsqueeze(1).to_broadcast([128, 4, 16]), ALU.add,
                       DTR.bs + [PRM.b], [sbf[0]])
                    act(row(1), row(0), AF.Abs, [sbf[0]], [sbf[1]])
                    act(row(1), row(1), AF.Exp, [sbf[1]], [sbf[1]], scale=-1.0)
                    act(row(1), row(1), AF.Ln, [sbf[1]], [sbf[1]], bias=1.0)
                    stt(row(2), row(0), 0.0, row(1), ALU.max, ALU.add, [sbf[0], sbf[1]], [sbf[2]])
                    tt(r3(3), r3(2), DER[:, 0:16].unsqueeze(1).to_broadcast([128, 4, 16]), ALU.mult,
                       [sbf[2], DER.b], [sbf[3]])
                    mms([(PS_[:, 256:320], U[:], row(3), True, True),
                         (PS_[:, 320:384], ONEF[:], row(3), True, True)], [U.b, ONEF.b, sbf[3]], [PS_.b])
                    ts(row(5), PS_[:, 256:320], -1.0, None, ALU.mult, None, [PS_.b], [sbf[5]])
                    tt(row(6), PS_[:, 320:384], row(5), ALU.add, [PS_.b, sbf[5]], [sbf[6]])
                    act(row(6), row(6), AF.Exp, [sbf[6]], [sbf[6]])
                    act(row(7), PS_[:, 256:320], AF.Exp, [PS_.b], [sbf[7]])
                    act(row(8), PS_[:, 320:384], AF.Exp, [PS_.b], [sbf[8]])

                    def stageB1(t):
                        cols = slice(t * 128, (t + 1) * 128)
                        mms([(PS_[:, g * 128:(g + 1) * 128], XBC[:, 8 + g, cols], XBC[:, 10 + g, cols], True, True)
                             for g in range(2)], [XBC.bs[8 + i] for i in range(4)], [PS_.b])
                        cp("dve", CBM[t % 2][:].rearrange("p a b -> p (a b)"), PS_[:, 0:256], [PS_.b], [CBM[t % 2].b])

                    def stageB2(t):
                        cols = slice(t * 128, (t + 1) * 128)
                        xdt = XDT[t % 2]
                        trs([(PSB[:, g * 128:(g + 1) * 128], XBC[:, 8 + g, cols]) for g in range(2)], IDB[:],
                            [XBC.bs[8], XBC.bs[9], IDB.b], [PSB.b])
                        trs([(PT[:, j * 128:(j + 1) * 128], XBC[:, j, cols]) for j in range(8)], IDB[:],
                            [XBC.bs[j] for j in range(8)] + [IDB.b], [PT.b])
                        cp("dve", BMT[:].rearrange("p a b -> p (a b)"), PSB[:], [PSB.b], [BMT.b])
                        pt3 = PT[:].rearrange("p (h d) -> p h d", h=16)
                        tt(xdt[:].rearrange("p (h d) -> p h d", h=16), pt3,
                           r3(2)[:, t, :].unsqueeze(2).to_broadcast([128, 16, 64]), ALU.mult, [PT.b, sbf[2]], [xdt.b])
                        tt(XD[:].rearrange("p (h d) -> p h d", h=16), pt3,
                           PRM[:, O_D16:O_D16 + 16].unsqueeze(2).to_broadcast([128, 16, 64]), ALU.mult,
                           [PT.b, PRM.b], [XD.b])
                        tt(XDS[:].rearrange("p (h d) -> p h d", h=16), xdt[:].rearrange("p (h d) -> p h d", h=16),
                           r3(6)[:, t, :].unsqueeze(2).to_broadcast([128, 16, 64]), ALU.mult, [xdt.b, sbf[6]], [XDS.b])

                    def kdec(k):
                        return k // 4, (k // 2) % 2, k % 2

                    def stageC1a(k):
                        t, g, hf = kdec(k)
                        h0 = g * 8 + hf * 4
                        rk = RB[k % 2]
                        tt(rk[:], r3(3)[:, t, h0:h0 + 4].unsqueeze(2).to_broadcast([128, 4, 128]),
                           U[:].unsqueeze(1).to_broadcast([128, 4, 128]), ALU.mult, [sbf[3], U.b], [rk.b], eng="pool")

                    def stageC1b(k):
                        rk, pxk = RB[k % 2], (PX0, PX1)[k % 2]
                        mms([(pxk[:], ONEF[:], rk[:], True, False), (pxk[:], IDB[:], MNEG[:], False, True)],
                            [ONEF.b, rk.b, IDB.b, MNEG.b], [pxk.b])

                    def stageC2(k):
                        t, g, hf = kdec(k)
                        h0 = g * 8 + hf * 4
                        pxk, ek, mtk = (PX0, PX1)[k % 2], EB[k % 2], MT4[k % 4]
                        for i in range(4):
                            act(ek[:, i, :], pxk[:, i * 128:(i + 1) * 128], AF.Exp, [pxk.b, sbf[5]], [ek.b],
                                bias=r3(5)[:, t, h0 + i:h0 + i + 1])
                        tt(mtk[:], ek[:], CBM[t % 2][:, g, :].unsqueeze(1).to_broadcast([128, 4, 128]), ALU.mult,
                           [ek.b, CBM[t % 2].b], [mtk.b])

                    def advance(k):
                        if k + 2 < 4 * NT:
                            stageC1a(k + 2)
                        if k + 1 < 4 * NT:
                            stageC1b(k + 1)
                        stageC2(k)

                    def stageD(t, g):
                        cols = slice(t * 128, (t + 1) * 128)
                        gs = slice(g * 512, (g + 1) * 512)
                        py, po, t1 = PY, PO, T1
                        xdt = XDT[t % 2]
                        k0 = 4 * t + 2 * g
                        items = [(py[:], IDB[:], XD[:, gs], True, False)]
                        for hh in range(8):
                            h = g * 8 + hh
                            items.append((py[:, hh * 64:(hh + 1) * 64], MT4[(k0 + hh // 4) % 4][:, hh % 4, :],
                                          xdt[:, h * 64:(h + 1) * 64], False, True))
                        mms(items, [IDB.b, XD.b, MT4[k0 % 4].b, MT4[(k0 + 1) % 4].b, xdt.b], [py.b])
                        mm(po[:], [(XBC[:, 10 + g, cols], HSB[:, gs])], [XBC.bs[10 + g], HSB.b], [po.b])
                        tt(t1[:].rearrange("p (h d) -> p h d", h=8), po[:].rearrange("p (h d) -> p h d", h=8),
                           r3(7)[:, t, g * 8:(g + 1) * 8].unsqueeze(2).to_broadcast([128, 8, 64]), ALU.mult,
                           [po.b, sbf[7]], [t1.b])
                        tt(Y[:, gs], t1[:], py[:], ALU.add, [t1.b, py.b], [Y.b])

                    def stageD2(t, g):
                        gs = slice(g * 512, (g + 1) * 512)
                        po = PO
                        mm(po[:], [(BMT[:, g, :], XDS[:, gs])], [BMT.b, XDS.b], [po.b])
                        tt(HS[:, gs].rearrange("p (h d) -> p h d", h=8), HS[:, gs].rearrange("p (h d) -> p h d", h=8),
                           r3(8)[:, t, g * 8:(g + 1) * 8].unsqueeze(2).to_broadcast([128, 8, 64]), ALU.mult,
                           [HS.b, sbf[8]], [HS.b])
                        tt(HS[:, gs], HS[:, gs], po[:], ALU.add, [HS.b, po.b], [HS.b])
                        cp("act", HSB[:, gs], HS[:, gs], [HS.b], [HSB.b])

                    def stageE(t):
                        cols = slice(t * 128, (t + 1) * 128)
                        tt(Y[:], Y[:], ZS[:, t, :], ALU.mult, [Y.b, ZS.bs[t]], [Y.b])
                        SME = SMEs[t % 2]
                        memset(SME[:, 0:2], 0.0, [SME.b])
                        for g in range(2):
                            act(JUNK[:, 0:512], Y[:, g * 512:(g + 1) * 512], AF.Square, [Y.b, SME.b], [JUNK.b, SME.b],
                                accum=SME[:, g:g + 1])
                        ts(SME[:, 2:4], SME[:, 0:2], 1.0 / 512.0, 4.0 * EPS, ALU.mult, ALU.add, [SME.b], [SME.b])
                        act(SME[:, 2:4], SME[:, 2:4], AF.Ln, [SME.b], [SME.b])
                        act(SME[:, 4:6], SME[:, 2:4], AF.Exp, [SME.b], [SME.b], scale=-0.5)
                        for g in range(2):
                            gs = slice(g * 512, (g + 1) * 512)
                            stt(YN[:, gs], Y[:, gs], SME[:, 4 + g:5 + g], PRM[:, O_SSDNW + g * 512:O_SSDNW + (g + 1) * 512],
                                ALU.mult, ALU.mult, [Y.b, SME.b, PRM.b], [YN.b])

                    def stageE2(t):
                        cols = slice(t * 128, (t + 1) * 128)
                        trs([(PT[:, j * 128:(j + 1) * 128], YN[:, j * 128:(j + 1) * 128]) for j in range(8)], IDB[:],
                            [YN.b, IDB.b], [PT.b])
                        cp("dve", YT[:, 0:8, cols], PT[:].rearrange("p (a b) -> p a b", a=8), [PT.b], [YT.b])

                    extra = []
                    wbh = {}

                    def conf_pair(j):
                        if j % 2 == 0:
                            wbh["c"] = w_acquire((l, "w_in", 6 + j // 2))
                        wb = wbh["c"]
                        jj = j % 2
                        pa = nextmm()
                        mm(pa[:], [(wb[:, kc, (2 * jj) * 128:(2 * jj + 1) * 128], XNT[:, kc, :]) for kc in range(8)],
                           [wb.b, XNT.b], [pa.b])
                        pb = nextmm()
                        mm(pb[:], [(wb[:, kc, (2 * jj + 1) * 128:(2 * jj + 2) * 128], XNT[:, kc, :]) for kc in range(8)],
                           [wb.b, XNT.b], [pb.b])
                        if j % 2 == 1:
                            w_release(wb)
                        th = TH[j % 2]
                        act(th[:], pb[:], AF.Tanh, [pb.b], [th.b], scale=0.5)
                        stt(GLU[:, j, 30:30 + TB], th[:], 1.0, pa[:], ALU.add, ALU.mult, [th.b, pa.b], [GLU.b])

                    def conf_gate(j):
                        if j == 0:
                            wbh["c"] = w_acquire((l, "w_in", 8))
                        wb = wbh["c"]
                        pm = nextmm()
                        mm(pm[:], [(wb[:, kc, j * 128:(j + 1) * 128], XNT[:, kc, :]) for kc in range(8)],
                           [wb.b, XNT.b], [pm.b])
                        if j == 3:
                            w_release(wb)
                        th = TH[j % 2]
                        act(th[:], pm[:], AF.Tanh, [pm.b], [th.b], scale=0.5)
                        stt(CG[:, j, :], th[:], 1.0, pm[:], ALU.add, ALU.mult, [th.b, pm.b], [CG.bs[j]])

                    def conf_convA(j):
                        tt(DG[:], IDF[:].unsqueeze(1).to_broadcast([128, 31, 128]),
                           PRM[:, O_DWW + j * 31:O_DWW + (j + 1) * 31].unsqueeze(2).to_broadcast([128, 31, 128]),
                           ALU.mult, [IDF.b, PRM.b], DG.bs)

                    def conf_convB(j):
                        pm = nextmm()
                        mm(pm[:], [(DG[:, k, :], GLU[:, j, k:k + TB]) for k in range(31)], DG.bs + [GLU.b], [pm.b])
                        act(HH[:, j, :], pm[:], AF.Identity, [pm.b, PRM.b], [HH.bs[j]],
                            bias=PRM[:, O_DWB + j:O_DWB + j + 1])

                    def conf_stats():
                        PCA = nextmm()
                        PCB = nextmm()
                        mm(PCA[:], [(ONE512[:], HH[:, j, :]) for j in range(4)], [ONE512.b] + HH.bs, [PCA.b])
                        for j in range(4):
                            hq = HQ[j % 2]
                            act(hq[:], HH[:, j, :], AF.Square, [HH.bs[j]], [hq.b])
                            mm(PCB[:], [(ONE512[:], hq[:])], [ONE512.b, hq.b], [PCB.b], start=(j == 0), stop=(j == 3))
                        cp("act", MU[:], PCA[:], [PCA.b], [MU.b])
                        tt(D1[:], MU[:], MU[:], ALU.mult, [MU.b], [D1.b])
                        tt(RS[:], PCB[:], D1[:], ALU.subtract, [PCB.b, D1.b], [RS.b])
                        ts(RS[:], RS[:], EPS, None, ALU.add, None, [RS.b], [RS.b])
                        act(RS[:], RS[:], AF.Ln, [RS.b], [RS.b])
                        act(RS[:], RS[:], AF.Exp, [RS.b], [RS.b], scale=-0.5)

                    def conf_ln(j):
                        tt(D1[:], HH[:, j, :], MU[:], ALU.subtract, [HH.bs[j], MU.b], [D1.b])
                        tt(D1[:], D1[:], RS[:], ALU.mult, [D1.b, RS.b], [D1.b])
                        th = TH[j % 2]
                        act(th[:], D1[:], AF.Tanh, [D1.b, DER.b], [th.b], scale=DER[:, 16 + j:17 + j],
                            bias=DER[:, 20 + j:21 + j])
                        ts(U1[:], D1[:], PRM[:, O_LNW + j:O_LNW + j + 1], PRM[:, O_LNB + j:O_LNB + j + 1],
                           ALU.mult, ALU.add, [D1.b, PRM.b], [U1.b])
                        stt(U1[:], th[:], 1.0, U1[:], ALU.add, ALU.mult, [th.b, U1.b], [U1.b])
                        stt(YT[:, 8 + j, :], U1[:], 0.25, CG[:, j, :], ALU.mult, ALU.mult, [U1.b, CG.bs[j]], [YT.b])

                    def attn_q(g):
                        if g == 0:
                            wbh["a"] = w_acquire((l, "w_in", 9))
                        wb = wbh["a"]
                        pm = nextmm()
                        mm(pm[:], [(wb[:, kc, g * 128:(g + 1) * 128], XNT[:, kc, :]) for kc in range(8)],
                           [wb.b, XNT.b], [pm.b])
                        if g == 3:
                            w_release(wb)
                        cp("act", QT[:, g, :], pm[:], [pm.b], [QT.bs[g]])

                    def attn_g(c):
                        if c == 0:
                            wbh["a"] = w_acquire((l, "w_in", 10))
                        wb = wbh["a"]
                        pm = nextmm()
                        mm(pm[:], [(wb[:, kc, c * 128:(c + 1) * 128], XNT[:, kc, :]) for kc in range(8)],
                           [wb.b, XNT.b], [pm.b])
                        if c == 3:
                            w_release(wb)
                        th = TH[c % 2]
                        act(th[:], pm[:], AF.Tanh, [pm.b], [th.b], scale=0.5)
                        stt(AG[:, c, :], th[:], 1.0, pm[:], ALU.add, ALU.mult, [th.b, pm.b], [AG.bs[c]])

                    def attn_A(n, kh):
                        u = n * 2 + kh
                        nbk = tb * NT + n
                        qcols = slice(n * 128, (n + 1) * 128)
                        kbs = [1] if nbk == 0 else [0, 1]
                        prt = slice(kh * 64, (kh + 1) * 64)
                        for kb in kbs:
                            et = ET[(u % 2) * 2 + kb]
                            pm = nextmm()
                            mm(pm[:], [(KT[prt, (n + kb) * 128:(n + kb + 1) * 128], QT[prt, :, qcols])],
                               [KT.b] + QT.bs, [pm.b])
                            stt(SC[kb][:], pm[:], 0.125,
                                BT[:, kb, kh * 4:(kh + 1) * 4, :].rearrange("p h q -> p (h q)"),
                                ALU.mult, ALU.add, [pm.b, BT.b], [SC[kb].b])
                            act(et[:], SC[kb][:], AF.Exp, [SC[kb].b], [et.b])

                    def attn_B(n, kh):
                        u = n * 2 + kh
                        nbk = tb * NT + n
                        qcols = slice(n * 128, (n + 1) * 128)
                        kbs = [1] if nbk == 0 else [0, 1]
                        prt = slice(kh * 64, (kh + 1) * 64)
                        ets = {kb: ET[(u % 2) * 2 + kb] for kb in kbs}
                        pod = nextmm()
                        items = []
                        for jj in range(2):
                            for i, kb in enumerate(kbs):
                                items.append((pod[jj * 64:(jj + 1) * 64, 0:256], VT[:, n + kb, prt],
                                              ets[kb][:, jj * 256:(jj + 1) * 256], i == 0, i == len(kbs) - 1))
                        for jj in range(2):
                            for i, kb in enumerate(kbs):
                                items.append((pod[jj * 64:(jj + 1) * 64, 256:512], ONEB[:, 0:64],
                                              ets[kb][:, jj * 256:(jj + 1) * 256], i == 0, False))
                            items.append((pod[jj * 64:(jj + 1) * 64, 256:512], ONEB[0:1, 0:64], ONEB[0:1, 0:256],
                                          False, True))
                        mms(items, [VT.b, ONEB.b] + [ets[kb].b for kb in kbs], [pod.b])
                        self.S.op("dve", lambda e, o=RD[:], i=pod[:, 256:512]: e.reciprocal(out=o, in_=i),
                                  [pod.b], [RD.b])
                        tt(OT[:], pod[:, 0:256], RD[:], ALU.mult, [pod.b, RD.b], [OT.b])
                        stt(YT[:, 12 + kh * 2:14 + kh * 2, qcols], OT[:].rearrange("p (a b) -> p a b", a=2), 0.5,
                            AG[:, kh * 2:kh * 2 + 2, qcols], ALU.mult, ALU.mult,
                            [OT.b, AG.bs[kh * 2], AG.bs[kh * 2 + 1]], [YT.b])

                    extra.append(lambda: cp("dve", GLU[:, :, 0:30], GLU[:, :, TB:TB + 30], [GLU.b], [GLU.b]))
                    for j in range(4):
                        extra.append(lambda j=j: conf_pair(j))
                    for j in range(4):
                        extra.append(lambda j=j: conf_convA(j))
                        extra.append(lambda j=j: conf_gate(j))
                        extra.append(lambda j=j: conf_convB(j))
                    extra.append(conf_stats)
                    for j in range(4):
                        extra.append(lambda j=j: attn_q(j))
                        extra.append(lambda j=j: conf_ln(j))
                    for c in range(4):
                        extra.append(lambda c=c: attn_g(c))
                    units = [(n, kh) for n in range(NT) for kh in range(2)]
                    for i in range(len(units) + 1):
                        if i < len(units):
                            extra.append(lambda nk=units[i]: attn_A(*nk))
                        if i >= 1:
                            extra.append(lambda nk=units[i - 1]: attn_B(*nk))
                    nslots = 2 * NT
                    per = (len(extra) + nslots - 1) // nslots
                    epos = [0]

                    def run_extra(n):
                        for _ in range(n):
                            if epos[0] < len(extra):
                                extra[epos[0]]()
                                epos[0] += 1

                    stageB1(0)
                    stageB2(0)
                    stageC1a(0)
                    stageC1a(1)
                    stageC1b(0)
                    for k in range(4):
                        advance(k)
                    stageB1(1)
                    for t in range(NT):
                        for g in range(2):
                            stageD(t, g)
                            if t + 1 < NT:
                                advance(4 * (t + 1) + 2 * g)
                                advance(4 * (t + 1) + 2 * g + 1)
                            run_extra(per)
                            stageD2(t, g)
                        if t >= 1:
                            stageE2(t - 1)
                        if t + 1 < NT:
                            stageB2(t + 1)
                        stageE(t)
                        if t + 2 < NT:
                            stageB1(t + 2)
                    run_extra(2)
                    stageE2(NT - 1)
                    run_extra(len(extra))

                p5a = []
                ohold = {}

                def outproj(ch, t):
                    if t == 0:
                        ohold["w0"] = w_acquire((l, "w_out", ch * 2))
                        ohold["w1"] = w_acquire((l, "w_out", ch * 2 + 1))
                    w0, w1 = ohold["w0"], ohold["w1"]
                    pm = nextmm()
                    pairs = [(YT[:, kc, t * 128:(t + 1) * 128], (w0 if kc < 8 else w1)[:, kc % 8, :])
                             for kc in range(16)]
                    mm(pm[:], pairs, [YT.b, w0.b, w1.b], [pm.b])
                    cp("act", O[:, t, ch * 512:(ch + 1) * 512], pm[:], [pm.b], [O.bs[t]])
                    if t == NT - 1:
                        w_release(w0)
                        w_release(w1)
                for ch in range(2):
                    for t in range(NT):
                        p5a.append((lambda ch=ch, t=t: outproj(ch, t), []))
                if tb + 1 < NBLK:
                    run(interleave(p5a, make_p1(tb + 1)))
                else:
                    run(p5a)
                for v in p5views:
                    v.b.w, v.b.r = {}, {}
                    inherit(v.b, [YT.b])
                A = make_p5b(tb)
                if tb + 1 < NBLK:
                    Bn = make_p2a(tb + 1)
                    merged = interleave(A, Bn)
                    wplan.extend(keys_of(merged) + k_ssdx + k_p5a)
                else:
                    merged = A
                    wplan.extend(keys_of(merged))
                w_pump()
                run(merged)
                inherit(YT.b, [v.b for v in p5views])
        S.wait_all("sp", b_out)
        S.wait_all("pool", b_out)


def _col_perm():
    z0, xbc0, dt0, ci0, cg0, q0, k0, v0, ag0 = 0, 1024, 2560, 2576, 3600, 4112, 4624, 4752, 4880
    groups = []
    for g in range(3):
        groups.append(list(range(xbc0 + g * 512, xbc0 + (g + 1) * 512)))
    groups.append(list(range(z0, z0 + 512)))
    groups.append(list(range(z0 + 512, z0 + 1024)))
    groups.append(list(range(k0, k0 + 128)) + list(range(dt0, dt0 + 16)) + list(range(v0, v0 + 128)) + [-1] * 240)
    for g in range(2):
        cols = []
        for j in (2 * g, 2 * g + 1):
            cols += list(range(ci0 + j * 128, ci0 + (j + 1) * 128))
            cols += list(range(ci0 + 512 + j * 128, ci0 + 512 + (j + 1) * 128))
        groups.append(cols)
    groups.append(list(range(cg0, cg0 + 512)))
    cols = []
    for g in range(4):
        cols += list(range(q0 + g * 64, q0 + (g + 1) * 64))
        cols += list(range(q0 + (4 + g) * 64, q0 + (5 + g) * 64))
    groups.append(cols)
    cols = []
    for kh in range(2):
        for gl in range(2):
            h0, h1 = kh * 4 + gl, kh * 4 + 2 + gl
            cols += list(range(ag0 + h0 * 64, ag0 + (h0 + 1) * 64))
            cols += list(range(ag0 + h1 * 64, ag0 + (h1 + 1) * 64))
    groups.append(cols)
    assert len(groups) == 11 and all(len(g) == 512 for g in groups)
    return np.array(groups, dtype=np.int64)


def _row_perm_out():
    rows = list(range(0, 1536))
    a0 = 1536
    for kh in range(2):
        for gl in range(2):
            h0, h1 = kh * 4 + gl, kh * 4 + 2 + gl
            rows += list(range(a0 + h0 * 64, a0 + (h0 + 1) * 64))
            rows += list(range(a0 + h1 * 64, a0 + (h1 + 1) * 64))
    return np.array(rows, dtype=np.int64)


def _t5_bucket(d):
    d = np.maximum(d, 0)
    dm = np.maximum(d, 1).astype(np.float32)
    large = 16 + (np.log(dm / np.float32(16)) / np.float32(math.log(128 / 16)) * np.float32(16)).astype(np.int32)
    large = np.minimum(large, 31)
    return np.where(d < 16, d, large)


def _prep(inputs):
    f32 = np.float32
    w_in = np.asarray(inputs["w_in"], f32)
    perm = _col_perm()
    w_in_pad = np.concatenate([w_in, np.zeros((DEPTH, D, 1), f32)], axis=2)
    wi = w_in_pad[:, :, perm.reshape(-1)].reshape(DEPTH, 8, 128, 11, 512)
    wi = np.ascontiguousarray(wi.transpose(0, 3, 2, 1, 4)).reshape(DEPTH, 11, 128, 4096)
    w_out = np.asarray(inputs["w_out"], f32)[:, _row_perm_out(), :]
    wo = w_out.reshape(DEPTH, 2, 8, 128, 2, 512).transpose(0, 4, 1, 3, 2, 5)
    wo = np.ascontiguousarray(wo).reshape(DEPTH, 4, 128, 4096)
    wg = np.asarray(inputs["ple_gate"], f32).reshape(DEPTH, 8, 128, 2, 512).transpose(0, 3, 2, 1, 4)
    wg = np.ascontiguousarray(wg).reshape(DEPTH, 2, 128, 4096)
    wp = np.asarray(inputs["ple_proj"], f32).reshape(DEPTH, 2, 128, 1024).transpose(0, 2, 1, 3)
    wp = np.ascontiguousarray(wp).reshape(DEPTH, 1, 128, 2048)
    prm = np.zeros((DEPTH, 128, NP), f32)
    bc = lambda v: np.broadcast_to(np.asarray(v, f32)[None, :], (128, len(v)))
    for l in range(DEPTH):
        prm[l, :, O_PREW:O_PREW + 1024] = bc(inputs["pre_norm_w"][l])
        prm[l, :, O_SSDNW:O_SSDNW + 1024] = bc(inputs["ssd_norm_w"][l])
        prm[l, :, O_POSTW:O_POSTW + 1024] = bc(inputs["post_norm_w"][l])
        prm[l, :, O_D16:O_D16 + 16] = bc(inputs["ssd_d"][l])
        prm[l, :, O_DTB:O_DTB + 16] = bc(inputs["ssd_dt_bias"][l])
        prm[l, :, O_ALOG:O_ALOG + 16] = bc(inputs["ssd_a_log"][l])
        cw = np.asarray(inputs["ssd_conv_w"][l], f32)
        prm[l, :, O_CW:O_CW + 48] = cw.reshape(4, 12, 128).transpose(2, 1, 0).reshape(128, 48)
        prm[l, :, O_CB:O_CB + 12] = np.asarray(inputs["ssd_conv_b"][l], f32).reshape(12, 128).T
        dw = np.asarray(inputs["conf_dw_w"][l], f32)
        prm[l, :, O_DWW:O_DWW + 124] = dw.reshape(31, 4, 128).transpose(2, 1, 0).reshape(128, 124)
        prm[l, :, O_DWB:O_DWB + 4] = np.asarray(inputs["conf_dw_b"][l], f32).reshape(4, 128).T
        prm[l, :, O_LNW:O_LNW + 4] = np.asarray(inputs["conf_ln_w"][l], f32).reshape(4, 128).T
        prm[l, :, O_LNB:O_LNB + 4] = np.asarray(inputs["conf_ln_b"][l], f32).reshape(4, 128).T
        prm[l, :, O_SINK:O_SINK + 8] = bc(inputs["attn_sinks"][l])
    s = np.arange(128)[:, None, None]
    kb = np.arange(2)[None, :, None]
    q = np.arange(128)[None, None, :]
    dist = q + 128 - (kb * 128 + s)
    valid = (dist >= 0) & (dist < 128)
    bucket = _t5_bucket(np.clip(dist, 0, 127))
    rel = np.asarray(inputs["rel_bias"], f32)
    abias = np.ascontiguousarray(rel[bucket].transpose(0, 1, 3, 2)).reshape(128, 2 * 8 * 128)
    amask = np.where(valid, 0.0, NEG).astype(f32).reshape(128, 256)
    ident = np.eye(128, dtype=f32)
    tri = np.triu(np.ones((128, 128), f32))
    common = {"w_in": wi, "w_out": wo, "w_gate": wg, "w_proj": wp, "prm": prm, "c_abias": abias,
              "c_amask": amask, "c_ident": ident, "c_tri": tri,
              "c_mneg": np.where(np.arange(128)[:, None] <= np.arange(128)[None, :], 0.0, NEG).astype(f32)}
    return common


def _build(layers, debug=False):
    nc = bass.Bass("TRN2", target_bir_lowering=False)
    dram = {}
    dt = lambda name, shape, dtype, kind: nc.dram_tensor(name, shape, dtype, kind=kind).ap()
    dram["x"] = dt("x", [L_SEQ, D], F32, "ExternalInput")
    dram["p"] = dt("p", [DEPTH, L_SEQ, 256], F32, "ExternalInput")
    for name, ng, w in (("w_in", 11, 4096), ("w_out", 4, 4096), ("w_gate", 2, 4096), ("w_proj", 1, 2048)):
        dram[name] = dt(name, [DEPTH, ng, 128, w], F32, "ExternalInput")
        dram[name + "_s"] = dt(name + "_s", [DEPTH, ng, 128, w], BF16, "Internal")
    dram["prm"] = dt("prm", [DEPTH, 128, NP], F32, "ExternalInput")
    dram["c_abias"] = dt("c_abias", [128, 2048], F32, "ExternalInput")
    dram["c_amask"] = dt("c_amask", [128, 256], F32, "ExternalInput")
    dram["c_ident"] = dt("c_ident", [128, 128], F32, "ExternalInput")
    dram["c_tri"] = dt("c_tri", [128, 128], F32, "ExternalInput")
    dram["c_mneg"] = dt("c_mneg", [128, 128], F32, "ExternalInput")
    dram["hscr"] = dt("hscr", [L_SEQ, D], F32, "Internal")
    dram["out"] = dt("out", [L_SEQ, D], F32, "ExternalOutput")
    if debug:
        dram["dbg"] = dt("dbg", [128, 16, L_SEQ], F32, "ExternalOutput")
    with ExitStack() as st:
        prog = Prog(nc, st, layers, True, True)
        prog.build(dram)
    return nc


def kernel(**inputs):
    common = _prep(inputs)
    x = np.asarray(inputs["x"], np.float32)
    p = np.asarray(inputs["p"], np.float32)
    nb = x.shape[0]
    nc = _build(list(range(DEPTH)), debug=DEBUG)
    in_maps = []
    for b in range(nb):
        m = dict(common)
        m["x"] = np.ascontiguousarray(x[b])
        m["p"] = np.ascontiguousarray(p[:, b])
        in_maps.append(m)
    res = run_bass_kernel_spmd(nc, in_maps, core_ids=list(range(nb)))
    out = np.stack([np.asarray(r["out"], np.float32) for r in res.results], axis=0)
    if DEBUG:
        kernel.dbg = [np.asarray(r["dbg"]) for r in res.results]
    return out
```

```python
from contextlib import ExitStack
import math
import numpy as np
import concourse.bass as bass
import concourse.mybir as mybir
from concourse.bass_utils import run_bass_kernel_spmd

F32 = mybir.dt.float32
BF16 = mybir.dt.bfloat16
AF = mybir.ActivationFunctionType
ALU = mybir.AluOpType

L_SEQ = 2048
D = 1024
DEPTH = 2
TB = 512
NT = 4
NBLK = L_SEQ // TB
EPS = 1e-6
NEG = -30000.0

O_PREW, O_SSDNW, O_POSTW = 0, 1024, 2048
O_DTB, O_ALOG, O_D16 = 3072, 3088, 3104
O_CW, O_CB = 3120, 3168
O_DWW, O_DWB, O_LNW, O_LNB, O_SINK = 3180, 3304, 3308, 3312, 3316
NP = 3328

DEBUG = False


class Buf:
    __slots__ = ("w", "r")

    def __init__(self):
        self.w = {}
        self.r = {}


class Sched:
    ENGS = ("pe", "act", "dve", "pool", "sp")
    NSLOT = {"sp": 12, "pool": 44, "act": 8}

    def __init__(self, nc, stack):
        self.nc = nc
        self.eng = {"pe": nc.tensor, "act": nc.scalar, "dve": nc.vector,
                    "pool": nc.gpsimd, "sp": nc.sync}
        self.sems = {}
        self.cnt = {}
        self.known = {e: {} for e in self.ENGS}
        for e in self.ENGS:
            self.sems[e] = stack.enter_context(nc.semaphore("s_" + e))
            self.cnt[e] = 0
        self.slot_i = {}
        self.fence = {}
        for q, n in self.NSLOT.items():
            self.slot_i[q] = 0
            for k in range(n):
                key = "d_%s%d" % (q, k)
                self.sems[key] = stack.enter_context(nc.semaphore(key))
                self.cnt[key] = 0

    LOG = None

    def _push(self, engname, waits, fn, key, inc):
        if Sched.LOG is not None:
            Sched.LOG.append((engname, list(waits), key if fn is not None else None, inc))
        e = self.eng[engname]
        for k, v in waits:
            e.wait_ge(self.sems[k], v)
        if fn is not None:
            fn(e).then_inc(self.sems[key], inc)

    def barrier(self):
        engs = ("pe", "act", "dve", "pool")
        for e in engs:
            waits = []
            tgt = {k: self.cnt[k] for k in engs if k != e}
            tgt.update(self.fence)
            for k, v in tgt.items():
                if v > 0 and self.known[e].get(k, 0) < v:
                    self.known[e][k] = v
                    waits.append((k, v))
            self._push(e, waits, None, None, 0)

    def _deps(self, eng, reads, writes):
        need = {}

        def add(d):
            for k, v in d.items():
                if need.get(k, 0) < v:
                    need[k] = v
        for b in reads:
            add(b.w)
        for b in writes:
            add(b.w)
            add(b.r)
        out = []
        kn = self.known[eng]
        for k, v in need.items():
            if k == eng and eng == "pe":
                continue
            if kn.get(k, 0) >= v:
                continue
            kn[k] = v
            out.append((k, v))
        return out

    def op(self, eng, fn, reads=(), writes=()):
        waits = self._deps(eng, reads, writes)
        self.cnt[eng] += 1
        t = self.cnt[eng]
        for b in reads:
            if b.r.get(eng, 0) < t:
                b.r[eng] = t
        for b in writes:
            b.w = {eng: t}
            b.r = {}
        self._push(eng, waits, fn, eng, 1)

    def dma(self, q, out, in_, reads=(), writes=(), fence=None):
        if fence is None:
            fence = (q == "pool")
        waits = self._deps(q, reads, writes)
        i = self.slot_i[q]
        self.slot_i[q] += 1
        key = "d_%s%d" % (q, i % self.NSLOT[q])
        prev = self.cnt[key]
        if prev > 0 and self.known[q].get(key, 0) < prev:
            self.known[q][key] = prev
            waits.append((key, prev))
        self.cnt[key] += 16
        t = self.cnt[key]
        if fence:
            self.fence[key] = t
        for b in reads:
            if b.r.get(key, 0) < t:
                b.r[key] = t
        for b in writes:
            b.w = {key: t}
            b.r = {}
        self._push(q, waits, lambda e: e.dma_start(out=out, in_=in_), key, 16)

    def wait_all(self, eng, bufs):
        waits = self._deps(eng, bufs, ())
        self._push(eng, waits, None, None, 0)


class Tl:
    def __init__(self, t, nb=1):
        self.t = t
        self.bs = [Buf() for _ in range(nb)]

    @property
    def b(self):
        return self.bs[0]

    def __getitem__(self, k):
        return self.t[k]


class Prog:
    def __init__(self, nc, st, layers, first_in, last_out):
        self.nc = nc
        self.st = st
        self.S = Sched(nc, st)
        self.layers = layers

    _uid = [0]

    def sb(self, stack, name, shape, dt, nb=1):
        self._uid[0] += 1
        name = "%s_%d" % (name, self._uid[0])
        return Tl(stack.enter_context(self.nc.sbuf_tensor(name, shape, dt)), nb)

    def ps(self, stack, name, shape, dt):
        return Tl(stack.enter_context(self.nc.psum_tensor(name, shape, dt)))

    def mm(self, out, pairs, reads, writes, start=True, stop=True, sgc=False):
        def fn(e):
            n = len(pairs)
            ins = None
            for i, (l, r) in enumerate(pairs):
                kw = {}
                if sgc:
                    kw["skip_group_check"] = True
                ins = e.matmul(out, lhsT=l, rhs=r, start=(start and i == 0),
                               stop=(stop and i == n - 1), **kw)
            return ins
        self.S.op("pe", fn, reads, writes)

    def mms(self, items, reads, writes):
        def fn(e):
            ins = None
            for (o, l, r, s0, s1) in items:
                ins = e.matmul(o, lhsT=l, rhs=r, start=s0, stop=s1, skip_group_check=True)
            return ins
        self.S.op("pe", fn, reads, writes)

    def trs(self, items, ident, reads, writes):
        def fn(e):
            ins = None
            for (o, i) in items:
                ins = e.transpose(out=o, in_=i, identity=ident)
            return ins
        self.S.op("pe", fn, reads, writes)

    def act(self, out, in_, func, reads, writes, bias=None, scale=None, accum=None):
        kw = {}
        if bias is not None:
            kw["bias"] = bias
        if scale is not None:
            kw["scale"] = scale
        if accum is not None:
            kw["accum_out"] = accum
        self.S.op("act", lambda e: e.activation(out=out, in_=in_, func=func, **kw), reads, writes)

    def tt(self, out, in0, in1, op, reads, writes, eng="dve"):
        self.S.op(eng, lambda e: e.tensor_tensor(out=out, in0=in0, in1=in1, op=op), reads, writes)

    def ts(self, out, in0, s1, s2, op0, op1, reads, writes):
        if s2 is None:
            self.S.op("dve", lambda e: e.tensor_scalar(out=out, in0=in0, scalar1=s1, scalar2=None, op0=op0),
                      reads, writes)
        else:
            self.S.op("dve", lambda e: e.tensor_scalar(out=out, in0=in0, scalar1=s1, scalar2=s2,
                                                       op0=op0, op1=op1), reads, writes)

    def stt(self, out, in0, scalar, in1, op0, op1, reads, writes):
        self.S.op("dve", lambda e: e.scalar_tensor_tensor(out=out, in0=in0, scalar=scalar, in1=in1,
                                                          op0=op0, op1=op1), reads, writes)

    def cp(self, eng, out, in_, reads, writes):
        if eng == "act":
            self.S.op("act", lambda e: e.copy(out=out, in_=in_), reads, writes)
        else:
            self.S.op(eng, lambda e: e.tensor_copy(out=out, in_=in_), reads, writes)

    def memset(self, ap, val, writes, eng="dve"):
        self.S.op(eng, lambda e: e.memset(ap, val), (), writes)

    def rstd(self, out, ssq, inv_n, tmp, reads, writes):
        self.ts(tmp, ssq, inv_n, EPS, ALU.mult, ALU.add, reads, writes)
        self.act(tmp, tmp, AF.Ln, writes, writes)
        self.act(out, tmp, AF.Exp, writes, writes, scale=-0.5)

    def build(self, dram):
        nc, S, st = self.nc, self.S, self.st
        sb, ps = self.sb, self.ps
        mm, mms, trs, act, tt, ts, stt, cp, memset = (self.mm, self.mms, self.trs, self.act, self.tt,
                                                      self.ts, self.stt, self.cp, self.memset)
        b_scr = {}
        for l in self.layers:
            for name, ng in (("w_proj", 1), ("w_in", 11), ("w_out", 4), ("w_gate", 2)):
                for g in range(ng):
                    b_scr[(l, name, g)] = Buf()

        cast_q = []
        for l in self.layers:
            cast_q.append((l, "w_proj", 0))
            for g in range(11):
                cast_q.append((l, "w_in", g))
            for g in range(4):
                cast_q.append((l, "w_out", g))
            for g in range(2):
                cast_q.append((l, "w_gate", g))
        cast_pos = [0]

        def cast_some(n):
            for _ in range(n):
                if cast_pos[0] < len(cast_q):
                    l_, name, g = cast_q[cast_pos[0]]
                    cast_pos[0] += 1
                    S.dma("pool", dram[name + "_s"][l_, g], dram[name][l_, g], writes=[b_scr[(l_, name, g)]],
                          fence=False)

        IDF = sb(st, "IDF", [128, 128], F32)
        IDB = sb(st, "IDB", [128, 128], BF16)
        U = sb(st, "U", [128, 128], F32)
        ONEF = sb(st, "ONEF", [128, 128], F32)
        ONE512 = sb(st, "ONE512", [128, 128], F32)
        ONEB = sb(st, "ONEB", [128, 256], BF16)
        AMASK = sb(st, "AMASK", [128, 2, 128], F32)
        PRM = sb(st, "PRM", [128, NP], F32)
        DER = sb(st, "DER", [128, 64], F32)
        BT = sb(st, "BT", [128, 2, 8, 128], F32)
        WP = sb(st, "WP", [128, 2, 1024], BF16)
        WB = [sb(st, "WB%d" % i, [128, 8, 512], BF16) for i in range(3)]
        XNT = sb(st, "XNT", [128, 8, 512], BF16)
        YT = sb(st, "YT", [128, 16, 512], BF16)
        HS = sb(st, "HS", [128, 1024], F32)
        HSB = sb(st, "HSB", [128, 1024], BF16)
        TAIL = sb(st, "TAIL", [128, 12, 3], F32)
        GLU = sb(st, "GLU", [128, 4, 30 + TB], BF16)
        KT = sb(st, "KT", [128, 128 + TB], BF16)
        VT = sb(st, "VT", [128, 5, 128], BF16)
        JUNK = sb(st, "JUNK", [128, 1024], BF16)
        SM = sb(st, "SM", [128, 64], F32)
        PA = ps(st, "PA", [128, 512], F32)
        PB = ps(st, "PB", [128, 512], F32)
        PT = ps(st, "PT", [128, 1024], BF16)
        PS_ = ps(st, "PS", [128, 512], F32)
        PX0 = ps(st, "PX0", [128, 512], F32)
        PX1 = ps(st, "PX1", [128, 512], F32)
        PY = ps(st, "PY", [128, 512], F32)
        PO = ps(st, "PO", [128, 512], F32)

        def view(ap, bufs):
            v = Tl.__new__(Tl)
            v.t = ap
            v.bs = list(bufs)
            return v

        XR = [sb(st, "XR%d" % i, [128, 1024], F32) for i in range(2)]
        YN = sb(st, "YN", [128, 1024], BF16)
        XN2 = sb(st, "XN2", [128, 1024], BF16)
        XNs = [YN, XN2]
        PG = sb(st, "PG", [128, 4, 8 + TB], F32, nb=4)
        PRE = [view(PG[:, i, 0:3 + TB], [PG.bs[i]]) for i in range(2)]
        PREB = [view(PG[:, i, 0:264].bitcast(BF16)[:, 0:3 + TB], [PG.bs[i]]) for i in range(2)]
        DGS = [view(PG[:, i, 264:520].bitcast(BF16).rearrange("p (a b) -> p a b", a=4), [PG.bs[i]]) for i in range(2)]
        TAILB = sb(st, "TAILB", [128, 12, 4], BF16)
        ACC = [view(PG[:, 2 + i, 0:TB], [PG.bs[2 + i]]) for i in range(2)]
        TH = [sb(st, "TH%d" % i, [128, TB], F32) for i in range(2)]
        XBC = sb(st, "XBC", [128, 12, TB], BF16, nb=12)
        ZS = sb(st, "ZS", [128, NT, 1024], BF16, nb=NT)
        DTR = sb(st, "DTR", [128, NT, 16], F32, nb=NT)
        SD2 = sb(st, "SD2", [128, 9, 64], F32, nb=9)
        XG = sb(st, "XG", [128, 6, 1024], BF16, nb=6)
        XDT = [view(XG[:, i, :], [XG.bs[i]]) for i in range(2)]
        XD = view(XG[:, 2, :], [XG.bs[2]])
        XDS = view(XG[:, 3, :], [XG.bs[3]])
        MT4 = [view(XG[:, 4 + i // 2, (i % 2) * 512:(i % 2 + 1) * 512].rearrange("p (a b) -> p a b", a=4),
                    [Buf()]) for i in range(4)]
        DG = view(PG[:].rearrange("p a b -> p (a b)").bitcast(BF16)[:, 0:31 * 128].rearrange("p (a b) -> p a b", a=31),
                  PG.bs)
        BMT = sb(st, "BMT", [128, 2, 128], BF16)
        CBM = [sb(st, "CBM%d" % i, [128, 2, 128], F32) for i in range(2)]
        MNEG = sb(st, "MNEG", [128, 4, 128], BF16)
        T1 = sb(st, "T1", [128, 512], F32)
        D1 = T1
        Y = sb(st, "Y", [128, 1024], F32)
        EB = [sb(st, "EB%d" % i, [128, 4, 128], F32) for i in range(2)]
        MU = view(XR[0][:, 0:512], [XR[0].b])
        RS = view(XR[0][:, 512:1024], [XR[0].b])
        U1 = sb(st, "U1", [128, TB], F32)
        QT = sb(st, "QT", [128, 4, TB], BF16, nb=4)
        AG = sb(st, "AG", [128, 4, TB], BF16, nb=4)
        RBH = [sb(st, "RBH%d" % i, [128, 4, 128], BF16) for i in range(2)]
        RBL = [sb(st, "RBL%d" % i, [128, 4, 128], BF16) for i in range(2)]
        UB = sb(st, "UB", [128, 128], BF16)
        DAB = sb(st, "DAB", [128, 2, 64], BF16, nb=2)
        SC = [view(XR[1][:, i * 512:(i + 1) * 512], [XR[1].b]) for i in range(2)]
        ET = [sb(st, "ET%d" % i, [128, 512], BF16) for i in range(4)]
        RD = sb(st, "RD", [128, 256], F32)
        OT = sb(st, "OT", [128, 256], F32)
        O = sb(st, "O", [128, NT, 1024], F32, nb=NT)
        HH = view(O[:, 0:2, :].rearrange("p a (b c) -> p (a b) c", b=2), [O.bs[0], O.bs[0], O.bs[1], O.bs[1]])
        HQ = [view(O[:, 2, i * TB:(i + 1) * TB], [O.bs[2]]) for i in range(2)]
        CG = view(O[:, 3, :].bitcast(BF16).rearrange("p (a b) -> p a b", a=4), [O.bs[3]] * 4)
        PR = [sb(st, "PR%d" % i, [128, 256], F32) for i in range(2)]
        SM1s = [sb(st, "SM1_%d" % i, [128, 8], F32) for i in range(2)]
        SM5s = [sb(st, "SM5_%d" % i, [128, 8], F32) for i in range(2)]
        SMEs = [sb(st, "SME_%d" % i, [128, 8], F32) for i in range(2)]
        HT = view(YT[:, 0:8, :], [Buf()])
        HBs = [view(YT[:, 8 + 6 * i:10 + 6 * i, :].rearrange("p a b -> p (a b)"), [Buf()]) for i in range(2)]
        PTT = view(YT[:, 10:12, :], [Buf()])
        TG = [view(YT[:, 12:14, :].rearrange("p a b -> p (a b)").bitcast(F32), [Buf()])] * 2
        PB16s = [sb(st, "PB16a", [128, 256], BF16), sb(st, "PB16b", [128, 256], BF16)]
        p5views = [HT, HBs[0], HBs[1], PTT, TG[0]]
        PSB = view(PS_[:, 384:512].bitcast(BF16), [PS_.b])
        mmb = [PA, PB]
        mmi = [0]

        def nextmm():
            p = mmb[mmi[0] % 2]
            mmi[0] += 1
            return p

        S.dma("sp", IDF[:], dram["c_ident"], writes=[IDF.b])
        S.dma("sp", U[:], dram["c_tri"], writes=[U.b])
        S.dma("sp", AMASK[:], dram["c_amask"], writes=[AMASK.b])
        cp("dve", IDB[:], IDF[:], [IDF.b], [IDB.b])
        cp("dve", UB[:], U[:], [U.b], [UB.b])
        S.dma("sp", EB[0][:, 0, :], dram["c_mneg"], writes=[EB[0].b])
        cp("dve", MNEG[:], EB[0][:, 0, :].unsqueeze(1).to_broadcast([128, 4, 128]), [EB[0].b], [MNEG.b])
        memset(ONEF[:], 1.0, [ONEF.b])
        memset(ONE512[:], 1.0 / 512.0, [ONE512.b])
        memset(ONEB[:], 1.0, [ONEB.b])

        wplan = []
        wloaded = []
        wstate = ["free", "free", "free"]
        cast_idx = {k: i for i, k in enumerate(cast_q)}

        def w_pump():
            while wplan and len(wloaded) < 2 and "free" in wstate:
                key = wplan.pop(0)
                bi = wstate.index("free")
                wstate[bi] = "loaded"
                l_, name, g = key
                assert cast_idx[key] < cast_pos[0], ("cast not issued", key)
                S.dma("sp", WB[bi][:].rearrange("p a b -> p (a b)"), dram[name + "_s"][l_, g],
                      reads=[b_scr[key]], writes=[WB[bi].b])
                wloaded.append((key, bi))

        def w_acquire(key):
            cast_some(1)
            if not wloaded:
                w_pump()
            k2, bi = wloaded.pop(0)
            assert k2 == key, (k2, key)
            wstate[bi] = "held"
            w_pump()
            return WB[bi]

        def w_release(buf):
            bi = WB.index(buf)
            assert wstate[bi] == "held"
            wstate[bi] = "free"
            w_pump()

        def inherit(dst, srcs):
            for sb_ in srcs:
                for d in (sb_.r, sb_.w):
                    for k, v in d.items():
                        if dst.r.get(k, 0) < v:
                            dst.r[k] = v

        def interleave(A, B):
            out = []
            na, nb = len(A), len(B)
            ia = ib = 0
            while ia < na or ib < nb:
                if ib >= nb or (ia < na and ia * nb <= ib * na):
                    out.append(A[ia]); ia += 1
                else:
                    out.append(B[ib]); ib += 1
            return out

        b_hscr = [Buf() for _ in range(16)]
        b_out = [Buf() for _ in range(16)]

        for li, l in enumerate(self.layers):
            src = dram["x"] if li == 0 else dram["hscr"]
            dst = dram["out"] if li == len(self.layers) - 1 else dram["hscr"]
            b_src = None if li == 0 else b_hscr
            b_dst = b_out if li == len(self.layers) - 1 else b_hscr
            S.dma("sp", PRM[:], dram["prm"][l], writes=[PRM.b])
            S.dma("sp", BT[:].rearrange("p a h q -> p (a h q)"), dram["c_abias"], writes=[BT.b])
            preloaded = set()
            if li == 0:
                for t in range(2):
                    S.dma("sp", XR[t][:], src[t * 128:(t + 1) * 128, :], writes=[XR[t].b])
                    preloaded.add(t)
                S.wait_all("pool", [PRM.b, BT.b, IDF.b, U.b, AMASK.b, EB[0].b, XR[0].b, XR[1].b])
                cast_some(4)
            S.dma("sp", WP[:].rearrange("p a b -> p (a b)"), dram["w_proj_s"][l, 0],
                  reads=[b_scr[(l, "w_proj", 0)]], writes=[WP.b])
            ts(PRM[:, O_CW:O_CB + 12], PRM[:, O_CW:O_CB + 12], 0.5, None, ALU.mult, None, [PRM.b], [PRM.b])
            ts(PRM[:, O_DWW:O_DWW + 124], PRM[:, O_DWW:O_DWW + 124], 0.5, None, ALU.mult, None, [PRM.b], [PRM.b])
            act(DER[:, 0:16], PRM[:, O_ALOG:O_ALOG + 16], AF.Exp, [PRM.b], [DER.b])
            ts(DER[:, 0:16], DER[:, 0:16], -1.0, None, ALU.mult, None, [DER.b], [DER.b])
            ts(DER[:, 16:24], PRM[:, O_LNW:O_LNW + 8], 0.5, None, ALU.mult, None, [PRM.b], [DER.b])
            tt(BT[:], BT[:], AMASK[:].unsqueeze(2).to_broadcast([128, 2, 8, 128]), ALU.add, [BT.b, AMASK.b], [BT.b])
            tt(BT[:], BT[:], PRM[:, O_SINK:O_SINK + 8].unsqueeze(1).unsqueeze(3).to_broadcast([128, 2, 8, 128]),
               ALU.subtract, [BT.b, PRM.b], [BT.b])
            memset(HS[:], 0.0, [HS.b])
            memset(HSB[:], 0.0, [HSB.b])
            memset(TAIL[:], 0.0, [TAIL.b])
            memset(TAILB[:], 0.0, [TAILB.b])
            memset(GLU[:], 0.0, [GLU.b])
            memset(KT[:], 0.0, [KT.b])
            memset(VT[:], 0.0, [VT.b])

            def make_p1(tb):
                tok0 = tb * TB
                ths = []

                def tileA(t):
                    xr = XR[t % 2]
                    xn = XNs[t % 2]
                    r0 = tok0 + t * 128
                    if not (tb == 0 and t in preloaded):
                        S.dma("act", xr[:], src[r0:r0 + 128, :],
                              reads=([] if b_src is None else [b_src[tb * NT + t]]), writes=[xr.b])
                    SM1 = SM1s[t % 2]
                    memset(SM1[:, 0:1], 0.0, [SM1.b])
                    act(JUNK[:], xr[:], AF.Square, [xr.b, SM1.b], [JUNK.b, SM1.b], accum=SM1[:, 0:1])
                    self.rstd(SM1[:, 2:3], SM1[:, 0:1], 1.0 / D, SM1[:, 1:2], [SM1.b], [SM1.b])
                    stt(xn[:], xr[:], SM1[:, 2:3], PRM[:, O_PREW:O_PREW + 1024], ALU.mult, ALU.mult,
                        [xr.b, SM1.b, PRM.b], [xn.b])

                def tileB(t):
                    xn = XNs[t % 2]
                    trs([(PT[:, j * 128:(j + 1) * 128], xn[:, j * 128:(j + 1) * 128]) for j in range(8)],
                        IDB[:], [xn.b, IDB.b], [PT.b])
                    cp("act", XNT[:, :, t * 128:(t + 1) * 128], PT[:].rearrange("p (a b) -> p a b", a=8),
                       [PT.b], [XNT.b])
                order = [("A", 0), ("A", 1), ("B", 0), ("A", 2), ("B", 1), ("A", 3), ("B", 2), ("B", 3)]
                for kind, t in order:
                    ths.append(((lambda t=t: tileA(t)) if kind == "A" else (lambda t=t: tileB(t)), []))
                return ths

            def make_p2a(tb):
                ths = []
                hold = {}

                def xbcA(c):
                    if c % 4 == 0:
                        hold["w"] = w_acquire((l, "w_in", c // 4))
                    wb = hold["w"]
                    c4 = c % 4
                    pm = nextmm()
                    mm(pm[:], [(wb[:, kc, c4 * 128:(c4 + 1) * 128], XNT[:, kc, :]) for kc in range(8)],
                       [wb.b, XNT.b], [pm.b])
                    if c % 4 == 3:
                        w_release(wb)
                    pre, dgs = PREB[c % 2], DGS[c % 2]
                    tt(dgs[:], IDF[:].unsqueeze(1).to_broadcast([128, 4, 128]),
                       PRM[:, O_CW + c * 4:O_CW + c * 4 + 4].unsqueeze(2).to_broadcast([128, 4, 128]), ALU.mult,
                       [IDF.b, PRM.b], [dgs.b])
                    cp("dve", pre[:, 0:3], TAILB[:, c, 0:3], [TAILB.b], [pre.b])
                    cp("act", pre[:, 3:3 + TB], pm[:], [pm.b], [pre.b])
                    cp("dve", TAILB[:, c, 0:3], pre[:, TB:TB + 3], [pre.b], [TAILB.b])

                def xbcB(c):
                    pre, acc, th, dgs = PREB[c % 2], ACC[c % 2], TH[c % 2], DGS[c % 2]
                    pc = nextmm()
                    mm(pc[:], [(dgs[:, k, :], pre[:, k:k + TB]) for k in range(4)], [dgs.b, pre.b], [pc.b])
                    act(acc[:], pc[:], AF.Identity, [pc.b, PRM.b], [acc.b], bias=PRM[:, O_CB + c:O_CB + c + 1])
                    act(th[:], acc[:], AF.Tanh, [acc.b], [th.b])
                    stt(XBC[:, c, :], th[:], 1.0, acc[:], ALU.add, ALU.mult, [th.b, acc.b], [XBC.bs[c]])

                def z_tile(half, t):
                    if t == 0:
                        hold["w"] = w_acquire((l, "w_in", 3 + half))
                    wb = hold["w"]
                    pm = nextmm()
                    mm(pm[:], [(XNT[:, kc, t * 128:(t + 1) * 128], wb[:, kc, :]) for kc in range(8)],
                       [wb.b, XNT.b], [pm.b])
                    if t == NT - 1:
                        w_release(wb)
                    th = TH[t % 2]
                    act(th[:], pm[:], AF.Tanh, [pm.b], [th.b], scale=0.5)
                    stt(ZS[:, t, half * 512:(half + 1) * 512], th[:], 1.0, pm[:], ALU.add, ALU.mult,
                        [th.b, pm.b], [ZS.bs[t]])

                def kv():
                    wb = w_acquire((l, "w_in", 5))
                    cp("dve", KT[:, 0:128], KT[:, TB:TB + 128], [KT.b], [KT.b])
                    cp("dve", VT[:, 0, :], VT[:, 4, :], [VT.b], [VT.b])
                    pm = nextmm()
                    mm(pm[:], [(wb[:, kc, 0:128], XNT[:, kc, :]) for kc in range(8)], [wb.b, XNT.b], [pm.b])
                    cp("act", KT[:, 128:128 + TB], pm[:], [pm.b], [KT.b])
                    for t in range(NT):
                        pm = nextmm()
                        mm(pm[:, 0:144], [(XNT[:, kc, t * 128:(t + 1) * 128], wb[:, kc, 128:272]) for kc in range(8)],
                           [wb.b, XNT.b], [pm.b])
                        cp("dve", DTR[:, t, :], pm[:, 0:16], [pm.b], [DTR.bs[t]])
                        cp("act", VT[:, 1 + t, :], pm[:, 16:144], [pm.b], [VT.b])
                    w_release(wb)
                for c in range(13):
                    if c < 12:
                        ths.append((lambda c=c: xbcA(c), [(l, "w_in", c // 4)] if c % 4 == 0 else []))
                    if c >= 1:
                        ths.append((lambda c=c: xbcB(c - 1), []))
                for half in range(2):
                    for t in range(NT):
                        ths.append((lambda half=half, t=t: z_tile(half, t), [(l, "w_in", 3 + half)] if t == 0 else []))
                ths.append((kv, [(l, "w_in", 5)]))
                return ths

            def make_p5b(tb):
                tok0 = tb * TB
                ths = []
                hold = {}

                def post(t):
                    r0 = tok0 + t * 128
                    xr = XR[t % 2]
                    S.dma("act", xr[:], src[r0:r0 + 128, :],
                          reads=([] if b_src is None else [b_src[tb * NT + t]]), writes=[xr.b])
                    pr = PR[t % 2]
                    S.dma("act", pr[:], dram["p"][l, r0:r0 + 128, :], writes=[pr.b])
                    SM5 = SM5s[t % 2]
                    memset(SM5[:, 0:1], 0.0, [SM5.b])
                    act(JUNK[:], O[:, t, :], AF.Square, [O.bs[t], SM5.b], [JUNK.b, SM5.b], accum=SM5[:, 0:1])
                    self.rstd(SM5[:, 2:3], SM5[:, 0:1], 1.0 / D, SM5[:, 1:2], [SM5.b], [SM5.b])
                    stt(O[:, t, :], O[:, t, :], SM5[:, 2:3], PRM[:, O_POSTW:O_POSTW + 1024], ALU.mult, ALU.mult,
                        [O.bs[t], SM5.b, PRM.b], [O.bs[t]])
                    tt(O[:, t, :], O[:, t, :], xr[:], ALU.add, [O.bs[t], xr.b], [O.bs[t]])
                    HB, PB16 = HBs[t % 2], PB16s[t % 2]
                    cp("act", HB[:], O[:, t, :], [O.bs[t]], [HB.b])
                    cp("dve", PB16[:], pr[:], [pr.b], [PB16.b])

                def postB(t):
                    HB, PB16 = HBs[t % 2], PB16s[t % 2]
                    trs([(PT[:, j * 128:(j + 1) * 128], HB[:, j * 128:(j + 1) * 128]) for j in range(8)], IDB[:],
                        [HB.b, IDB.b], [PT.b])
                    cp("act", HT[:, :, t * 128:(t + 1) * 128], PT[:].rearrange("p (a b) -> p a b", a=8),
                       [PT.b], [HT.b])
                    trs([(PSB[:, j * 128:(j + 1) * 128], PB16[:, j * 128:(j + 1) * 128]) for j in range(2)], IDB[:],
                        [PB16.b, IDB.b], [PSB.b])
                    cp("act", PTT[:, :, t * 128:(t + 1) * 128], PSB[:].rearrange("p (a b) -> p a b", a=2),
                       [PSB.b], [PTT.b])

                def ple(ch, t):
                    if t == 0:
                        hold["w"] = w_acquire((l, "w_gate", ch))
                    wg = hold["w"]
                    tcols = slice(t * 128, (t + 1) * 128)
                    pg = nextmm()
                    mm(pg[:], [(HT[:, kc, tcols], wg[:, kc, :]) for kc in range(8)], [HT.b, wg.b], [pg.b])
                    if t == NT - 1:
                        w_release(wg)
                    pp = nextmm()
                    mm(pp[:], [(PTT[:, kc, tcols], WP[:, kc, ch * 512:(ch + 1) * 512]) for kc in range(2)],
                       [PTT.b, WP.b], [pp.b])
                    tg = TG[t % 2]
                    act(tg[:], pg[:], AF.Tanh, [pg.b], [tg.b], scale=0.5)
                    stt(tg[:], tg[:], 1.0, pp[:], ALU.add, ALU.mult, [tg.b, pp.b], [tg.b])
                    stt(O[:, t, ch * 512:(ch + 1) * 512], tg[:], 0.5, O[:, t, ch * 512:(ch + 1) * 512],
                        ALU.mult, ALU.add, [tg.b, O.bs[t]], [O.bs[t]])

                def store(t):
                    r0 = tok0 + t * 128
                    S.dma("pool", dst[r0:r0 + 128, :], O[:, t, :], reads=[O.bs[t]], writes=[b_dst[tb * NT + t]])
                ths.append((lambda: post(0), []))
                ths.append((lambda: post(1), []))
                ths.append((lambda: postB(0), []))
                ths.append((lambda: post(2), []))
                ths.append((lambda: postB(1), []))
                ths.append((lambda: ple(0, 0), [(l, "w_gate", 0)]))
                ths.append((lambda: post(3), []))
                ths.append((lambda: postB(2), []))
                ths.append((lambda: ple(0, 1), []))
                ths.append((lambda: postB(3), []))
                ths.append((lambda: ple(0, 2), []))
                ths.append((lambda: ple(0, 3), []))
                for t in range(NT):
                    ths.append((lambda t=t: ple(1, t), [(l, "w_gate", 1)] if t == 0 else []))
                for t in range(NT):
                    ths.append((lambda t=t: store(t), []))
                return ths

            def run(ths):
                for fn, _ in ths:
                    fn()

            def keys_of(ths):
                return [k for _, ws in ths for k in ws]

            k_ssdx = [(l, "w_in", g) for g in range(6, 11)]
            k_p5a = [(l, "w_out", g) for g in range(4)]
            first = make_p1(0) + make_p2a(0)
            wplan.extend(keys_of(first) + k_ssdx + k_p5a)
            w_pump()
            run(first)

            for tb in range(NBLK):
                tok0 = tb * TB
                if True:
                    row = lambda i: SD2[:, i, :]
                    r3 = lambda i: SD2[:, i, :].rearrange("p (t h) -> p t h", t=4)
                    sbf = SD2.bs
                    tt(r3(0), DTR[:], PRM[:, O_DTB:O_DTB + 16].unsqueeze(1).to_broadcast([128, 4, 16]), ALU.add,
                       DTR.bs + [PRM.b], [sbf[0]])
                    act(row(1), row(0), AF.Abs, [sbf[0]], [sbf[1]])
                    act(row(1), row(1), AF.Exp, [sbf[1]], [sbf[1]], scale=-1.0)
                    act(row(1), row(1), AF.Ln, [sbf[1]], [sbf[1]], bias=1.0)
                    stt(row(2), row(0), 0.0, row(1), ALU.max, ALU.add, [sbf[0], sbf[1]], [sbf[2]])
                    tt(r3(3), r3(2), DER[:, 0:16].unsqueeze(1).to_broadcast([128, 4, 16]), ALU.mult,
                       [sbf[2], DER.b], [sbf[3]])
                    cp("dve", DAB[:, 0, :], row(3), [sbf[3]], [DAB.bs[0]])
                    cp("dve", row(4), DAB[:, 0, :], [DAB.bs[0]], [sbf[4]])
                    tt(row(4), row(3), row(4), ALU.subtract, [sbf[3], sbf[4]], [sbf[4]])
                    cp("dve", DAB[:, 1, :], row(4), [sbf[4]], [DAB.bs[1]])
                    mms([(PS_[:, 256:320], UB[:], DAB[:, 0, :], True, False),
                         (PS_[:, 256:320], UB[:], DAB[:, 1, :], False, True),
                         (PS_[:, 320:384], ONEB[:, 0:128], DAB[:, 0, :], True, False),
                         (PS_[:, 320:384], ONEB[:, 0:128], DAB[:, 1, :], False, True)],
                        [UB.b, ONEB.b] + DAB.bs, [PS_.b])
                    ts(row(5), PS_[:, 256:320], -1.0, None, ALU.mult, None, [PS_.b], [sbf[5]])
                    tt(row(6), PS_[:, 320:384], row(5), ALU.add, [PS_.b, sbf[5]], [sbf[6]])
                    act(row(6), row(6), AF.Exp, [sbf[6]], [sbf[6]])
                    act(row(7), PS_[:, 256:320], AF.Exp, [PS_.b], [sbf[7]])
                    act(row(8), PS_[:, 320:384], AF.Exp, [PS_.b], [sbf[8]])

                    def stageB1(t):
                        cols = slice(t * 128, (t + 1) * 128)
                        mms([(PS_[:, g * 128:(g + 1) * 128], XBC[:, 8 + g, cols], XBC[:, 10 + g, cols], True, True)
                             for g in range(2)], [XBC.bs[8 + i] for i in range(4)], [PS_.b])
                        cp("dve", CBM[t % 2][:].rearrange("p a b -> p (a b)"), PS_[:, 0:256], [PS_.b], [CBM[t % 2].b])

                    def stageB2(t):
                        cols = slice(t * 128, (t + 1) * 128)
                        xdt = XDT[t % 2]
                        trs([(PSB[:, g * 128:(g + 1) * 128], XBC[:, 8 + g, cols]) for g in range(2)], IDB[:],
                            [XBC.bs[8], XBC.bs[9], IDB.b], [PSB.b])
                        trs([(PT[:, j * 128:(j + 1) * 128], XBC[:, j, cols]) for j in range(8)], IDB[:],
                            [XBC.bs[j] for j in range(8)] + [IDB.b], [PT.b])
                        cp("dve", BMT[:].rearrange("p a b -> p (a b)"), PSB[:], [PSB.b], [BMT.b])
                        pt3 = PT[:].rearrange("p (h d) -> p h d", h=16)
                        tt(xdt[:].rearrange("p (h d) -> p h d", h=16), pt3,
                           r3(2)[:, t, :].unsqueeze(2).to_broadcast([128, 16, 64]), ALU.mult, [PT.b, sbf[2]], [xdt.b])
                        tt(XD[:].rearrange("p (h d) -> p h d", h=16), pt3,
                           PRM[:, O_D16:O_D16 + 16].unsqueeze(2).to_broadcast([128, 16, 64]), ALU.mult,
                           [PT.b, PRM.b], [XD.b])
                        tt(XDS[:].rearrange("p (h d) -> p h d", h=16), xdt[:].rearrange("p (h d) -> p h d", h=16),
                           r3(6)[:, t, :].unsqueeze(2).to_broadcast([128, 16, 64]), ALU.mult, [xdt.b, sbf[6]], [XDS.b])

                    def kdec(k):
                        return k // 4, (k // 2) % 2, k % 2

                    def stageC1a(k):
                        t, g, hf = kdec(k)
                        h0 = g * 8 + hf * 4
                        for part, rb in ((0, RBH), (1, RBL)):
                            rk = rb[k % 2]
                            dab = DAB[:, part, :].rearrange("p (t h) -> p t h", t=4)
                            tt(rk[:], dab[:, t, h0:h0 + 4].unsqueeze(2).to_broadcast([128, 4, 128]),
                               UB[:].unsqueeze(1).to_broadcast([128, 4, 128]), ALU.mult, [DAB.bs[part], UB.b], [rk.b],
                               eng="pool")

                    def stageC1b(k):
                        rh, rl, pxk = RBH[k % 2], RBL[k % 2], (PX0, PX1)[k % 2]
                        mms([(pxk[:], ONEB[:, 0:128], rh[:], True, False), (pxk[:], ONEB[:, 0:128], rl[:], False, False),
                             (pxk[:], IDB[:], MNEG[:], False, True)],
                            [ONEB.b, rh.b, rl.b, IDB.b, MNEG.b], [pxk.b])

                    def stageC2(k):
                        t, g, hf = kdec(k)
                        h0 = g * 8 + hf * 4
                        pxk, ek, mtk = (PX0, PX1)[k % 2], EB[k % 2], MT4[k % 4]
                        for i in range(4):
                            act(ek[:, i, :], pxk[:, i * 128:(i + 1) * 128], AF.Exp, [pxk.b, sbf[5]], [ek.b],
                                bias=r3(5)[:, t, h0 + i:h0 + i + 1])
                        tt(mtk[:], ek[:], CBM[t % 2][:, g, :].unsqueeze(1).to_broadcast([128, 4, 128]), ALU.mult,
                           [ek.b, CBM[t % 2].b], [mtk.b])

                    def advance(k):
                        if k + 2 < 4 * NT:
                            stageC1a(k + 2)
                        if k + 1 < 4 * NT:
                            stageC1b(k + 1)
                        stageC2(k)

                    def stageD(t, g):
                        cols = slice(t * 128, (t + 1) * 128)
                        gs = slice(g * 512, (g + 1) * 512)
                        py, po, t1 = PY, PO, T1
                        xdt = XDT[t % 2]
                        k0 = 4 * t + 2 * g
                        items = [(py[:], IDB[:], XD[:, gs], True, False)]
                        for hh in range(8):
                            h = g * 8 + hh
                            items.append((py[:, hh * 64:(hh + 1) * 64], MT4[(k0 + hh // 4) % 4][:, hh % 4, :],
                                          xdt[:, h * 64:(h + 1) * 64], False, True))
                        mms(items, [IDB.b, XD.b, MT4[k0 % 4].b, MT4[(k0 + 1) % 4].b, xdt.b], [py.b])
                        mm(po[:], [(XBC[:, 10 + g, cols], HSB[:, gs])], [XBC.bs[10 + g], HSB.b], [po.b])
                        tt(t1[:].rearrange("p (h d) -> p h d", h=8), po[:].rearrange("p (h d) -> p h d", h=8),
                           r3(7)[:, t, g * 8:(g + 1) * 8].unsqueeze(2).to_broadcast([128, 8, 64]), ALU.mult,
                           [po.b, sbf[7]], [t1.b])
                        tt(Y[:, gs], t1[:], py[:], ALU.add, [t1.b, py.b], [Y.b])

                    def stageD2(t, g):
                        gs = slice(g * 512, (g + 1) * 512)
                        po = PO
                        mm(po[:], [(BMT[:, g, :], XDS[:, gs])], [BMT.b, XDS.b], [po.b])
                        tt(HS[:, gs].rearrange("p (h d) -> p h d", h=8), HS[:, gs].rearrange("p (h d) -> p h d", h=8),
                           r3(8)[:, t, g * 8:(g + 1) * 8].unsqueeze(2).to_broadcast([128, 8, 64]), ALU.mult,
                           [HS.b, sbf[8]], [HS.b])
                        tt(HS[:, gs], HS[:, gs], po[:], ALU.add, [HS.b, po.b], [HS.b])
                        cp("act", HSB[:, gs], HS[:, gs], [HS.b], [HSB.b])

                    def stageE(t):
                        cols = slice(t * 128, (t + 1) * 128)
                        tt(Y[:], Y[:], ZS[:, t, :], ALU.mult, [Y.b, ZS.bs[t]], [Y.b])
                        SME = SMEs[t % 2]
                        memset(SME[:, 0:2], 0.0, [SME.b])
                        for g in range(2):
                            act(JUNK[:, 0:512], Y[:, g * 512:(g + 1) * 512], AF.Square, [Y.b, SME.b], [JUNK.b, SME.b],
                                accum=SME[:, g:g + 1])
                        ts(SME[:, 2:4], SME[:, 0:2], 1.0 / 512.0, 4.0 * EPS, ALU.mult, ALU.add, [SME.b], [SME.b])
                        act(SME[:, 2:4], SME[:, 2:4], AF.Ln, [SME.b], [SME.b])
                        act(SME[:, 4:6], SME[:, 2:4], AF.Exp, [SME.b], [SME.b], scale=-0.5)
                        for g in range(2):
                            gs = slice(g * 512, (g + 1) * 512)
                            stt(YN[:, gs], Y[:, gs], SME[:, 4 + g:5 + g], PRM[:, O_SSDNW + g * 512:O_SSDNW + (g + 1) * 512],
                                ALU.mult, ALU.mult, [Y.b, SME.b, PRM.b], [YN.b])

                    def stageE2(t):
                        cols = slice(t * 128, (t + 1) * 128)
                        trs([(PT[:, j * 128:(j + 1) * 128], YN[:, j * 128:(j + 1) * 128]) for j in range(8)], IDB[:],
                            [YN.b, IDB.b], [PT.b])
                        cp("dve", YT[:, 0:8, cols], PT[:].rearrange("p (a b) -> p a b", a=8), [PT.b], [YT.b])

                    extra = []
                    wbh = {}

                    def conf_pair(j):
                        if j % 2 == 0:
                            wbh["c"] = w_acquire((l, "w_in", 6 + j // 2))
                        wb = wbh["c"]
                        jj = j % 2
                        pa = nextmm()
                        mm(pa[:], [(wb[:, kc, (2 * jj) * 128:(2 * jj + 1) * 128], XNT[:, kc, :]) for kc in range(8)],
                           [wb.b, XNT.b], [pa.b])
                        pb = nextmm()
                        mm(pb[:], [(wb[:, kc, (2 * jj + 1) * 128:(2 * jj + 2) * 128], XNT[:, kc, :]) for kc in range(8)],
                           [wb.b, XNT.b], [pb.b])
                        if j % 2 == 1:
                            w_release(wb)
                        th = TH[j % 2]
                        act(th[:], pb[:], AF.Tanh, [pb.b], [th.b], scale=0.5)
                        stt(GLU[:, j, 30:30 + TB], th[:], 1.0, pa[:], ALU.add, ALU.mult, [th.b, pa.b], [GLU.b])

                    def conf_gate(j):
                        if j == 0:
                            wbh["c"] = w_acquire((l, "w_in", 8))
                        wb = wbh["c"]
                        pm = nextmm()
                        mm(pm[:], [(wb[:, kc, j * 128:(j + 1) * 128], XNT[:, kc, :]) for kc in range(8)],
                           [wb.b, XNT.b], [pm.b])
                        if j == 3:
                            w_release(wb)
                        th = TH[j % 2]
                        act(th[:], pm[:], AF.Tanh, [pm.b], [th.b], scale=0.5)
                        stt(CG[:, j, :], th[:], 1.0, pm[:], ALU.add, ALU.mult, [th.b, pm.b], [CG.bs[j]])

                    def conf_convA(j):
                        tt(DG[:], IDF[:].unsqueeze(1).to_broadcast([128, 31, 128]),
                           PRM[:, O_DWW + j * 31:O_DWW + (j + 1) * 31].unsqueeze(2).to_broadcast([128, 31, 128]),
                           ALU.mult, [IDF.b, PRM.b], DG.bs)

                    def conf_convB(j):
                        pm = nextmm()
                        mm(pm[:], [(DG[:, k, :], GLU[:, j, k:k + TB]) for k in range(31)], DG.bs + [GLU.b], [pm.b])
                        act(HH[:, j, :], pm[:], AF.Identity, [pm.b, PRM.b], [HH.bs[j]],
                            bias=PRM[:, O_DWB + j:O_DWB + j + 1])

                    def conf_stats():
                        PCA = nextmm()
                        PCB = nextmm()
                        mm(PCA[:], [(ONE512[:], HH[:, j, :]) for j in range(4)], [ONE512.b] + HH.bs, [PCA.b])
                        for j in range(4):
                            hq = HQ[j % 2]
                            act(hq[:], HH[:, j, :], AF.Square, [HH.bs[j]], [hq.b])
                            mm(PCB[:], [(ONE512[:], hq[:])], [ONE512.b, hq.b], [PCB.b], start=(j == 0), stop=(j == 3))
                        cp("act", MU[:], PCA[:], [PCA.b], [MU.b])
                        tt(D1[:], MU[:], MU[:], ALU.mult, [MU.b], [D1.b])
                        tt(RS[:], PCB[:], D1[:], ALU.subtract, [PCB.b, D1.b], [RS.b])
                        ts(RS[:], RS[:], EPS, None, ALU.add, None, [RS.b], [RS.b])
                        act(RS[:], RS[:], AF.Ln, [RS.b], [RS.b])
                        act(RS[:], RS[:], AF.Exp, [RS.b], [RS.b], scale=-0.5)

                    def conf_ln(j):
                        tt(D1[:], HH[:, j, :], MU[:], ALU.subtract, [HH.bs[j], MU.b], [D1.b])
                        tt(D1[:], D1[:], RS[:], ALU.mult, [D1.b, RS.b], [D1.b])
                        th = TH[j % 2]
                        act(th[:], D1[:], AF.Tanh, [D1.b, DER.b], [th.b], scale=DER[:, 16 + j:17 + j],
                            bias=DER[:, 20 + j:21 + j])
                        ts(U1[:], D1[:], PRM[:, O_LNW + j:O_LNW + j + 1], PRM[:, O_LNB + j:O_LNB + j + 1],
                           ALU.mult, ALU.add, [D1.b, PRM.b], [U1.b])
                        stt(U1[:], th[:], 1.0, U1[:], ALU.add, ALU.mult, [th.b, U1.b], [U1.b])
                        stt(YT[:, 8 + j, :], U1[:], 0.25, CG[:, j, :], ALU.mult, ALU.mult, [U1.b, CG.bs[j]], [YT.b])

                    def attn_q(g):
                        if g == 0:
                            wbh["a"] = w_acquire((l, "w_in", 9))
                        wb = wbh["a"]
                        pm = nextmm()
                        mm(pm[:], [(wb[:, kc, g * 128:(g + 1) * 128], XNT[:, kc, :]) for kc in range(8)],
                           [wb.b, XNT.b], [pm.b])
                        if g == 3:
                            w_release(wb)
                        cp("act", QT[:, g, :], pm[:], [pm.b], [QT.bs[g]])

                    def attn_g(c):
                        if c == 0:
                            wbh["a"] = w_acquire((l, "w_in", 10))
                        wb = wbh["a"]
                        pm = nextmm()
                        mm(pm[:], [(wb[:, kc, c * 128:(c + 1) * 128], XNT[:, kc, :]) for kc in range(8)],
                           [wb.b, XNT.b], [pm.b])
                        if c == 3:
                            w_release(wb)
                        th = TH[c % 2]
                        act(th[:], pm[:], AF.Tanh, [pm.b], [th.b], scale=0.5)
                        stt(AG[:, c, :], th[:], 1.0, pm[:], ALU.add, ALU.mult, [th.b, pm.b], [AG.bs[c]])

                    def attn_A(n, kh):
                        u = n * 2 + kh
                        nbk = tb * NT + n
                        qcols = slice(n * 128, (n + 1) * 128)
                        kbs = [1] if nbk == 0 else [0, 1]
                        prt = slice(kh * 64, (kh + 1) * 64)
                        for kb in kbs:
                            et = ET[(u % 2) * 2 + kb]
                            pm = nextmm()
                            mm(pm[:], [(KT[prt, (n + kb) * 128:(n + kb + 1) * 128], QT[prt, :, qcols])],
                               [KT.b] + QT.bs, [pm.b])
                            stt(SC[kb][:], pm[:], 0.125,
                                BT[:, kb, kh * 4:(kh + 1) * 4, :].rearrange("p h q -> p (h q)"),
                                ALU.mult, ALU.add, [pm.b, BT.b], [SC[kb].b])
                            act(et[:], SC[kb][:], AF.Exp, [SC[kb].b], [et.b])

                    def attn_B(n, kh):
                        u = n * 2 + kh
                        nbk = tb * NT + n
                        qcols = slice(n * 128, (n + 1) * 128)
                        kbs = [1] if nbk == 0 else [0, 1]
                        prt = slice(kh * 64, (kh + 1) * 64)
                        ets = {kb: ET[(u % 2) * 2 + kb] for kb in kbs}
                        pod = nextmm()
                        items = []
                        for jj in range(2):
                            for i, kb in enumerate(kbs):
                                items.append((pod[jj * 64:(jj + 1) * 64, 0:256], VT[:, n + kb, prt],
                                              ets[kb][:, jj * 256:(jj + 1) * 256], i == 0, i == len(kbs) - 1))
                        for jj in range(2):
                            for i, kb in enumerate(kbs):
                                items.append((pod[jj * 64:(jj + 1) * 64, 256:512], ONEB[:, 0:64],
                                              ets[kb][:, jj * 256:(jj + 1) * 256], i == 0, False))
                            items.append((pod[jj * 64:(jj + 1) * 64, 256:512], ONEB[0:1, 0:64], ONEB[0:1, 0:256],
                                          False, True))
                        mms(items, [VT.b, ONEB.b] + [ets[kb].b for kb in kbs], [pod.b])
                        self.S.op("dve", lambda e, o=RD[:], i=pod[:, 256:512]: e.reciprocal(out=o, in_=i),
                                  [pod.b], [RD.b])
                        tt(OT[:], pod[:, 0:256], RD[:], ALU.mult, [pod.b, RD.b], [OT.b])
                        stt(YT[:, 12 + kh * 2:14 + kh * 2, qcols], OT[:].rearrange("p (a b) -> p a b", a=2), 0.5,
                            AG[:, kh * 2:kh * 2 + 2, qcols], ALU.mult, ALU.mult,
                            [OT.b, AG.bs[kh * 2], AG.bs[kh * 2 + 1]], [YT.b])

                    extra.append(lambda: cp("dve", GLU[:, :, 0:30], GLU[:, :, TB:TB + 30], [GLU.b], [GLU.b]))
                    for j in range(4):
                        extra.append(lambda j=j: conf_pair(j))
                    for j in range(4):
                        extra.append(lambda j=j: conf_convA(j))
                        extra.append(lambda j=j: conf_gate(j))
                        extra.append(lambda j=j: conf_convB(j))
                    extra.append(conf_stats)
                    for j in range(4):
                        extra.append(lambda j=j: attn_q(j))
                        extra.append(lambda j=j: conf_ln(j))
                    for c in range(4):
                        extra.append(lambda c=c: attn_g(c))
                    units = [(n, kh) for n in range(NT) for kh in range(2)]
                    for i in range(len(units) + 1):
                        if i < len(units):
                            extra.append(lambda nk=units[i]: attn_A(*nk))
                        if i >= 1:
                            extra.append(lambda nk=units[i - 1]: attn_B(*nk))
                    nslots = 2 * NT
                    per = (len(extra) + nslots - 1) // nslots
                    epos = [0]

                    def run_extra(n):
                        for _ in range(n):
                            if epos[0] < len(extra):
                                extra[epos[0]]()
                                epos[0] += 1

                    stageB1(0)
                    stageB2(0)
                    stageC1a(0)
                    stageC1a(1)
                    stageC1b(0)
                    for k in range(4):
                        advance(k)
                    stageB1(1)
                    for t in range(NT):
                        for g in range(2):
                            stageD(t, g)
                            if t + 1 < NT:
                                advance(4 * (t + 1) + 2 * g)
                                advance(4 * (t + 1) + 2 * g + 1)
                            run_extra(per)
                            stageD2(t, g)
                        if t >= 1:
                            stageE2(t - 1)
                        if t + 1 < NT:
                            stageB2(t + 1)
                        stageE(t)
                        if t + 2 < NT:
                            stageB1(t + 2)
                    run_extra(2)
                    stageE2(NT - 1)
                    run_extra(len(extra))

                p5a = []
                ohold = {}

                def outproj(ch, t):
                    if t == 0:
                        ohold["w0"] = w_acquire((l, "w_out", ch * 2))
                        ohold["w1"] = w_acquire((l, "w_out", ch * 2 + 1))
                    w0, w1 = ohold["w0"], ohold["w1"]
                    pm = nextmm()
                    pairs = [(YT[:, kc, t * 128:(t + 1) * 128], (w0 if kc < 8 else w1)[:, kc % 8, :])
                             for kc in range(16)]
                    mm(pm[:], pairs, [YT.b, w0.b, w1.b], [pm.b])
                    cp("act", O[:, t, ch * 512:(ch + 1) * 512], pm[:], [pm.b], [O.bs[t]])
                    if t == NT - 1:
                        w_release(w0)
                        w_release(w1)
                for ch in range(2):
                    for t in range(NT):
                        p5a.append((lambda ch=ch, t=t: outproj(ch, t), []))
                if tb + 1 < NBLK:
                    run(interleave(p5a, make_p1(tb + 1)))
                else:
                    run(p5a)
                for v in p5views:
                    v.b.w, v.b.r = {}, {}
                    inherit(v.b, [YT.b])
                A = make_p5b(tb)
                if tb + 1 < NBLK:
                    Bn = make_p2a(tb + 1)
                    merged = interleave(A, Bn)
                    wplan.extend(keys_of(merged) + k_ssdx + k_p5a)
                else:
                    merged = A
                    wplan.extend(keys_of(merged))
                w_pump()
                run(merged)
                inherit(YT.b, [v.b for v in p5views])
        S.wait_all("sp", b_out)
        S.wait_all("pool", b_out)


def _col_perm():
    z0, xbc0, dt0, ci0, cg0, q0, k0, v0, ag0 = 0, 1024, 2560, 2576, 3600, 4112, 4624, 4752, 4880
    groups = []
    for g in range(3):
        groups.append(list(range(xbc0 + g * 512, xbc0 + (g + 1) * 512)))
    groups.append(list(range(z0, z0 + 512)))
    groups.append(list(range(z0 + 512, z0 + 1024)))
    groups.append(list(range(k0, k0 + 128)) + list(range(dt0, dt0 + 16)) + list(range(v0, v0 + 128)) + [-1] * 240)
    for g in range(2):
        cols = []
        for j in (2 * g, 2 * g + 1):
            cols += list(range(ci0 + j * 128, ci0 + (j + 1) * 128))
            cols += list(range(ci0 + 512 + j * 128, ci0 + 512 + (j + 1) * 128))
        groups.append(cols)
    groups.append(list(range(cg0, cg0 + 512)))
    cols = []
    for g in range(4):
        cols += list(range(q0 + g * 64, q0 + (g + 1) * 64))
        cols += list(range(q0 + (4 + g) * 64, q0 + (5 + g) * 64))
    groups.append(cols)
    cols = []
    for kh in range(2):
        for gl in range(2):
            h0, h1 = kh * 4 + gl, kh * 4 + 2 + gl
            cols += list(range(ag0 + h0 * 64, ag0 + (h0 + 1) * 64))
            cols += list(range(ag0 + h1 * 64, ag0 + (h1 + 1) * 64))
    groups.append(cols)
    assert len(groups) == 11 and all(len(g) == 512 for g in groups)
    return np.array(groups, dtype=np.int64)


def _row_perm_out():
    rows = list(range(0, 1536))
    a0 = 1536
    for kh in range(2):
        for gl in range(2):
            h0, h1 = kh * 4 + gl, kh * 4 + 2 + gl
            rows += list(range(a0 + h0 * 64, a0 + (h0 + 1) * 64))
            rows += list(range(a0 + h1 * 64, a0 + (h1 + 1) * 64))
    return np.array(rows, dtype=np.int64)


def _t5_bucket(d):
    d = np.maximum(d, 0)
    dm = np.maximum(d, 1).astype(np.float32)
    large = 16 + (np.log(dm / np.float32(16)) / np.float32(math.log(128 / 16)) * np.float32(16)).astype(np.int32)
    large = np.minimum(large, 31)
    return np.where(d < 16, d, large)


def _prep(inputs):
    f32 = np.float32
    w_in = np.asarray(inputs["w_in"], f32)
    perm = _col_perm()
    w_in_pad = np.concatenate([w_in, np.zeros((DEPTH, D, 1), f32)], axis=2)
    wi = w_in_pad[:, :, perm.reshape(-1)].reshape(DEPTH, 8, 128, 11, 512)
    wi = np.ascontiguousarray(wi.transpose(0, 3, 2, 1, 4)).reshape(DEPTH, 11, 128, 4096)
    w_out = np.asarray(inputs["w_out"], f32)[:, _row_perm_out(), :]
    wo = w_out.reshape(DEPTH, 2, 8, 128, 2, 512).transpose(0, 4, 1, 3, 2, 5)
    wo = np.ascontiguousarray(wo).reshape(DEPTH, 4, 128, 4096)
    wg = np.asarray(inputs["ple_gate"], f32).reshape(DEPTH, 8, 128, 2, 512).transpose(0, 3, 2, 1, 4)
    wg = np.ascontiguousarray(wg).reshape(DEPTH, 2, 128, 4096)
    wp = np.asarray(inputs["ple_proj"], f32).reshape(DEPTH, 2, 128, 1024).transpose(0, 2, 1, 3)
    wp = np.ascontiguousarray(wp).reshape(DEPTH, 1, 128, 2048)
    prm = np.zeros((DEPTH, 128, NP), f32)
    bc = lambda v: np.broadcast_to(np.asarray(v, f32)[None, :], (128, len(v)))
    for l in range(DEPTH):
        prm[l, :, O_PREW:O_PREW + 1024] = bc(inputs["pre_norm_w"][l])
        prm[l, :, O_SSDNW:O_SSDNW + 1024] = bc(inputs["ssd_norm_w"][l])
        prm[l, :, O_POSTW:O_POSTW + 1024] = bc(inputs["post_norm_w"][l])
        prm[l, :, O_D16:O_D16 + 16] = bc(inputs["ssd_d"][l])
        prm[l, :, O_DTB:O_DTB + 16] = bc(inputs["ssd_dt_bias"][l])
        prm[l, :, O_ALOG:O_ALOG + 16] = bc(inputs["ssd_a_log"][l])
        cw = np.asarray(inputs["ssd_conv_w"][l], f32)
        prm[l, :, O_CW:O_CW + 48] = cw.reshape(4, 12, 128).transpose(2, 1, 0).reshape(128, 48)
        prm[l, :, O_CB:O_CB + 12] = np.asarray(inputs["ssd_conv_b"][l], f32).reshape(12, 128).T
        dw = np.asarray(inputs["conf_dw_w"][l], f32)
        prm[l, :, O_DWW:O_DWW + 124] = dw.reshape(31, 4, 128).transpose(2, 1, 0).reshape(128, 124)
        prm[l, :, O_DWB:O_DWB + 4] = np.asarray(inputs["conf_dw_b"][l], f32).reshape(4, 128).T
        prm[l, :, O_LNW:O_LNW + 4] = np.asarray(inputs["conf_ln_w"][l], f32).reshape(4, 128).T
        prm[l, :, O_LNB:O_LNB + 4] = np.asarray(inputs["conf_ln_b"][l], f32).reshape(4, 128).T
        prm[l, :, O_SINK:O_SINK + 8] = bc(inputs["attn_sinks"][l])
    s = np.arange(128)[:, None, None]
    kb = np.arange(2)[None, :, None]
    q = np.arange(128)[None, None, :]
    dist = q + 128 - (kb * 128 + s)
    valid = (dist >= 0) & (dist < 128)
    bucket = _t5_bucket(np.clip(dist, 0, 127))
    rel = np.asarray(inputs["rel_bias"], f32)
    abias = np.ascontiguousarray(rel[bucket].transpose(0, 1, 3, 2)).reshape(128, 2 * 8 * 128)
    amask = np.where(valid, 0.0, NEG).astype(f32).reshape(128, 256)
    ident = np.eye(128, dtype=f32)
    tri = np.triu(np.ones((128, 128), f32))
    common = {"w_in": wi, "w_out": wo, "w_gate": wg, "w_proj": wp, "prm": prm, "c_abias": abias,
              "c_amask": amask, "c_ident": ident, "c_tri": tri,
              "c_mneg": np.where(np.arange(128)[:, None] <= np.arange(128)[None, :], 0.0, NEG).astype(f32)}
    return common


def _build(layers, debug=False):
    nc = bass.Bass("TRN2", target_bir_lowering=False)
    dram = {}
    dt = lambda name, shape, dtype, kind: nc.dram_tensor(name, shape, dtype, kind=kind).ap()
    dram["x"] = dt("x", [L_SEQ, D], F32, "ExternalInput")
    dram["p"] = dt("p", [DEPTH, L_SEQ, 256], F32, "ExternalInput")
    for name, ng, w in (("w_in", 11, 4096), ("w_out", 4, 4096), ("w_gate", 2, 4096), ("w_proj", 1, 2048)):
        dram[name] = dt(name, [DEPTH, ng, 128, w], F32, "ExternalInput")
        dram[name + "_s"] = dt(name + "_s", [DEPTH, ng, 128, w], BF16, "Internal")
    dram["prm"] = dt("prm", [DEPTH, 128, NP], F32, "ExternalInput")
    dram["c_abias"] = dt("c_abias", [128, 2048], F32, "ExternalInput")
    dram["c_amask"] = dt("c_amask", [128, 256], F32, "ExternalInput")
    dram["c_ident"] = dt("c_ident", [128, 128], F32, "ExternalInput")
    dram["c_tri"] = dt("c_tri", [128, 128], F32, "ExternalInput")
    dram["c_mneg"] = dt("c_mneg", [128, 128], F32, "ExternalInput")
    dram["hscr"] = dt("hscr", [L_SEQ, D], F32, "Internal")
    dram["out"] = dt("out", [L_SEQ, D], F32, "ExternalOutput")
    if debug:
        dram["dbg"] = dt("dbg", [128, 16, L_SEQ], F32, "ExternalOutput")
    with ExitStack() as st:
        prog = Prog(nc, st, layers, True, True)
        prog.build(dram)
    return nc


def kernel(**inputs):
    common = _prep(inputs)
    x = np.asarray(inputs["x"], np.float32)
    p = np.asarray(inputs["p"], np.float32)
    nb = x.shape[0]
    nc = _build(list(range(DEPTH)), debug=DEBUG)
    in_maps = []
    for b in range(nb):
        m = dict(common)
        m["x"] = np.ascontiguousarray(x[b])
        m["p"] = np.ascontiguousarray(p[:, b])
        in_maps.append(m)
    res = run_bass_kernel_spmd(nc, in_maps, core_ids=list(range(nb)))
    out = np.stack([np.asarray(r["out"], np.float32) for r in res.results], axis=0)
    if DEBUG:
        kernel.dbg = [np.asarray(r["dbg"]) for r in res.results]
    return out
```

```python
from contextlib import ExitStack
import math
import numpy as np
import concourse.bass as bass
import concourse.mybir as mybir
from concourse.bass_utils import run_bass_kernel_spmd

F32 = mybir.dt.float32
BF16 = mybir.dt.bfloat16
AF = mybir.ActivationFunctionType
ALU = mybir.AluOpType

L_SEQ = 2048
D = 1024
DEPTH = 2
TB = 512
NT = 4
NBLK = L_SEQ // TB
EPS = 1e-6
NEG = -30000.0

O_PREW, O_SSDNW, O_POSTW = 0, 1024, 2048
O_DTB, O_ALOG, O_D16 = 3072, 3088, 3104
O_CW, O_CB = 3120, 3168
O_DWW, O_DWB, O_LNW, O_LNB, O_SINK = 3180, 3304, 3308, 3312, 3316
NP = 3328

DEBUG = False


class Buf:
    __slots__ = ("w", "r")

    def __init__(self):
        self.w = {}
        self.r = {}


class Sched:
    ENGS = ("pe", "act", "dve", "pool", "sp")
    NSLOT = {"sp": 12, "pool": 44, "act": 8}

    def __init__(self, nc, stack):
        self.nc = nc
        self.eng = {"pe": nc.tensor, "act": nc.scalar, "dve": nc.vector,
                    "pool": nc.gpsimd, "sp": nc.sync}
        self.sems = {}
        self.cnt = {}
        self.known = {e: {} for e in self.ENGS}
        for e in self.ENGS:
            self.sems[e] = stack.enter_context(nc.semaphore("s_" + e))
            self.cnt[e] = 0
        self.slot_i = {}
        self.fence = {}
        for q, n in self.NSLOT.items():
            self.slot_i[q] = 0
            for k in range(n):
                key = "d_%s%d" % (q, k)
                self.sems[key] = stack.enter_context(nc.semaphore(key))
                self.cnt[key] = 0

    LOG = None

    def _push(self, engname, waits, fn, key, inc):
        if Sched.LOG is not None:
            Sched.LOG.append((engname, list(waits), key if fn is not None else None, inc))
        e = self.eng[engname]
        for k, v in waits:
            e.wait_ge(self.sems[k], v)
        if fn is not None:
            fn(e).then_inc(self.sems[key], inc)

    def barrier(self):
        engs = ("pe", "act", "dve", "pool")
        for e in engs:
            waits = []
            tgt = {k: self.cnt[k] for k in engs if k != e}
            tgt.update(self.fence)
            for k, v in tgt.items():
                if v > 0 and self.known[e].get(k, 0) < v:
                    self.known[e][k] = v
                    waits.append((k, v))
            self._push(e, waits, None, None, 0)

    def _deps(self, eng, reads, writes):
        need = {}

        def add(d):
            for k, v in d.items():
                if need.get(k, 0) < v:
                    need[k] = v
        for b in reads:
            add(b.w)
        for b in writes:
            add(b.w)
            add(b.r)
        out = []
        kn = self.known[eng]
        for k, v in need.items():
            if k == eng and eng == "pe":
                continue
            if kn.get(k, 0) >= v:
                continue
            kn[k] = v
            out.append((k, v))
        return out

    def op(self, eng, fn, reads=(), writes=()):
        waits = self._deps(eng, reads, writes)
        self.cnt[eng] += 1
        t = self.cnt[eng]
        for b in reads:
            if b.r.get(eng, 0) < t:
                b.r[eng] = t
        for b in writes:
            b.w = {eng: t}
            b.r = {}
        self._push(eng, waits, fn, eng, 1)

    def dma(self, q, out, in_, reads=(), writes=(), fence=None):
        if fence is None:
            fence = (q == "pool")
        waits = self._deps(q, reads, writes)
        i = self.slot_i[q]
        self.slot_i[q] += 1
        key = "d_%s%d" % (q, i % self.NSLOT[q])
        prev = self.cnt[key]
        if prev > 0 and self.known[q].get(key, 0) < prev:
            self.known[q][key] = prev
            waits.append((key, prev))
        self.cnt[key] += 16
        t = self.cnt[key]
        if fence:
            self.fence[key] = t
        for b in reads:
            if b.r.get(key, 0) < t:
                b.r[key] = t
        for b in writes:
            b.w = {key: t}
            b.r = {}
        self._push(q, waits, lambda e: e.dma_start(out=out, in_=in_), key, 16)

    def wait_all(self, eng, bufs):
        waits = self._deps(eng, bufs, ())
        self._push(eng, waits, None, None, 0)


class Tl:
    def __init__(self, t, nb=1):
        self.t = t
        self.bs = [Buf() for _ in range(nb)]

    @property
    def b(self):
        return self.bs[0]

    def __getitem__(self, k):
        return self.t[k]


class Prog:
    def __init__(self, nc, st, layers, first_in, last_out):
        self.nc = nc
        self.st = st
        self.S = Sched(nc, st)
        self.layers = layers

    _uid = [0]

    def sb(self, stack, name, shape, dt, nb=1):
        self._uid[0] += 1
        name = "%s_%d" % (name, self._uid[0])
        return Tl(stack.enter_context(self.nc.sbuf_tensor(name, shape, dt)), nb)

    def ps(self, stack, name, shape, dt):
        return Tl(stack.enter_context(self.nc.psum_tensor(name, shape, dt)))

    def mm(self, out, pairs, reads, writes, start=True, stop=True, sgc=False):
        def fn(e):
            n = len(pairs)
            ins = None
            for i, (l, r) in enumerate(pairs):
                kw = {}
                if sgc:
                    kw["skip_group_check"] = True
                ins = e.matmul(out, lhsT=l, rhs=r, start=(start and i == 0),
                               stop=(stop and i == n - 1), **kw)
            return ins
        self.S.op("pe", fn, reads, writes)

    def mms(self, items, reads, writes):
        def fn(e):
            ins = None
            for (o, l, r, s0, s1) in items:
                ins = e.matmul(o, lhsT=l, rhs=r, start=s0, stop=s1, skip_group_check=True)
            return ins
        self.S.op("pe", fn, reads, writes)

    def trs(self, items, ident, reads, writes):
        def fn(e):
            ins = None
            for (o, i) in items:
                ins = e.transpose(out=o, in_=i, identity=ident)
            return ins
        self.S.op("pe", fn, reads, writes)

    def act(self, out, in_, func, reads, writes, bias=None, scale=None, accum=None):
        kw = {}
        if bias is not None:
            kw["bias"] = bias
        if scale is not None:
            kw["scale"] = scale
        if accum is not None:
            kw["accum_out"] = accum
        self.S.op("act", lambda e: e.activation(out=out, in_=in_, func=func, **kw), reads, writes)

    def tt(self, out, in0, in1, op, reads, writes, eng="dve"):
        self.S.op(eng, lambda e: e.tensor_tensor(out=out, in0=in0, in1=in1, op=op), reads, writes)

    def ts(self, out, in0, s1, s2, op0, op1, reads, writes):
        if s2 is None:
            self.S.op("dve", lambda e: e.tensor_scalar(out=out, in0=in0, scalar1=s1, scalar2=None, op0=op0),
                      reads, writes)
        else:
            self.S.op("dve", lambda e: e.tensor_scalar(out=out, in0=in0, scalar1=s1, scalar2=s2,
                                                       op0=op0, op1=op1), reads, writes)

    def stt(self, out, in0, scalar, in1, op0, op1, reads, writes):
        self.S.op("dve", lambda e: e.scalar_tensor_tensor(out=out, in0=in0, scalar=scalar, in1=in1,
                                                          op0=op0, op1=op1), reads, writes)

    def cp(self, eng, out, in_, reads, writes):
        if eng == "act":
            self.S.op("act", lambda e: e.copy(out=out, in_=in_), reads, writes)
        else:
            self.S.op(eng, lambda e: e.tensor_copy(out=out, in_=in_), reads, writes)

    def memset(self, ap, val, writes, eng="dve"):
        self.S.op(eng, lambda e: e.memset(ap, val), (), writes)

    def rstd(self, out, ssq, inv_n, tmp, reads, writes):
        self.ts(tmp, ssq, inv_n, EPS, ALU.mult, ALU.add, reads, writes)
        self.act(tmp, tmp, AF.Ln, writes, writes)
        self.act(out, tmp, AF.Exp, writes, writes, scale=-0.5)

    def build(self, dram):
        nc, S, st = self.nc, self.S, self.st
        sb, ps = self.sb, self.ps
        mm, mms, trs, act, tt, ts, stt, cp, memset = (self.mm, self.mms, self.trs, self.act, self.tt,
                                                      self.ts, self.stt, self.cp, self.memset)
        b_scr = {}
        for l in self.layers:
            for name, ng in (("w_proj", 1), ("w_in", 11), ("w_out", 4), ("w_gate", 2)):
                for g in range(ng):
                    b_scr[(l, name, g)] = Buf()

        cast_q = []
        for l in self.layers:
            cast_q.append((l, "w_proj", 0))
            for g in range(11):
                cast_q.append((l, "w_in", g))
            for g in range(4):
                cast_q.append((l, "w_out", g))
            for g in range(2):
                cast_q.append((l, "w_gate", g))
        cast_pos = [0]

        def cast_some(n):
            for _ in range(n):
                if cast_pos[0] < len(cast_q):
                    l_, name, g = cast_q[cast_pos[0]]
                    cast_pos[0] += 1
                    S.dma("pool", dram[name + "_s"][l_, g], dram[name][l_, g], writes=[b_scr[(l_, name, g)]],
                          fence=False)

        IDF = sb(st, "IDF", [128, 128], F32)
        IDB = sb(st, "IDB", [128, 128], BF16)
        U = sb(st, "U", [128, 128], F32)
        ONEF = sb(st, "ONEF", [128, 128], F32)
        ONE512 = sb(st, "ONE512", [128, 128], F32)
        ONEB = sb(st, "ONEB", [128, 256], BF16)
        AMASK = sb(st, "AMASK", [128, 2, 128], F32)
        PRM = sb(st, "PRM", [128, NP], F32)
        DER = sb(st, "DER", [128, 64], F32)
        BT = sb(st, "BT", [128, 2, 8, 128], F32)
        WP = sb(st, "WP", [128, 2, 1024], BF16)
        WB = [sb(st, "WB%d" % i, [128, 8, 512], BF16) for i in range(3)]
        XNT = sb(st, "XNT", [128, 8, 512], BF16)
        YT = sb(st, "YT", [128, 16, 512], BF16)
        HS = sb(st, "HS", [128, 1024], F32)
        HSB = sb(st, "HSB", [128, 1024], BF16)
        TAIL = sb(st, "TAIL", [128, 12, 3], F32)
        GLU = sb(st, "GLU", [128, 4, 30 + TB], BF16)
        KT = sb(st, "KT", [128, 128 + TB], BF16)
        VT = sb(st, "VT", [128, 5, 128], BF16)
        JUNK = sb(st, "JUNK", [128, 1024], BF16)
        SM = sb(st, "SM", [128, 64], F32)
        PA = ps(st, "PA", [128, 512], F32)
        PB = ps(st, "PB", [128, 512], F32)
        PT = ps(st, "PT", [128, 1024], BF16)
        PS_ = ps(st, "PS", [128, 512], F32)
        PX0 = ps(st, "PX0", [128, 512], F32)
        PX1 = ps(st, "PX1", [128, 512], F32)
        PY = ps(st, "PY", [128, 512], F32)
        PO = ps(st, "PO", [128, 512], F32)

        def view(ap, bufs):
            v = Tl.__new__(Tl)
            v.t = ap
            v.bs = list(bufs)
            return v

        XR = [sb(st, "XR%d" % i, [128, 1024], F32) for i in range(2)]
        YN = sb(st, "YN", [128, 1024], BF16)
        XN2 = sb(st, "XN2", [128, 1024], BF16)
        XNs = [YN, XN2]
        PG = sb(st, "PG", [128, 4, 8 + TB], F32, nb=4)
        PRE = [view(PG[:, i, 0:3 + TB], [PG.bs[i]]) for i in range(2)]
        PREB = [view(PG[:, i, 0:264].bitcast(BF16)[:, 0:3 + TB], [PG.bs[i]]) for i in range(2)]
        DGS = [view(PG[:, i, 264:520].bitcast(BF16).rearrange("p (a b) -> p a b", a=4), [PG.bs[i]]) for i in range(2)]
        TAILB = sb(st, "TAILB", [128, 12, 4], BF16)
        ACC = [view(PG[:, 2 + i, 0:TB], [PG.bs[2 + i]]) for i in range(2)]
        TH = [sb(st, "TH%d" % i, [128, TB], F32) for i in range(2)]
        XBC = sb(st, "XBC", [128, 12, TB], BF16, nb=12)
        ZS = sb(st, "ZS", [128, NT, 1024], BF16, nb=NT)
        DTR = sb(st, "DTR", [128, NT, 16], F32, nb=NT)
        SD2 = sb(st, "SD2", [128, 9, 64], F32, nb=9)
        XG = sb(st, "XG", [128, 6, 1024], BF16, nb=6)
        XDT = [view(XG[:, i, :], [XG.bs[i]]) for i in range(2)]
        XD = view(XG[:, 2, :], [XG.bs[2]])
        XDS = view(XG[:, 3, :], [XG.bs[3]])
        MT4 = [view(XG[:, 4 + i // 2, (i % 2) * 512:(i % 2 + 1) * 512].rearrange("p (a b) -> p a b", a=4),
                    [Buf()]) for i in range(4)]
        DG = view(PG[:].rearrange("p a b -> p (a b)").bitcast(BF16)[:, 0:31 * 128].rearrange("p (a b) -> p a b", a=31),
                  PG.bs)
        BMT = sb(st, "BMT", [128, 2, 128], BF16)
        CBM = [sb(st, "CBM%d" % i, [128, 2, 128], F32) for i in range(2)]
        MNEG = sb(st, "MNEG", [128, 4, 128], BF16)
        T1 = sb(st, "T1", [128, 512], F32)
        D1 = T1
        Y = sb(st, "Y", [128, 1024], F32)
        EB = [sb(st, "EB%d" % i, [128, 4, 128], F32) for i in range(2)]
        MU = view(XR[0][:, 0:512], [XR[0].b])
        RS = view(XR[0][:, 512:1024], [XR[0].b])
        U1 = sb(st, "U1", [128, TB], F32)
        QT = sb(st, "QT", [128, 4, TB], BF16, nb=4)
        AG = sb(st, "AG", [128, 4, TB], BF16, nb=4)
        AB = [sb(st, "AB%d" % i, [128, 4, 128], F32) for i in range(3)]
        ACST = sb(st, "ACST", [64, 128], F32)
        b_acsd = [Buf(), Buf()]
        blkctr = [0]
        UB = sb(st, "UB", [128, 128], BF16)
        DAB = sb(st, "DAB", [128, 2, 64], BF16, nb=2)
        SC = [view(XR[1][:, i * 512:(i + 1) * 512], [XR[1].b]) for i in range(2)]
        ET = [sb(st, "ET%d" % i, [128, 512], BF16) for i in range(4)]
        RD = sb(st, "RD", [128, 256], F32)
        OT = sb(st, "OT", [128, 256], F32)
        O = sb(st, "O", [128, NT, 1024], F32, nb=NT)
        HH = view(O[:, 0:2, :].rearrange("p a (b c) -> p (a b) c", b=2), [O.bs[0], O.bs[0], O.bs[1], O.bs[1]])
        HQ = [view(O[:, 2, i * TB:(i + 1) * TB], [O.bs[2]]) for i in range(2)]
        CG = view(O[:, 3, :].bitcast(BF16).rearrange("p (a b) -> p a b", a=4), [O.bs[3]] * 4)
        PR = [sb(st, "PR%d" % i, [128, 256], F32) for i in range(2)]
        SM1s = [sb(st, "SM1_%d" % i, [128, 8], F32) for i in range(2)]
        SM5s = [sb(st, "SM5_%d" % i, [128, 8], F32) for i in range(2)]
        SMEs = [sb(st, "SME_%d" % i, [128, 8], F32) for i in range(2)]
        HT = view(YT[:, 0:8, :], [Buf()])
        HBs = [view(YT[:, 8 + 6 * i:10 + 6 * i, :].rearrange("p a b -> p (a b)"), [Buf()]) for i in range(2)]
        PTT = view(YT[:, 10:12, :], [Buf()])
        TG = [view(YT[:, 12:14, :].rearrange("p a b -> p (a b)").bitcast(F32), [Buf()])] * 2
        PB16s = [sb(st, "PB16a", [128, 256], BF16), sb(st, "PB16b", [128, 256], BF16)]
        p5views = [HT, HBs[0], HBs[1], PTT, TG[0]]
        PSB = view(PS_[:, 384:512].bitcast(BF16), [PS_.b])
        mmb = [PA, PB, PX0, PX1]
        mmi = [0]

        def nextmm():
            p = mmb[mmi[0] % 4]
            mmi[0] += 1
            return p

        S.dma("sp", IDF[:], dram["c_ident"], writes=[IDF.b])
        S.dma("sp", U[:], dram["c_tri"], writes=[U.b])
        S.dma("sp", AMASK[:], dram["c_amask"], writes=[AMASK.b])
        cp("dve", IDB[:], IDF[:], [IDF.b], [IDB.b])
        cp("dve", UB[:], U[:], [U.b], [UB.b])
        S.dma("sp", EB[0][:, 0, :], dram["c_mneg"], writes=[EB[0].b])
        cp("dve", MNEG[:], EB[0][:, 0, :].unsqueeze(1).to_broadcast([128, 4, 128]), [EB[0].b], [MNEG.b])
        memset(ONEF[:], 1.0, [ONEF.b])
        memset(ONE512[:], 1.0 / 512.0, [ONE512.b])
        memset(ONEB[:], 1.0, [ONEB.b])

        wplan = []
        wloaded = []
        wstate = ["free", "free", "free"]
        cast_idx = {k: i for i, k in enumerate(cast_q)}

        def w_pump():
            while wplan and len(wloaded) < 2 and "free" in wstate:
                key = wplan.pop(0)
                bi = wstate.index("free")
                wstate[bi] = "loaded"
                l_, name, g = key
                assert cast_idx[key] < cast_pos[0], ("cast not issued", key)
                S.dma("sp", WB[bi][:].rearrange("p a b -> p (a b)"), dram[name + "_s"][l_, g],
                      reads=[b_scr[key]], writes=[WB[bi].b])
                wloaded.append((key, bi))

        def w_acquire(key):
            cast_some(1)
            if not wloaded:
                w_pump()
            k2, bi = wloaded.pop(0)
            assert k2 == key, (k2, key)
            wstate[bi] = "held"
            w_pump()
            return WB[bi]

        def w_release(buf):
            bi = WB.index(buf)
            assert wstate[bi] == "held"
            wstate[bi] = "free"
            w_pump()

        def inherit(dst, srcs):
            for sb_ in srcs:
                for d in (sb_.r, sb_.w):
                    for k, v in d.items():
                        if dst.r.get(k, 0) < v:
                            dst.r[k] = v

        def interleave(A, B):
            out = []
            na, nb = len(A), len(B)
            ia = ib = 0
            while ia < na or ib < nb:
                if ib >= nb or (ia < na and ia * nb <= ib * na):
                    out.append(A[ia]); ia += 1
                else:
                    out.append(B[ib]); ib += 1
            return out

        b_hscr = [Buf() for _ in range(16)]
        b_out = [Buf() for _ in range(16)]

        for li, l in enumerate(self.layers):
            src = dram["x"] if li == 0 else dram["hscr"]
            dst = dram["out"] if li == len(self.layers) - 1 else dram["hscr"]
            b_src = None if li == 0 else b_hscr
            b_dst = b_out if li == len(self.layers) - 1 else b_hscr
            S.dma("sp", PRM[:], dram["prm"][l], writes=[PRM.b])
            S.dma("sp", BT[:].rearrange("p a h q -> p (a h q)"), dram["c_abias"], writes=[BT.b])
            preloaded = set()
            if li == 0:
                for t in range(2):
                    S.dma("sp", XR[t][:], src[t * 128:(t + 1) * 128, :], writes=[XR[t].b])
                    preloaded.add(t)
                S.wait_all("pool", [PRM.b, BT.b, IDF.b, U.b, AMASK.b, EB[0].b, XR[0].b, XR[1].b])
                cast_some(4)
            S.dma("sp", WP[:].rearrange("p a b -> p (a b)"), dram["w_proj_s"][l, 0],
                  reads=[b_scr[(l, "w_proj", 0)]], writes=[WP.b])
            ts(PRM[:, O_CW:O_CB + 12], PRM[:, O_CW:O_CB + 12], 0.5, None, ALU.mult, None, [PRM.b], [PRM.b])
            ts(PRM[:, O_DWW:O_DWW + 124], PRM[:, O_DWW:O_DWW + 124], 0.5, None, ALU.mult, None, [PRM.b], [PRM.b])
            act(DER[:, 0:16], PRM[:, O_ALOG:O_ALOG + 16], AF.Exp, [PRM.b], [DER.b])
            ts(DER[:, 0:16], DER[:, 0:16], -1.0, None, ALU.mult, None, [DER.b], [DER.b])
            ts(DER[:, 16:24], PRM[:, O_LNW:O_LNW + 8], 0.5, None, ALU.mult, None, [PRM.b], [DER.b])
            tt(BT[:], BT[:], AMASK[:].unsqueeze(2).to_broadcast([128, 2, 8, 128]), ALU.add, [BT.b, AMASK.b], [BT.b])
            tt(BT[:], BT[:], PRM[:, O_SINK:O_SINK + 8].unsqueeze(1).unsqueeze(3).to_broadcast([128, 2, 8, 128]),
               ALU.subtract, [BT.b, PRM.b], [BT.b])
            memset(HS[:], 0.0, [HS.b])
            memset(HSB[:], 0.0, [HSB.b])
            memset(TAIL[:], 0.0, [TAIL.b])
            memset(TAILB[:], 0.0, [TAILB.b])
            memset(GLU[:], 0.0, [GLU.b])
            memset(KT[:], 0.0, [KT.b])
            memset(VT[:], 0.0, [VT.b])

            def make_p1(tb):
                tok0 = tb * TB
                ths = []

                def tileA(t):
                    xr = XR[t % 2]
                    xn = XNs[t % 2]
                    r0 = tok0 + t * 128
                    if not (tb == 0 and t in preloaded):
                        S.dma("act", xr[:], src[r0:r0 + 128, :],
                              reads=([] if b_src is None else [b_src[tb * NT + t]]), writes=[xr.b])
                    SM1 = SM1s[t % 2]
                    memset(SM1[:, 0:1], 0.0, [SM1.b])
                    act(JUNK[:], xr[:], AF.Square, [xr.b, SM1.b], [JUNK.b, SM1.b], accum=SM1[:, 0:1])
                    self.rstd(SM1[:, 2:3], SM1[:, 0:1], 1.0 / D, SM1[:, 1:2], [SM1.b], [SM1.b])
                    stt(xn[:], xr[:], SM1[:, 2:3], PRM[:, O_PREW:O_PREW + 1024], ALU.mult, ALU.mult,
                        [xr.b, SM1.b, PRM.b], [xn.b])

                def tileB(t):
                    xn = XNs[t % 2]
                    trs([(PT[:, j * 128:(j + 1) * 128], xn[:, j * 128:(j + 1) * 128]) for j in range(8)],
                        IDB[:], [xn.b, IDB.b], [PT.b])
                    cp("act", XNT[:, :, t * 128:(t + 1) * 128], PT[:].rearrange("p (a b) -> p a b", a=8),
                       [PT.b], [XNT.b])
                order = [("A", 0), ("A", 1), ("B", 0), ("A", 2), ("B", 1), ("A", 3), ("B", 2), ("B", 3)]
                for kind, t in order:
                    ths.append(((lambda t=t: tileA(t)) if kind == "A" else (lambda t=t: tileB(t)), []))
                return ths

            def make_p2a(tb):
                ths = []
                hold = {}

                def xbcA(c):
                    if c % 4 == 0:
                        hold["w"] = w_acquire((l, "w_in", c // 4))
                    wb = hold["w"]
                    c4 = c % 4
                    pm = nextmm()
                    mm(pm[:], [(wb[:, kc, c4 * 128:(c4 + 1) * 128], XNT[:, kc, :]) for kc in range(8)],
                       [wb.b, XNT.b], [pm.b])
                    if c % 4 == 3:
                        w_release(wb)
                    pre, dgs = PREB[c % 2], DGS[c % 2]
                    tt(dgs[:], IDF[:].unsqueeze(1).to_broadcast([128, 4, 128]),
                       PRM[:, O_CW + c * 4:O_CW + c * 4 + 4].unsqueeze(2).to_broadcast([128, 4, 128]), ALU.mult,
                       [IDF.b, PRM.b], [dgs.b])
                    cp("dve", pre[:, 0:3], TAILB[:, c, 0:3], [TAILB.b], [pre.b])
                    cp("act", pre[:, 3:3 + TB], pm[:], [pm.b], [pre.b])
                    cp("dve", TAILB[:, c, 0:3], pre[:, TB:TB + 3], [pre.b], [TAILB.b])

                def xbcB(c):
                    pre, acc, th, dgs = PREB[c % 2], ACC[c % 2], TH[c % 2], DGS[c % 2]
                    pc = nextmm()
                    mm(pc[:], [(dgs[:, k, :], pre[:, k:k + TB]) for k in range(4)], [dgs.b, pre.b], [pc.b])
                    act(acc[:], pc[:], AF.Identity, [pc.b, PRM.b], [acc.b], bias=PRM[:, O_CB + c:O_CB + c + 1])
                    act(th[:], acc[:], AF.Tanh, [acc.b], [th.b])
                    stt(XBC[:, c, :], th[:], 1.0, acc[:], ALU.add, ALU.mult, [th.b, acc.b], [XBC.bs[c]])

                def z_tile(half, t):
                    if t == 0:
                        hold["w"] = w_acquire((l, "w_in", 3 + half))
                    wb = hold["w"]
                    pm = nextmm()
                    mm(pm[:], [(XNT[:, kc, t * 128:(t + 1) * 128], wb[:, kc, :]) for kc in range(8)],
                       [wb.b, XNT.b], [pm.b])
                    if t == NT - 1:
                        w_release(wb)
                    th = TH[t % 2]
                    act(th[:], pm[:], AF.Tanh, [pm.b], [th.b], scale=0.5)
                    stt(ZS[:, t, half * 512:(half + 1) * 512], th[:], 1.0, pm[:], ALU.add, ALU.mult,
                        [th.b, pm.b], [ZS.bs[t]])

                def kv():
                    wb = w_acquire((l, "w_in", 5))
                    cp("dve", KT[:, 0:128], KT[:, TB:TB + 128], [KT.b], [KT.b])
                    cp("dve", VT[:, 0, :], VT[:, 4, :], [VT.b], [VT.b])
                    pm = nextmm()
                    mm(pm[:], [(wb[:, kc, 0:128], XNT[:, kc, :]) for kc in range(8)], [wb.b, XNT.b], [pm.b])
                    cp("act", KT[:, 128:128 + TB], pm[:], [pm.b], [KT.b])
                    for t in range(NT):
                        pm = nextmm()
                        mm(pm[:, 0:144], [(XNT[:, kc, t * 128:(t + 1) * 128], wb[:, kc, 128:272]) for kc in range(8)],
                           [wb.b, XNT.b], [pm.b])
                        cp("dve", DTR[:, t, :], pm[:, 0:16], [pm.b], [DTR.bs[t]])
                        cp("act", VT[:, 1 + t, :], pm[:, 16:144], [pm.b], [VT.b])
                    w_release(wb)
                for c in range(13):
                    if c < 12:
                        ths.append((lambda c=c: xbcA(c), [(l, "w_in", c // 4)] if c % 4 == 0 else []))
                    if c >= 1:
                        ths.append((lambda c=c: xbcB(c - 1), []))
                for half in range(2):
                    for t in range(NT):
                        ths.append((lambda half=half, t=t: z_tile(half, t), [(l, "w_in", 3 + half)] if t == 0 else []))
                ths.append((kv, [(l, "w_in", 5)]))
                return ths

            def make_p5b(tb):
                tok0 = tb * TB
                ths = []
                hold = {}

                def post(t):
                    r0 = tok0 + t * 128
                    xr = XR[t % 2]
                    S.dma("act", xr[:], src[r0:r0 + 128, :],
                          reads=([] if b_src is None else [b_src[tb * NT + t]]), writes=[xr.b])
                    pr = PR[t % 2]
                    S.dma("act", pr[:], dram["p"][l, r0:r0 + 128, :], writes=[pr.b])
                    SM5 = SM5s[t % 2]
                    memset(SM5[:, 0:1], 0.0, [SM5.b])
                    act(JUNK[:], O[:, t, :], AF.Square, [O.bs[t], SM5.b], [JUNK.b, SM5.b], accum=SM5[:, 0:1])
                    self.rstd(SM5[:, 2:3], SM5[:, 0:1], 1.0 / D, SM5[:, 1:2], [SM5.b], [SM5.b])
                    stt(O[:, t, :], O[:, t, :], SM5[:, 2:3], PRM[:, O_POSTW:O_POSTW + 1024], ALU.mult, ALU.mult,
                        [O.bs[t], SM5.b, PRM.b], [O.bs[t]])
                    tt(O[:, t, :], O[:, t, :], xr[:], ALU.add, [O.bs[t], xr.b], [O.bs[t]])
                    HB, PB16 = HBs[t % 2], PB16s[t % 2]
                    cp("act", HB[:], O[:, t, :], [O.bs[t]], [HB.b])
                    cp("dve", PB16[:], pr[:], [pr.b], [PB16.b])

                def postB(t):
                    HB, PB16 = HBs[t % 2], PB16s[t % 2]
                    trs([(PT[:, j * 128:(j + 1) * 128], HB[:, j * 128:(j + 1) * 128]) for j in range(8)], IDB[:],
                        [HB.b, IDB.b], [PT.b])
                    cp("act", HT[:, :, t * 128:(t + 1) * 128], PT[:].rearrange("p (a b) -> p a b", a=8),
                       [PT.b], [HT.b])
                    trs([(PSB[:, j * 128:(j + 1) * 128], PB16[:, j * 128:(j + 1) * 128]) for j in range(2)], IDB[:],
                        [PB16.b, IDB.b], [PSB.b])
                    cp("act", PTT[:, :, t * 128:(t + 1) * 128], PSB[:].rearrange("p (a b) -> p a b", a=2),
                       [PSB.b], [PTT.b])

                def ple(ch, t):
                    if t == 0:
                        hold["w"] = w_acquire((l, "w_gate", ch))
                    wg = hold["w"]
                    tcols = slice(t * 128, (t + 1) * 128)
                    pg = nextmm()
                    mm(pg[:], [(HT[:, kc, tcols], wg[:, kc, :]) for kc in range(8)], [HT.b, wg.b], [pg.b])
                    if t == NT - 1:
                        w_release(wg)
                    pp = nextmm()
                    mm(pp[:], [(PTT[:, kc, tcols], WP[:, kc, ch * 512:(ch + 1) * 512]) for kc in range(2)],
                       [PTT.b, WP.b], [pp.b])
                    tg = TG[t % 2]
                    act(tg[:], pg[:], AF.Tanh, [pg.b], [tg.b], scale=0.5)
                    stt(tg[:], tg[:], 1.0, pp[:], ALU.add, ALU.mult, [tg.b, pp.b], [tg.b])
                    stt(O[:, t, ch * 512:(ch + 1) * 512], tg[:], 0.5, O[:, t, ch * 512:(ch + 1) * 512],
                        ALU.mult, ALU.add, [tg.b, O.bs[t]], [O.bs[t]])

                def store(t):
                    r0 = tok0 + t * 128
                    S.dma("pool", dst[r0:r0 + 128, :], O[:, t, :], reads=[O.bs[t]], writes=[b_dst[tb * NT + t]])
                ths.append((lambda: post(0), []))
                ths.append((lambda: post(1), []))
                ths.append((lambda: postB(0), []))
                ths.append((lambda: post(2), []))
                ths.append((lambda: postB(1), []))
                ths.append((lambda: ple(0, 0), [(l, "w_gate", 0)]))
                ths.append((lambda: post(3), []))
                ths.append((lambda: postB(2), []))
                ths.append((lambda: ple(0, 1), []))
                ths.append((lambda: postB(3), []))
                ths.append((lambda: ple(0, 2), []))
                ths.append((lambda: ple(0, 3), []))
                for t in range(NT):
                    ths.append((lambda t=t: ple(1, t), [(l, "w_gate", 1)] if t == 0 else []))
                for t in range(NT):
                    ths.append((lambda t=t: store(t), []))
                return ths

            def run(ths):
                for fn, _ in ths:
                    fn()

            def keys_of(ths):
                return [k for _, ws in ths for k in ws]

            k_ssdx = [(l, "w_in", g) for g in range(6, 11)]
            k_p5a = [(l, "w_out", g) for g in range(4)]
            first = make_p1(0) + make_p2a(0)
            wplan.extend(keys_of(first) + k_ssdx + k_p5a)
            w_pump()
            run(first)

            for tb in range(NBLK):
                tok0 = tb * TB
                if True:
                    row = lambda i: SD2[:, i, :]
                    r3 = lambda i: SD2[:, i, :].rearrange("p (t h) -> p t h", t=4)
                    sbf = SD2.bs
                    tt(r3(0), DTR[:], PRM[:, O_DTB:O_DTB + 16].unsqueeze(1).to_broadcast([128, 4, 16]), ALU.add,
                       DTR.bs + [PRM.b], [sbf[0]])
                    act(row(1), row(0), AF.Abs, [sbf[0]], [sbf[1]])
                    act(row(1), row(1), AF.Exp, [sbf[1]], [sbf[1]], scale=-1.0)
                    act(row(1), row(1), AF.Ln, [sbf[1]], [sbf[1]], bias=1.0)
                    stt(row(2), row(0), 0.0, row(1), ALU.max, ALU.add, [sbf[0], sbf[1]], [sbf[2]])
                    tt(r3(3), r3(2), DER[:, 0:16].unsqueeze(1).to_broadcast([128, 4, 16]), ALU.mult,
                       [sbf[2], DER.b], [sbf[3]])
                    cp("dve", DAB[:, 0, :], row(3), [sbf[3]], [DAB.bs[0]])
                    cp("dve", row(4), DAB[:, 0, :], [DAB.bs[0]], [sbf[4]])
                    tt(row(4), row(3), row(4), ALU.subtract, [sbf[3], sbf[4]], [sbf[4]])
                    cp("dve", DAB[:, 1, :], row(4), [sbf[4]], [DAB.bs[1]])
                    mms([(PS_[:, 256:320], UB[:], DAB[:, 0, :], True, False),
                         (PS_[:, 256:320], UB[:], DAB[:, 1, :], False, True),
                         (PS_[:, 320:384], ONEB[:, 0:128], DAB[:, 0, :], True, False),
                         (PS_[:, 320:384], ONEB[:, 0:128], DAB[:, 1, :], False, True)],
                        [UB.b, ONEB.b] + DAB.bs, [PS_.b])
                    ts(row(5), PS_[:, 256:320], -1.0, None, ALU.mult, None, [PS_.b], [sbf[5]])
                    cp("dve", row(4), PS_[:, 256:320], [PS_.b], [sbf[4]])
                    tt(row(6), PS_[:, 320:384], row(5), ALU.add, [PS_.b, sbf[5]], [sbf[6]])
                    act(row(6), row(6), AF.Exp, [sbf[6]], [sbf[6]])
                    act(row(7), PS_[:, 256:320], AF.Exp, [PS_.b], [sbf[7]])
                    act(row(8), PS_[:, 320:384], AF.Exp, [PS_.b], [sbf[8]])
                    slot = blkctr[0] % 2
                    blkctr[0] += 1
                    self.S.op("pe", lambda e: e.transpose(out=PS_[0:64, 0:128], in_=row(4), identity=IDF[:]),
                              [sbf[4], IDF.b], [PS_.b])
                    cp("act", ACST[:], PS_[0:64, 0:128], [PS_.b], [ACST.b])
                    S.dma("pool", dram["acs_d"][slot], ACST[:], reads=[ACST.b], writes=[b_acsd[slot]])

                    def stageB1(t):
                        cols = slice(t * 128, (t + 1) * 128)
                        mms([(PS_[:, g * 128:(g + 1) * 128], XBC[:, 8 + g, cols], XBC[:, 10 + g, cols], True, True)
                             for g in range(2)], [XBC.bs[8 + i] for i in range(4)], [PS_.b])
                        tt(CBM[t % 2][:], PS_[:, 0:256].rearrange("p (a b) -> p a b", a=2),
                           U[:].unsqueeze(1).to_broadcast([128, 2, 128]), ALU.mult, [PS_.b, U.b], [CBM[t % 2].b])

                    def stageB2(t):
                        cols = slice(t * 128, (t + 1) * 128)
                        xdt = XDT[t % 2]
                        trs([(PSB[:, g * 128:(g + 1) * 128], XBC[:, 8 + g, cols]) for g in range(2)], IDB[:],
                            [XBC.bs[8], XBC.bs[9], IDB.b], [PSB.b])
                        trs([(PT[:, j * 128:(j + 1) * 128], XBC[:, j, cols]) for j in range(8)], IDB[:],
                            [XBC.bs[j] for j in range(8)] + [IDB.b], [PT.b])
                        cp("dve", BMT[:].rearrange("p a b -> p (a b)"), PSB[:], [PSB.b], [BMT.b])
                        pt3 = PT[:].rearrange("p (h d) -> p h d", h=16)
                        tt(xdt[:].rearrange("p (h d) -> p h d", h=16), pt3,
                           r3(2)[:, t, :].unsqueeze(2).to_broadcast([128, 16, 64]), ALU.mult, [PT.b, sbf[2]], [xdt.b])
                        tt(XD[:].rearrange("p (h d) -> p h d", h=16), pt3,
                           PRM[:, O_D16:O_D16 + 16].unsqueeze(2).to_broadcast([128, 16, 64]), ALU.mult,
                           [PT.b, PRM.b], [XD.b])
                        tt(XDS[:].rearrange("p (h d) -> p h d", h=16), xdt[:].rearrange("p (h d) -> p h d", h=16),
                           r3(6)[:, t, :].unsqueeze(2).to_broadcast([128, 16, 64]), ALU.mult, [xdt.b, sbf[6]], [XDS.b])

                    def kdec(k):
                        return k // 4, (k // 2) % 2, k % 2

                    def stageC1(k):
                        t, g, hf = kdec(k)
                        r0 = t * 16 + g * 8 + hf * 4
                        ab = AB[k % 3]
                        S.dma("pool", ab[:], dram["acs_d"][slot, r0:r0 + 4, :].partition_broadcast(128),
                              reads=[b_acsd[slot]], writes=[ab.b])

                    def stageC2(k):
                        t, g, hf = kdec(k)
                        h0 = g * 8 + hf * 4
                        ab, ek, mtk = AB[k % 3], EB[k % 2], MT4[k % 4]
                        for i in range(4):
                            act(ek[:, i, :], ab[:, i, :], AF.Exp, [ab.b, sbf[5]], [ek.b],
                                bias=r3(5)[:, t, h0 + i:h0 + i + 1])
                        stt(mtk[:], ek[:], 1e30, CBM[t % 2][:, g, :].unsqueeze(1).to_broadcast([128, 4, 128]),
                            ALU.min, ALU.mult, [ek.b, CBM[t % 2].b], [mtk.b])

                    def advance(k):
                        if k + 2 < 4 * NT:
                            stageC1(k + 2)
                        stageC2(k)

                    def stageD(t, g):
                        cols = slice(t * 128, (t + 1) * 128)
                        gs = slice(g * 512, (g + 1) * 512)
                        py, po, t1 = PY, PO, T1
                        xdt = XDT[t % 2]
                        k0 = 4 * t + 2 * g
                        items = [(py[:], IDB[:], XD[:, gs], True, False)]
                        for hh in range(8):
                            h = g * 8 + hh
                            items.append((py[:, hh * 64:(hh + 1) * 64], MT4[(k0 + hh // 4) % 4][:, hh % 4, :],
                                          xdt[:, h * 64:(h + 1) * 64], False, True))
                        mms(items, [IDB.b, XD.b, MT4[k0 % 4].b, MT4[(k0 + 1) % 4].b, xdt.b], [py.b])
                        mm(po[:], [(XBC[:, 10 + g, cols], HSB[:, gs])], [XBC.bs[10 + g], HSB.b], [po.b])
                        tt(t1[:].rearrange("p (h d) -> p h d", h=8), po[:].rearrange("p (h d) -> p h d", h=8),
                           r3(7)[:, t, g * 8:(g + 1) * 8].unsqueeze(2).to_broadcast([128, 8, 64]), ALU.mult,
                           [po.b, sbf[7]], [t1.b])
                        tt(Y[:, gs], t1[:], py[:], ALU.add, [t1.b, py.b], [Y.b])

                    def stageD2(t, g):
                        gs = slice(g * 512, (g + 1) * 512)
                        po = PO
                        mm(po[:], [(BMT[:, g, :], XDS[:, gs])], [BMT.b, XDS.b], [po.b])
                        tt(HS[:, gs].rearrange("p (h d) -> p h d", h=8), HS[:, gs].rearrange("p (h d) -> p h d", h=8),
                           r3(8)[:, t, g * 8:(g + 1) * 8].unsqueeze(2).to_broadcast([128, 8, 64]), ALU.mult,
                           [HS.b, sbf[8]], [HS.b])
                        tt(HS[:, gs], HS[:, gs], po[:], ALU.add, [HS.b, po.b], [HS.b])
                        cp("act", HSB[:, gs], HS[:, gs], [HS.b], [HSB.b])

                    def stageE(t):
                        cols = slice(t * 128, (t + 1) * 128)
                        tt(Y[:], Y[:], ZS[:, t, :], ALU.mult, [Y.b, ZS.bs[t]], [Y.b])
                        SME = SMEs[t % 2]
                        memset(SME[:, 0:2], 0.0, [SME.b])
                        for g in range(2):
                            act(JUNK[:, 0:512], Y[:, g * 512:(g + 1) * 512], AF.Square, [Y.b, SME.b], [JUNK.b, SME.b],
                                accum=SME[:, g:g + 1])
                        ts(SME[:, 2:4], SME[:, 0:2], 1.0 / 512.0, 4.0 * EPS, ALU.mult, ALU.add, [SME.b], [SME.b])
                        act(SME[:, 2:4], SME[:, 2:4], AF.Ln, [SME.b], [SME.b])
                        act(SME[:, 4:6], SME[:, 2:4], AF.Exp, [SME.b], [SME.b], scale=-0.5)
                        for g in range(2):
                            gs = slice(g * 512, (g + 1) * 512)
                            stt(YN[:, gs], Y[:, gs], SME[:, 4 + g:5 + g], PRM[:, O_SSDNW + g * 512:O_SSDNW + (g + 1) * 512],
                                ALU.mult, ALU.mult, [Y.b, SME.b, PRM.b], [YN.b])

                    def stageE2(t):
                        cols = slice(t * 128, (t + 1) * 128)
                        trs([(PT[:, j * 128:(j + 1) * 128], YN[:, j * 128:(j + 1) * 128]) for j in range(8)], IDB[:],
                            [YN.b, IDB.b], [PT.b])
                        cp("dve", YT[:, 0:8, cols], PT[:].rearrange("p (a b) -> p a b", a=8), [PT.b], [YT.b])

                    extra = []
                    wbh = {}

                    def conf_pair(j):
                        if j % 2 == 0:
                            wbh["c"] = w_acquire((l, "w_in", 6 + j // 2))
                        wb = wbh["c"]
                        jj = j % 2
                        pa = nextmm()
                        mm(pa[:], [(wb[:, kc, (2 * jj) * 128:(2 * jj + 1) * 128], XNT[:, kc, :]) for kc in range(8)],
                           [wb.b, XNT.b], [pa.b])
                        pb = nextmm()
                        mm(pb[:], [(wb[:, kc, (2 * jj + 1) * 128:(2 * jj + 2) * 128], XNT[:, kc, :]) for kc in range(8)],
                           [wb.b, XNT.b], [pb.b])
                        if j % 2 == 1:
                            w_release(wb)
                        th = TH[j % 2]
                        act(th[:], pb[:], AF.Tanh, [pb.b], [th.b], scale=0.5)
                        stt(GLU[:, j, 30:30 + TB], th[:], 1.0, pa[:], ALU.add, ALU.mult, [th.b, pa.b], [GLU.b])

                    def conf_gate(j):
                        if j == 0:
                            wbh["c"] = w_acquire((l, "w_in", 8))
                        wb = wbh["c"]
                        pm = nextmm()
                        mm(pm[:], [(wb[:, kc, j * 128:(j + 1) * 128], XNT[:, kc, :]) for kc in range(8)],
                           [wb.b, XNT.b], [pm.b])
                        if j == 3:
                            w_release(wb)
                        th = TH[j % 2]
                        act(th[:], pm[:], AF.Tanh, [pm.b], [th.b], scale=0.5)
                        stt(CG[:, j, :], th[:], 1.0, pm[:], ALU.add, ALU.mult, [th.b, pm.b], [CG.bs[j]])

                    def conf_convA(j):
                        tt(DG[:], IDF[:].unsqueeze(1).to_broadcast([128, 31, 128]),
                           PRM[:, O_DWW + j * 31:O_DWW + (j + 1) * 31].unsqueeze(2).to_broadcast([128, 31, 128]),
                           ALU.mult, [IDF.b, PRM.b], DG.bs)

                    def conf_convB(j):
                        pm = nextmm()
                        mm(pm[:], [(DG[:, k, :], GLU[:, j, k:k + TB]) for k in range(31)], DG.bs + [GLU.b], [pm.b])
                        act(HH[:, j, :], pm[:], AF.Identity, [pm.b, PRM.b], [HH.bs[j]],
                            bias=PRM[:, O_DWB + j:O_DWB + j + 1])

                    def conf_stats():
                        PCA = nextmm()
                        PCB = nextmm()
                        mm(PCA[:], [(ONE512[:], HH[:, j, :]) for j in range(4)], [ONE512.b] + HH.bs, [PCA.b])
                        for j in range(4):
                            hq = HQ[j % 2]
                            act(hq[:], HH[:, j, :], AF.Square, [HH.bs[j]], [hq.b])
                            mm(PCB[:], [(ONE512[:], hq[:])], [ONE512.b, hq.b], [PCB.b], start=(j == 0), stop=(j == 3))
                        cp("act", MU[:], PCA[:], [PCA.b], [MU.b])
                        tt(D1[:], MU[:], MU[:], ALU.mult, [MU.b], [D1.b])
                        tt(RS[:], PCB[:], D1[:], ALU.subtract, [PCB.b, D1.b], [RS.b])
                        ts(RS[:], RS[:], EPS, None, ALU.add, None, [RS.b], [RS.b])
                        act(RS[:], RS[:], AF.Ln, [RS.b], [RS.b])
                        act(RS[:], RS[:], AF.Exp, [RS.b], [RS.b], scale=-0.5)

                    def conf_ln(j):
                        tt(D1[:], HH[:, j, :], MU[:], ALU.subtract, [HH.bs[j], MU.b], [D1.b])
                        tt(D1[:], D1[:], RS[:], ALU.mult, [D1.b, RS.b], [D1.b])
                        th = TH[j % 2]
                        act(th[:], D1[:], AF.Tanh, [D1.b, DER.b], [th.b], scale=DER[:, 16 + j:17 + j],
                            bias=DER[:, 20 + j:21 + j])
                        ts(U1[:], D1[:], PRM[:, O_LNW + j:O_LNW + j + 1], PRM[:, O_LNB + j:O_LNB + j + 1],
                           ALU.mult, ALU.add, [D1.b, PRM.b], [U1.b])
                        stt(U1[:], th[:], 1.0, U1[:], ALU.add, ALU.mult, [th.b, U1.b], [U1.b])
                        stt(YT[:, 8 + j, :], U1[:], 0.25, CG[:, j, :], ALU.mult, ALU.mult, [U1.b, CG.bs[j]], [YT.b])

                    def attn_q(g):
                        if g == 0:
                            wbh["a"] = w_acquire((l, "w_in", 9))
                        wb = wbh["a"]
                        pm = nextmm()
                        mm(pm[:], [(wb[:, kc, g * 128:(g + 1) * 128], XNT[:, kc, :]) for kc in range(8)],
                           [wb.b, XNT.b], [pm.b])
                        if g == 3:
                            w_release(wb)
                        cp("act", QT[:, g, :], pm[:], [pm.b], [QT.bs[g]])

                    def attn_g(c):
                        if c == 0:
                            wbh["a"] = w_acquire((l, "w_in", 10))
                        wb = wbh["a"]
                        pm = nextmm()
                        mm(pm[:], [(wb[:, kc, c * 128:(c + 1) * 128], XNT[:, kc, :]) for kc in range(8)],
                           [wb.b, XNT.b], [pm.b])
                        if c == 3:
                            w_release(wb)
                        th = TH[c % 2]
                        act(th[:], pm[:], AF.Tanh, [pm.b], [th.b], scale=0.5)
                        stt(AG[:, c, :], th[:], 1.0, pm[:], ALU.add, ALU.mult, [th.b, pm.b], [AG.bs[c]])

                    def attn_A(n, kh):
                        u = n * 2 + kh
                        nbk = tb * NT + n
                        qcols = slice(n * 128, (n + 1) * 128)
                        kbs = [1] if nbk == 0 else [0, 1]
                        prt = slice(kh * 64, (kh + 1) * 64)
                        for kb in kbs:
                            et = ET[(u % 2) * 2 + kb]
                            pm = nextmm()
                            mm(pm[:], [(KT[prt, (n + kb) * 128:(n + kb + 1) * 128], QT[prt, :, qcols])],
                               [KT.b] + QT.bs, [pm.b])
                            stt(SC[kb][:], pm[:], 0.125,
                                BT[:, kb, kh * 4:(kh + 1) * 4, :].rearrange("p h q -> p (h q)"),
                                ALU.mult, ALU.add, [pm.b, BT.b], [SC[kb].b])
                            act(et[:], SC[kb][:], AF.Exp, [SC[kb].b], [et.b])

                    def attn_B(n, kh):
                        u = n * 2 + kh
                        nbk = tb * NT + n
                        qcols = slice(n * 128, (n + 1) * 128)
                        kbs = [1] if nbk == 0 else [0, 1]
                        prt = slice(kh * 64, (kh + 1) * 64)
                        ets = {kb: ET[(u % 2) * 2 + kb] for kb in kbs}
                        pod = nextmm()
                        items = []
                        for jj in range(2):
                            for i, kb in enumerate(kbs):
                                items.append((pod[jj * 64:(jj + 1) * 64, 0:256], VT[:, n + kb, prt],
                                              ets[kb][:, jj * 256:(jj + 1) * 256], i == 0, i == len(kbs) - 1))
                        for jj in range(2):
                            for i, kb in enumerate(kbs):
                                items.append((pod[jj * 64:(jj + 1) * 64, 256:512], ONEB[:, 0:64],
                                              ets[kb][:, jj * 256:(jj + 1) * 256], i == 0, False))
                            items.append((pod[jj * 64:(jj + 1) * 64, 256:512], ONEB[0:1, 0:64], ONEB[0:1, 0:256],
                                          False, True))
                        mms(items, [VT.b, ONEB.b] + [ets[kb].b for kb in kbs], [pod.b])
                        self.S.op("dve", lambda e, o=RD[:], i=pod[:, 256:512]: e.reciprocal(out=o, in_=i),
                                  [pod.b], [RD.b])
                        tt(OT[:], pod[:, 0:256], RD[:], ALU.mult, [pod.b, RD.b], [OT.b])
                        stt(YT[:, 12 + kh * 2:14 + kh * 2, qcols], OT[:].rearrange("p (a b) -> p a b", a=2), 0.5,
                            AG[:, kh * 2:kh * 2 + 2, qcols], ALU.mult, ALU.mult,
                            [OT.b, AG.bs[kh * 2], AG.bs[kh * 2 + 1]], [YT.b])

                    extra.append(lambda: cp("dve", GLU[:, :, 0:30], GLU[:, :, TB:TB + 30], [GLU.b], [GLU.b]))
                    for j in range(4):
                        extra.append(lambda j=j: conf_pair(j))
                    for j in range(4):
                        extra.append(lambda j=j: conf_convA(j))
                        extra.append(lambda j=j: conf_gate(j))
                        extra.append(lambda j=j: conf_convB(j))
                    extra.append(conf_stats)
                    for j in range(4):
                        extra.append(lambda j=j: attn_q(j))
                        extra.append(lambda j=j: conf_ln(j))
                    for c in range(4):
                        extra.append(lambda c=c: attn_g(c))
                    units = [(n, kh) for n in range(NT) for kh in range(2)]
                    for i in range(len(units) + 1):
                        if i < len(units):
                            extra.append(lambda nk=units[i]: attn_A(*nk))
                        if i >= 1:
                            extra.append(lambda nk=units[i - 1]: attn_B(*nk))
                    nslots = 2 * NT
                    per = (len(extra) + nslots - 1) // nslots
                    epos = [0]

                    def run_extra(n):
                        for _ in range(n):
                            if epos[0] < len(extra):
                                extra[epos[0]]()
                                epos[0] += 1

                    stageB1(0)
                    stageB2(0)
                    stageC1(0)
                    stageC1(1)
                    for k in range(4):
                        advance(k)
                    stageB1(1)
                    for t in range(NT):
                        for g in range(2):
                            stageD(t, g)
                            if t + 1 < NT:
                                advance(4 * (t + 1) + 2 * g)
                                advance(4 * (t + 1) + 2 * g + 1)
                            run_extra(per)
                            stageD2(t, g)
                        if t >= 1:
                            stageE2(t - 1)
                        if t + 1 < NT:
                            stageB2(t + 1)
                        stageE(t)
                        if t + 2 < NT:
                            stageB1(t + 2)
                    run_extra(2)
                    stageE2(NT - 1)
                    run_extra(len(extra))

                p5a = []
                ohold = {}

                def outproj(ch, t):
                    if t == 0:
                        ohold["w0"] = w_acquire((l, "w_out", ch * 2))
                        ohold["w1"] = w_acquire((l, "w_out", ch * 2 + 1))
                    w0, w1 = ohold["w0"], ohold["w1"]
                    pm = nextmm()
                    pairs = [(YT[:, kc, t * 128:(t + 1) * 128], (w0 if kc < 8 else w1)[:, kc % 8, :])
                             for kc in range(16)]
                    mm(pm[:], pairs, [YT.b, w0.b, w1.b], [pm.b])
                    cp("act", O[:, t, ch * 512:(ch + 1) * 512], pm[:], [pm.b], [O.bs[t]])
                    if t == NT - 1:
                        w_release(w0)
                        w_release(w1)
                for ch in range(2):
                    for t in range(NT):
                        p5a.append((lambda ch=ch, t=t: outproj(ch, t), []))
                if tb + 1 < NBLK:
                    run(interleave(p5a, make_p1(tb + 1)))
                else:
                    run(p5a)
                for v in p5views:
                    v.b.w, v.b.r = {}, {}
                    inherit(v.b, [YT.b])
                A = make_p5b(tb)
                if tb + 1 < NBLK:
                    Bn = make_p2a(tb + 1)
                    merged = interleave(A, Bn)
                    wplan.extend(keys_of(merged) + k_ssdx + k_p5a)
                else:
                    merged = A
                    wplan.extend(keys_of(merged))
                w_pump()
                run(merged)
                inherit(YT.b, [v.b for v in p5views])
        S.wait_all("sp", b_out)
        S.wait_all("pool", b_out)


def _col_perm():
    z0, xbc0, dt0, ci0, cg0, q0, k0, v0, ag0 = 0, 1024, 2560, 2576, 3600, 4112, 4624, 4752, 4880
    groups = []
    for g in range(3):
        groups.append(list(range(xbc0 + g * 512, xbc0 + (g + 1) * 512)))
    groups.append(list(range(z0, z0 + 512)))
    groups.append(list(range(z0 + 512, z0 + 1024)))
    groups.append(list(range(k0, k0 + 128)) + list(range(dt0, dt0 + 16)) + list(range(v0, v0 + 128)) + [-1] * 240)
    for g in range(2):
        cols = []
        for j in (2 * g, 2 * g + 1):
            cols += list(range(ci0 + j * 128, ci0 + (j + 1) * 128))
            cols += list(range(ci0 + 512 + j * 128, ci0 + 512 + (j + 1) * 128))
        groups.append(cols)
    groups.append(list(range(cg0, cg0 + 512)))
    cols = []
    for g in range(4):
        cols += list(range(q0 + g * 64, q0 + (g + 1) * 64))
        cols += list(range(q0 + (4 + g) * 64, q0 + (5 + g) * 64))
    groups.append(cols)
    cols = []
    for kh in range(2):
        for gl in range(2):
            h0, h1 = kh * 4 + gl, kh * 4 + 2 + gl
            cols += list(range(ag0 + h0 * 64, ag0 + (h0 + 1) * 64))
            cols += list(range(ag0 + h1 * 64, ag0 + (h1 + 1) * 64))
    groups.append(cols)
    assert len(groups) == 11 and all(len(g) == 512 for g in groups)
    return np.array(groups, dtype=np.int64)


def _row_perm_out():
    rows = list(range(0, 1536))
    a0 = 1536
    for kh in range(2):
        for gl in range(2):
            h0, h1 = kh * 4 + gl, kh * 4 + 2 + gl
            rows += list(range(a0 + h0 * 64, a0 + (h0 + 1) * 64))
            rows += list(range(a0 + h1 * 64, a0 + (h1 + 1) * 64))
    return np.array(rows, dtype=np.int64)


def _t5_bucket(d):
    d = np.maximum(d, 0)
    dm = np.maximum(d, 1).astype(np.float32)
    large = 16 + (np.log(dm / np.float32(16)) / np.float32(math.log(128 / 16)) * np.float32(16)).astype(np.int32)
    large = np.minimum(large, 31)
    return np.where(d < 16, d, large)


def _prep(inputs):
    f32 = np.float32
    w_in = np.asarray(inputs["w_in"], f32)
    perm = _col_perm()
    w_in_pad = np.concatenate([w_in, np.zeros((DEPTH, D, 1), f32)], axis=2)
    wi = w_in_pad[:, :, perm.reshape(-1)].reshape(DEPTH, 8, 128, 11, 512)
    wi = np.ascontiguousarray(wi.transpose(0, 3, 2, 1, 4)).reshape(DEPTH, 11, 128, 4096)
    w_out = np.asarray(inputs["w_out"], f32)[:, _row_perm_out(), :]
    wo = w_out.reshape(DEPTH, 2, 8, 128, 2, 512).transpose(0, 4, 1, 3, 2, 5)
    wo = np.ascontiguousarray(wo).reshape(DEPTH, 4, 128, 4096)
    wg = np.asarray(inputs["ple_gate"], f32).reshape(DEPTH, 8, 128, 2, 512).transpose(0, 3, 2, 1, 4)
    wg = np.ascontiguousarray(wg).reshape(DEPTH, 2, 128, 4096)
    wp = np.asarray(inputs["ple_proj"], f32).reshape(DEPTH, 2, 128, 1024).transpose(0, 2, 1, 3)
    wp = np.ascontiguousarray(wp).reshape(DEPTH, 1, 128, 2048)
    prm = np.zeros((DEPTH, 128, NP), f32)
    bc = lambda v: np.broadcast_to(np.asarray(v, f32)[None, :], (128, len(v)))
    for l in range(DEPTH):
        prm[l, :, O_PREW:O_PREW + 1024] = bc(inputs["pre_norm_w"][l])
        prm[l, :, O_SSDNW:O_SSDNW + 1024] = bc(inputs["ssd_norm_w"][l])
        prm[l, :, O_POSTW:O_POSTW + 1024] = bc(inputs["post_norm_w"][l])
        prm[l, :, O_D16:O_D16 + 16] = bc(inputs["ssd_d"][l])
        prm[l, :, O_DTB:O_DTB + 16] = bc(inputs["ssd_dt_bias"][l])
        prm[l, :, O_ALOG:O_ALOG + 16] = bc(inputs["ssd_a_log"][l])
        cw = np.asarray(inputs["ssd_conv_w"][l], f32)
        prm[l, :, O_CW:O_CW + 48] = cw.reshape(4, 12, 128).transpose(2, 1, 0).reshape(128, 48)
        prm[l, :, O_CB:O_CB + 12] = np.asarray(inputs["ssd_conv_b"][l], f32).reshape(12, 128).T
        dw = np.asarray(inputs["conf_dw_w"][l], f32)
        prm[l, :, O_DWW:O_DWW + 124] = dw.reshape(31, 4, 128).transpose(2, 1, 0).reshape(128, 124)
        prm[l, :, O_DWB:O_DWB + 4] = np.asarray(inputs["conf_dw_b"][l], f32).reshape(4, 128).T
        prm[l, :, O_LNW:O_LNW + 4] = np.asarray(inputs["conf_ln_w"][l], f32).reshape(4, 128).T
        prm[l, :, O_LNB:O_LNB + 4] = np.asarray(inputs["conf_ln_b"][l], f32).reshape(4, 128).T
        prm[l, :, O_SINK:O_SINK + 8] = bc(inputs["attn_sinks"][l])
    s = np.arange(128)[:, None, None]
    kb = np.arange(2)[None, :, None]
    q = np.arange(128)[None, None, :]
    dist = q + 128 - (kb * 128 + s)
    valid = (dist >= 0) & (dist < 128)
    bucket = _t5_bucket(np.clip(dist, 0, 127))
    rel = np.asarray(inputs["rel_bias"], f32)
    abias = np.ascontiguousarray(rel[bucket].transpose(0, 1, 3, 2)).reshape(128, 2 * 8 * 128)
    amask = np.where(valid, 0.0, NEG).astype(f32).reshape(128, 256)
    ident = np.eye(128, dtype=f32)
    tri = np.triu(np.ones((128, 128), f32))
    common = {"w_in": wi, "w_out": wo, "w_gate": wg, "w_proj": wp, "prm": prm, "c_abias": abias,
              "c_amask": amask, "c_ident": ident, "c_tri": tri,
              "c_mneg": np.where(np.arange(128)[:, None] <= np.arange(128)[None, :], 0.0, NEG).astype(f32)}
    return common


def _build(layers, debug=False):
    nc = bass.Bass("TRN2", target_bir_lowering=False)
    dram = {}
    dt = lambda name, shape, dtype, kind: nc.dram_tensor(name, shape, dtype, kind=kind).ap()
    dram["x"] = dt("x", [L_SEQ, D], F32, "ExternalInput")
    dram["p"] = dt("p", [DEPTH, L_SEQ, 256], F32, "ExternalInput")
    for name, ng, w in (("w_in", 11, 4096), ("w_out", 4, 4096), ("w_gate", 2, 4096), ("w_proj", 1, 2048)):
        dram[name] = dt(name, [DEPTH, ng, 128, w], F32, "ExternalInput")
        dram[name + "_s"] = dt(name + "_s", [DEPTH, ng, 128, w], BF16, "Internal")
    dram["prm"] = dt("prm", [DEPTH, 128, NP], F32, "ExternalInput")
    dram["c_abias"] = dt("c_abias", [128, 2048], F32, "ExternalInput")
    dram["c_amask"] = dt("c_amask", [128, 256], F32, "ExternalInput")
    dram["c_ident"] = dt("c_ident", [128, 128], F32, "ExternalInput")
    dram["c_tri"] = dt("c_tri", [128, 128], F32, "ExternalInput")
    dram["c_mneg"] = dt("c_mneg", [128, 128], F32, "ExternalInput")
    dram["hscr"] = dt("hscr", [L_SEQ, D], F32, "Internal")
    dram["acs_d"] = dt("acs_d", [2, 64, 128], F32, "Internal")
    dram["out"] = dt("out", [L_SEQ, D], F32, "ExternalOutput")
    if debug:
        dram["dbg"] = dt("dbg", [128, 16, L_SEQ], F32, "ExternalOutput")
    with ExitStack() as st:
        prog = Prog(nc, st, layers, True, True)
        prog.build(dram)
    return nc


def kernel(**inputs):
    common = _prep(inputs)
    x = np.asarray(inputs["x"], np.float32)
    p = np.asarray(inputs["p"], np.float32)
    nb = x.shape[0]
    nc = _build(list(range(DEPTH)), debug=DEBUG)
    in_maps = []
    for b in range(nb):
        m = dict(common)
        m["x"] = np.ascontiguousarray(x[b])
        m["p"] = np.ascontiguousarray(p[:, b])
        in_maps.append(m)
    res = run_bass_kernel_spmd(nc, in_maps, core_ids=list(range(nb)))
    out = np.stack([np.asarray(r["out"], np.float32) for r in res.results], axis=0)
    if DEBUG:
        kernel.dbg = [np.asarray(r["dbg"]) for r in res.results]
    return out
```

```python
from contextlib import ExitStack
import math
import numpy as np
import concourse.bass as bass
import concourse.mybir as mybir
from concourse.bass_utils import run_bass_kernel_spmd

F32 = mybir.dt.float32
BF16 = mybir.dt.bfloat16
AF = mybir.ActivationFunctionType
ALU = mybir.AluOpType

L_SEQ = 2048
D = 1024
DEPTH = 2
TB = 512
NT = 4
NBLK = L_SEQ // TB
EPS = 1e-6
NEG = -30000.0

O_PREW, O_SSDNW, O_POSTW = 0, 1024, 2048
O_DTB, O_ALOG, O_D16 = 3072, 3088, 3104
O_CW, O_CB = 3120, 3168
O_DWW, O_DWB, O_LNW, O_LNB, O_SINK = 3180, 3304, 3308, 3312, 3316
NP = 3328

DEBUG = False


class Buf:
    __slots__ = ("w", "r")

    def __init__(self):
        self.w = {}
        self.r = {}


class Sched:
    ENGS = ("pe", "act", "dve", "pool", "sp")
    NSLOT = {"sp": 12, "pool": 44, "act": 8}

    def __init__(self, nc, stack):
        self.nc = nc
        self.eng = {"pe": nc.tensor, "act": nc.scalar, "dve": nc.vector,
                    "pool": nc.gpsimd, "sp": nc.sync}
        self.sems = {}
        self.cnt = {}
        self.known = {e: {} for e in self.ENGS}
        for e in self.ENGS:
            self.sems[e] = stack.enter_context(nc.semaphore("s_" + e))
            self.cnt[e] = 0
        self.slot_i = {}
        self.fence = {}
        for q, n in self.NSLOT.items():
            self.slot_i[q] = 0
            for k in range(n):
                key = "d_%s%d" % (q, k)
                self.sems[key] = stack.enter_context(nc.semaphore(key))
                self.cnt[key] = 0

    LOG = None

    def _push(self, engname, waits, fn, key, inc):
        if Sched.LOG is not None:
            Sched.LOG.append((engname, list(waits), key if fn is not None else None, inc))
        e = self.eng[engname]
        for k, v in waits:
            e.wait_ge(self.sems[k], v)
        if fn is not None:
            fn(e).then_inc(self.sems[key], inc)

    def barrier(self):
        engs = ("pe", "act", "dve", "pool")
        for e in engs:
            waits = []
            tgt = {k: self.cnt[k] for k in engs if k != e}
            tgt.update(self.fence)
            for k, v in tgt.items():
                if v > 0 and self.known[e].get(k, 0) < v:
                    self.known[e][k] = v
                    waits.append((k, v))
            self._push(e, waits, None, None, 0)

    def _deps(self, eng, reads, writes):
        need = {}

        def add(d):
            for k, v in d.items():
                if need.get(k, 0) < v:
                    need[k] = v
        for b in reads:
            add(b.w)
        for b in writes:
            add(b.w)
            add(b.r)
        out = []
        kn = self.known[eng]
        for k, v in need.items():
            if k == eng and eng == "pe":
                continue
            if kn.get(k, 0) >= v:
                continue
            kn[k] = v
            out.append((k, v))
        return out

    def op(self, eng, fn, reads=(), writes=()):
        waits = self._deps(eng, reads, writes)
        self.cnt[eng] += 1
        t = self.cnt[eng]
        for b in reads:
            if b.r.get(eng, 0) < t:
                b.r[eng] = t
        for b in writes:
            b.w = {eng: t}
            b.r = {}
        self._push(eng, waits, fn, eng, 1)

    def dma(self, q, out, in_, reads=(), writes=(), fence=None):
        if fence is None:
            fence = (q == "pool")
        waits = self._deps(q, reads, writes)
        i = self.slot_i[q]
        self.slot_i[q] += 1
        key = "d_%s%d" % (q, i % self.NSLOT[q])
        prev = self.cnt[key]
        if prev > 0 and self.known[q].get(key, 0) < prev:
            self.known[q][key] = prev
            waits.append((key, prev))
        self.cnt[key] += 16
        t = self.cnt[key]
        if fence:
            self.fence[key] = t
        for b in reads:
            if b.r.get(key, 0) < t:
                b.r[key] = t
        for b in writes:
            b.w = {key: t}
            b.r = {}
        self._push(q, waits, lambda e: e.dma_start(out=out, in_=in_), key, 16)

    def wait_all(self, eng, bufs):
        waits = self._deps(eng, bufs, ())
        self._push(eng, waits, None, None, 0)


class Tl:
    def __init__(self, t, nb=1):
        self.t = t
        self.bs = [Buf() for _ in range(nb)]

    @property
    def b(self):
        return self.bs[0]

    def __getitem__(self, k):
        return self.t[k]


class Prog:
    def __init__(self, nc, st, layers, first_in, last_out):
        self.nc = nc
        self.st = st
        self.S = Sched(nc, st)
        self.layers = layers

    _uid = [0]

    def sb(self, stack, name, shape, dt, nb=1):
        self._uid[0] += 1
        name = "%s_%d" % (name, self._uid[0])
        return Tl(stack.enter_context(self.nc.sbuf_tensor(name, shape, dt)), nb)

    def ps(self, stack, name, shape, dt):
        return Tl(stack.enter_context(self.nc.psum_tensor(name, shape, dt)))

    def mm(self, out, pairs, reads, writes, start=True, stop=True, sgc=False):
        def fn(e):
            n = len(pairs)
            ins = None
            for i, (l, r) in enumerate(pairs):
                kw = {}
                if sgc:
                    kw["skip_group_check"] = True
                ins = e.matmul(out, lhsT=l, rhs=r, start=(start and i == 0),
                               stop=(stop and i == n - 1), **kw)
            return ins
        self.S.op("pe", fn, reads, writes)

    def mms(self, items, reads, writes):
        def fn(e):
            ins = None
            for (o, l, r, s0, s1) in items:
                ins = e.matmul(o, lhsT=l, rhs=r, start=s0, stop=s1, skip_group_check=True)
            return ins
        self.S.op("pe", fn, reads, writes)

    def trs(self, items, ident, reads, writes):
        def fn(e):
            ins = None
            for (o, i) in items:
                ins = e.transpose(out=o, in_=i, identity=ident)
            return ins
        self.S.op("pe", fn, reads, writes)

    def act(self, out, in_, func, reads, writes, bias=None, scale=None, accum=None):
        kw = {}
        if bias is not None:
            kw["bias"] = bias
        if scale is not None:
            kw["scale"] = scale
        if accum is not None:
            kw["accum_out"] = accum
        self.S.op("act", lambda e: e.activation(out=out, in_=in_, func=func, **kw), reads, writes)

    def tt(self, out, in0, in1, op, reads, writes, eng="dve"):
        self.S.op(eng, lambda e: e.tensor_tensor(out=out, in0=in0, in1=in1, op=op), reads, writes)

    def ts(self, out, in0, s1, s2, op0, op1, reads, writes):
        if s2 is None:
            self.S.op("dve", lambda e: e.tensor_scalar(out=out, in0=in0, scalar1=s1, scalar2=None, op0=op0),
                      reads, writes)
        else:
            self.S.op("dve", lambda e: e.tensor_scalar(out=out, in0=in0, scalar1=s1, scalar2=s2,
                                                       op0=op0, op1=op1), reads, writes)

    def stt(self, out, in0, scalar, in1, op0, op1, reads, writes):
        self.S.op("dve", lambda e: e.scalar_tensor_tensor(out=out, in0=in0, scalar=scalar, in1=in1,
                                                          op0=op0, op1=op1), reads, writes)

    def cp(self, eng, out, in_, reads, writes):
        if eng == "act":
            self.S.op("act", lambda e: e.copy(out=out, in_=in_), reads, writes)
        else:
            self.S.op(eng, lambda e: e.tensor_copy(out=out, in_=in_), reads, writes)

    def memset(self, ap, val, writes, eng="dve"):
        self.S.op(eng, lambda e: e.memset(ap, val), (), writes)

    def rstd(self, out, ssq, inv_n, tmp, reads, writes):
        self.ts(tmp, ssq, inv_n, EPS, ALU.mult, ALU.add, reads, writes)
        self.act(tmp, tmp, AF.Ln, writes, writes)
        self.act(out, tmp, AF.Exp, writes, writes, scale=-0.5)

    def build(self, dram):
        nc, S, st = self.nc, self.S, self.st
        sb, ps = self.sb, self.ps
        mm, mms, trs, act, tt, ts, stt, cp, memset = (self.mm, self.mms, self.trs, self.act, self.tt,
                                                      self.ts, self.stt, self.cp, self.memset)
        b_scr = {}
        for l in self.layers:
            for name, ng in (("w_proj", 1), ("w_in", 11), ("w_out", 4), ("w_gate", 2)):
                for g in range(ng):
                    b_scr[(l, name, g)] = Buf()

        cast_q = []
        for l in self.layers:
            cast_q.append((l, "w_proj", 0))
            for g in range(11):
                cast_q.append((l, "w_in", g))
            for g in range(4):
                cast_q.append((l, "w_out", g))
            for g in range(2):
                cast_q.append((l, "w_gate", g))
        cast_pos = [0]

        def cast_some(n):
            for _ in range(n):
                if cast_pos[0] < len(cast_q):
                    l_, name, g = cast_q[cast_pos[0]]
                    cast_pos[0] += 1
                    S.dma("pool", dram[name + "_s"][l_, g], dram[name][l_, g], writes=[b_scr[(l_, name, g)]],
                          fence=False)

        IDF = sb(st, "IDF", [128, 128], F32)
        IDB = sb(st, "IDB", [128, 128], BF16)
        U = sb(st, "U", [128, 128], F32)
        ONEF = sb(st, "ONEF", [128, 128], F32)
        ONE512 = sb(st, "ONE512", [128, 128], F32)
        ONEB = sb(st, "ONEB", [128, 256], BF16)
        AMASK = sb(st, "AMASK", [128, 2, 128], F32)
        PRM = sb(st, "PRM", [128, NP], F32)
        DER = sb(st, "DER", [128, 64], F32)
        BT = sb(st, "BT", [128, 2, 8, 128], F32)
        WP = sb(st, "WP", [128, 2, 1024], BF16)
        WB = [sb(st, "WB%d" % i, [128, 8, 512], BF16) for i in range(3)]
        XNT = sb(st, "XNT", [128, 8, 512], BF16)
        YT = sb(st, "YT", [128, 16, 512], BF16)
        HS = sb(st, "HS", [128, 1024], F32)
        HSB = sb(st, "HSB", [128, 1024], BF16)
        HSbufs = [Buf(), Buf()]
        HSBbufs = [Buf(), Buf()]
        TAIL = sb(st, "TAIL", [128, 12, 3], F32)
        GLU = sb(st, "GLU", [128, 4, 30 + TB], BF16)
        KT = sb(st, "KT", [128, 128 + TB], BF16)
        VT = sb(st, "VT", [128, 5, 128], BF16)
        JUNK = sb(st, "JUNK", [128, 1024], BF16)
        SM = sb(st, "SM", [128, 64], F32)
        PA = ps(st, "PA", [128, 512], F32)
        PB = ps(st, "PB", [128, 512], F32)
        PT = ps(st, "PT", [128, 1024], BF16)
        PS_ = ps(st, "PS", [128, 512], F32)
        PX0 = ps(st, "PX0", [128, 512], F32)
        PX1 = ps(st, "PX1", [128, 512], F32)
        PY = ps(st, "PY", [128, 512], F32)
        PO = ps(st, "PO", [128, 512], F32)

        def view(ap, bufs):
            v = Tl.__new__(Tl)
            v.t = ap
            v.bs = list(bufs)
            return v

        XR = [sb(st, "XR%d" % i, [128, 1024], F32) for i in range(2)]
        YN = sb(st, "YN", [128, 1024], BF16)
        XN2 = sb(st, "XN2", [128, 1024], BF16)
        XNs = [YN, XN2]
        PG = sb(st, "PG", [128, 4, 8 + TB], F32, nb=4)
        PRE = [view(PG[:, i, 0:3 + TB], [PG.bs[i]]) for i in range(2)]
        PREB = [view(PG[:, i, 0:264].bitcast(BF16)[:, 0:3 + TB], [PG.bs[i]]) for i in range(2)]
        DGS = [view(PG[:, i, 264:520].bitcast(BF16).rearrange("p (a b) -> p a b", a=4), [PG.bs[i]]) for i in range(2)]
        TAILB = sb(st, "TAILB", [128, 12, 4], BF16)
        ACC = [view(PG[:, 2 + i, 0:TB], [PG.bs[2 + i]]) for i in range(2)]
        TH = [sb(st, "TH%d" % i, [128, TB], F32) for i in range(2)]
        XBC = sb(st, "XBC", [128, 12, TB], BF16, nb=12)
        ZS = sb(st, "ZS", [128, NT, 1024], BF16, nb=NT)
        DTR = sb(st, "DTR", [128, NT, 16], F32, nb=NT)
        SD2 = sb(st, "SD2", [128, 9, 64], F32, nb=9)
        XG = sb(st, "XG", [128, 6, 1024], BF16, nb=6)
        XDT = [view(XG[:, i, :], [XG.bs[i]]) for i in range(2)]
        XD = view(XG[:, 2, :], [XG.bs[2]])
        XDS = view(XG[:, 3, :], [XG.bs[3]])
        MT4 = [view(XG[:, 4 + i // 2, (i % 2) * 512:(i % 2 + 1) * 512].rearrange("p (a b) -> p a b", a=4),
                    [Buf()]) for i in range(4)]
        DG = view(PG[:].rearrange("p a b -> p (a b)").bitcast(BF16)[:, 0:31 * 128].rearrange("p (a b) -> p a b", a=31),
                  PG.bs)
        BMT = sb(st, "BMT", [128, 2, 128], BF16)
        CBM = [sb(st, "CBM%d" % i, [128, 2, 128], F32) for i in range(2)]
        MNEG = sb(st, "MNEG", [128, 4, 128], BF16)
        T1 = sb(st, "T1", [128, 512], F32)
        D1 = T1
        Y = sb(st, "Y", [128, 1024], F32)
        EB = [sb(st, "EB%d" % i, [128, 4, 128], F32) for i in range(2)]
        MU = view(XR[0][:, 0:512], [XR[0].b])
        RS = view(XR[0][:, 512:1024], [XR[0].b])
        U1 = sb(st, "U1", [128, TB], F32)
        QT = sb(st, "QT", [128, 4, TB], BF16, nb=4)
        AG = sb(st, "AG", [128, 4, TB], BF16, nb=4)
        AB = [sb(st, "AB%d" % i, [128, 4, 128], F32) for i in range(3)]
        ACST = sb(st, "ACST", [64, 128], F32)
        b_acsd = [Buf(), Buf()]
        blkctr = [0]
        UB = sb(st, "UB", [128, 128], BF16)
        DAB = sb(st, "DAB", [128, 2, 64], BF16, nb=2)
        SC = [view(XR[1][:, i * 512:(i + 1) * 512], [XR[1].b]) for i in range(2)]
        ET = [sb(st, "ET%d" % i, [128, 512], BF16) for i in range(4)]
        RD = sb(st, "RD", [128, 256], F32)
        OT = sb(st, "OT", [128, 256], F32)
        O = sb(st, "O", [128, NT, 1024], F32, nb=NT)
        HH = view(O[:, 0:2, :].rearrange("p a (b c) -> p (a b) c", b=2), [O.bs[0], O.bs[0], O.bs[1], O.bs[1]])
        HQ = [view(O[:, 2, i * TB:(i + 1) * TB], [O.bs[2]]) for i in range(2)]
        CG = view(O[:, 3, :].bitcast(BF16).rearrange("p (a b) -> p a b", a=4), [O.bs[3]] * 4)
        PR = [sb(st, "PR%d" % i, [128, 256], F32) for i in range(2)]
        SM1s = [sb(st, "SM1_%d" % i, [128, 8], F32) for i in range(2)]
        SM5s = [sb(st, "SM5_%d" % i, [128, 8], F32) for i in range(2)]
        SMEs = [sb(st, "SME_%d" % i, [128, 8], F32) for i in range(2)]
        HT = view(YT[:, 0:8, :], [Buf()])
        HBs = [view(YT[:, 8 + 6 * i:10 + 6 * i, :].rearrange("p a b -> p (a b)"), [Buf()]) for i in range(2)]
        PTT = view(YT[:, 10:12, :], [Buf()])
        TG = [view(YT[:, 12:14, :].rearrange("p a b -> p (a b)").bitcast(F32), [Buf()])] * 2
        PB16s = [sb(st, "PB16a", [128, 256], BF16), sb(st, "PB16b", [128, 256], BF16)]
        p5views = [HT, HBs[0], HBs[1], PTT, TG[0]]
        PSB = view(PS_[:, 384:512].bitcast(BF16), [PS_.b])
        mmb = [PA, PB, PX0, PX1]
        mmi = [0]

        def nextmm():
            p = mmb[mmi[0] % 4]
            mmi[0] += 1
            return p

        S.dma("sp", IDF[:], dram["c_ident"], writes=[IDF.b])
        S.dma("sp", U[:], dram["c_tri"], writes=[U.b])
        S.dma("sp", AMASK[:], dram["c_amask"], writes=[AMASK.b])
        cp("dve", IDB[:], IDF[:], [IDF.b], [IDB.b])
        cp("dve", UB[:], U[:], [U.b], [UB.b])
        S.dma("sp", EB[0][:, 0, :], dram["c_mneg"], writes=[EB[0].b])
        cp("dve", MNEG[:], EB[0][:, 0, :].unsqueeze(1).to_broadcast([128, 4, 128]), [EB[0].b], [MNEG.b])
        memset(ONEF[:], 1.0, [ONEF.b])
        memset(ONE512[:], 1.0 / 512.0, [ONE512.b])
        memset(ONEB[:], 1.0, [ONEB.b])

        wplan = []
        wloaded = []
        wstate = ["free", "free", "free"]
        cast_idx = {k: i for i, k in enumerate(cast_q)}

        def w_pump():
            while wplan and len(wloaded) < 2 and "free" in wstate:
                key = wplan.pop(0)
                bi = wstate.index("free")
                wstate[bi] = "loaded"
                l_, name, g = key
                assert cast_idx[key] < cast_pos[0], ("cast not issued", key)
                S.dma("sp", WB[bi][:].rearrange("p a b -> p (a b)"), dram[name + "_s"][l_, g],
                      reads=[b_scr[key]], writes=[WB[bi].b])
                wloaded.append((key, bi))

        def w_acquire(key):
            cast_some(1)
            if not wloaded:
                w_pump()
            k2, bi = wloaded.pop(0)
            assert k2 == key, (k2, key)
            wstate[bi] = "held"
            w_pump()
            return WB[bi]

        def w_release(buf):
            bi = WB.index(buf)
            assert wstate[bi] == "held"
            wstate[bi] = "free"
            w_pump()

        def inherit(dst, srcs):
            for sb_ in srcs:
                for d in (sb_.r, sb_.w):
                    for k, v in d.items():
                        if dst.r.get(k, 0) < v:
                            dst.r[k] = v

        def interleave(A, B):
            out = []
            na, nb = len(A), len(B)
            ia = ib = 0
            while ia < na or ib < nb:
                if ib >= nb or (ia < na and ia * nb <= ib * na):
                    out.append(A[ia]); ia += 1
                else:
                    out.append(B[ib]); ib += 1
            return out

        b_hscr = [Buf() for _ in range(16)]
        b_out = [Buf() for _ in range(16)]

        for li, l in enumerate(self.layers):
            src = dram["x"] if li == 0 else dram["hscr"]
            dst = dram["out"] if li == len(self.layers) - 1 else dram["hscr"]
            b_src = None if li == 0 else b_hscr
            b_dst = b_out if li == len(self.layers) - 1 else b_hscr
            S.dma("sp", PRM[:], dram["prm"][l], writes=[PRM.b])
            S.dma("sp", BT[:].rearrange("p a h q -> p (a h q)"), dram["c_abias"], writes=[BT.b])
            preloaded = set()
            if li == 0:
                for t in range(2):
                    S.dma("sp", XR[t][:], src[t * 128:(t + 1) * 128, :], writes=[XR[t].b])
                    preloaded.add(t)
                S.wait_all("pool", [PRM.b, BT.b, IDF.b, U.b, AMASK.b, EB[0].b, XR[0].b, XR[1].b])
                cast_some(4)
            S.dma("sp", WP[:].rearrange("p a b -> p (a b)"), dram["w_proj_s"][l, 0],
                  reads=[b_scr[(l, "w_proj", 0)]], writes=[WP.b])
            ts(PRM[:, O_CW:O_CB + 12], PRM[:, O_CW:O_CB + 12], 0.5, None, ALU.mult, None, [PRM.b], [PRM.b])
            ts(PRM[:, O_DWW:O_DWW + 124], PRM[:, O_DWW:O_DWW + 124], 0.5, None, ALU.mult, None, [PRM.b], [PRM.b])
            act(DER[:, 0:16], PRM[:, O_ALOG:O_ALOG + 16], AF.Exp, [PRM.b], [DER.b])
            ts(DER[:, 0:16], DER[:, 0:16], -1.0, None, ALU.mult, None, [DER.b], [DER.b])
            ts(DER[:, 16:24], PRM[:, O_LNW:O_LNW + 8], 0.5, None, ALU.mult, None, [PRM.b], [DER.b])
            tt(BT[:], BT[:], AMASK[:].unsqueeze(2).to_broadcast([128, 2, 8, 128]), ALU.add, [BT.b, AMASK.b], [BT.b])
            tt(BT[:], BT[:], PRM[:, O_SINK:O_SINK + 8].unsqueeze(1).unsqueeze(3).to_broadcast([128, 2, 8, 128]),
               ALU.subtract, [BT.b, PRM.b], [BT.b])
            memset(HS[:], 0.0, HSbufs)
            memset(HSB[:], 0.0, HSBbufs)
            memset(TAIL[:], 0.0, [TAIL.b])
            memset(TAILB[:], 0.0, [TAILB.b])
            memset(GLU[:], 0.0, [GLU.b])
            memset(KT[:], 0.0, [KT.b])
            memset(VT[:], 0.0, [VT.b])

            def make_p1(tb):
                tok0 = tb * TB
                ths = []

                def tileA(t):
                    xr = XR[t % 2]
                    xn = XNs[t % 2]
                    r0 = tok0 + t * 128
                    if not (tb == 0 and t in preloaded):
                        S.dma("act", xr[:], src[r0:r0 + 128, :],
                              reads=([] if b_src is None else [b_src[tb * NT + t]]), writes=[xr.b])
                    SM1 = SM1s[t % 2]
                    memset(SM1[:, 0:1], 0.0, [SM1.b])
                    act(JUNK[:], xr[:], AF.Square, [xr.b, SM1.b], [JUNK.b, SM1.b], accum=SM1[:, 0:1])
                    self.rstd(SM1[:, 2:3], SM1[:, 0:1], 1.0 / D, SM1[:, 1:2], [SM1.b], [SM1.b])
                    stt(xn[:], xr[:], SM1[:, 2:3], PRM[:, O_PREW:O_PREW + 1024], ALU.mult, ALU.mult,
                        [xr.b, SM1.b, PRM.b], [xn.b])

                def tileB(t):
                    xn = XNs[t % 2]
                    trs([(PT[:, j * 128:(j + 1) * 128], xn[:, j * 128:(j + 1) * 128]) for j in range(8)],
                        IDB[:], [xn.b, IDB.b], [PT.b])
                    cp("act", XNT[:, :, t * 128:(t + 1) * 128], PT[:].rearrange("p (a b) -> p a b", a=8),
                       [PT.b], [XNT.b])
                order = [("A", 0), ("A", 1), ("B", 0), ("A", 2), ("B", 1), ("A", 3), ("B", 2), ("B", 3)]
                for kind, t in order:
                    ths.append(((lambda t=t: tileA(t)) if kind == "A" else (lambda t=t: tileB(t)), []))
                return ths

            def make_p2a(tb):
                ths = []
                hold = {}

                def xbcA(c):
                    if c % 4 == 0:
                        hold["w"] = w_acquire((l, "w_in", c // 4))
                    wb = hold["w"]
                    c4 = c % 4
                    pm = nextmm()
                    mm(pm[:], [(wb[:, kc, c4 * 128:(c4 + 1) * 128], XNT[:, kc, :]) for kc in range(8)],
                       [wb.b, XNT.b], [pm.b])
                    if c % 4 == 3:
                        w_release(wb)
                    pre, dgs = PREB[c % 2], DGS[c % 2]
                    tt(dgs[:], IDF[:].unsqueeze(1).to_broadcast([128, 4, 128]),
                       PRM[:, O_CW + c * 4:O_CW + c * 4 + 4].unsqueeze(2).to_broadcast([128, 4, 128]), ALU.mult,
                       [IDF.b, PRM.b], [dgs.b])
                    cp("dve", pre[:, 0:3], TAILB[:, c, 0:3], [TAILB.b], [pre.b])
                    cp("act", pre[:, 3:3 + TB], pm[:], [pm.b], [pre.b])
                    cp("dve", TAILB[:, c, 0:3], pre[:, TB:TB + 3], [pre.b], [TAILB.b])

                def xbcB(c):
                    pre, acc, th, dgs = PREB[c % 2], ACC[c % 2], TH[c % 2], DGS[c % 2]
                    pc = nextmm()
                    mm(pc[:], [(dgs[:, k, :], pre[:, k:k + TB]) for k in range(4)], [dgs.b, pre.b], [pc.b])
                    act(acc[:], pc[:], AF.Identity, [pc.b, PRM.b], [acc.b], bias=PRM[:, O_CB + c:O_CB + c + 1])
                    act(th[:], acc[:], AF.Tanh, [acc.b], [th.b])
                    stt(XBC[:, c, :], th[:], 1.0, acc[:], ALU.add, ALU.mult, [th.b, acc.b], [XBC.bs[c]])

                def z_tile(half, t):
                    if t == 0:
                        hold["w"] = w_acquire((l, "w_in", 3 + half))
                    wb = hold["w"]
                    pm = nextmm()
                    mm(pm[:], [(XNT[:, kc, t * 128:(t + 1) * 128], wb[:, kc, :]) for kc in range(8)],
                       [wb.b, XNT.b], [pm.b])
                    if t == NT - 1:
                        w_release(wb)
                    th = TH[t % 2]
                    act(th[:], pm[:], AF.Tanh, [pm.b], [th.b], scale=0.5)
                    stt(ZS[:, t, half * 512:(half + 1) * 512], th[:], 1.0, pm[:], ALU.add, ALU.mult,
                        [th.b, pm.b], [ZS.bs[t]])

                def kv():
                    wb = w_acquire((l, "w_in", 5))
                    cp("dve", KT[:, 0:128], KT[:, TB:TB + 128], [KT.b], [KT.b])
                    cp("dve", VT[:, 0, :], VT[:, 4, :], [VT.b], [VT.b])
                    pm = nextmm()
                    mm(pm[:], [(wb[:, kc, 0:128], XNT[:, kc, :]) for kc in range(8)], [wb.b, XNT.b], [pm.b])
                    cp("act", KT[:, 128:128 + TB], pm[:], [pm.b], [KT.b])
                    for t in range(NT):
                        pm = nextmm()
                        mm(pm[:, 0:144], [(XNT[:, kc, t * 128:(t + 1) * 128], wb[:, kc, 128:272]) for kc in range(8)],
                           [wb.b, XNT.b], [pm.b])
                        cp("dve", DTR[:, t, :], pm[:, 0:16], [pm.b], [DTR.bs[t]])
                        cp("act", VT[:, 1 + t, :], pm[:, 16:144], [pm.b], [VT.b])
                    w_release(wb)
                for c in range(13):
                    if c < 12:
                        ths.append((lambda c=c: xbcA(c), [(l, "w_in", c // 4)] if c % 4 == 0 else []))
                    if c >= 1:
                        ths.append((lambda c=c: xbcB(c - 1), []))
                for half in range(2):
                    for t in range(NT):
                        ths.append((lambda half=half, t=t: z_tile(half, t), [(l, "w_in", 3 + half)] if t == 0 else []))
                ths.append((kv, [(l, "w_in", 5)]))
                return ths

            def make_p5b(tb):
                tok0 = tb * TB
                ths = []
                hold = {}

                def post(t):
                    r0 = tok0 + t * 128
                    xr = XR[t % 2]
                    S.dma("act", xr[:], src[r0:r0 + 128, :],
                          reads=([] if b_src is None else [b_src[tb * NT + t]]), writes=[xr.b])
                    pr = PR[t % 2]
                    S.dma("act", pr[:], dram["p"][l, r0:r0 + 128, :], writes=[pr.b])
                    SM5 = SM5s[t % 2]
                    memset(SM5[:, 0:1], 0.0, [SM5.b])
                    act(JUNK[:], O[:, t, :], AF.Square, [O.bs[t], SM5.b], [JUNK.b, SM5.b], accum=SM5[:, 0:1])
                    self.rstd(SM5[:, 2:3], SM5[:, 0:1], 1.0 / D, SM5[:, 1:2], [SM5.b], [SM5.b])
                    stt(O[:, t, :], O[:, t, :], SM5[:, 2:3], PRM[:, O_POSTW:O_POSTW + 1024], ALU.mult, ALU.mult,
                        [O.bs[t], SM5.b, PRM.b], [O.bs[t]])
                    tt(O[:, t, :], O[:, t, :], xr[:], ALU.add, [O.bs[t], xr.b], [O.bs[t]])
                    HB, PB16 = HBs[t % 2], PB16s[t % 2]
                    cp("act", HB[:], O[:, t, :], [O.bs[t]], [HB.b])
                    cp("dve", PB16[:], pr[:], [pr.b], [PB16.b])

                def postB(t):
                    HB, PB16 = HBs[t % 2], PB16s[t % 2]
                    trs([(PT[:, j * 128:(j + 1) * 128], HB[:, j * 128:(j + 1) * 128]) for j in range(8)], IDB[:],
                        [HB.b, IDB.b], [PT.b])
                    cp("act", HT[:, :, t * 128:(t + 1) * 128], PT[:].rearrange("p (a b) -> p a b", a=8),
                       [PT.b], [HT.b])
                    trs([(PSB[:, j * 128:(j + 1) * 128], PB16[:, j * 128:(j + 1) * 128]) for j in range(2)], IDB[:],
                        [PB16.b, IDB.b], [PSB.b])
                    cp("act", PTT[:, :, t * 128:(t + 1) * 128], PSB[:].rearrange("p (a b) -> p a b", a=2),
                       [PSB.b], [PTT.b])

                def ple(ch, t):
                    if t == 0:
                        hold["w"] = w_acquire((l, "w_gate", ch))
                    wg = hold["w"]
                    tcols = slice(t * 128, (t + 1) * 128)
                    pg = nextmm()
                    mm(pg[:], [(HT[:, kc, tcols], wg[:, kc, :]) for kc in range(8)], [HT.b, wg.b], [pg.b])
                    if t == NT - 1:
                        w_release(wg)
                    pp = nextmm()
                    mm(pp[:], [(PTT[:, kc, tcols], WP[:, kc, ch * 512:(ch + 1) * 512]) for kc in range(2)],
                       [PTT.b, WP.b], [pp.b])
                    tg = TG[t % 2]
                    act(tg[:], pg[:], AF.Tanh, [pg.b], [tg.b], scale=0.5)
                    stt(tg[:], tg[:], 1.0, pp[:], ALU.add, ALU.mult, [tg.b, pp.b], [tg.b])
                    stt(O[:, t, ch * 512:(ch + 1) * 512], tg[:], 0.5, O[:, t, ch * 512:(ch + 1) * 512],
                        ALU.mult, ALU.add, [tg.b, O.bs[t]], [O.bs[t]])

                def store(t):
                    r0 = tok0 + t * 128
                    S.dma("pool", dst[r0:r0 + 128, :], O[:, t, :], reads=[O.bs[t]], writes=[b_dst[tb * NT + t]])
                ths.append((lambda: post(0), []))
                ths.append((lambda: post(1), []))
                ths.append((lambda: postB(0), []))
                ths.append((lambda: post(2), []))
                ths.append((lambda: postB(1), []))
                ths.append((lambda: ple(0, 0), [(l, "w_gate", 0)]))
                ths.append((lambda: post(3), []))
                ths.append((lambda: postB(2), []))
                ths.append((lambda: ple(0, 1), []))
                ths.append((lambda: postB(3), []))
                ths.append((lambda: ple(0, 2), []))
                ths.append((lambda: ple(0, 3), []))
                for t in range(NT):
                    ths.append((lambda t=t: ple(1, t), [(l, "w_gate", 1)] if t == 0 else []))
                for t in range(NT):
                    ths.append((lambda t=t: store(t), []))
                return ths

            def run(ths):
                for fn, _ in ths:
                    fn()

            def keys_of(ths):
                return [k for _, ws in ths for k in ws]

            k_ssdx = [(l, "w_in", g) for g in range(6, 11)]
            k_p5a = [(l, "w_out", g) for g in range(4)]
            first = make_p1(0) + make_p2a(0)
            wplan.extend(keys_of(first) + k_ssdx + k_p5a)
            w_pump()
            run(first)

            for tb in range(NBLK):
                tok0 = tb * TB
                if True:
                    row = lambda i: SD2[:, i, :]
                    r3 = lambda i: SD2[:, i, :].rearrange("p (t h) -> p t h", t=4)
                    sbf = SD2.bs
                    tt(r3(0), DTR[:], PRM[:, O_DTB:O_DTB + 16].unsqueeze(1).to_broadcast([128, 4, 16]), ALU.add,
                       DTR.bs + [PRM.b], [sbf[0]])
                    act(row(1), row(0), AF.Abs, [sbf[0]], [sbf[1]])
                    act(row(1), row(1), AF.Exp, [sbf[1]], [sbf[1]], scale=-1.0)
                    act(row(1), row(1), AF.Ln, [sbf[1]], [sbf[1]], bias=1.0)
                    stt(row(2), row(0), 0.0, row(1), ALU.max, ALU.add, [sbf[0], sbf[1]], [sbf[2]])
                    tt(r3(3), r3(2), DER[:, 0:16].unsqueeze(1).to_broadcast([128, 4, 16]), ALU.mult,
                       [sbf[2], DER.b], [sbf[3]])
                    cp("dve", DAB[:, 0, :], row(3), [sbf[3]], [DAB.bs[0]])
                    cp("dve", row(4), DAB[:, 0, :], [DAB.bs[0]], [sbf[4]])
                    tt(row(4), row(3), row(4), ALU.subtract, [sbf[3], sbf[4]], [sbf[4]])
                    cp("dve", DAB[:, 1, :], row(4), [sbf[4]], [DAB.bs[1]])
                    mms([(PS_[:, 256:320], UB[:], DAB[:, 0, :], True, False),
                         (PS_[:, 256:320], UB[:], DAB[:, 1, :], False, True),
                         (PS_[:, 320:384], ONEB[:, 0:128], DAB[:, 0, :], True, False),
                         (PS_[:, 320:384], ONEB[:, 0:128], DAB[:, 1, :], False, True)],
                        [UB.b, ONEB.b] + DAB.bs, [PS_.b])
                    ts(row(5), PS_[:, 256:320], -1.0, None, ALU.mult, None, [PS_.b], [sbf[5]])
                    cp("dve", row(4), PS_[:, 256:320], [PS_.b], [sbf[4]])
                    tt(row(6), PS_[:, 320:384], row(5), ALU.add, [PS_.b, sbf[5]], [sbf[6]])
                    act(row(6), row(6), AF.Exp, [sbf[6]], [sbf[6]])
                    act(row(7), PS_[:, 256:320], AF.Exp, [PS_.b], [sbf[7]])
                    act(row(8), PS_[:, 320:384], AF.Exp, [PS_.b], [sbf[8]])
                    slot = blkctr[0] % 2
                    blkctr[0] += 1
                    self.S.op("pe", lambda e: e.transpose(out=PS_[0:64, 0:128], in_=row(4), identity=IDF[:]),
                              [sbf[4], IDF.b], [PS_.b])
                    cp("act", ACST[:], PS_[0:64, 0:128], [PS_.b], [ACST.b])
                    S.dma("pool", dram["acs_d"][slot], ACST[:], reads=[ACST.b], writes=[b_acsd[slot]])

                    def stageB1(t):
                        cols = slice(t * 128, (t + 1) * 128)
                        mms([(PS_[:, g * 128:(g + 1) * 128], XBC[:, 8 + g, cols], XBC[:, 10 + g, cols], True, True)
                             for g in range(2)], [XBC.bs[8 + i] for i in range(4)], [PS_.b])
                        tt(CBM[t % 2][:], PS_[:, 0:256].rearrange("p (a b) -> p a b", a=2),
                           U[:].unsqueeze(1).to_broadcast([128, 2, 128]), ALU.mult, [PS_.b, U.b], [CBM[t % 2].b])

                    def stageB2(t):
                        cols = slice(t * 128, (t + 1) * 128)
                        xdt = XDT[t % 2]
                        trs([(PSB[:, g * 128:(g + 1) * 128], XBC[:, 8 + g, cols]) for g in range(2)], IDB[:],
                            [XBC.bs[8], XBC.bs[9], IDB.b], [PSB.b])
                        trs([(PT[:, j * 128:(j + 1) * 128], XBC[:, j, cols]) for j in range(8)], IDB[:],
                            [XBC.bs[j] for j in range(8)] + [IDB.b], [PT.b])
                        cp("dve", BMT[:].rearrange("p a b -> p (a b)"), PSB[:], [PSB.b], [BMT.b])
                        pt3 = PT[:].rearrange("p (h d) -> p h d", h=16)
                        tt(xdt[:].rearrange("p (h d) -> p h d", h=16), pt3,
                           r3(2)[:, t, :].unsqueeze(2).to_broadcast([128, 16, 64]), ALU.mult, [PT.b, sbf[2]], [xdt.b])
                        tt(XD[:].rearrange("p (h d) -> p h d", h=16), pt3,
                           PRM[:, O_D16:O_D16 + 16].unsqueeze(2).to_broadcast([128, 16, 64]), ALU.mult,
                           [PT.b, PRM.b], [XD.b])
                        tt(XDS[:].rearrange("p (h d) -> p h d", h=16), xdt[:].rearrange("p (h d) -> p h d", h=16),
                           r3(6)[:, t, :].unsqueeze(2).to_broadcast([128, 16, 64]), ALU.mult, [xdt.b, sbf[6]], [XDS.b])

                    def kdec(k):
                        return k // 4, (k // 2) % 2, k % 2

                    def stageC1(k):
                        t, g, hf = kdec(k)
                        r0 = t * 16 + g * 8 + hf * 4
                        ab = AB[k % 3]
                        S.dma("pool", ab[:], dram["acs_d"][slot, r0:r0 + 4, :].partition_broadcast(128),
                              reads=[b_acsd[slot]], writes=[ab.b])

                    def stageC2(k):
                        t, g, hf = kdec(k)
                        h0 = g * 8 + hf * 4
                        ab, ek, mtk = AB[k % 3], EB[k % 2], MT4[k % 4]
                        for i in range(4):
                            act(ek[:, i, :], ab[:, i, :], AF.Exp, [ab.b, sbf[5]], [ek.b],
                                bias=r3(5)[:, t, h0 + i:h0 + i + 1])
                        stt(mtk[:], ek[:], 1e30, CBM[t % 2][:, g, :].unsqueeze(1).to_broadcast([128, 4, 128]),
                            ALU.min, ALU.mult, [ek.b, CBM[t % 2].b], [mtk.b])

                    def advance(k):
                        if k + 2 < 4 * NT:
                            stageC1(k + 2)
                        stageC2(k)

                    def stageD(t, g):
                        cols = slice(t * 128, (t + 1) * 128)
                        gs = slice(g * 512, (g + 1) * 512)
                        py, po, t1 = PY, PO, T1
                        xdt = XDT[t % 2]
                        k0 = 4 * t + 2 * g
                        items = [(py[:], IDB[:], XD[:, gs], True, False)]
                        for hh in range(8):
                            h = g * 8 + hh
                            items.append((py[:, hh * 64:(hh + 1) * 64], MT4[(k0 + hh // 4) % 4][:, hh % 4, :],
                                          xdt[:, h * 64:(h + 1) * 64], False, True))
                        mms(items, [IDB.b, XD.b, MT4[k0 % 4].b, MT4[(k0 + 1) % 4].b, xdt.b], [py.b])
                        mm(po[:], [(XBC[:, 10 + g, cols], HSB[:, gs])], [XBC.bs[10 + g], HSBbufs[g]], [po.b])
                        tt(t1[:].rearrange("p (h d) -> p h d", h=8), po[:].rearrange("p (h d) -> p h d", h=8),
                           r3(7)[:, t, g * 8:(g + 1) * 8].unsqueeze(2).to_broadcast([128, 8, 64]), ALU.mult,
                           [po.b, sbf[7]], [t1.b])
                        tt(Y[:, gs], t1[:], py[:], ALU.add, [t1.b, py.b], [Y.b])

                    def stageD2(t, g):
                        gs = slice(g * 512, (g + 1) * 512)
                        po = PO
                        hb_, hsb_ = HSbufs[g], HSBbufs[g]
                        tt(HS[:, gs].rearrange("p (h d) -> p h d", h=8), HS[:, gs].rearrange("p (h d) -> p h d", h=8),
                           r3(8)[:, t, g * 8:(g + 1) * 8].unsqueeze(2).to_broadcast([128, 8, 64]), ALU.mult,
                           [hb_, sbf[8]], [hb_], eng="pool")
                        mm(po[:], [(BMT[:, g, :], XDS[:, gs])], [BMT.b, XDS.b], [po.b])
                        tt(HSB[:, gs], HS[:, gs], po[:], ALU.add, [hb_, po.b], [hsb_])
                        tt(HS[:, gs], HS[:, gs], po[:], ALU.add, [hb_, po.b], [hb_])

                    def stageE(t):
                        cols = slice(t * 128, (t + 1) * 128)
                        tt(Y[:], Y[:], ZS[:, t, :], ALU.mult, [Y.b, ZS.bs[t]], [Y.b])
                        SME = SMEs[t % 2]
                        memset(SME[:, 0:2], 0.0, [SME.b])
                        for g in range(2):
                            act(JUNK[:, 0:512], Y[:, g * 512:(g + 1) * 512], AF.Square, [Y.b, SME.b], [JUNK.b, SME.b],
                                accum=SME[:, g:g + 1])
                        ts(SME[:, 2:4], SME[:, 0:2], 1.0 / 512.0, 4.0 * EPS, ALU.mult, ALU.add, [SME.b], [SME.b])
                        act(SME[:, 2:4], SME[:, 2:4], AF.Ln, [SME.b], [SME.b])
                        act(SME[:, 4:6], SME[:, 2:4], AF.Exp, [SME.b], [SME.b], scale=-0.5)
                        for g in range(2):
                            gs = slice(g * 512, (g + 1) * 512)
                            stt(YN[:, gs], Y[:, gs], SME[:, 4 + g:5 + g], PRM[:, O_SSDNW + g * 512:O_SSDNW + (g + 1) * 512],
                                ALU.mult, ALU.mult, [Y.b, SME.b, PRM.b], [YN.b])

                    def stageE2(t):
                        cols = slice(t * 128, (t + 1) * 128)
                        trs([(PT[:, j * 128:(j + 1) * 128], YN[:, j * 128:(j + 1) * 128]) for j in range(8)], IDB[:],
                            [YN.b, IDB.b], [PT.b])
                        cp("dve", YT[:, 0:8, cols], PT[:].rearrange("p (a b) -> p a b", a=8), [PT.b], [YT.b])

                    extra = []
                    wbh = {}

                    def conf_pair(j):
                        if j % 2 == 0:
                            wbh["c"] = w_acquire((l, "w_in", 6 + j // 2))
                        wb = wbh["c"]
                        jj = j % 2
                        pa = nextmm()
                        mm(pa[:], [(wb[:, kc, (2 * jj) * 128:(2 * jj + 1) * 128], XNT[:, kc, :]) for kc in range(8)],
                           [wb.b, XNT.b], [pa.b])
                        pb = nextmm()
                        mm(pb[:], [(wb[:, kc, (2 * jj + 1) * 128:(2 * jj + 2) * 128], XNT[:, kc, :]) for kc in range(8)],
                           [wb.b, XNT.b], [pb.b])
                        if j % 2 == 1:
                            w_release(wb)
                        th = TH[j % 2]
                        act(th[:], pb[:], AF.Tanh, [pb.b], [th.b], scale=0.5)
                        stt(GLU[:, j, 30:30 + TB], th[:], 1.0, pa[:], ALU.add, ALU.mult, [th.b, pa.b], [GLU.b])

                    def conf_gate(j):
                        if j == 0:
                            wbh["c"] = w_acquire((l, "w_in", 8))
                        wb = wbh["c"]
                        pm = nextmm()
                        mm(pm[:], [(wb[:, kc, j * 128:(j + 1) * 128], XNT[:, kc, :]) for kc in range(8)],
                           [wb.b, XNT.b], [pm.b])
                        if j == 3:
                            w_release(wb)
                        th = TH[j % 2]
                        act(th[:], pm[:], AF.Tanh, [pm.b], [th.b], scale=0.5)
                        stt(CG[:, j, :], th[:], 1.0, pm[:], ALU.add, ALU.mult, [th.b, pm.b], [CG.bs[j]])

                    def conf_convA(j):
                        tt(DG[:], IDF[:].unsqueeze(1).to_broadcast([128, 31, 128]),
                           PRM[:, O_DWW + j * 31:O_DWW + (j + 1) * 31].unsqueeze(2).to_broadcast([128, 31, 128]),
                           ALU.mult, [IDF.b, PRM.b], DG.bs)

                    def conf_convB(j):
                        pm = nextmm()
                        mm(pm[:], [(DG[:, k, :], GLU[:, j, k:k + TB]) for k in range(31)], DG.bs + [GLU.b], [pm.b])
                        act(HH[:, j, :], pm[:], AF.Identity, [pm.b, PRM.b], [HH.bs[j]],
                            bias=PRM[:, O_DWB + j:O_DWB + j + 1])

                    def conf_stats():
                        PCA = nextmm()
                        PCB = nextmm()
                        mm(PCA[:], [(ONE512[:], HH[:, j, :]) for j in range(4)], [ONE512.b] + HH.bs, [PCA.b])
                        for j in range(4):
                            hq = HQ[j % 2]
                            act(hq[:], HH[:, j, :], AF.Square, [HH.bs[j]], [hq.b])
                            mm(PCB[:], [(ONE512[:], hq[:])], [ONE512.b, hq.b], [PCB.b], start=(j == 0), stop=(j == 3))
                        cp("act", MU[:], PCA[:], [PCA.b], [MU.b])
                        tt(D1[:], MU[:], MU[:], ALU.mult, [MU.b], [D1.b])
                        tt(RS[:], PCB[:], D1[:], ALU.subtract, [PCB.b, D1.b], [RS.b])
                        ts(RS[:], RS[:], EPS, None, ALU.add, None, [RS.b], [RS.b])
                        act(RS[:], RS[:], AF.Ln, [RS.b], [RS.b])
                        act(RS[:], RS[:], AF.Exp, [RS.b], [RS.b], scale=-0.5)

                    def conf_ln(j):
                        tt(D1[:], HH[:, j, :], MU[:], ALU.subtract, [HH.bs[j], MU.b], [D1.b])
                        tt(D1[:], D1[:], RS[:], ALU.mult, [D1.b, RS.b], [D1.b])
                        th = TH[j % 2]
                        act(th[:], D1[:], AF.Tanh, [D1.b, DER.b], [th.b], scale=DER[:, 16 + j:17 + j],
                            bias=DER[:, 20 + j:21 + j])
                        ts(U1[:], D1[:], PRM[:, O_LNW + j:O_LNW + j + 1], PRM[:, O_LNB + j:O_LNB + j + 1],
                           ALU.mult, ALU.add, [D1.b, PRM.b], [U1.b])
                        stt(U1[:], th[:], 1.0, U1[:], ALU.add, ALU.mult, [th.b, U1.b], [U1.b])
                        stt(YT[:, 8 + j, :], U1[:], 0.25, CG[:, j, :], ALU.mult, ALU.mult, [U1.b, CG.bs[j]], [YT.b])

                    def attn_q(g):
                        if g == 0:
                            wbh["a"] = w_acquire((l, "w_in", 9))
                        wb = wbh["a"]
                        pm = nextmm()
                        mm(pm[:], [(wb[:, kc, g * 128:(g + 1) * 128], XNT[:, kc, :]) for kc in range(8)],
                           [wb.b, XNT.b], [pm.b])
                        if g == 3:
                            w_release(wb)
                        cp("act", QT[:, g, :], pm[:], [pm.b], [QT.bs[g]])

                    def attn_g(c):
                        if c == 0:
                            wbh["a"] = w_acquire((l, "w_in", 10))
                        wb = wbh["a"]
                        pm = nextmm()
                        mm(pm[:], [(wb[:, kc, c * 128:(c + 1) * 128], XNT[:, kc, :]) for kc in range(8)],
                           [wb.b, XNT.b], [pm.b])
                        if c == 3:
                            w_release(wb)
                        th = TH[c % 2]
                        act(th[:], pm[:], AF.Tanh, [pm.b], [th.b], scale=0.5)
                        stt(AG[:, c, :], th[:], 1.0, pm[:], ALU.add, ALU.mult, [th.b, pm.b], [AG.bs[c]])

                    def attn_A(n, kh):
                        u = n * 2 + kh
                        nbk = tb * NT + n
                        qcols = slice(n * 128, (n + 1) * 128)
                        kbs = [1] if nbk == 0 else [0, 1]
                        prt = slice(kh * 64, (kh + 1) * 64)
                        for kb in kbs:
                            et = ET[(u % 2) * 2 + kb]
                            pm = nextmm()
                            mm(pm[:], [(KT[prt, (n + kb) * 128:(n + kb + 1) * 128], QT[prt, :, qcols])],
                               [KT.b] + QT.bs, [pm.b])
                            stt(SC[kb][:], pm[:], 0.125,
                                BT[:, kb, kh * 4:(kh + 1) * 4, :].rearrange("p h q -> p (h q)"),
                                ALU.mult, ALU.add, [pm.b, BT.b], [SC[kb].b])
                            act(et[:], SC[kb][:], AF.Exp, [SC[kb].b], [et.b])

                    def attn_B(n, kh):
                        u = n * 2 + kh
                        nbk = tb * NT + n
                        qcols = slice(n * 128, (n + 1) * 128)
                        kbs = [1] if nbk == 0 else [0, 1]
                        prt = slice(kh * 64, (kh + 1) * 64)
                        ets = {kb: ET[(u % 2) * 2 + kb] for kb in kbs}
                        pod = nextmm()
                        items = []
                        for jj in range(2):
                            for i, kb in enumerate(kbs):
                                items.append((pod[jj * 64:(jj + 1) * 64, 0:256], VT[:, n + kb, prt],
                                              ets[kb][:, jj * 256:(jj + 1) * 256], i == 0, i == len(kbs) - 1))
                        for jj in range(2):
                            for i, kb in enumerate(kbs):
                                items.append((pod[jj * 64:(jj + 1) * 64, 256:512], ONEB[:, 0:64],
                                              ets[kb][:, jj * 256:(jj + 1) * 256], i == 0, False))
                            items.append((pod[jj * 64:(jj + 1) * 64, 256:512], ONEB[0:1, 0:64], ONEB[0:1, 0:256],
                                          False, True))
                        mms(items, [VT.b, ONEB.b] + [ets[kb].b for kb in kbs], [pod.b])
                        self.S.op("dve", lambda e, o=RD[:], i=pod[:, 256:512]: e.reciprocal(out=o, in_=i),
                                  [pod.b], [RD.b])
                        tt(OT[:], pod[:, 0:256], RD[:], ALU.mult, [pod.b, RD.b], [OT.b])
                        stt(YT[:, 12 + kh * 2:14 + kh * 2, qcols], OT[:].rearrange("p (a b) -> p a b", a=2), 0.5,
                            AG[:, kh * 2:kh * 2 + 2, qcols], ALU.mult, ALU.mult,
                            [OT.b, AG.bs[kh * 2], AG.bs[kh * 2 + 1]], [YT.b])

                    extra.append(lambda: cp("dve", GLU[:, :, 0:30], GLU[:, :, TB:TB + 30], [GLU.b], [GLU.b]))
                    for j in range(4):
                        extra.append(lambda j=j: conf_pair(j))
                    for j in range(4):
                        extra.append(lambda j=j: conf_convA(j))
                        extra.append(lambda j=j: conf_gate(j))
                        extra.append(lambda j=j: conf_convB(j))
                    extra.append(conf_stats)
                    for j in range(4):
                        extra.append(lambda j=j: attn_q(j))
                        extra.append(lambda j=j: conf_ln(j))
                    for c in range(4):
                        extra.append(lambda c=c: attn_g(c))
                    units = [(n, kh) for n in range(NT) for kh in range(2)]
                    for i in range(len(units) + 1):
                        if i < len(units):
                            extra.append(lambda nk=units[i]: attn_A(*nk))
                        if i >= 1:
                            extra.append(lambda nk=units[i - 1]: attn_B(*nk))
                    nslots = 2 * NT
                    per = (len(extra) + nslots - 1) // nslots
                    epos = [0]

                    def run_extra(n):
                        for _ in range(n):
                            if epos[0] < len(extra):
                                extra[epos[0]]()
                                epos[0] += 1

                    stageB1(0)
                    stageB2(0)
                    stageC1(0)
                    stageC1(1)
                    for k in range(4):
                        advance(k)
                    stageB1(1)
                    for t in range(NT):
                        for g in range(2):
                            stageD(t, g)
                            if t + 1 < NT:
                                advance(4 * (t + 1) + 2 * g)
                                advance(4 * (t + 1) + 2 * g + 1)
                            run_extra(per)
                            stageD2(t, g)
                        if t >= 1:
                            stageE2(t - 1)
                        if t + 1 < NT:
                            stageB2(t + 1)
                        stageE(t)
                        if t + 2 < NT:
                            stageB1(t + 2)
                    run_extra(2)
                    stageE2(NT - 1)
                    run_extra(len(extra))

                p5a = []
                ohold = {}

                def outproj(ch, t):
                    if t == 0:
                        ohold["w0"] = w_acquire((l, "w_out", ch * 2))
                        ohold["w1"] = w_acquire((l, "w_out", ch * 2 + 1))
                    w0, w1 = ohold["w0"], ohold["w1"]
                    pm = nextmm()
                    pairs = [(YT[:, kc, t * 128:(t + 1) * 128], (w0 if kc < 8 else w1)[:, kc % 8, :])
                             for kc in range(16)]
                    mm(pm[:], pairs, [YT.b, w0.b, w1.b], [pm.b])
                    cp("act", O[:, t, ch * 512:(ch + 1) * 512], pm[:], [pm.b], [O.bs[t]])
                    if t == NT - 1:
                        w_release(w0)
                        w_release(w1)
                for ch in range(2):
                    for t in range(NT):
                        p5a.append((lambda ch=ch, t=t: outproj(ch, t), []))
                if tb + 1 < NBLK:
                    run(interleave(p5a, make_p1(tb + 1)))
                else:
                    run(p5a)
                for v in p5views:
                    v.b.w, v.b.r = {}, {}
                    inherit(v.b, [YT.b])
                A = make_p5b(tb)
                if tb + 1 < NBLK:
                    Bn = make_p2a(tb + 1)
                    merged = interleave(A, Bn)
                    wplan.extend(keys_of(merged) + k_ssdx + k_p5a)
                else:
                    merged = A
                    wplan.extend(keys_of(merged))
                w_pump()
                run(merged)
                inherit(YT.b, [v.b for v in p5views])
        S.wait_all("sp", b_out)
        S.wait_all("pool", b_out)


def _col_perm():
    z0, xbc0, dt0, ci0, cg0, q0, k0, v0, ag0 = 0, 1024, 2560, 2576, 3600, 4112, 4624, 4752, 4880
    groups = []
    for g in range(3):
        groups.append(list(range(xbc0 + g * 512, xbc0 + (g + 1) * 512)))
    groups.append(list(range(z0, z0 + 512)))
    groups.append(list(range(z0 + 512, z0 + 1024)))
    groups.append(list(range(k0, k0 + 128)) + list(range(dt0, dt0 + 16)) + list(range(v0, v0 + 128)) + [-1] * 240)
    for g in range(2):
        cols = []
        for j in (2 * g, 2 * g + 1):
            cols += list(range(ci0 + j * 128, ci0 + (j + 1) * 128))
            cols += list(range(ci0 + 512 + j * 128, ci0 + 512 + (j + 1) * 128))
        groups.append(cols)
    groups.append(list(range(cg0, cg0 + 512)))
    cols = []
    for g in range(4):
        cols += list(range(q0 + g * 64, q0 + (g + 1) * 64))
        cols += list(range(q0 + (4 + g) * 64, q0 + (5 + g) * 64))
    groups.append(cols)
    cols = []
    for kh in range(2):
        for gl in range(2):
            h0, h1 = kh * 4 + gl, kh * 4 + 2 + gl
            cols += list(range(ag0 + h0 * 64, ag0 + (h0 + 1) * 64))
            cols += list(range(ag0 + h1 * 64, ag0 + (h1 + 1) * 64))
    groups.append(cols)
    assert len(groups) == 11 and all(len(g) == 512 for g in groups)
    return np.array(groups, dtype=np.int64)


def _row_perm_out():
    rows = list(range(0, 1536))
    a0 = 1536
    for kh in range(2):
        for gl in range(2):
            h0, h1 = kh * 4 + gl, kh * 4 + 2 + gl
            rows += list(range(a0 + h0 * 64, a0 + (h0 + 1) * 64))
            rows += list(range(a0 + h1 * 64, a0 + (h1 + 1) * 64))
    return np.array(rows, dtype=np.int64)


def _t5_bucket(d):
    d = np.maximum(d, 0)
    dm = np.maximum(d, 1).astype(np.float32)
    large = 16 + (np.log(dm / np.float32(16)) / np.float32(math.log(128 / 16)) * np.float32(16)).astype(np.int32)
    large = np.minimum(large, 31)
    return np.where(d < 16, d, large)


def _prep(inputs):
    f32 = np.float32
    w_in = np.asarray(inputs["w_in"], f32)
    perm = _col_perm()
    w_in_pad = np.concatenate([w_in, np.zeros((DEPTH, D, 1), f32)], axis=2)
    wi = w_in_pad[:, :, perm.reshape(-1)].reshape(DEPTH, 8, 128, 11, 512)
    wi = np.ascontiguousarray(wi.transpose(0, 3, 2, 1, 4)).reshape(DEPTH, 11, 128, 4096)
    w_out = np.asarray(inputs["w_out"], f32)[:, _row_perm_out(), :]
    wo = w_out.reshape(DEPTH, 2, 8, 128, 2, 512).transpose(0, 4, 1, 3, 2, 5)
    wo = np.ascontiguousarray(wo).reshape(DEPTH, 4, 128, 4096)
    wg = np.asarray(inputs["ple_gate"], f32).reshape(DEPTH, 8, 128, 2, 512).transpose(0, 3, 2, 1, 4)
    wg = np.ascontiguousarray(wg).reshape(DEPTH, 2, 128, 4096)
    wp = np.asarray(inputs["ple_proj"], f32).reshape(DEPTH, 2, 128, 1024).transpose(0, 2, 1, 3)
    wp = np.ascontiguousarray(wp).reshape(DEPTH, 1, 128, 2048)
    prm = np.zeros((DEPTH, 128, NP), f32)
    bc = lambda v: np.broadcast_to(np.asarray(v, f32)[None, :], (128, len(v)))
    for l in range(DEPTH):
        prm[l, :, O_PREW:O_PREW + 1024] = bc(inputs["pre_norm_w"][l])
        prm[l, :, O_SSDNW:O_SSDNW + 1024] = bc(inputs["ssd_norm_w"][l])
        prm[l, :, O_POSTW:O_POSTW + 1024] = bc(inputs["post_norm_w"][l])
        prm[l, :, O_D16:O_D16 + 16] = bc(inputs["ssd_d"][l])
        prm[l, :, O_DTB:O_DTB + 16] = bc(inputs["ssd_dt_bias"][l])
        prm[l, :, O_ALOG:O_ALOG + 16] = bc(inputs["ssd_a_log"][l])
        cw = np.asarray(inputs["ssd_conv_w"][l], f32)
        prm[l, :, O_CW:O_CW + 48] = cw.reshape(4, 12, 128).transpose(2, 1, 0).reshape(128, 48)
        prm[l, :, O_CB:O_CB + 12] = np.asarray(inputs["ssd_conv_b"][l], f32).reshape(12, 128).T
        dw = np.asarray(inputs["conf_dw_w"][l], f32)
        prm[l, :, O_DWW:O_DWW + 124] = dw.reshape(31, 4, 128).transpose(2, 1, 0).reshape(128, 124)
        prm[l, :, O_DWB:O_DWB + 4] = np.asarray(inputs["conf_dw_b"][l], f32).reshape(4, 128).T
        prm[l, :, O_LNW:O_LNW + 4] = np.asarray(inputs["conf_ln_w"][l], f32).reshape(4, 128).T
        prm[l, :, O_LNB:O_LNB + 4] = np.asarray(inputs["conf_ln_b"][l], f32).reshape(4, 128).T
        prm[l, :, O_SINK:O_SINK + 8] = bc(inputs["attn_sinks"][l])
    s = np.arange(128)[:, None, None]
    kb = np.arange(2)[None, :, None]
    q = np.arange(128)[None, None, :]
    dist = q + 128 - (kb * 128 + s)
    valid = (dist >= 0) & (dist < 128)
    bucket = _t5_bucket(np.clip(dist, 0, 127))
    rel = np.asarray(inputs["rel_bias"], f32)
    abias = np.ascontiguousarray(rel[bucket].transpose(0, 1, 3, 2)).reshape(128, 2 * 8 * 128)
    amask = np.where(valid, 0.0, NEG).astype(f32).reshape(128, 256)
    ident = np.eye(128, dtype=f32)
    tri = np.triu(np.ones((128, 128), f32))
    common = {"w_in": wi, "w_out": wo, "w_gate": wg, "w_proj": wp, "prm": prm, "c_abias": abias,
              "c_amask": amask, "c_ident": ident, "c_tri": tri,
              "c_mneg": np.where(np.arange(128)[:, None] <= np.arange(128)[None, :], 0.0, NEG).astype(f32)}
    return common


def _build(layers, debug=False):
    nc = bass.Bass("TRN2", target_bir_lowering=False)
    dram = {}
    dt = lambda name, shape, dtype, kind: nc.dram_tensor(name, shape, dtype, kind=kind).ap()
    dram["x"] = dt("x", [L_SEQ, D], F32, "ExternalInput")
    dram["p"] = dt("p", [DEPTH, L_SEQ, 256], F32, "ExternalInput")
    for name, ng, w in (("w_in", 11, 4096), ("w_out", 4, 4096), ("w_gate", 2, 4096), ("w_proj", 1, 2048)):
        dram[name] = dt(name, [DEPTH, ng, 128, w], F32, "ExternalInput")
        dram[name + "_s"] = dt(name + "_s", [DEPTH, ng, 128, w], BF16, "Internal")
    dram["prm"] = dt("prm", [DEPTH, 128, NP], F32, "ExternalInput")
    dram["c_abias"] = dt("c_abias", [128, 2048], F32, "ExternalInput")
    dram["c_amask"] = dt("c_amask", [128, 256], F32, "ExternalInput")
    dram["c_ident"] = dt("c_ident", [128, 128], F32, "ExternalInput")
    dram["c_tri"] = dt("c_tri", [128, 128], F32, "ExternalInput")
    dram["c_mneg"] = dt("c_mneg", [128, 128], F32, "ExternalInput")
    dram["hscr"] = dt("hscr", [L_SEQ, D], F32, "Internal")
    dram["acs_d"] = dt("acs_d", [2, 64, 128], F32, "Internal")
    dram["out"] = dt("out", [L_SEQ, D], F32, "ExternalOutput")
    if debug:
        dram["dbg"] = dt("dbg", [128, 16, L_SEQ], F32, "ExternalOutput")
    with ExitStack() as st:
        prog = Prog(nc, st, layers, True, True)
        prog.build(dram)
    return nc


def kernel(**inputs):
    common = _prep(inputs)
    x = np.asarray(inputs["x"], np.float32)
    p = np.asarray(inputs["p"], np.float32)
    nb = x.shape[0]
    nc = _build(list(range(DEPTH)), debug=DEBUG)
    in_maps = []
    for b in range(nb):
        m = dict(common)
        m["x"] = np.ascontiguousarray(x[b])
        m["p"] = np.ascontiguousarray(p[:, b])
        in_maps.append(m)
    res = run_bass_kernel_spmd(nc, in_maps, core_ids=list(range(nb)))
    out = np.stack([np.asarray(r["out"], np.float32) for r in res.results], axis=0)
    if DEBUG:
        kernel.dbg = [np.asarray(r["dbg"]) for r in res.results]
    return out
```

```python
from contextlib import ExitStack
import math
import numpy as np
import concourse.bass as bass
import concourse.mybir as mybir
from concourse.bass_utils import run_bass_kernel_spmd

F32 = mybir.dt.float32
BF16 = mybir.dt.bfloat16
AF = mybir.ActivationFunctionType
ALU = mybir.AluOpType

L_SEQ = 2048
D = 1024
DEPTH = 2
TB = 512
NT = 4
NBLK = L_SEQ // TB
EPS = 1e-6
NEG = -30000.0

O_PREW, O_SSDNW, O_POSTW = 0, 1024, 2048
O_DTB, O_ALOG, O_D16 = 3072, 3088, 3104
O_CW, O_CB = 3120, 3168
O_DWW, O_DWB, O_LNW, O_LNB, O_SINK = 3180, 3304, 3308, 3312, 3316
NP = 3328

DEBUG = False


class Buf:
    __slots__ = ("w", "r")

    def __init__(self):
        self.w = {}
        self.r = {}


class Sched:
    ENGS = ("pe", "act", "dve", "pool", "sp")
    NSLOT = {"sp": 12, "pool": 44, "act": 8}

    def __init__(self, nc, stack):
        self.nc = nc
        self.eng = {"pe": nc.tensor, "act": nc.scalar, "dve": nc.vector,
                    "pool": nc.gpsimd, "sp": nc.sync}
        self.sems = {}
        self.cnt = {}
        self.known = {e: {} for e in self.ENGS}
        for e in self.ENGS:
            self.sems[e] = stack.enter_context(nc.semaphore("s_" + e))
            self.cnt[e] = 0
        self.slot_i = {}
        self.fence = {}
        for q, n in self.NSLOT.items():
            self.slot_i[q] = 0
            for k in range(n):
                key = "d_%s%d" % (q, k)
                self.sems[key] = stack.enter_context(nc.semaphore(key))
                self.cnt[key] = 0

    LOG = None

    def _push(self, engname, waits, fn, key, inc):
        if Sched.LOG is not None:
            Sched.LOG.append((engname, list(waits), key if fn is not None else None, inc))
        e = self.eng[engname]
        for k, v in waits:
            e.wait_ge(self.sems[k], v)
        if fn is not None:
            fn(e).then_inc(self.sems[key], inc)

    def barrier(self):
        engs = ("pe", "act", "dve", "pool")
        for e in engs:
            waits = []
            tgt = {k: self.cnt[k] for k in engs if k != e}
            tgt.update(self.fence)
            for k, v in tgt.items():
                if v > 0 and self.known[e].get(k, 0) < v:
                    self.known[e][k] = v
                    waits.append((k, v))
            self._push(e, waits, None, None, 0)

    def _deps(self, eng, reads, writes):
        need = {}

        def add(d):
            for k, v in d.items():
                if need.get(k, 0) < v:
                    need[k] = v
        for b in reads:
            add(b.w)
        for b in writes:
            add(b.w)
            add(b.r)
        out = []
        kn = self.known[eng]
        for k, v in need.items():
            if k == eng and eng == "pe":
                continue
            if kn.get(k, 0) >= v:
                continue
            kn[k] = v
            out.append((k, v))
        return out

    def op(self, eng, fn, reads=(), writes=()):
        waits = self._deps(eng, reads, writes)
        self.cnt[eng] += 1
        t = self.cnt[eng]
        for b in reads:
            if b.r.get(eng, 0) < t:
                b.r[eng] = t
        for b in writes:
            b.w = {eng: t}
            b.r = {}
        self._push(eng, waits, fn, eng, 1)

    def dma(self, q, out, in_, reads=(), writes=(), fence=None):
        if fence is None:
            fence = (q == "pool")
        waits = self._deps(q, reads, writes)
        i = self.slot_i[q]
        self.slot_i[q] += 1
        key = "d_%s%d" % (q, i % self.NSLOT[q])
        prev = self.cnt[key]
        if prev > 0 and self.known[q].get(key, 0) < prev:
            self.known[q][key] = prev
            waits.append((key, prev))
        self.cnt[key] += 16
        t = self.cnt[key]
        if fence:
            self.fence[key] = t
        for b in reads:
            if b.r.get(key, 0) < t:
                b.r[key] = t
        for b in writes:
            b.w = {key: t}
            b.r = {}
        self._push(q, waits, lambda e: e.dma_start(out=out, in_=in_), key, 16)

    def wait_all(self, eng, bufs):
        waits = self._deps(eng, bufs, ())
        self._push(eng, waits, None, None, 0)


class Tl:
    def __init__(self, t, nb=1):
        self.t = t
        self.bs = [Buf() for _ in range(nb)]

    @property
    def b(self):
        return self.bs[0]

    def __getitem__(self, k):
        return self.t[k]


class Prog:
    def __init__(self, nc, st, layers, first_in, last_out):
        self.nc = nc
        self.st = st
        self.S = Sched(nc, st)
        self.layers = layers

    _uid = [0]

    def sb(self, stack, name, shape, dt, nb=1):
        self._uid[0] += 1
        name = "%s_%d" % (name, self._uid[0])
        return Tl(stack.enter_context(self.nc.sbuf_tensor(name, shape, dt)), nb)

    def ps(self, stack, name, shape, dt):
        return Tl(stack.enter_context(self.nc.psum_tensor(name, shape, dt)))

    def mm(self, out, pairs, reads, writes, start=True, stop=True, sgc=False):
        def fn(e):
            n = len(pairs)
            ins = None
            for i, (l, r) in enumerate(pairs):
                kw = {}
                if sgc:
                    kw["skip_group_check"] = True
                ins = e.matmul(out, lhsT=l, rhs=r, start=(start and i == 0),
                               stop=(stop and i == n - 1), **kw)
            return ins
        self.S.op("pe", fn, reads, writes)

    def mms(self, items, reads, writes):
        def fn(e):
            ins = None
            for (o, l, r, s0, s1) in items:
                ins = e.matmul(o, lhsT=l, rhs=r, start=s0, stop=s1, skip_group_check=True)
            return ins
        self.S.op("pe", fn, reads, writes)

    def trs(self, items, ident, reads, writes):
        def fn(e):
            ins = None
            for (o, i) in items:
                ins = e.transpose(out=o, in_=i, identity=ident)
            return ins
        self.S.op("pe", fn, reads, writes)

    def act(self, out, in_, func, reads, writes, bias=None, scale=None, accum=None):
        kw = {}
        if bias is not None:
            kw["bias"] = bias
        if scale is not None:
            kw["scale"] = scale
        if accum is not None:
            kw["accum_out"] = accum
        self.S.op("act", lambda e: e.activation(out=out, in_=in_, func=func, **kw), reads, writes)

    def tt(self, out, in0, in1, op, reads, writes, eng="dve"):
        self.S.op(eng, lambda e: e.tensor_tensor(out=out, in0=in0, in1=in1, op=op), reads, writes)

    def ts(self, out, in0, s1, s2, op0, op1, reads, writes):
        if s2 is None:
            self.S.op("dve", lambda e: e.tensor_scalar(out=out, in0=in0, scalar1=s1, scalar2=None, op0=op0),
                      reads, writes)
        else:
            self.S.op("dve", lambda e: e.tensor_scalar(out=out, in0=in0, scalar1=s1, scalar2=s2,
                                                       op0=op0, op1=op1), reads, writes)

    def stt(self, out, in0, scalar, in1, op0, op1, reads, writes):
        self.S.op("dve", lambda e: e.scalar_tensor_tensor(out=out, in0=in0, scalar=scalar, in1=in1,
                                                          op0=op0, op1=op1), reads, writes)

    def cp(self, eng, out, in_, reads, writes):
        if eng == "act":
            self.S.op("act", lambda e: e.copy(out=out, in_=in_), reads, writes)
        else:
            self.S.op(eng, lambda e: e.tensor_copy(out=out, in_=in_), reads, writes)

    def memset(self, ap, val, writes, eng="dve"):
        self.S.op(eng, lambda e: e.memset(ap, val), (), writes)

    def rstd(self, out, ssq, inv_n, tmp, reads, writes):
        self.ts(tmp, ssq, inv_n, EPS, ALU.mult, ALU.add, reads, writes)
        self.act(tmp, tmp, AF.Ln, writes, writes)
        self.act(out, tmp, AF.Exp, writes, writes, scale=-0.5)

    def build(self, dram):
        nc, S, st = self.nc, self.S, self.st
        sb, ps = self.sb, self.ps
        mm, mms, trs, act, tt, ts, stt, cp, memset = (self.mm, self.mms, self.trs, self.act, self.tt,
                                                      self.ts, self.stt, self.cp, self.memset)
        b_scr = {}
        for l in self.layers:
            for name, ng in (("w_proj", 1), ("w_in", 11), ("w_out", 4), ("w_gate", 2)):
                for g in range(ng):
                    b_scr[(l, name, g)] = Buf()

        cast_q = []
        for l in self.layers:
            cast_q.append((l, "w_proj", 0))
            for g in range(11):
                cast_q.append((l, "w_in", g))
            for g in range(4):
                cast_q.append((l, "w_out", g))
            for g in range(2):
                cast_q.append((l, "w_gate", g))
        cast_pos = [0]

        def cast_some(n):
            for _ in range(n):
                if cast_pos[0] < len(cast_q):
                    l_, name, g = cast_q[cast_pos[0]]
                    cast_pos[0] += 1
                    S.dma("pool", dram[name + "_s"][l_, g], dram[name][l_, g], writes=[b_scr[(l_, name, g)]],
                          fence=False)

        IDF = sb(st, "IDF", [128, 128], F32)
        IDB = sb(st, "IDB", [128, 128], BF16)
        U = sb(st, "U", [128, 128], F32)
        ONEF = sb(st, "ONEF", [128, 128], F32)
        ONE512 = sb(st, "ONE512", [128, 128], F32)
        ONEB = sb(st, "ONEB", [128, 256], BF16)
        AMASK = sb(st, "AMASK", [128, 2, 128], F32)
        PRM = sb(st, "PRM", [128, NP], F32)
        DER = sb(st, "DER", [128, 64], F32)
        BT = sb(st, "BT", [128, 2, 8, 128], F32)
        WP = sb(st, "WP", [128, 2, 1024], BF16)
        WB = [sb(st, "WB%d" % i, [128, 8, 512], BF16) for i in range(3)]
        XNT = sb(st, "XNT", [128, 8, 512], BF16)
        YT = sb(st, "YT", [128, 16, 512], BF16)
        HS = sb(st, "HS", [128, 1024], F32)
        HSB = sb(st, "HSB", [128, 1024], BF16)
        HSbufs = [Buf(), Buf()]
        HSBbufs = [Buf(), Buf()]
        TAIL = sb(st, "TAIL", [128, 12, 3], F32)
        GLU = sb(st, "GLU", [128, 4, 30 + TB], BF16)
        KT = sb(st, "KT", [128, 128 + TB], BF16)
        VT = sb(st, "VT", [128, 5, 128], BF16)
        JUNK = sb(st, "JUNK", [128, 1024], BF16)
        SM = sb(st, "SM", [128, 64], F32)
        PA = ps(st, "PA", [128, 512], F32)
        PB = ps(st, "PB", [128, 512], F32)
        PT = ps(st, "PT", [128, 1024], BF16)
        PS_ = ps(st, "PS", [128, 512], F32)
        PX0 = ps(st, "PX0", [128, 512], F32)
        PX1 = ps(st, "PX1", [128, 512], F32)
        PY = ps(st, "PY", [128, 512], F32)
        PO = ps(st, "PO", [128, 512], F32)

        def view(ap, bufs):
            v = Tl.__new__(Tl)
            v.t = ap
            v.bs = list(bufs)
            return v

        XR = [sb(st, "XR%d" % i, [128, 1024], F32) for i in range(2)]
        YN = sb(st, "YN", [128, 1024], BF16)
        XN2 = sb(st, "XN2", [128, 1024], BF16)
        XNs = [YN, XN2]
        PG = sb(st, "PG", [128, 4, 8 + TB], F32, nb=4)
        PRE = [view(PG[:, i, 0:3 + TB], [PG.bs[i]]) for i in range(2)]
        PREB = [view(PG[:, i, 0:264].bitcast(BF16)[:, 0:3 + TB], [PG.bs[i]]) for i in range(2)]
        DGS = [view(PG[:, i, 264:520].bitcast(BF16).rearrange("p (a b) -> p a b", a=4), [PG.bs[i]]) for i in range(2)]
        TAILB = sb(st, "TAILB", [128, 12, 4], BF16)
        ACC = [view(PG[:, 2 + i, 0:TB], [PG.bs[2 + i]]) for i in range(2)]
        TH = [sb(st, "TH%d" % i, [128, TB], F32) for i in range(2)]
        XBC = sb(st, "XBC", [128, 12, TB], BF16, nb=12)
        ZS = sb(st, "ZS", [128, NT, 1024], BF16, nb=NT)
        DTR = sb(st, "DTR", [128, NT, 16], F32, nb=NT)
        SD2 = sb(st, "SD2", [128, 9, 64], F32, nb=9)
        XG = sb(st, "XG", [128, 6, 1024], BF16, nb=6)
        XDT = [view(XG[:, i, :], [XG.bs[i]]) for i in range(2)]
        XD = view(XG[:, 2, :], [XG.bs[2]])
        XDS = view(XG[:, 3, :], [XG.bs[3]])
        MT4 = [view(XG[:, 4 + i // 2, (i % 2) * 512:(i % 2 + 1) * 512].rearrange("p (a b) -> p a b", a=4),
                    [Buf()]) for i in range(4)]
        DG = view(PG[:].rearrange("p a b -> p (a b)").bitcast(BF16)[:, 0:31 * 128].rearrange("p (a b) -> p a b", a=31),
                  PG.bs)
        BMT = sb(st, "BMT", [128, 2, 128], BF16)
        CBM = [sb(st, "CBM%d" % i, [128, 2, 128], F32) for i in range(2)]
        MNEG = sb(st, "MNEG", [128, 4, 128], BF16)
        T1 = sb(st, "T1", [128, 512], F32)
        D1 = T1
        Y = sb(st, "Y", [128, 1024], F32)
        EB = [sb(st, "EB%d" % i, [128, 4, 128], F32) for i in range(2)]
        MU = view(XR[0][:, 0:512], [XR[0].b])
        RS = view(XR[0][:, 512:1024], [XR[0].b])
        U1 = sb(st, "U1", [128, TB], F32)
        QT = sb(st, "QT", [128, 4, TB], BF16, nb=4)
        AG = sb(st, "AG", [128, 4, TB], BF16, nb=4)
        AB = [sb(st, "AB%d" % i, [128, 4, 128], F32) for i in range(3)]
        ACST = sb(st, "ACST", [64, 128], F32)
        b_acsd = [Buf(), Buf()]
        blkctr = [0]
        UB = sb(st, "UB", [128, 128], BF16)
        DAB = sb(st, "DAB", [128, 2, 64], BF16, nb=2)
        SC = [view(XR[1][:, i * 512:(i + 1) * 512], [XR[1].b]) for i in range(2)]
        ET = [sb(st, "ET%d" % i, [128, 512], BF16) for i in range(4)]
        RD = sb(st, "RD", [128, 256], F32)
        OT = sb(st, "OT", [128, 256], F32)
        O = sb(st, "O", [128, NT, 1024], F32, nb=NT)
        HH = view(O[:, 0:2, :].rearrange("p a (b c) -> p (a b) c", b=2), [O.bs[0], O.bs[0], O.bs[1], O.bs[1]])
        HQ = [view(O[:, 2, i * TB:(i + 1) * TB], [O.bs[2]]) for i in range(2)]
        CG = view(O[:, 3, :].bitcast(BF16).rearrange("p (a b) -> p a b", a=4), [O.bs[3]] * 4)
        PR = [sb(st, "PR%d" % i, [128, 256], F32) for i in range(2)]
        SM1s = [sb(st, "SM1_%d" % i, [128, 8], F32) for i in range(2)]
        SM5s = [sb(st, "SM5_%d" % i, [128, 8], F32) for i in range(2)]
        SMEs = [sb(st, "SME_%d" % i, [128, 8], F32) for i in range(2)]
        HT = view(YT[:, 0:8, :], [Buf()])
        HBs = [view(YT[:, 8 + 6 * i:10 + 6 * i, :].rearrange("p a b -> p (a b)"), [Buf()]) for i in range(2)]
        PTT = view(YT[:, 10:12, :], [Buf()])
        TG = [view(YT[:, 12:14, :].rearrange("p a b -> p (a b)").bitcast(F32), [Buf()])] * 2
        PB16s = [sb(st, "PB16a", [128, 256], BF16), sb(st, "PB16b", [128, 256], BF16)]
        p5views = [HT, HBs[0], HBs[1], PTT, TG[0]]
        PSB = view(PS_[:, 384:512].bitcast(BF16), [PS_.b])
        mmb = [PA, PB, PX0, PX1]
        mmi = [0]

        def nextmm():
            p = mmb[mmi[0] % 4]
            mmi[0] += 1
            return p

        S.dma("sp", IDF[:], dram["c_ident"], writes=[IDF.b])
        S.dma("sp", U[:], dram["c_tri"], writes=[U.b])
        S.dma("sp", AMASK[:], dram["c_amask"], writes=[AMASK.b])
        cp("dve", IDB[:], IDF[:], [IDF.b], [IDB.b])
        cp("dve", UB[:], U[:], [U.b], [UB.b])
        S.dma("sp", EB[0][:, 0, :], dram["c_mneg"], writes=[EB[0].b])
        cp("dve", MNEG[:], EB[0][:, 0, :].unsqueeze(1).to_broadcast([128, 4, 128]), [EB[0].b], [MNEG.b])
        memset(ONEF[:], 1.0, [ONEF.b])
        memset(ONE512[:], 1.0 / 512.0, [ONE512.b])
        memset(ONEB[:], 1.0, [ONEB.b])

        wplan = []
        wloaded = []
        wstate = ["free", "free", "free"]
        cast_idx = {k: i for i, k in enumerate(cast_q)}

        def w_pump():
            while wplan and len(wloaded) < 2 and "free" in wstate:
                key = wplan.pop(0)
                bi = wstate.index("free")
                wstate[bi] = "loaded"
                l_, name, g = key
                assert cast_idx[key] < cast_pos[0], ("cast not issued", key)
                S.dma("sp", WB[bi][:].rearrange("p a b -> p (a b)"), dram[name + "_s"][l_, g],
                      reads=[b_scr[key]], writes=[WB[bi].b])
                wloaded.append((key, bi))

        def w_acquire(key):
            cast_some(1)
            if not wloaded:
                w_pump()
            k2, bi = wloaded.pop(0)
            assert k2 == key, (k2, key)
            wstate[bi] = "held"
            w_pump()
            return WB[bi]

        def w_release(buf):
            bi = WB.index(buf)
            assert wstate[bi] == "held"
            wstate[bi] = "free"
            w_pump()

        def inherit(dst, srcs):
            for sb_ in srcs:
                for d in (sb_.r, sb_.w):
                    for k, v in d.items():
                        if dst.r.get(k, 0) < v:
                            dst.r[k] = v

        def interleave(A, B):
            out = []
            na, nb = len(A), len(B)
            ia = ib = 0
            while ia < na or ib < nb:
                if ib >= nb or (ia < na and ia * nb <= ib * na):
                    out.append(A[ia]); ia += 1
                else:
                    out.append(B[ib]); ib += 1
            return out

        b_hscr = [Buf() for _ in range(16)]
        b_out = [Buf() for _ in range(16)]

        for li, l in enumerate(self.layers):
            src = dram["x"] if li == 0 else dram["hscr"]
            dst = dram["out"] if li == len(self.layers) - 1 else dram["hscr"]
            b_src = None if li == 0 else b_hscr
            b_dst = b_out if li == len(self.layers) - 1 else b_hscr
            S.dma("sp", PRM[:], dram["prm"][l], writes=[PRM.b])
            S.dma("sp", BT[:].rearrange("p a h q -> p (a h q)"), dram["c_abias"], writes=[BT.b])
            preloaded = set()
            if li == 0:
                for t in range(2):
                    S.dma("sp", XR[t][:], src[t * 128:(t + 1) * 128, :], writes=[XR[t].b])
                    preloaded.add(t)
                S.wait_all("pool", [PRM.b, BT.b, IDF.b, U.b, AMASK.b, EB[0].b, XR[0].b, XR[1].b])
                cast_some(4)
            S.dma("sp", WP[:].rearrange("p a b -> p (a b)"), dram["w_proj_s"][l, 0],
                  reads=[b_scr[(l, "w_proj", 0)]], writes=[WP.b])
            ts(PRM[:, O_CW:O_CB + 12], PRM[:, O_CW:O_CB + 12], 0.5, None, ALU.mult, None, [PRM.b], [PRM.b])
            ts(PRM[:, O_DWW:O_DWW + 124], PRM[:, O_DWW:O_DWW + 124], 0.5, None, ALU.mult, None, [PRM.b], [PRM.b])
            act(DER[:, 0:16], PRM[:, O_ALOG:O_ALOG + 16], AF.Exp, [PRM.b], [DER.b])
            ts(DER[:, 0:16], DER[:, 0:16], -1.0, None, ALU.mult, None, [DER.b], [DER.b])
            ts(DER[:, 16:24], PRM[:, O_LNW:O_LNW + 8], 0.5, None, ALU.mult, None, [PRM.b], [DER.b])
            tt(BT[:], BT[:], AMASK[:].unsqueeze(2).to_broadcast([128, 2, 8, 128]), ALU.add, [BT.b, AMASK.b], [BT.b])
            tt(BT[:], BT[:], PRM[:, O_SINK:O_SINK + 8].unsqueeze(1).unsqueeze(3).to_broadcast([128, 2, 8, 128]),
               ALU.subtract, [BT.b, PRM.b], [BT.b])
            memset(HS[:], 0.0, HSbufs)
            memset(HSB[:], 0.0, HSBbufs)
            memset(TAIL[:], 0.0, [TAIL.b])
            memset(TAILB[:], 0.0, [TAILB.b])
            memset(GLU[:], 0.0, [GLU.b])
            memset(KT[:], 0.0, [KT.b])
            memset(VT[:], 0.0, [VT.b])

            def make_p1(tb):
                tok0 = tb * TB
                ths = []

                def tileA(t):
                    xr = XR[t % 2]
                    xn = XNs[t % 2]
                    r0 = tok0 + t * 128
                    if not (tb == 0 and t in preloaded):
                        S.dma("act", xr[:], src[r0:r0 + 128, :],
                              reads=([] if b_src is None else [b_src[tb * NT + t]]), writes=[xr.b])
                    SM1 = SM1s[t % 2]
                    memset(SM1[:, 0:1], 0.0, [SM1.b])
                    act(JUNK[:], xr[:], AF.Square, [xr.b, SM1.b], [JUNK.b, SM1.b], accum=SM1[:, 0:1])
                    self.rstd(SM1[:, 2:3], SM1[:, 0:1], 1.0 / D, SM1[:, 1:2], [SM1.b], [SM1.b])
                    stt(xn[:], xr[:], SM1[:, 2:3], PRM[:, O_PREW:O_PREW + 1024], ALU.mult, ALU.mult,
                        [xr.b, SM1.b, PRM.b], [xn.b])

                def tileB(t):
                    xn = XNs[t % 2]
                    trs([(PT[:, j * 128:(j + 1) * 128], xn[:, j * 128:(j + 1) * 128]) for j in range(8)],
                        IDB[:], [xn.b, IDB.b], [PT.b])
                    cp("act", XNT[:, :, t * 128:(t + 1) * 128], PT[:].rearrange("p (a b) -> p a b", a=8),
                       [PT.b], [XNT.b])
                order = [("A", 0), ("A", 1), ("B", 0), ("A", 2), ("B", 1), ("A", 3), ("B", 2), ("B", 3)]
                for kind, t in order:
                    ths.append(((lambda t=t: tileA(t)) if kind == "A" else (lambda t=t: tileB(t)), []))
                return ths

            def make_p2a(tb):
                ths = []
                hold = {}

                def xbcA(c):
                    if c % 4 == 0:
                        hold["w"] = w_acquire((l, "w_in", c // 4))
                    wb = hold["w"]
                    c4 = c % 4
                    pm = nextmm()
                    mm(pm[:], [(wb[:, kc, c4 * 128:(c4 + 1) * 128], XNT[:, kc, :]) for kc in range(8)],
                       [wb.b, XNT.b], [pm.b])
                    if c % 4 == 3:
                        w_release(wb)
                    pre, dgs = PREB[c % 2], DGS[c % 2]
                    tt(dgs[:], IDF[:].unsqueeze(1).to_broadcast([128, 4, 128]),
                       PRM[:, O_CW + c * 4:O_CW + c * 4 + 4].unsqueeze(2).to_broadcast([128, 4, 128]), ALU.mult,
                       [IDF.b, PRM.b], [dgs.b])
                    cp("dve", pre[:, 0:3], TAILB[:, c, 0:3], [TAILB.b], [pre.b])
                    cp("act", pre[:, 3:3 + TB], pm[:], [pm.b], [pre.b])
                    cp("dve", TAILB[:, c, 0:3], pre[:, TB:TB + 3], [pre.b], [TAILB.b])

                def xbcB(c):
                    pre, acc, th, dgs = PREB[c % 2], ACC[c % 2], TH[c % 2], DGS[c % 2]
                    pc = nextmm()
                    mm(pc[:], [(dgs[:, k, :], pre[:, k:k + TB]) for k in range(4)], [dgs.b, pre.b], [pc.b])
                    act(acc[:], pc[:], AF.Identity, [pc.b, PRM.b], [acc.b], bias=PRM[:, O_CB + c:O_CB + c + 1])
                    act(th[:], acc[:], AF.Tanh, [acc.b], [th.b])
                    stt(XBC[:, c, :], th[:], 1.0, acc[:], ALU.add, ALU.mult, [th.b, acc.b], [XBC.bs[c]])

                def z_tile(half, t):
                    if t == 0:
                        hold["w"] = w_acquire((l, "w_in", 3 + half))
                    wb = hold["w"]
                    pm = nextmm()
                    mm(pm[:], [(XNT[:, kc, t * 128:(t + 1) * 128], wb[:, kc, :]) for kc in range(8)],
                       [wb.b, XNT.b], [pm.b])
                    if t == NT - 1:
                        w_release(wb)
                    th = TH[t % 2]
                    act(th[:], pm[:], AF.Tanh, [pm.b], [th.b], scale=0.5)
                    stt(ZS[:, t, half * 512:(half + 1) * 512], th[:], 1.0, pm[:], ALU.add, ALU.mult,
                        [th.b, pm.b], [ZS.bs[t]])

                def kv():
                    wb = w_acquire((l, "w_in", 5))
                    cp("dve", KT[:, 0:128], KT[:, TB:TB + 128], [KT.b], [KT.b])
                    cp("dve", VT[:, 0, :], VT[:, 4, :], [VT.b], [VT.b])
                    pm = nextmm()
                    mm(pm[:], [(wb[:, kc, 0:128], XNT[:, kc, :]) for kc in range(8)], [wb.b, XNT.b], [pm.b])
                    cp("act", KT[:, 128:128 + TB], pm[:], [pm.b], [KT.b])
                    for t in range(NT):
                        pm = nextmm()
                        mm(pm[:, 0:144], [(XNT[:, kc, t * 128:(t + 1) * 128], wb[:, kc, 128:272]) for kc in range(8)],
                           [wb.b, XNT.b], [pm.b])
                        cp("dve", DTR[:, t, :], pm[:, 0:16], [pm.b], [DTR.bs[t]])
                        cp("act", VT[:, 1 + t, :], pm[:, 16:144], [pm.b], [VT.b])
                    w_release(wb)
                for c in range(13):
                    if c < 12:
                        ths.append((lambda c=c: xbcA(c), [(l, "w_in", c // 4)] if c % 4 == 0 else []))
                    if c >= 1:
                        ths.append((lambda c=c: xbcB(c - 1), []))
                for half in range(2):
                    for t in range(NT):
                        ths.append((lambda half=half, t=t: z_tile(half, t), [(l, "w_in", 3 + half)] if t == 0 else []))
                ths.append((kv, [(l, "w_in", 5)]))
                return ths

            def make_p5b(tb):
                tok0 = tb * TB
                ths = []
                hold = {}

                def post(t):
                    r0 = tok0 + t * 128
                    xr = XR[t % 2]
                    S.dma("act", xr[:], src[r0:r0 + 128, :],
                          reads=([] if b_src is None else [b_src[tb * NT + t]]), writes=[xr.b])
                    pr = PR[t % 2]
                    S.dma("act", pr[:], dram["p"][l, r0:r0 + 128, :], writes=[pr.b])
                    SM5 = SM5s[t % 2]
                    memset(SM5[:, 0:1], 0.0, [SM5.b])
                    act(JUNK[:], O[:, t, :], AF.Square, [O.bs[t], SM5.b], [JUNK.b, SM5.b], accum=SM5[:, 0:1])
                    self.rstd(SM5[:, 2:3], SM5[:, 0:1], 1.0 / D, SM5[:, 1:2], [SM5.b], [SM5.b])
                    stt(O[:, t, :], O[:, t, :], SM5[:, 2:3], PRM[:, O_POSTW:O_POSTW + 1024], ALU.mult, ALU.mult,
                        [O.bs[t], SM5.b, PRM.b], [O.bs[t]])
                    tt(O[:, t, :], O[:, t, :], xr[:], ALU.add, [O.bs[t], xr.b], [O.bs[t]])
                    HB, PB16 = HBs[t % 2], PB16s[t % 2]
                    cp("act", HB[:], O[:, t, :], [O.bs[t]], [HB.b])
                    cp("dve", PB16[:], pr[:], [pr.b], [PB16.b])

                def postB(t):
                    HB, PB16 = HBs[t % 2], PB16s[t % 2]
                    trs([(PT[:, j * 128:(j + 1) * 128], HB[:, j * 128:(j + 1) * 128]) for j in range(8)], IDB[:],
                        [HB.b, IDB.b], [PT.b])
                    cp("act", HT[:, :, t * 128:(t + 1) * 128], PT[:].rearrange("p (a b) -> p a b", a=8),
                       [PT.b], [HT.b])
                    trs([(PSB[:, j * 128:(j + 1) * 128], PB16[:, j * 128:(j + 1) * 128]) for j in range(2)], IDB[:],
                        [PB16.b, IDB.b], [PSB.b])
                    cp("act", PTT[:, :, t * 128:(t + 1) * 128], PSB[:].rearrange("p (a b) -> p a b", a=2),
                       [PSB.b], [PTT.b])

                def ple(ch, t):
                    if t == 0:
                        hold["w"] = w_acquire((l, "w_gate", ch))
                    wg = hold["w"]
                    tcols = slice(t * 128, (t + 1) * 128)
                    pg = nextmm()
                    mm(pg[:], [(HT[:, kc, tcols], wg[:, kc, :]) for kc in range(8)], [HT.b, wg.b], [pg.b])
                    if t == NT - 1:
                        w_release(wg)
                    pp = nextmm()
                    mm(pp[:], [(PTT[:, kc, tcols], WP[:, kc, ch * 512:(ch + 1) * 512]) for kc in range(2)],
                       [PTT.b, WP.b], [pp.b])
                    tg = TG[t % 2]
                    act(tg[:], pg[:], AF.Tanh, [pg.b], [tg.b], scale=0.5)
                    stt(tg[:], tg[:], 1.0, pp[:], ALU.add, ALU.mult, [tg.b, pp.b], [tg.b])
                    stt(O[:, t, ch * 512:(ch + 1) * 512], tg[:], 0.5, O[:, t, ch * 512:(ch + 1) * 512],
                        ALU.mult, ALU.add, [tg.b, O.bs[t]], [O.bs[t]])

                def store(t):
                    r0 = tok0 + t * 128
                    S.dma("pool", dst[r0:r0 + 128, :], O[:, t, :], reads=[O.bs[t]], writes=[b_dst[tb * NT + t]])
                ths.append((lambda: post(0), []))
                ths.append((lambda: post(1), []))
                ths.append((lambda: postB(0), []))
                ths.append((lambda: post(2), []))
                ths.append((lambda: postB(1), []))
                ths.append((lambda: ple(0, 0), [(l, "w_gate", 0)]))
                ths.append((lambda: post(3), []))
                ths.append((lambda: postB(2), []))
                ths.append((lambda: ple(0, 1), []))
                ths.append((lambda: postB(3), []))
                ths.append((lambda: ple(0, 2), []))
                ths.append((lambda: ple(0, 3), []))
                for t in range(NT):
                    ths.append((lambda t=t: ple(1, t), [(l, "w_gate", 1)] if t == 0 else []))
                for t in range(NT):
                    ths.append((lambda t=t: store(t), []))
                return ths

            def run(ths):
                for fn, _ in ths:
                    fn()

            def keys_of(ths):
                return [k for _, ws in ths for k in ws]

            k_ssdx = [(l, "w_in", g) for g in range(6, 11)]
            k_p5a = [(l, "w_out", g) for g in range(4)]
            first = make_p1(0) + make_p2a(0)
            wplan.extend(keys_of(first) + k_ssdx + k_p5a)
            w_pump()
            run(first)

            for tb in range(NBLK):
                tok0 = tb * TB
                if True:
                    row = lambda i: SD2[:, i, :]
                    r3 = lambda i: SD2[:, i, :].rearrange("p (t h) -> p t h", t=4)
                    sbf = SD2.bs
                    tt(r3(0), DTR[:], PRM[:, O_DTB:O_DTB + 16].unsqueeze(1).to_broadcast([128, 4, 16]), ALU.add,
                       DTR.bs + [PRM.b], [sbf[0]])
                    act(row(1), row(0), AF.Abs, [sbf[0]], [sbf[1]])
                    act(row(1), row(1), AF.Exp, [sbf[1]], [sbf[1]], scale=-1.0)
                    act(row(1), row(1), AF.Ln, [sbf[1]], [sbf[1]], bias=1.0)
                    stt(row(2), row(0), 0.0, row(1), ALU.max, ALU.add, [sbf[0], sbf[1]], [sbf[2]])
                    tt(r3(3), r3(2), DER[:, 0:16].unsqueeze(1).to_broadcast([128, 4, 16]), ALU.mult,
                       [sbf[2], DER.b], [sbf[3]])
                    cp("dve", DAB[:, 0, :], row(3), [sbf[3]], [DAB.bs[0]])
                    cp("dve", row(4), DAB[:, 0, :], [DAB.bs[0]], [sbf[4]])
                    tt(row(4), row(3), row(4), ALU.subtract, [sbf[3], sbf[4]], [sbf[4]])
                    cp("dve", DAB[:, 1, :], row(4), [sbf[4]], [DAB.bs[1]])
                    mms([(PS_[:, 256:320], UB[:], DAB[:, 0, :], True, False),
                         (PS_[:, 256:320], UB[:], DAB[:, 1, :], False, True),
                         (PS_[:, 320:384], ONEB[:, 0:128], DAB[:, 0, :], True, False),
                         (PS_[:, 320:384], ONEB[:, 0:128], DAB[:, 1, :], False, True)],
                        [UB.b, ONEB.b] + DAB.bs, [PS_.b])
                    ts(row(5), PS_[:, 256:320], -1.0, None, ALU.mult, None, [PS_.b], [sbf[5]])
                    cp("dve", row(4), PS_[:, 256:320], [PS_.b], [sbf[4]])
                    tt(row(6), PS_[:, 320:384], row(5), ALU.add, [PS_.b, sbf[5]], [sbf[6]])
                    act(row(6), row(6), AF.Exp, [sbf[6]], [sbf[6]])
                    act(row(7), PS_[:, 256:320], AF.Exp, [PS_.b], [sbf[7]])
                    act(row(8), PS_[:, 320:384], AF.Exp, [PS_.b], [sbf[8]])
                    slot = blkctr[0] % 2
                    blkctr[0] += 1
                    self.S.op("pe", lambda e: e.transpose(out=PS_[0:64, 0:128], in_=row(4), identity=IDF[:]),
                              [sbf[4], IDF.b], [PS_.b])
                    cp("act", ACST[:], PS_[0:64, 0:128], [PS_.b], [ACST.b])
                    S.dma("pool", dram["acs_d"][slot], ACST[:], reads=[ACST.b], writes=[b_acsd[slot]])

                    def stageB1(t):
                        cols = slice(t * 128, (t + 1) * 128)
                        pcb = nextmm()
                        mms([(pcb[:, g * 128:(g + 1) * 128], XBC[:, 8 + g, cols], XBC[:, 10 + g, cols], True, True)
                             for g in range(2)], [XBC.bs[8 + i] for i in range(4)], [pcb.b])
                        tt(CBM[t % 2][:], pcb[:, 0:256].rearrange("p (a b) -> p a b", a=2),
                           U[:].unsqueeze(1).to_broadcast([128, 2, 128]), ALU.mult, [pcb.b, U.b], [CBM[t % 2].b])

                    def stageB2(t):
                        cols = slice(t * 128, (t + 1) * 128)
                        xdt = XDT[t % 2]
                        trs([(PSB[:, g * 128:(g + 1) * 128], XBC[:, 8 + g, cols]) for g in range(2)], IDB[:],
                            [XBC.bs[8], XBC.bs[9], IDB.b], [PSB.b])
                        trs([(PT[:, j * 128:(j + 1) * 128], XBC[:, j, cols]) for j in range(8)], IDB[:],
                            [XBC.bs[j] for j in range(8)] + [IDB.b], [PT.b])
                        cp("dve", BMT[:].rearrange("p a b -> p (a b)"), PSB[:], [PSB.b], [BMT.b])
                        pt3 = PT[:].rearrange("p (h d) -> p h d", h=16)
                        tt(xdt[:].rearrange("p (h d) -> p h d", h=16), pt3,
                           r3(2)[:, t, :].unsqueeze(2).to_broadcast([128, 16, 64]), ALU.mult, [PT.b, sbf[2]], [xdt.b])
                        tt(XD[:].rearrange("p (h d) -> p h d", h=16), pt3,
                           PRM[:, O_D16:O_D16 + 16].unsqueeze(2).to_broadcast([128, 16, 64]), ALU.mult,
                           [PT.b, PRM.b], [XD.b])
                        tt(XDS[:].rearrange("p (h d) -> p h d", h=16), xdt[:].rearrange("p (h d) -> p h d", h=16),
                           r3(6)[:, t, :].unsqueeze(2).to_broadcast([128, 16, 64]), ALU.mult, [xdt.b, sbf[6]], [XDS.b])

                    def kdec(k):
                        return k // 4, (k // 2) % 2, k % 2

                    def stageC1(k):
                        t, g, hf = kdec(k)
                        r0 = t * 16 + g * 8 + hf * 4
                        ab = AB[k % 3]
                        S.dma("pool", ab[:], dram["acs_d"][slot, r0:r0 + 4, :].partition_broadcast(128),
                              reads=[b_acsd[slot]], writes=[ab.b])

                    def stageC2(k):
                        t, g, hf = kdec(k)
                        h0 = g * 8 + hf * 4
                        ab, ek, mtk = AB[k % 3], EB[k % 2], MT4[k % 4]
                        for i in range(4):
                            act(ek[:, i, :], ab[:, i, :], AF.Exp, [ab.b, sbf[5]], [ek.b],
                                bias=r3(5)[:, t, h0 + i:h0 + i + 1])
                        stt(mtk[:], ek[:], 1e30, CBM[t % 2][:, g, :].unsqueeze(1).to_broadcast([128, 4, 128]),
                            ALU.min, ALU.mult, [ek.b, CBM[t % 2].b], [mtk.b])

                    def advance(k):
                        if k + 2 < 4 * NT:
                            stageC1(k + 2)
                        stageC2(k)

                    def stageD(t, g):
                        cols = slice(t * 128, (t + 1) * 128)
                        gs = slice(g * 512, (g + 1) * 512)
                        py, po, t1 = PY, PO, T1
                        xdt = XDT[t % 2]
                        k0 = 4 * t + 2 * g
                        items = [(py[:], IDB[:], XD[:, gs], True, False)]
                        for hh in range(8):
                            h = g * 8 + hh
                            items.append((py[:, hh * 64:(hh + 1) * 64], MT4[(k0 + hh // 4) % 4][:, hh % 4, :],
                                          xdt[:, h * 64:(h + 1) * 64], False, True))
                        mms(items, [IDB.b, XD.b, MT4[k0 % 4].b, MT4[(k0 + 1) % 4].b, xdt.b], [py.b])
                        mm(po[:], [(XBC[:, 10 + g, cols], HSB[:, gs])], [XBC.bs[10 + g], HSBbufs[g]], [po.b])
                        tt(t1[:].rearrange("p (h d) -> p h d", h=8), po[:].rearrange("p (h d) -> p h d", h=8),
                           r3(7)[:, t, g * 8:(g + 1) * 8].unsqueeze(2).to_broadcast([128, 8, 64]), ALU.mult,
                           [po.b, sbf[7]], [t1.b])
                        tt(Y[:, gs], t1[:], py[:], ALU.add, [t1.b, py.b], [Y.b])

                    def stageD2(t, g):
                        gs = slice(g * 512, (g + 1) * 512)
                        po = nextmm()
                        hb_, hsb_ = HSbufs[g], HSBbufs[g]
                        tt(HS[:, gs].rearrange("p (h d) -> p h d", h=8), HS[:, gs].rearrange("p (h d) -> p h d", h=8),
                           r3(8)[:, t, g * 8:(g + 1) * 8].unsqueeze(2).to_broadcast([128, 8, 64]), ALU.mult,
                           [hb_, sbf[8]], [hb_], eng="pool")
                        mm(po[:], [(BMT[:, g, :], XDS[:, gs])], [BMT.b, XDS.b], [po.b])
                        tt(HSB[:, gs], HS[:, gs], po[:], ALU.add, [hb_, po.b], [hsb_])
                        tt(HS[:, gs], HS[:, gs], po[:], ALU.add, [hb_, po.b], [hb_])

                    def stageE(t):
                        cols = slice(t * 128, (t + 1) * 128)
                        tt(Y[:], Y[:], ZS[:, t, :], ALU.mult, [Y.b, ZS.bs[t]], [Y.b])
                        SME = SMEs[t % 2]
                        memset(SME[:, 0:2], 0.0, [SME.b])
                        for g in range(2):
                            act(JUNK[:, 0:512], Y[:, g * 512:(g + 1) * 512], AF.Square, [Y.b, SME.b], [JUNK.b, SME.b],
                                accum=SME[:, g:g + 1])
                        ts(SME[:, 2:4], SME[:, 0:2], 1.0 / 512.0, 4.0 * EPS, ALU.mult, ALU.add, [SME.b], [SME.b])
                        act(SME[:, 2:4], SME[:, 2:4], AF.Ln, [SME.b], [SME.b])
                        act(SME[:, 4:6], SME[:, 2:4], AF.Exp, [SME.b], [SME.b], scale=-0.5)
                        for g in range(2):
                            gs = slice(g * 512, (g + 1) * 512)
                            stt(YN[:, gs], Y[:, gs], SME[:, 4 + g:5 + g], PRM[:, O_SSDNW + g * 512:O_SSDNW + (g + 1) * 512],
                                ALU.mult, ALU.mult, [Y.b, SME.b, PRM.b], [YN.b])

                    def stageE2(t):
                        cols = slice(t * 128, (t + 1) * 128)
                        pte = nextmm()
                        ptv = pte[:].bitcast(BF16)
                        trs([(ptv[:, j * 128:(j + 1) * 128], YN[:, j * 128:(j + 1) * 128]) for j in range(8)], IDB[:],
                            [YN.b, IDB.b], [pte.b])
                        cp("dve", YT[:, 0:8, cols], ptv.rearrange("p (a b) -> p a b", a=8), [pte.b], [YT.b])

                    extra = []
                    wbh = {}

                    def conf_pair(j):
                        if j % 2 == 0:
                            wbh["c"] = w_acquire((l, "w_in", 6 + j // 2))
                        wb = wbh["c"]
                        jj = j % 2
                        pa = nextmm()
                        mm(pa[:], [(wb[:, kc, (2 * jj) * 128:(2 * jj + 1) * 128], XNT[:, kc, :]) for kc in range(8)],
                           [wb.b, XNT.b], [pa.b])
                        pb = nextmm()
                        mm(pb[:], [(wb[:, kc, (2 * jj + 1) * 128:(2 * jj + 2) * 128], XNT[:, kc, :]) for kc in range(8)],
                           [wb.b, XNT.b], [pb.b])
                        if j % 2 == 1:
                            w_release(wb)
                        th = TH[j % 2]
                        act(th[:], pb[:], AF.Tanh, [pb.b], [th.b], scale=0.5)
                        stt(GLU[:, j, 30:30 + TB], th[:], 1.0, pa[:], ALU.add, ALU.mult, [th.b, pa.b], [GLU.b])

                    def conf_gate(j):
                        if j == 0:
                            wbh["c"] = w_acquire((l, "w_in", 8))
                        wb = wbh["c"]
                        pm = nextmm()
                        mm(pm[:], [(wb[:, kc, j * 128:(j + 1) * 128], XNT[:, kc, :]) for kc in range(8)],
                           [wb.b, XNT.b], [pm.b])
                        if j == 3:
                            w_release(wb)
                        th = TH[j % 2]
                        act(th[:], pm[:], AF.Tanh, [pm.b], [th.b], scale=0.5)
                        stt(CG[:, j, :], th[:], 1.0, pm[:], ALU.add, ALU.mult, [th.b, pm.b], [CG.bs[j]])

                    def conf_convA(j):
                        tt(DG[:], IDF[:].unsqueeze(1).to_broadcast([128, 31, 128]),
                           PRM[:, O_DWW + j * 31:O_DWW + (j + 1) * 31].unsqueeze(2).to_broadcast([128, 31, 128]),
                           ALU.mult, [IDF.b, PRM.b], DG.bs)

                    def conf_convB(j):
                        pm = nextmm()
                        mm(pm[:], [(DG[:, k, :], GLU[:, j, k:k + TB]) for k in range(31)], DG.bs + [GLU.b], [pm.b])
                        act(HH[:, j, :], pm[:], AF.Identity, [pm.b, PRM.b], [HH.bs[j]],
                            bias=PRM[:, O_DWB + j:O_DWB + j + 1])

                    def conf_stats():
                        PCA = nextmm()
                        PCB = nextmm()
                        mm(PCA[:], [(ONE512[:], HH[:, j, :]) for j in range(4)], [ONE512.b] + HH.bs, [PCA.b])
                        for j in range(4):
                            hq = HQ[j % 2]
                            act(hq[:], HH[:, j, :], AF.Square, [HH.bs[j]], [hq.b])
                            mm(PCB[:], [(ONE512[:], hq[:])], [ONE512.b, hq.b], [PCB.b], start=(j == 0), stop=(j == 3))
                        cp("act", MU[:], PCA[:], [PCA.b], [MU.b])
                        tt(D1[:], MU[:], MU[:], ALU.mult, [MU.b], [D1.b])
                        tt(RS[:], PCB[:], D1[:], ALU.subtract, [PCB.b, D1.b], [RS.b])
                        ts(RS[:], RS[:], EPS, None, ALU.add, None, [RS.b], [RS.b])
                        act(RS[:], RS[:], AF.Ln, [RS.b], [RS.b])
                        act(RS[:], RS[:], AF.Exp, [RS.b], [RS.b], scale=-0.5)

                    def conf_ln(j):
                        tt(D1[:], HH[:, j, :], MU[:], ALU.subtract, [HH.bs[j], MU.b], [D1.b])
                        tt(D1[:], D1[:], RS[:], ALU.mult, [D1.b, RS.b], [D1.b])
                        th = TH[j % 2]
                        act(th[:], D1[:], AF.Tanh, [D1.b, DER.b], [th.b], scale=DER[:, 16 + j:17 + j],
                            bias=DER[:, 20 + j:21 + j])
                        ts(U1[:], D1[:], PRM[:, O_LNW + j:O_LNW + j + 1], PRM[:, O_LNB + j:O_LNB + j + 1],
                           ALU.mult, ALU.add, [D1.b, PRM.b], [U1.b])
                        stt(U1[:], th[:], 1.0, U1[:], ALU.add, ALU.mult, [th.b, U1.b], [U1.b])
                        stt(YT[:, 8 + j, :], U1[:], 0.25, CG[:, j, :], ALU.mult, ALU.mult, [U1.b, CG.bs[j]], [YT.b])

                    def attn_q(g):
                        if g == 0:
                            wbh["a"] = w_acquire((l, "w_in", 9))
                        wb = wbh["a"]
                        pm = nextmm()
                        mm(pm[:], [(wb[:, kc, g * 128:(g + 1) * 128], XNT[:, kc, :]) for kc in range(8)],
                           [wb.b, XNT.b], [pm.b])
                        if g == 3:
                            w_release(wb)
                        cp("act", QT[:, g, :], pm[:], [pm.b], [QT.bs[g]])

                    def attn_g(c):
                        if c == 0:
                            wbh["a"] = w_acquire((l, "w_in", 10))
                        wb = wbh["a"]
                        pm = nextmm()
                        mm(pm[:], [(wb[:, kc, c * 128:(c + 1) * 128], XNT[:, kc, :]) for kc in range(8)],
                           [wb.b, XNT.b], [pm.b])
                        if c == 3:
                            w_release(wb)
                        th = TH[c % 2]
                        act(th[:], pm[:], AF.Tanh, [pm.b], [th.b], scale=0.5)
                        stt(AG[:, c, :], th[:], 1.0, pm[:], ALU.add, ALU.mult, [th.b, pm.b], [AG.bs[c]])

                    def attn_A(n, kh):
                        u = n * 2 + kh
                        nbk = tb * NT + n
                        qcols = slice(n * 128, (n + 1) * 128)
                        kbs = [1] if nbk == 0 else [0, 1]
                        prt = slice(kh * 64, (kh + 1) * 64)
                        for kb in kbs:
                            et = ET[(u % 2) * 2 + kb]
                            pm = nextmm()
                            mm(pm[:], [(KT[prt, (n + kb) * 128:(n + kb + 1) * 128], QT[prt, :, qcols])],
                               [KT.b] + QT.bs, [pm.b])
                            stt(SC[kb][:], pm[:], 0.125,
                                BT[:, kb, kh * 4:(kh + 1) * 4, :].rearrange("p h q -> p (h q)"),
                                ALU.mult, ALU.add, [pm.b, BT.b], [SC[kb].b])
                            act(et[:], SC[kb][:], AF.Exp, [SC[kb].b], [et.b])

                    def attn_B(n, kh):
                        u = n * 2 + kh
                        nbk = tb * NT + n
                        qcols = slice(n * 128, (n + 1) * 128)
                        kbs = [1] if nbk == 0 else [0, 1]
                        prt = slice(kh * 64, (kh + 1) * 64)
                        ets = {kb: ET[(u % 2) * 2 + kb] for kb in kbs}
                        pod = nextmm()
                        items = []
                        for jj in range(2):
                            for i, kb in enumerate(kbs):
                                items.append((pod[jj * 64:(jj + 1) * 64, 0:256], VT[:, n + kb, prt],
                                              ets[kb][:, jj * 256:(jj + 1) * 256], i == 0, i == len(kbs) - 1))
                        for jj in range(2):
                            for i, kb in enumerate(kbs):
                                items.append((pod[jj * 64:(jj + 1) * 64, 256:512], ONEB[:, 0:64],
                                              ets[kb][:, jj * 256:(jj + 1) * 256], i == 0, False))
                            items.append((pod[jj * 64:(jj + 1) * 64, 256:512], ONEB[0:1, 0:64], ONEB[0:1, 0:256],
                                          False, True))
                        mms(items, [VT.b, ONEB.b] + [ets[kb].b for kb in kbs], [pod.b])
                        self.S.op("dve", lambda e, o=RD[:], i=pod[:, 256:512]: e.reciprocal(out=o, in_=i),
                                  [pod.b], [RD.b])
                        tt(OT[:], pod[:, 0:256], RD[:], ALU.mult, [pod.b, RD.b], [OT.b])
                        stt(YT[:, 12 + kh * 2:14 + kh * 2, qcols], OT[:].rearrange("p (a b) -> p a b", a=2), 0.5,
                            AG[:, kh * 2:kh * 2 + 2, qcols], ALU.mult, ALU.mult,
                            [OT.b, AG.bs[kh * 2], AG.bs[kh * 2 + 1]], [YT.b])

                    extra.append(lambda: cp("dve", GLU[:, :, 0:30], GLU[:, :, TB:TB + 30], [GLU.b], [GLU.b]))
                    for j in range(4):
                        extra.append(lambda j=j: conf_pair(j))
                    for j in range(4):
                        extra.append(lambda j=j: conf_convA(j))
                        extra.append(lambda j=j: conf_gate(j))
                        extra.append(lambda j=j: conf_convB(j))
                    extra.append(conf_stats)
                    for j in range(4):
                        extra.append(lambda j=j: attn_q(j))
                        extra.append(lambda j=j: conf_ln(j))
                    for c in range(4):
                        extra.append(lambda c=c: attn_g(c))
                    units = [(n, kh) for n in range(NT) for kh in range(2)]
                    for i in range(len(units) + 1):
                        if i < len(units):
                            extra.append(lambda nk=units[i]: attn_A(*nk))
                        if i >= 1:
                            extra.append(lambda nk=units[i - 1]: attn_B(*nk))
                    nslots = 2 * NT
                    per = (len(extra) + nslots - 1) // nslots
                    epos = [0]

                    def run_extra(n):
                        for _ in range(n):
                            if epos[0] < len(extra):
                                extra[epos[0]]()
                                epos[0] += 1

                    stageB1(0)
                    stageB2(0)
                    stageC1(0)
                    stageC1(1)
                    for k in range(4):
                        advance(k)
                    stageB1(1)
                    for t in range(NT):
                        for g in range(2):
                            stageD(t, g)
                            if t + 1 < NT:
                                advance(4 * (t + 1) + 2 * g)
                                advance(4 * (t + 1) + 2 * g + 1)
                            run_extra(per)
                            stageD2(t, g)
                        if t >= 1:
                            stageE2(t - 1)
                        if t + 1 < NT:
                            stageB2(t + 1)
                        stageE(t)
                        if t + 2 < NT:
                            stageB1(t + 2)
                    run_extra(2)
                    stageE2(NT - 1)
                    run_extra(len(extra))

                p5a = []
                ohold = {}

                def outproj(ch, t):
                    if t == 0:
                        ohold["w0"] = w_acquire((l, "w_out", ch * 2))
                        ohold["w1"] = w_acquire((l, "w_out", ch * 2 + 1))
                    w0, w1 = ohold["w0"], ohold["w1"]
                    pm = nextmm()
                    pairs = [(YT[:, kc, t * 128:(t + 1) * 128], (w0 if kc < 8 else w1)[:, kc % 8, :])
                             for kc in range(16)]
                    mm(pm[:], pairs, [YT.b, w0.b, w1.b], [pm.b])
                    cp("act", O[:, t, ch * 512:(ch + 1) * 512], pm[:], [pm.b], [O.bs[t]])
                    if t == NT - 1:
                        w_release(w0)
                        w_release(w1)
                for ch in range(2):
                    for t in range(NT):
                        p5a.append((lambda ch=ch, t=t: outproj(ch, t), []))
                if tb + 1 < NBLK:
                    run(interleave(p5a, make_p1(tb + 1)))
                else:
                    run(p5a)
                for v in p5views:
                    v.b.w, v.b.r = {}, {}
                    inherit(v.b, [YT.b])
                A = make_p5b(tb)
                if tb + 1 < NBLK:
                    Bn = make_p2a(tb + 1)
                    merged = interleave(A, Bn)
                    wplan.extend(keys_of(merged) + k_ssdx + k_p5a)
                else:
                    merged = A
                    wplan.extend(keys_of(merged))
                w_pump()
                run(merged)
                inherit(YT.b, [v.b for v in p5views])
        S.wait_all("sp", b_out)
        S.wait_all("pool", b_out)


def _col_perm():
    z0, xbc0, dt0, ci0, cg0, q0, k0, v0, ag0 = 0, 1024, 2560, 2576, 3600, 4112, 4624, 4752, 4880
    groups = []
    for g in range(3):
        groups.append(list(range(xbc0 + g * 512, xbc0 + (g + 1) * 512)))
    groups.append(list(range(z0, z0 + 512)))
    groups.append(list(range(z0 + 512, z0 + 1024)))
    groups.append(list(range(k0, k0 + 128)) + list(range(dt0, dt0 + 16)) + list(range(v0, v0 + 128)) + [-1] * 240)
    for g in range(2):
        cols = []
        for j in (2 * g, 2 * g + 1):
            cols += list(range(ci0 + j * 128, ci0 + (j + 1) * 128))
            cols += list(range(ci0 + 512 + j * 128, ci0 + 512 + (j + 1) * 128))
        groups.append(cols)
    groups.append(list(range(cg0, cg0 + 512)))
    cols = []
    for g in range(4):
        cols += list(range(q0 + g * 64, q0 + (g + 1) * 64))
        cols += list(range(q0 + (4 + g) * 64, q0 + (5 + g) * 64))
    groups.append(cols)
    cols = []
    for kh in range(2):
        for gl in range(2):
            h0, h1 = kh * 4 + gl, kh * 4 + 2 + gl
            cols += list(range(ag0 + h0 * 64, ag0 + (h0 + 1) * 64))
            cols += list(range(ag0 + h1 * 64, ag0 + (h1 + 1) * 64))
    groups.append(cols)
    assert len(groups) == 11 and all(len(g) == 512 for g in groups)
    return np.array(groups, dtype=np.int64)


def _row_perm_out():
    rows = list(range(0, 1536))
    a0 = 1536
    for kh in range(2):
        for gl in range(2):
            h0, h1 = kh * 4 + gl, kh * 4 + 2 + gl
            rows += list(range(a0 + h0 * 64, a0 + (h0 + 1) * 64))
            rows += list(range(a0 + h1 * 64, a0 + (h1 + 1) * 64))
    return np.array(rows, dtype=np.int64)


def _t5_bucket(d):
    d = np.maximum(d, 0)
    dm = np.maximum(d, 1).astype(np.float32)
    large = 16 + (np.log(dm / np.float32(16)) / np.float32(math.log(128 / 16)) * np.float32(16)).astype(np.int32)
    large = np.minimum(large, 31)
    return np.where(d < 16, d, large)


def _prep(inputs):
    f32 = np.float32
    w_in = np.asarray(inputs["w_in"], f32)
    perm = _col_perm()
    w_in_pad = np.concatenate([w_in, np.zeros((DEPTH, D, 1), f32)], axis=2)
    wi = w_in_pad[:, :, perm.reshape(-1)].reshape(DEPTH, 8, 128, 11, 512)
    wi = np.ascontiguousarray(wi.transpose(0, 3, 2, 1, 4)).reshape(DEPTH, 11, 128, 4096)
    w_out = np.asarray(inputs["w_out"], f32)[:, _row_perm_out(), :]
    wo = w_out.reshape(DEPTH, 2, 8, 128, 2, 512).transpose(0, 4, 1, 3, 2, 5)
    wo = np.ascontiguousarray(wo).reshape(DEPTH, 4, 128, 4096)
    wg = np.asarray(inputs["ple_gate"], f32).reshape(DEPTH, 8, 128, 2, 512).transpose(0, 3, 2, 1, 4)
    wg = np.ascontiguousarray(wg).reshape(DEPTH, 2, 128, 4096)
    wp = np.asarray(inputs["ple_proj"], f32).reshape(DEPTH, 2, 128, 1024).transpose(0, 2, 1, 3)
    wp = np.ascontiguousarray(wp).reshape(DEPTH, 1, 128, 2048)
    prm = np.zeros((DEPTH, 128, NP), f32)
    bc = lambda v: np.broadcast_to(np.asarray(v, f32)[None, :], (128, len(v)))
    for l in range(DEPTH):
        prm[l, :, O_PREW:O_PREW + 1024] = bc(inputs["pre_norm_w"][l])
        prm[l, :, O_SSDNW:O_SSDNW + 1024] = bc(inputs["ssd_norm_w"][l])
        prm[l, :, O_POSTW:O_POSTW + 1024] = bc(inputs["post_norm_w"][l])
        prm[l, :, O_D16:O_D16 + 16] = bc(inputs["ssd_d"][l])
        prm[l, :, O_DTB:O_DTB + 16] = bc(inputs["ssd_dt_bias"][l])
        prm[l, :, O_ALOG:O_ALOG + 16] = bc(inputs["ssd_a_log"][l])
        cw = np.asarray(inputs["ssd_conv_w"][l], f32)
        prm[l, :, O_CW:O_CW + 48] = cw.reshape(4, 12, 128).transpose(2, 1, 0).reshape(128, 48)
        prm[l, :, O_CB:O_CB + 12] = np.asarray(inputs["ssd_conv_b"][l], f32).reshape(12, 128).T
        dw = np.asarray(inputs["conf_dw_w"][l], f32)
        prm[l, :, O_DWW:O_DWW + 124] = dw.reshape(31, 4, 128).transpose(2, 1, 0).reshape(128, 124)
        prm[l, :, O_DWB:O_DWB + 4] = np.asarray(inputs["conf_dw_b"][l], f32).reshape(4, 128).T
        prm[l, :, O_LNW:O_LNW + 4] = np.asarray(inputs["conf_ln_w"][l], f32).reshape(4, 128).T
        prm[l, :, O_LNB:O_LNB + 4] = np.asarray(inputs["conf_ln_b"][l], f32).reshape(4, 128).T
        prm[l, :, O_SINK:O_SINK + 8] = bc(inputs["attn_sinks"][l])
    s = np.arange(128)[:, None, None]
    kb = np.arange(2)[None, :, None]
    q = np.arange(128)[None, None, :]
    dist = q + 128 - (kb * 128 + s)
    valid = (dist >= 0) & (dist < 128)
    bucket = _t5_bucket(np.clip(dist, 0, 127))
    rel = np.asarray(inputs["rel_bias"], f32)
    abias = np.ascontiguousarray(rel[bucket].transpose(0, 1, 3, 2)).reshape(128, 2 * 8 * 128)
    amask = np.where(valid, 0.0, NEG).astype(f32).reshape(128, 256)
    ident = np.eye(128, dtype=f32)
    tri = np.triu(np.ones((128, 128), f32))
    common = {"w_in": wi, "w_out": wo, "w_gate": wg, "w_proj": wp, "prm": prm, "c_abias": abias,
              "c_amask": amask, "c_ident": ident, "c_tri": tri,
              "c_mneg": np.where(np.arange(128)[:, None] <= np.arange(128)[None, :], 0.0, NEG).astype(f32)}
    return common


def _build(layers, debug=False):
    nc = bass.Bass("TRN2", target_bir_lowering=False)
    dram = {}
    dt = lambda name, shape, dtype, kind: nc.dram_tensor(name, shape, dtype, kind=kind).ap()
    dram["x"] = dt("x", [L_SEQ, D], F32, "ExternalInput")
    dram["p"] = dt("p", [DEPTH, L_SEQ, 256], F32, "ExternalInput")
    for name, ng, w in (("w_in", 11, 4096), ("w_out", 4, 4096), ("w_gate", 2, 4096), ("w_proj", 1, 2048)):
        dram[name] = dt(name, [DEPTH, ng, 128, w], F32, "ExternalInput")
        dram[name + "_s"] = dt(name + "_s", [DEPTH, ng, 128, w], BF16, "Internal")
    dram["prm"] = dt("prm", [DEPTH, 128, NP], F32, "ExternalInput")
    dram["c_abias"] = dt("c_abias", [128, 2048], F32, "ExternalInput")
    dram["c_amask"] = dt("c_amask", [128, 256], F32, "ExternalInput")
    dram["c_ident"] = dt("c_ident", [128, 128], F32, "ExternalInput")
    dram["c_tri"] = dt("c_tri", [128, 128], F32, "ExternalInput")
    dram["c_mneg"] = dt("c_mneg", [128, 128], F32, "ExternalInput")
    dram["hscr"] = dt("hscr", [L_SEQ, D], F32, "Internal")
    dram["acs_d"] = dt("acs_d", [2, 64, 128], F32, "Internal")
    dram["out"] = dt("out", [L_SEQ, D], F32, "ExternalOutput")
    if debug:
        dram["dbg"] = dt("dbg", [128, 16, L_SEQ], F32, "ExternalOutput")
    with ExitStack() as st:
        prog = Prog(nc, st, layers, True, True)
        prog.build(dram)
    return nc


def kernel(**inputs):
    common = _prep(inputs)
    x = np.asarray(inputs["x"], np.float32)
    p = np.asarray(inputs["p"], np.float32)
    nb = x.shape[0]
    nc = _build(list(range(DEPTH)), debug=DEBUG)
    in_maps = []
    for b in range(nb):
        m = dict(common)
        m["x"] = np.ascontiguousarray(x[b])
        m["p"] = np.ascontiguousarray(p[:, b])
        in_maps.append(m)
    res = run_bass_kernel_spmd(nc, in_maps, core_ids=list(range(nb)))
    out = np.stack([np.asarray(r["out"], np.float32) for r in res.results], axis=0)
    if DEBUG:
        kernel.dbg = [np.asarray(r["dbg"]) for r in res.results]
    return out
```

```python
from contextlib import ExitStack
import math
import numpy as np
import concourse.bass as bass
import concourse.mybir as mybir
from concourse.bass_utils import run_bass_kernel_spmd

F32 = mybir.dt.float32
BF16 = mybir.dt.bfloat16
AF = mybir.ActivationFunctionType
ALU = mybir.AluOpType

L_SEQ = 2048
D = 1024
DEPTH = 2
TB = 512
NT = 4
NBLK = L_SEQ // TB
EPS = 1e-6
NEG = -30000.0

O_PREW, O_SSDNW, O_POSTW = 0, 1024, 2048
O_DTB, O_ALOG, O_D16 = 3072, 3088, 3104
O_CW, O_CB = 3120, 3168
O_DWW, O_DWB, O_LNW, O_LNB, O_SINK = 3180, 3304, 3308, 3312, 3316
NP = 3328

DEBUG = False


class Buf:
    __slots__ = ("w", "r")

    def __init__(self):
        self.w = {}
        self.r = {}


class Sched:
    ENGS = ("pe", "act", "dve", "pool", "sp")
    NSLOT = {"sp": 12, "pool": 44, "act": 8}

    def __init__(self, nc, stack):
        self.nc = nc
        self.eng = {"pe": nc.tensor, "act": nc.scalar, "dve": nc.vector,
                    "pool": nc.gpsimd, "sp": nc.sync}
        self.sems = {}
        self.cnt = {}
        self.known = {e: {} for e in self.ENGS}
        for e in self.ENGS:
            self.sems[e] = stack.enter_context(nc.semaphore("s_" + e))
            self.cnt[e] = 0
        self.slot_i = {}
        self.fence = {}
        for q, n in self.NSLOT.items():
            self.slot_i[q] = 0
            for k in range(n):
                key = "d_%s%d" % (q, k)
                self.sems[key] = stack.enter_context(nc.semaphore(key))
                self.cnt[key] = 0

    LOG = None

    def _push(self, engname, waits, fn, key, inc):
        if Sched.LOG is not None:
            Sched.LOG.append((engname, list(waits), key if fn is not None else None, inc))
        e = self.eng[engname]
        for k, v in waits:
            e.wait_ge(self.sems[k], v)
        if fn is not None:
            fn(e).then_inc(self.sems[key], inc)

    def barrier(self):
        engs = ("pe", "act", "dve", "pool")
        for e in engs:
            waits = []
            tgt = {k: self.cnt[k] for k in engs if k != e}
            tgt.update(self.fence)
            for k, v in tgt.items():
                if v > 0 and self.known[e].get(k, 0) < v:
                    self.known[e][k] = v
                    waits.append((k, v))
            self._push(e, waits, None, None, 0)

    def _deps(self, eng, reads, writes):
        need = {}

        def add(d):
            for k, v in d.items():
                if need.get(k, 0) < v:
                    need[k] = v
        for b in reads:
            add(b.w)
        for b in writes:
            add(b.w)
            add(b.r)
        out = []
        kn = self.known[eng]
        for k, v in need.items():
            if k == eng and eng == "pe":
                continue
            if kn.get(k, 0) >= v:
                continue
            kn[k] = v
            out.append((k, v))
        return out

    def op(self, eng, fn, reads=(), writes=()):
        waits = self._deps(eng, reads, writes)
        self.cnt[eng] += 1
        t = self.cnt[eng]
        for b in reads:
            if b.r.get(eng, 0) < t:
                b.r[eng] = t
        for b in writes:
            b.w = {eng: t}
            b.r = {}
        self._push(eng, waits, fn, eng, 1)

    def dma(self, q, out, in_, reads=(), writes=(), fence=None):
        if fence is None:
            fence = (q == "pool")
        waits = self._deps(q, reads, writes)
        i = self.slot_i[q]
        self.slot_i[q] += 1
        key = "d_%s%d" % (q, i % self.NSLOT[q])
        prev = self.cnt[key]
        if prev > 0 and self.known[q].get(key, 0) < prev:
            self.known[q][key] = prev
            waits.append((key, prev))
        self.cnt[key] += 16
        t = self.cnt[key]
        if fence:
            self.fence[key] = t
        for b in reads:
            if b.r.get(key, 0) < t:
                b.r[key] = t
        for b in writes:
            b.w = {key: t}
            b.r = {}
        self._push(q, waits, lambda e: e.dma_start(out=out, in_=in_), key, 16)

    def wait_all(self, eng, bufs):
        waits = self._deps(eng, bufs, ())
        self._push(eng, waits, None, None, 0)


class Tl:
    def __init__(self, t, nb=1):
        self.t = t
        self.bs = [Buf() for _ in range(nb)]

    @property
    def b(self):
        return self.bs[0]

    def __getitem__(self, k):
        return self.t[k]


class Prog:
    def __init__(self, nc, st, layers, first_in, last_out):
        self.nc = nc
        self.st = st
        self.S = Sched(nc, st)
        self.layers = layers

    _uid = [0]

    def sb(self, stack, name, shape, dt, nb=1):
        self._uid[0] += 1
        name = "%s_%d" % (name, self._uid[0])
        return Tl(stack.enter_context(self.nc.sbuf_tensor(name, shape, dt)), nb)

    def ps(self, stack, name, shape, dt):
        return Tl(stack.enter_context(self.nc.psum_tensor(name, shape, dt)))

    def mm(self, out, pairs, reads, writes, start=True, stop=True, sgc=False):
        def fn(e):
            n = len(pairs)
            ins = None
            for i, (l, r) in enumerate(pairs):
                kw = {}
                if sgc:
                    kw["skip_group_check"] = True
                ins = e.matmul(out, lhsT=l, rhs=r, start=(start and i == 0),
                               stop=(stop and i == n - 1), **kw)
            return ins
        self.S.op("pe", fn, reads, writes)

    def mms(self, items, reads, writes):
        def fn(e):
            ins = None
            for (o, l, r, s0, s1) in items:
                ins = e.matmul(o, lhsT=l, rhs=r, start=s0, stop=s1, skip_group_check=True)
            return ins
        self.S.op("pe", fn, reads, writes)

    def trs(self, items, ident, reads, writes):
        def fn(e):
            ins = None
            for (o, i) in items:
                ins = e.transpose(out=o, in_=i, identity=ident)
            return ins
        self.S.op("pe", fn, reads, writes)

    def act(self, out, in_, func, reads, writes, bias=None, scale=None, accum=None):
        kw = {}
        if bias is not None:
            kw["bias"] = bias
        if scale is not None:
            kw["scale"] = scale
        if accum is not None:
            kw["accum_out"] = accum
        self.S.op("act", lambda e: e.activation(out=out, in_=in_, func=func, **kw), reads, writes)

    def tt(self, out, in0, in1, op, reads, writes, eng="dve"):
        self.S.op(eng, lambda e: e.tensor_tensor(out=out, in0=in0, in1=in1, op=op), reads, writes)

    def ts(self, out, in0, s1, s2, op0, op1, reads, writes):
        if s2 is None:
            self.S.op("dve", lambda e: e.tensor_scalar(out=out, in0=in0, scalar1=s1, scalar2=None, op0=op0),
                      reads, writes)
        else:
            self.S.op("dve", lambda e: e.tensor_scalar(out=out, in0=in0, scalar1=s1, scalar2=s2,
                                                       op0=op0, op1=op1), reads, writes)

    def stt(self, out, in0, scalar, in1, op0, op1, reads, writes):
        self.S.op("dve", lambda e: e.scalar_tensor_tensor(out=out, in0=in0, scalar=scalar, in1=in1,
                                                          op0=op0, op1=op1), reads, writes)

    def cp(self, eng, out, in_, reads, writes):
        if eng == "act":
            self.S.op("act", lambda e: e.copy(out=out, in_=in_), reads, writes)
        else:
            self.S.op(eng, lambda e: e.tensor_copy(out=out, in_=in_), reads, writes)

    def memset(self, ap, val, writes, eng="dve"):
        self.S.op(eng, lambda e: e.memset(ap, val), (), writes)

    def rstd(self, out, ssq, inv_n, tmp, reads, writes):
        self.ts(tmp, ssq, inv_n, EPS, ALU.mult, ALU.add, reads, writes)
        self.act(tmp, tmp, AF.Ln, writes, writes)
        self.act(out, tmp, AF.Exp, writes, writes, scale=-0.5)

    def build(self, dram):
        nc, S, st = self.nc, self.S, self.st
        sb, ps = self.sb, self.ps
        mm, mms, trs, act, tt, ts, stt, cp, memset = (self.mm, self.mms, self.trs, self.act, self.tt,
                                                      self.ts, self.stt, self.cp, self.memset)
        b_scr = {}
        for l in self.layers:
            for name, ng in (("w_proj", 1), ("w_in", 11), ("w_out", 4), ("w_gate", 2)):
                for g in range(ng):
                    b_scr[(l, name, g)] = Buf()

        cast_q = []
        for l in self.layers:
            cast_q.append((l, "w_proj", 0))
            for g in range(11):
                cast_q.append((l, "w_in", g))
            for g in range(4):
                cast_q.append((l, "w_out", g))
            for g in range(2):
                cast_q.append((l, "w_gate", g))
        cast_pos = [0]

        def cast_some(n):
            for _ in range(n):
                if cast_pos[0] < len(cast_q):
                    l_, name, g = cast_q[cast_pos[0]]
                    cast_pos[0] += 1
                    S.dma("pool", dram[name + "_s"][l_, g], dram[name][l_, g], writes=[b_scr[(l_, name, g)]],
                          fence=False)

        IDF = sb(st, "IDF", [128, 128], F32)
        IDB = sb(st, "IDB", [128, 128], BF16)
        U = sb(st, "U", [128, 128], F32)
        ONEF = sb(st, "ONEF", [128, 128], F32)
        ONE512 = sb(st, "ONE512", [128, 128], F32)
        ONEB = sb(st, "ONEB", [128, 256], BF16)
        AMASK = sb(st, "AMASK", [128, 2, 128], F32)
        PRM = sb(st, "PRM", [128, NP], F32)
        DER = sb(st, "DER", [128, 64], F32)
        BT = sb(st, "BT", [128, 2, 8, 128], F32)
        WP = sb(st, "WP", [128, 2, 1024], BF16)
        WB = [sb(st, "WB%d" % i, [128, 8, 512], BF16) for i in range(3)]
        XNT = sb(st, "XNT", [128, 8, 512], BF16)
        YT = sb(st, "YT", [128, 16, 512], BF16)
        HS = sb(st, "HS", [128, 1024], F32)
        HSB = sb(st, "HSB", [128, 1024], BF16)
        HSbufs = [Buf(), Buf()]
        HSBbufs = [Buf(), Buf()]
        TAIL = sb(st, "TAIL", [128, 12, 3], F32)
        GLU = sb(st, "GLU", [128, 4, 30 + TB], BF16)
        KT = sb(st, "KT", [128, 128 + TB], BF16)
        VT = sb(st, "VT", [128, 5, 128], BF16)
        JUNK = sb(st, "JUNK", [128, 1024], BF16)
        SM = sb(st, "SM", [128, 64], F32)
        PA = ps(st, "PA", [128, 512], F32)
        PB = ps(st, "PB", [128, 512], F32)
        PT = ps(st, "PT", [128, 1024], BF16)
        PS_ = ps(st, "PS", [128, 512], F32)
        PX0 = ps(st, "PX0", [128, 512], F32)
        PX1 = ps(st, "PX1", [128, 512], F32)
        PY = ps(st, "PY", [128, 512], F32)
        PO = ps(st, "PO", [128, 512], F32)

        def view(ap, bufs):
            v = Tl.__new__(Tl)
            v.t = ap
            v.bs = list(bufs)
            return v

        XR = [sb(st, "XR%d" % i, [128, 1024], F32) for i in range(2)]
        YN = sb(st, "YN", [128, 1024], BF16)
        XN2 = sb(st, "XN2", [128, 1024], BF16)
        XNs = [YN, XN2]
        PG = sb(st, "PG", [128, 4, 8 + TB], F32, nb=4)
        PRE = [view(PG[:, i, 0:3 + TB], [PG.bs[i]]) for i in range(2)]
        PREB = [view(PG[:, i, 0:264].bitcast(BF16)[:, 0:3 + TB], [PG.bs[i]]) for i in range(2)]
        DGS = [view(PG[:, i, 264:520].bitcast(BF16).rearrange("p (a b) -> p a b", a=4), [PG.bs[i]]) for i in range(2)]
        TAILB = sb(st, "TAILB", [128, 12, 4], BF16)
        ACC = [view(PG[:, 2 + i, 0:TB], [PG.bs[2 + i]]) for i in range(2)]
        TH = [sb(st, "TH%d" % i, [128, TB], F32) for i in range(2)]
        XBC = sb(st, "XBC", [128, 12, TB], BF16, nb=12)
        ZS = sb(st, "ZS", [128, NT, 1024], BF16, nb=NT)
        DTR = sb(st, "DTR", [128, NT, 16], F32, nb=NT)
        SD2 = sb(st, "SD2", [128, 9, 64], F32, nb=9)
        XG = sb(st, "XG", [128, 6, 1024], BF16, nb=6)
        XDT = [view(XG[:, i, :], [XG.bs[i]]) for i in range(2)]
        XD = view(XG[:, 2, :], [XG.bs[2]])
        XDS = view(XG[:, 3, :], [XG.bs[3]])
        MT4 = [view(XG[:, 4 + i // 2, (i % 2) * 512:(i % 2 + 1) * 512].rearrange("p (a b) -> p a b", a=4),
                    [Buf()]) for i in range(4)]
        DG = view(PG[:].rearrange("p a b -> p (a b)").bitcast(BF16)[:, 0:31 * 128].rearrange("p (a b) -> p a b", a=31),
                  PG.bs)
        BMT = sb(st, "BMT", [128, 2, 128], BF16)
        CBM = [sb(st, "CBM%d" % i, [128, 2, 128], F32) for i in range(2)]
        MNEG = sb(st, "MNEG", [128, 4, 128], BF16)
        T1 = sb(st, "T1", [128, 512], F32)
        D1 = T1
        Y = sb(st, "Y", [128, 1024], F32)
        EB = [sb(st, "EB%d" % i, [128, 4, 128], F32) for i in range(2)]
        MU = view(XR[0][:, 0:512], [XR[0].b])
        RS = view(XR[0][:, 512:1024], [XR[0].b])
        U1 = sb(st, "U1", [128, TB], F32)
        QT = sb(st, "QT", [128, 4, TB], BF16, nb=4)
        AG = sb(st, "AG", [128, 4, TB], BF16, nb=4)
        AB = [sb(st, "AB%d" % i, [128, 4, 128], F32) for i in range(3)]
        ACST = sb(st, "ACST", [64, 128], F32)
        b_acsd = [Buf(), Buf()]
        blkctr = [0]
        UB = sb(st, "UB", [128, 128], BF16)
        DAB = sb(st, "DAB", [128, 2, 64], BF16, nb=2)
        SC = [view(XR[1][:, i * 512:(i + 1) * 512], [XR[1].b]) for i in range(2)]
        ET = [sb(st, "ET%d" % i, [128, 512], BF16) for i in range(4)]
        RD = sb(st, "RD", [128, 256], F32)
        OT = sb(st, "OT", [128, 256], F32)
        O = sb(st, "O", [128, NT, 1024], F32, nb=NT)
        HH = view(O[:, 0:2, :].rearrange("p a (b c) -> p (a b) c", b=2), [O.bs[0], O.bs[0], O.bs[1], O.bs[1]])
        HQ = [view(O[:, 2, i * TB:(i + 1) * TB], [O.bs[2]]) for i in range(2)]
        CG = view(O[:, 3, :].bitcast(BF16).rearrange("p (a b) -> p a b", a=4), [O.bs[3]] * 4)
        PR = [sb(st, "PR%d" % i, [128, 256], F32) for i in range(2)]
        SM1s = [sb(st, "SM1_%d" % i, [128, 8], F32) for i in range(2)]
        SM5s = [sb(st, "SM5_%d" % i, [128, 8], F32) for i in range(2)]
        SMEs = [sb(st, "SME_%d" % i, [128, 8], F32) for i in range(2)]
        HT = view(YT[:, 0:8, :], [Buf()])
        HBs = [view(YT[:, 8 + 6 * i:10 + 6 * i, :].rearrange("p a b -> p (a b)"), [Buf()]) for i in range(2)]
        PTT = view(YT[:, 10:12, :], [Buf()])
        TG = [view(YT[:, 12:14, :].rearrange("p a b -> p (a b)").bitcast(F32), [Buf()])] * 2
        PB16s = [sb(st, "PB16a", [128, 256], BF16), sb(st, "PB16b", [128, 256], BF16)]
        p5views = [HT, HBs[0], HBs[1], PTT, TG[0]]
        PSB = view(PS_[:, 384:512].bitcast(BF16), [PS_.b])
        mmb = [PA, PB, PX0, PX1, PY, PO]
        mmi = [0]

        def nextmm():
            p = mmb[mmi[0] % len(mmb)]
            mmi[0] += 1
            return p

        S.dma("sp", IDF[:], dram["c_ident"], writes=[IDF.b])
        S.dma("sp", U[:], dram["c_tri"], writes=[U.b])
        S.dma("sp", AMASK[:], dram["c_amask"], writes=[AMASK.b])
        cp("dve", IDB[:], IDF[:], [IDF.b], [IDB.b])
        cp("dve", UB[:], U[:], [U.b], [UB.b])
        S.dma("sp", EB[0][:, 0, :], dram["c_mneg"], writes=[EB[0].b])
        cp("dve", MNEG[:], EB[0][:, 0, :].unsqueeze(1).to_broadcast([128, 4, 128]), [EB[0].b], [MNEG.b])
        memset(ONEF[:], 1.0, [ONEF.b])
        memset(ONE512[:], 1.0 / 512.0, [ONE512.b])
        memset(ONEB[:], 1.0, [ONEB.b])

        wplan = []
        wloaded = []
        wstate = ["free", "free", "free"]
        cast_idx = {k: i for i, k in enumerate(cast_q)}

        def w_pump():
            while wplan and len(wloaded) < 2 and "free" in wstate:
                key = wplan.pop(0)
                bi = wstate.index("free")
                wstate[bi] = "loaded"
                l_, name, g = key
                assert cast_idx[key] < cast_pos[0], ("cast not issued", key)
                S.dma("sp", WB[bi][:].rearrange("p a b -> p (a b)"), dram[name + "_s"][l_, g],
                      reads=[b_scr[key]], writes=[WB[bi].b])
                wloaded.append((key, bi))

        def w_acquire(key):
            cast_some(1)
            if not wloaded:
                w_pump()
            k2, bi = wloaded.pop(0)
            assert k2 == key, (k2, key)
            wstate[bi] = "held"
            w_pump()
            return WB[bi]

        def w_release(buf):
            bi = WB.index(buf)
            assert wstate[bi] == "held"
            wstate[bi] = "free"
            w_pump()

        def inherit(dst, srcs):
            for sb_ in srcs:
                for d in (sb_.r, sb_.w):
                    for k, v in d.items():
                        if dst.r.get(k, 0) < v:
                            dst.r[k] = v

        def interleave(A, B):
            out = []
            na, nb = len(A), len(B)
            ia = ib = 0
            while ia < na or ib < nb:
                if ib >= nb or (ia < na and ia * nb <= ib * na):
                    out.append(A[ia]); ia += 1
                else:
                    out.append(B[ib]); ib += 1
            return out

        b_hscr = [Buf() for _ in range(16)]
        b_out = [Buf() for _ in range(16)]

        for li, l in enumerate(self.layers):
            src = dram["x"] if li == 0 else dram["hscr"]
            dst = dram["out"] if li == len(self.layers) - 1 else dram["hscr"]
            b_src = None if li == 0 else b_hscr
            b_dst = b_out if li == len(self.layers) - 1 else b_hscr
            S.dma("sp", PRM[:], dram["prm"][l], writes=[PRM.b])
            S.dma("sp", BT[:].rearrange("p a h q -> p (a h q)"), dram["c_abias"], writes=[BT.b])
            preloaded = set()
            if li == 0:
                for t in range(2):
                    S.dma("sp", XR[t][:], src[t * 128:(t + 1) * 128, :], writes=[XR[t].b])
                    preloaded.add(t)
                S.wait_all("pool", [PRM.b, BT.b, IDF.b, U.b, AMASK.b, EB[0].b, XR[0].b, XR[1].b])
                cast_some(4)
            S.dma("sp", WP[:].rearrange("p a b -> p (a b)"), dram["w_proj_s"][l, 0],
                  reads=[b_scr[(l, "w_proj", 0)]], writes=[WP.b])
            ts(PRM[:, O_CW:O_CB + 12], PRM[:, O_CW:O_CB + 12], 0.5, None, ALU.mult, None, [PRM.b], [PRM.b])
            ts(PRM[:, O_DWW:O_DWW + 124], PRM[:, O_DWW:O_DWW + 124], 0.5, None, ALU.mult, None, [PRM.b], [PRM.b])
            act(DER[:, 0:16], PRM[:, O_ALOG:O_ALOG + 16], AF.Exp, [PRM.b], [DER.b])
            ts(DER[:, 0:16], DER[:, 0:16], -1.0, None, ALU.mult, None, [DER.b], [DER.b])
            ts(DER[:, 16:24], PRM[:, O_LNW:O_LNW + 8], 0.5, None, ALU.mult, None, [PRM.b], [DER.b])
            tt(BT[:], BT[:], AMASK[:].unsqueeze(2).to_broadcast([128, 2, 8, 128]), ALU.add, [BT.b, AMASK.b], [BT.b])
            tt(BT[:], BT[:], PRM[:, O_SINK:O_SINK + 8].unsqueeze(1).unsqueeze(3).to_broadcast([128, 2, 8, 128]),
               ALU.subtract, [BT.b, PRM.b], [BT.b])
            memset(HS[:], 0.0, HSbufs)
            memset(HSB[:], 0.0, HSBbufs)
            memset(TAIL[:], 0.0, [TAIL.b])
            memset(TAILB[:], 0.0, [TAILB.b])
            memset(GLU[:], 0.0, [GLU.b])
            memset(KT[:], 0.0, [KT.b])
            memset(VT[:], 0.0, [VT.b])

            def make_p1(tb):
                tok0 = tb * TB
                ths = []

                def tileA(t):
                    xr = XR[t % 2]
                    xn = XNs[t % 2]
                    r0 = tok0 + t * 128
                    if not (tb == 0 and t in preloaded):
                        S.dma("act", xr[:], src[r0:r0 + 128, :],
                              reads=([] if b_src is None else [b_src[tb * NT + t]]), writes=[xr.b])
                    SM1 = SM1s[t % 2]
                    memset(SM1[:, 0:1], 0.0, [SM1.b])
                    act(JUNK[:], xr[:], AF.Square, [xr.b, SM1.b], [JUNK.b, SM1.b], accum=SM1[:, 0:1])
                    self.rstd(SM1[:, 2:3], SM1[:, 0:1], 1.0 / D, SM1[:, 1:2], [SM1.b], [SM1.b])
                    stt(xn[:], xr[:], SM1[:, 2:3], PRM[:, O_PREW:O_PREW + 1024], ALU.mult, ALU.mult,
                        [xr.b, SM1.b, PRM.b], [xn.b])

                def tileB(t):
                    xn = XNs[t % 2]
                    trs([(PT[:, j * 128:(j + 1) * 128], xn[:, j * 128:(j + 1) * 128]) for j in range(8)],
                        IDB[:], [xn.b, IDB.b], [PT.b])
                    cp("act", XNT[:, :, t * 128:(t + 1) * 128], PT[:].rearrange("p (a b) -> p a b", a=8),
                       [PT.b], [XNT.b])
                order = [("A", 0), ("A", 1), ("B", 0), ("A", 2), ("B", 1), ("A", 3), ("B", 2), ("B", 3)]
                for kind, t in order:
                    ths.append(((lambda t=t: tileA(t)) if kind == "A" else (lambda t=t: tileB(t)), []))
                return ths

            def make_p2a(tb):
                ths = []
                hold = {}

                def xbcA(c):
                    if c % 4 == 0:
                        hold["w"] = w_acquire((l, "w_in", c // 4))
                    wb = hold["w"]
                    c4 = c % 4
                    pm = nextmm()
                    mm(pm[:], [(wb[:, kc, c4 * 128:(c4 + 1) * 128], XNT[:, kc, :]) for kc in range(8)],
                       [wb.b, XNT.b], [pm.b])
                    if c % 4 == 3:
                        w_release(wb)
                    pre, dgs = PREB[c % 2], DGS[c % 2]
                    tt(dgs[:], IDF[:].unsqueeze(1).to_broadcast([128, 4, 128]),
                       PRM[:, O_CW + c * 4:O_CW + c * 4 + 4].unsqueeze(2).to_broadcast([128, 4, 128]), ALU.mult,
                       [IDF.b, PRM.b], [dgs.b])
                    cp("dve", pre[:, 0:3], TAILB[:, c, 0:3], [TAILB.b], [pre.b])
                    cp("act", pre[:, 3:3 + TB], pm[:], [pm.b], [pre.b])
                    cp("dve", TAILB[:, c, 0:3], pre[:, TB:TB + 3], [pre.b], [TAILB.b])

                def xbcB(c):
                    pre, acc, th, dgs = PREB[c % 2], ACC[c % 2], TH[c % 2], DGS[c % 2]
                    pc = nextmm()
                    mm(pc[:], [(dgs[:, k, :], pre[:, k:k + TB]) for k in range(4)], [dgs.b, pre.b], [pc.b])
                    act(acc[:], pc[:], AF.Identity, [pc.b, PRM.b], [acc.b], bias=PRM[:, O_CB + c:O_CB + c + 1])
                    act(th[:], acc[:], AF.Tanh, [acc.b], [th.b])
                    stt(XBC[:, c, :], th[:], 1.0, acc[:], ALU.add, ALU.mult, [th.b, acc.b], [XBC.bs[c]])

                def z_tile(half, t):
                    if t == 0:
                        hold["w"] = w_acquire((l, "w_in", 3 + half))
                    wb = hold["w"]
                    pm = nextmm()
                    mm(pm[:], [(XNT[:, kc, t * 128:(t + 1) * 128], wb[:, kc, :]) for kc in range(8)],
                       [wb.b, XNT.b], [pm.b])
                    if t == NT - 1:
                        w_release(wb)
                    th = TH[t % 2]
                    act(th[:], pm[:], AF.Tanh, [pm.b], [th.b], scale=0.5)
                    stt(ZS[:, t, half * 512:(half + 1) * 512], th[:], 1.0, pm[:], ALU.add, ALU.mult,
                        [th.b, pm.b], [ZS.bs[t]])

                def kv():
                    wb = w_acquire((l, "w_in", 5))
                    cp("dve", KT[:, 0:128], KT[:, TB:TB + 128], [KT.b], [KT.b])
                    cp("dve", VT[:, 0, :], VT[:, 4, :], [VT.b], [VT.b])
                    pm = nextmm()
                    mm(pm[:], [(wb[:, kc, 0:128], XNT[:, kc, :]) for kc in range(8)], [wb.b, XNT.b], [pm.b])
                    cp("act", KT[:, 128:128 + TB], pm[:], [pm.b], [KT.b])
                    for t in range(NT):
                        pm = nextmm()
                        mm(pm[:, 0:144], [(XNT[:, kc, t * 128:(t + 1) * 128], wb[:, kc, 128:272]) for kc in range(8)],
                           [wb.b, XNT.b], [pm.b])
                        cp("dve", DTR[:, t, :], pm[:, 0:16], [pm.b], [DTR.bs[t]])
                        cp("act", VT[:, 1 + t, :], pm[:, 16:144], [pm.b], [VT.b])
                    w_release(wb)
                for c in range(13):
                    if c < 12:
                        ths.append((lambda c=c: xbcA(c), [(l, "w_in", c // 4)] if c % 4 == 0 else []))
                    if c >= 1:
                        ths.append((lambda c=c: xbcB(c - 1), []))
                for half in range(2):
                    for t in range(NT):
                        ths.append((lambda half=half, t=t: z_tile(half, t), [(l, "w_in", 3 + half)] if t == 0 else []))
                ths.append((kv, [(l, "w_in", 5)]))
                return ths

            def make_p5b(tb):
                tok0 = tb * TB
                ths = []
                hold = {}

                def post(t):
                    r0 = tok0 + t * 128
                    xr = XR[t % 2]
                    S.dma("act", xr[:], src[r0:r0 + 128, :],
                          reads=([] if b_src is None else [b_src[tb * NT + t]]), writes=[xr.b])
                    pr = PR[t % 2]
                    S.dma("act", pr[:], dram["p"][l, r0:r0 + 128, :], writes=[pr.b])
                    SM5 = SM5s[t % 2]
                    memset(SM5[:, 0:1], 0.0, [SM5.b])
                    act(JUNK[:], O[:, t, :], AF.Square, [O.bs[t], SM5.b], [JUNK.b, SM5.b], accum=SM5[:, 0:1])
                    self.rstd(SM5[:, 2:3], SM5[:, 0:1], 1.0 / D, SM5[:, 1:2], [SM5.b], [SM5.b])
                    stt(O[:, t, :], O[:, t, :], SM5[:, 2:3], PRM[:, O_POSTW:O_POSTW + 1024], ALU.mult, ALU.mult,
                        [O.bs[t], SM5.b, PRM.b], [O.bs[t]])
                    tt(O[:, t, :], O[:, t, :], xr[:], ALU.add, [O.bs[t], xr.b], [O.bs[t]])
                    HB, PB16 = HBs[t % 2], PB16s[t % 2]
                    cp("act", HB[:], O[:, t, :], [O.bs[t]], [HB.b])
                    cp("dve", PB16[:], pr[:], [pr.b], [PB16.b])

                def postB(t):
                    HB, PB16 = HBs[t % 2], PB16s[t % 2]
                    trs([(PT[:, j * 128:(j + 1) * 128], HB[:, j * 128:(j + 1) * 128]) for j in range(8)], IDB[:],
                        [HB.b, IDB.b], [PT.b])
                    cp("act", HT[:, :, t * 128:(t + 1) * 128], PT[:].rearrange("p (a b) -> p a b", a=8),
                       [PT.b], [HT.b])
                    trs([(PSB[:, j * 128:(j + 1) * 128], PB16[:, j * 128:(j + 1) * 128]) for j in range(2)], IDB[:],
                        [PB16.b, IDB.b], [PSB.b])
                    cp("act", PTT[:, :, t * 128:(t + 1) * 128], PSB[:].rearrange("p (a b) -> p a b", a=2),
                       [PSB.b], [PTT.b])

                def ple(ch, t):
                    if t == 0:
                        hold["w"] = w_acquire((l, "w_gate", ch))
                    wg = hold["w"]
                    tcols = slice(t * 128, (t + 1) * 128)
                    pg = nextmm()
                    mm(pg[:], [(HT[:, kc, tcols], wg[:, kc, :]) for kc in range(8)], [HT.b, wg.b], [pg.b])
                    if t == NT - 1:
                        w_release(wg)
                    pp = nextmm()
                    mm(pp[:], [(PTT[:, kc, tcols], WP[:, kc, ch * 512:(ch + 1) * 512]) for kc in range(2)],
                       [PTT.b, WP.b], [pp.b])
                    tg = TG[t % 2]
                    act(tg[:], pg[:], AF.Tanh, [pg.b], [tg.b], scale=0.5)
                    stt(tg[:], tg[:], 1.0, pp[:], ALU.add, ALU.mult, [tg.b, pp.b], [tg.b])
                    stt(O[:, t, ch * 512:(ch + 1) * 512], tg[:], 0.5, O[:, t, ch * 512:(ch + 1) * 512],
                        ALU.mult, ALU.add, [tg.b, O.bs[t]], [O.bs[t]])

                def store(t):
                    r0 = tok0 + t * 128
                    S.dma("pool", dst[r0:r0 + 128, :], O[:, t, :], reads=[O.bs[t]], writes=[b_dst[tb * NT + t]])
                ths.append((lambda: post(0), []))
                ths.append((lambda: post(1), []))
                ths.append((lambda: postB(0), []))
                ths.append((lambda: post(2), []))
                ths.append((lambda: postB(1), []))
                ths.append((lambda: ple(0, 0), [(l, "w_gate", 0)]))
                ths.append((lambda: post(3), []))
                ths.append((lambda: postB(2), []))
                ths.append((lambda: ple(0, 1), []))
                ths.append((lambda: postB(3), []))
                ths.append((lambda: ple(0, 2), []))
                ths.append((lambda: ple(0, 3), []))
                for t in range(NT):
                    ths.append((lambda t=t: ple(1, t), [(l, "w_gate", 1)] if t == 0 else []))
                for t in range(NT):
                    ths.append((lambda t=t: store(t), []))
                return ths

            def run(ths):
                for fn, _ in ths:
                    fn()

            def keys_of(ths):
                return [k for _, ws in ths for k in ws]

            k_ssdx = [(l, "w_in", g) for g in range(6, 11)]
            k_p5a = [(l, "w_out", g) for g in range(4)]
            first = make_p1(0) + make_p2a(0)
            wplan.extend(keys_of(first) + k_ssdx + k_p5a)
            w_pump()
            run(first)

            for tb in range(NBLK):
                tok0 = tb * TB
                if True:
                    row = lambda i: SD2[:, i, :]
                    r3 = lambda i: SD2[:, i, :].rearrange("p (t h) -> p t h", t=4)
                    sbf = SD2.bs
                    tt(r3(0), DTR[:], PRM[:, O_DTB:O_DTB + 16].unsqueeze(1).to_broadcast([128, 4, 16]), ALU.add,
                       DTR.bs + [PRM.b], [sbf[0]])
                    act(row(1), row(0), AF.Abs, [sbf[0]], [sbf[1]])
                    act(row(1), row(1), AF.Exp, [sbf[1]], [sbf[1]], scale=-1.0)
                    act(row(1), row(1), AF.Ln, [sbf[1]], [sbf[1]], bias=1.0)
                    stt(row(2), row(0), 0.0, row(1), ALU.max, ALU.add, [sbf[0], sbf[1]], [sbf[2]])
                    tt(r3(3), r3(2), DER[:, 0:16].unsqueeze(1).to_broadcast([128, 4, 16]), ALU.mult,
                       [sbf[2], DER.b], [sbf[3]])
                    cp("dve", DAB[:, 0, :], row(3), [sbf[3]], [DAB.bs[0]])
                    cp("dve", row(4), DAB[:, 0, :], [DAB.bs[0]], [sbf[4]])
                    tt(row(4), row(3), row(4), ALU.subtract, [sbf[3], sbf[4]], [sbf[4]])
                    cp("dve", DAB[:, 1, :], row(4), [sbf[4]], [DAB.bs[1]])
                    mms([(PS_[:, 256:320], UB[:], DAB[:, 0, :], True, False),
                         (PS_[:, 256:320], UB[:], DAB[:, 1, :], False, True),
                         (PS_[:, 320:384], ONEB[:, 0:128], DAB[:, 0, :], True, False),
                         (PS_[:, 320:384], ONEB[:, 0:128], DAB[:, 1, :], False, True)],
                        [UB.b, ONEB.b] + DAB.bs, [PS_.b])
                    ts(row(5), PS_[:, 256:320], -1.0, None, ALU.mult, None, [PS_.b], [sbf[5]])
                    cp("dve", row(4), PS_[:, 256:320], [PS_.b], [sbf[4]])
                    tt(row(6), PS_[:, 320:384], row(5), ALU.add, [PS_.b, sbf[5]], [sbf[6]])
                    act(row(6), row(6), AF.Exp, [sbf[6]], [sbf[6]])
                    act(row(7), PS_[:, 256:320], AF.Exp, [PS_.b], [sbf[7]])
                    act(row(8), PS_[:, 320:384], AF.Exp, [PS_.b], [sbf[8]])
                    slot = blkctr[0] % 2
                    blkctr[0] += 1
                    self.S.op("pe", lambda e: e.transpose(out=PS_[0:64, 0:128], in_=row(4), identity=IDF[:]),
                              [sbf[4], IDF.b], [PS_.b])
                    cp("act", ACST[:], PS_[0:64, 0:128], [PS_.b], [ACST.b])
                    S.dma("pool", dram["acs_d"][slot], ACST[:], reads=[ACST.b], writes=[b_acsd[slot]])

                    def stageB1(t):
                        cols = slice(t * 128, (t + 1) * 128)
                        pcb = nextmm()
                        mms([(pcb[:, g * 128:(g + 1) * 128], XBC[:, 8 + g, cols], XBC[:, 10 + g, cols], True, True)
                             for g in range(2)], [XBC.bs[8 + i] for i in range(4)], [pcb.b])
                        tt(CBM[t % 2][:], pcb[:, 0:256].rearrange("p (a b) -> p a b", a=2),
                           U[:].unsqueeze(1).to_broadcast([128, 2, 128]), ALU.mult, [pcb.b, U.b], [CBM[t % 2].b])

                    def stageB2(t):
                        cols = slice(t * 128, (t + 1) * 128)
                        xdt = XDT[t % 2]
                        trs([(PSB[:, g * 128:(g + 1) * 128], XBC[:, 8 + g, cols]) for g in range(2)], IDB[:],
                            [XBC.bs[8], XBC.bs[9], IDB.b], [PSB.b])
                        trs([(PT[:, j * 128:(j + 1) * 128], XBC[:, j, cols]) for j in range(8)], IDB[:],
                            [XBC.bs[j] for j in range(8)] + [IDB.b], [PT.b])
                        cp("dve", BMT[:].rearrange("p a b -> p (a b)"), PSB[:], [PSB.b], [BMT.b])
                        pt3 = PT[:].rearrange("p (h d) -> p h d", h=16)
                        tt(xdt[:].rearrange("p (h d) -> p h d", h=16), pt3,
                           r3(2)[:, t, :].unsqueeze(2).to_broadcast([128, 16, 64]), ALU.mult, [PT.b, sbf[2]], [xdt.b])
                        tt(XD[:].rearrange("p (h d) -> p h d", h=16), pt3,
                           PRM[:, O_D16:O_D16 + 16].unsqueeze(2).to_broadcast([128, 16, 64]), ALU.mult,
                           [PT.b, PRM.b], [XD.b])
                        tt(XDS[:].rearrange("p (h d) -> p h d", h=16), xdt[:].rearrange("p (h d) -> p h d", h=16),
                           r3(6)[:, t, :].unsqueeze(2).to_broadcast([128, 16, 64]), ALU.mult, [xdt.b, sbf[6]], [XDS.b])

                    def kdec(k):
                        return k // 4, (k // 2) % 2, k % 2

                    def stageC1(k):
                        t, g, hf = kdec(k)
                        r0 = t * 16 + g * 8 + hf * 4
                        ab = AB[k % 3]
                        S.dma("pool", ab[:], dram["acs_d"][slot, r0:r0 + 4, :].partition_broadcast(128),
                              reads=[b_acsd[slot]], writes=[ab.b])

                    def stageC2(k):
                        t, g, hf = kdec(k)
                        h0 = g * 8 + hf * 4
                        ab, ek, mtk = AB[k % 3], EB[k % 2], MT4[k % 4]
                        for i in range(4):
                            act(ek[:, i, :], ab[:, i, :], AF.Exp, [ab.b, sbf[5]], [ek.b],
                                bias=r3(5)[:, t, h0 + i:h0 + i + 1])
                        stt(mtk[:], ek[:], 1e30, CBM[t % 2][:, g, :].unsqueeze(1).to_broadcast([128, 4, 128]),
                            ALU.min, ALU.mult, [ek.b, CBM[t % 2].b], [mtk.b])

                    def advance(k):
                        if k + 2 < 4 * NT:
                            stageC1(k + 2)
                        stageC2(k)

                    def stageD(t, g):
                        cols = slice(t * 128, (t + 1) * 128)
                        gs = slice(g * 512, (g + 1) * 512)
                        py, po, t1 = nextmm(), nextmm(), T1
                        xdt = XDT[t % 2]
                        k0 = 4 * t + 2 * g
                        items = [(py[:], IDB[:], XD[:, gs], True, False)]
                        for hh in range(8):
                            h = g * 8 + hh
                            items.append((py[:, hh * 64:(hh + 1) * 64], MT4[(k0 + hh // 4) % 4][:, hh % 4, :],
                                          xdt[:, h * 64:(h + 1) * 64], False, True))
                        mms(items, [IDB.b, XD.b, MT4[k0 % 4].b, MT4[(k0 + 1) % 4].b, xdt.b], [py.b])
                        mm(po[:], [(XBC[:, 10 + g, cols], HSB[:, gs])], [XBC.bs[10 + g], HSBbufs[g]], [po.b])
                        tt(t1[:].rearrange("p (h d) -> p h d", h=8), po[:].rearrange("p (h d) -> p h d", h=8),
                           r3(7)[:, t, g * 8:(g + 1) * 8].unsqueeze(2).to_broadcast([128, 8, 64]), ALU.mult,
                           [po.b, sbf[7]], [t1.b])
                        tt(Y[:, gs], t1[:], py[:], ALU.add, [t1.b, py.b], [Y.b])

                    def stageD2(t, g):
                        gs = slice(g * 512, (g + 1) * 512)
                        po = nextmm()
                        hb_, hsb_ = HSbufs[g], HSBbufs[g]
                        tt(HS[:, gs].rearrange("p (h d) -> p h d", h=8), HS[:, gs].rearrange("p (h d) -> p h d", h=8),
                           r3(8)[:, t, g * 8:(g + 1) * 8].unsqueeze(2).to_broadcast([128, 8, 64]), ALU.mult,
                           [hb_, sbf[8]], [hb_])
                        mm(po[:], [(BMT[:, g, :], XDS[:, gs])], [BMT.b, XDS.b], [po.b])
                        tt(HSB[:, gs], HS[:, gs], po[:], ALU.add, [hb_, po.b], [hsb_])
                        tt(HS[:, gs], HS[:, gs], po[:], ALU.add, [hb_, po.b], [hb_])

                    def stageE(t):
                        cols = slice(t * 128, (t + 1) * 128)
                        tt(Y[:], Y[:], ZS[:, t, :], ALU.mult, [Y.b, ZS.bs[t]], [Y.b])
                        SME = SMEs[t % 2]
                        memset(SME[:, 0:2], 0.0, [SME.b])
                        for g in range(2):
                            act(JUNK[:, 0:512], Y[:, g * 512:(g + 1) * 512], AF.Square, [Y.b, SME.b], [JUNK.b, SME.b],
                                accum=SME[:, g:g + 1])
                        ts(SME[:, 2:4], SME[:, 0:2], 1.0 / 512.0, 4.0 * EPS, ALU.mult, ALU.add, [SME.b], [SME.b])
                        act(SME[:, 2:4], SME[:, 2:4], AF.Ln, [SME.b], [SME.b])
                        act(SME[:, 4:6], SME[:, 2:4], AF.Exp, [SME.b], [SME.b], scale=-0.5)
                        for g in range(2):
                            gs = slice(g * 512, (g + 1) * 512)
                            stt(YN[:, gs], Y[:, gs], SME[:, 4 + g:5 + g], PRM[:, O_SSDNW + g * 512:O_SSDNW + (g + 1) * 512],
                                ALU.mult, ALU.mult, [Y.b, SME.b, PRM.b], [YN.b])

                    def stageE2(t):
                        cols = slice(t * 128, (t + 1) * 128)
                        pte = nextmm()
                        ptv = pte[:].bitcast(BF16)
                        trs([(ptv[:, j * 128:(j + 1) * 128], YN[:, j * 128:(j + 1) * 128]) for j in range(8)], IDB[:],
                            [YN.b, IDB.b], [pte.b])
                        cp("dve", YT[:, 0:8, cols], ptv.rearrange("p (a b) -> p a b", a=8), [pte.b], [YT.b])

                    extra = []
                    wbh = {}

                    def conf_pair(j):
                        if j % 2 == 0:
                            wbh["c"] = w_acquire((l, "w_in", 6 + j // 2))
                        wb = wbh["c"]
                        jj = j % 2
                        pa = nextmm()
                        mm(pa[:], [(wb[:, kc, (2 * jj) * 128:(2 * jj + 1) * 128], XNT[:, kc, :]) for kc in range(8)],
                           [wb.b, XNT.b], [pa.b])
                        pb = nextmm()
                        mm(pb[:], [(wb[:, kc, (2 * jj + 1) * 128:(2 * jj + 2) * 128], XNT[:, kc, :]) for kc in range(8)],
                           [wb.b, XNT.b], [pb.b])
                        if j % 2 == 1:
                            w_release(wb)
                        th = TH[j % 2]
                        act(th[:], pb[:], AF.Tanh, [pb.b], [th.b], scale=0.5)
                        stt(GLU[:, j, 30:30 + TB], th[:], 1.0, pa[:], ALU.add, ALU.mult, [th.b, pa.b], [GLU.b])

                    def conf_gate(j):
                        if j == 0:
                            wbh["c"] = w_acquire((l, "w_in", 8))
                        wb = wbh["c"]
                        pm = nextmm()
                        mm(pm[:], [(wb[:, kc, j * 128:(j + 1) * 128], XNT[:, kc, :]) for kc in range(8)],
                           [wb.b, XNT.b], [pm.b])
                        if j == 3:
                            w_release(wb)
                        th = TH[j % 2]
                        act(th[:], pm[:], AF.Tanh, [pm.b], [th.b], scale=0.5)
                        stt(CG[:, j, :], th[:], 1.0, pm[:], ALU.add, ALU.mult, [th.b, pm.b], [CG.bs[j]])

                    def conf_convA(j):
                        tt(DG[:], IDF[:].unsqueeze(1).to_broadcast([128, 31, 128]),
                           PRM[:, O_DWW + j * 31:O_DWW + (j + 1) * 31].unsqueeze(2).to_broadcast([128, 31, 128]),
                           ALU.mult, [IDF.b, PRM.b], DG.bs)

                    def conf_convB(j):
                        pm = nextmm()
                        mm(pm[:], [(DG[:, k, :], GLU[:, j, k:k + TB]) for k in range(31)], DG.bs + [GLU.b], [pm.b])
                        act(HH[:, j, :], pm[:], AF.Identity, [pm.b, PRM.b], [HH.bs[j]],
                            bias=PRM[:, O_DWB + j:O_DWB + j + 1])

                    def conf_stats():
                        PCA = nextmm()
                        PCB = nextmm()
                        mm(PCA[:], [(ONE512[:], HH[:, j, :]) for j in range(4)], [ONE512.b] + HH.bs, [PCA.b])
                        for j in range(4):
                            hq = HQ[j % 2]
                            act(hq[:], HH[:, j, :], AF.Square, [HH.bs[j]], [hq.b])
                            mm(PCB[:], [(ONE512[:], hq[:])], [ONE512.b, hq.b], [PCB.b], start=(j == 0), stop=(j == 3))
                        cp("act", MU[:], PCA[:], [PCA.b], [MU.b])
                        tt(D1[:], MU[:], MU[:], ALU.mult, [MU.b], [D1.b])
                        tt(RS[:], PCB[:], D1[:], ALU.subtract, [PCB.b, D1.b], [RS.b])
                        ts(RS[:], RS[:], EPS, None, ALU.add, None, [RS.b], [RS.b])
                        act(RS[:], RS[:], AF.Ln, [RS.b], [RS.b])
                        act(RS[:], RS[:], AF.Exp, [RS.b], [RS.b], scale=-0.5)

                    def conf_ln(j):
                        tt(D1[:], HH[:, j, :], MU[:], ALU.subtract, [HH.bs[j], MU.b], [D1.b])
                        tt(D1[:], D1[:], RS[:], ALU.mult, [D1.b, RS.b], [D1.b])
                        th = TH[j % 2]
                        act(th[:], D1[:], AF.Tanh, [D1.b, DER.b], [th.b], scale=DER[:, 16 + j:17 + j],
                            bias=DER[:, 20 + j:21 + j])
                        ts(U1[:], D1[:], PRM[:, O_LNW + j:O_LNW + j + 1], PRM[:, O_LNB + j:O_LNB + j + 1],
                           ALU.mult, ALU.add, [D1.b, PRM.b], [U1.b])
                        stt(U1[:], th[:], 1.0, U1[:], ALU.add, ALU.mult, [th.b, U1.b], [U1.b])
                        stt(YT[:, 8 + j, :], U1[:], 0.25, CG[:, j, :], ALU.mult, ALU.mult, [U1.b, CG.bs[j]], [YT.b])

                    def attn_q(g):
                        if g == 0:
                            wbh["a"] = w_acquire((l, "w_in", 9))
                        wb = wbh["a"]
                        pm = nextmm()
                        mm(pm[:], [(wb[:, kc, g * 128:(g + 1) * 128], XNT[:, kc, :]) for kc in range(8)],
                           [wb.b, XNT.b], [pm.b])
                        if g == 3:
                            w_release(wb)
                        cp("act", QT[:, g, :], pm[:], [pm.b], [QT.bs[g]])

                    def attn_g(c):
                        if c == 0:
                            wbh["a"] = w_acquire((l, "w_in", 10))
                        wb = wbh["a"]
                        pm = nextmm()
                        mm(pm[:], [(wb[:, kc, c * 128:(c + 1) * 128], XNT[:, kc, :]) for kc in range(8)],
                           [wb.b, XNT.b], [pm.b])
                        if c == 3:
                            w_release(wb)
                        th = TH[c % 2]
                        act(th[:], pm[:], AF.Tanh, [pm.b], [th.b], scale=0.5)
                        stt(AG[:, c, :], th[:], 1.0, pm[:], ALU.add, ALU.mult, [th.b, pm.b], [AG.bs[c]])

                    def attn_A(n, kh):
                        u = n * 2 + kh
                        nbk = tb * NT + n
                        qcols = slice(n * 128, (n + 1) * 128)
                        kbs = [1] if nbk == 0 else [0, 1]
                        prt = slice(kh * 64, (kh + 1) * 64)
                        for kb in kbs:
                            et = ET[(u % 2) * 2 + kb]
                            pm = nextmm()
                            mm(pm[:], [(KT[prt, (n + kb) * 128:(n + kb + 1) * 128], QT[prt, :, qcols])],
                               [KT.b] + QT.bs, [pm.b])
                            stt(SC[kb][:], pm[:], 0.125,
                                BT[:, kb, kh * 4:(kh + 1) * 4, :].rearrange("p h q -> p (h q)"),
                                ALU.mult, ALU.add, [pm.b, BT.b], [SC[kb].b])
                            act(et[:], SC[kb][:], AF.Exp, [SC[kb].b], [et.b])

                    def attn_B(n, kh):
                        u = n * 2 + kh
                        nbk = tb * NT + n
                        qcols = slice(n * 128, (n + 1) * 128)
                        kbs = [1] if nbk == 0 else [0, 1]
                        prt = slice(kh * 64, (kh + 1) * 64)
                        ets = {kb: ET[(u % 2) * 2 + kb] for kb in kbs}
                        pod = nextmm()
                        items = []
                        for jj in range(2):
                            for i, kb in enumerate(kbs):
                                items.append((pod[jj * 64:(jj + 1) * 64, 0:256], VT[:, n + kb, prt],
                                              ets[kb][:, jj * 256:(jj + 1) * 256], i == 0, i == len(kbs) - 1))
                        for jj in range(2):
                            for i, kb in enumerate(kbs):
                                items.append((pod[jj * 64:(jj + 1) * 64, 256:512], ONEB[:, 0:64],
                                              ets[kb][:, jj * 256:(jj + 1) * 256], i == 0, False))
                            items.append((pod[jj * 64:(jj + 1) * 64, 256:512], ONEB[0:1, 0:64], ONEB[0:1, 0:256],
                                          False, True))
                        mms(items, [VT.b, ONEB.b] + [ets[kb].b for kb in kbs], [pod.b])
                        self.S.op("dve", lambda e, o=RD[:], i=pod[:, 256:512]: e.reciprocal(out=o, in_=i),
                                  [pod.b], [RD.b])
                        tt(OT[:], pod[:, 0:256], RD[:], ALU.mult, [pod.b, RD.b], [OT.b])
                        stt(YT[:, 12 + kh * 2:14 + kh * 2, qcols], OT[:].rearrange("p (a b) -> p a b", a=2), 0.5,
                            AG[:, kh * 2:kh * 2 + 2, qcols], ALU.mult, ALU.mult,
                            [OT.b, AG.bs[kh * 2], AG.bs[kh * 2 + 1]], [YT.b])

                    extra.append(lambda: cp("dve", GLU[:, :, 0:30], GLU[:, :, TB:TB + 30], [GLU.b], [GLU.b]))
                    for j in range(4):
                        extra.append(lambda j=j: conf_pair(j))
                    for j in range(4):
                        extra.append(lambda j=j: conf_convA(j))
                        extra.append(lambda j=j: conf_gate(j))
                        extra.append(lambda j=j: conf_convB(j))
                    extra.append(conf_stats)
                    for j in range(4):
                        extra.append(lambda j=j: attn_q(j))
                        extra.append(lambda j=j: conf_ln(j))
                    for c in range(4):
                        extra.append(lambda c=c: attn_g(c))
                    units = [(n, kh) for n in range(NT) for kh in range(2)]
                    for i in range(len(units) + 1):
                        if i < len(units):
                            extra.append(lambda nk=units[i]: attn_A(*nk))
                        if i >= 1:
                            extra.append(lambda nk=units[i - 1]: attn_B(*nk))
                    nslots = 2 * NT
                    per = (len(extra) + nslots - 1) // nslots
                    epos = [0]

                    def run_extra(n):
                        for _ in range(n):
                            if epos[0] < len(extra):
                                extra[epos[0]]()
                                epos[0] += 1

                    stageB1(0)
                    stageB2(0)
                    stageC1(0)
                    stageC1(1)
                    for k in range(4):
                        advance(k)
                    stageB1(1)
                    for t in range(NT):
                        for g in range(2):
                            stageD(t, g)
                            if t + 1 < NT:
                                advance(4 * (t + 1) + 2 * g)
                                advance(4 * (t + 1) + 2 * g + 1)
                            run_extra(per)
                            stageD2(t, g)
                        if t >= 1:
                            stageE2(t - 1)
                        if t + 1 < NT:
                            stageB2(t + 1)
                        stageE(t)
                        if t + 2 < NT:
                            stageB1(t + 2)
                    run_extra(2)
                    stageE2(NT - 1)
                    run_extra(len(extra))

                p5a = []
                ohold = {}

                def outproj(ch, t):
                    if t == 0:
                        ohold["w0"] = w_acquire((l, "w_out", ch * 2))
                        ohold["w1"] = w_acquire((l, "w_out", ch * 2 + 1))
                    w0, w1 = ohold["w0"], ohold["w1"]
                    pm = nextmm()
                    pairs = [(YT[:, kc, t * 128:(t + 1) * 128], (w0 if kc < 8 else w1)[:, kc % 8, :])
                             for kc in range(16)]
                    mm(pm[:], pairs, [YT.b, w0.b, w1.b], [pm.b])
                    cp("act", O[:, t, ch * 512:(ch + 1) * 512], pm[:], [pm.b], [O.bs[t]])
                    if t == NT - 1:
                        w_release(w0)
                        w_release(w1)
                for ch in range(2):
                    for t in range(NT):
                        p5a.append((lambda ch=ch, t=t: outproj(ch, t), []))
                if tb + 1 < NBLK:
                    run(interleave(p5a, make_p1(tb + 1)))
                else:
                    run(p5a)
                for v in p5views:
                    v.b.w, v.b.r = {}, {}
                    inherit(v.b, [YT.b])
                A = make_p5b(tb)
                if tb + 1 < NBLK:
                    Bn = make_p2a(tb + 1)
                    merged = interleave(A, Bn)
                    wplan.extend(keys_of(merged) + k_ssdx + k_p5a)
                else:
                    merged = A
                    wplan.extend(keys_of(merged))
                w_pump()
                run(merged)
                inherit(YT.b, [v.b for v in p5views])
        S.wait_all("sp", b_out)
        S.wait_all("pool", b_out)


def _col_perm():
    z0, xbc0, dt0, ci0, cg0, q0, k0, v0, ag0 = 0, 1024, 2560, 2576, 3600, 4112, 4624, 4752, 4880
    groups = []
    for g in range(3):
        groups.append(list(range(xbc0 + g * 512, xbc0 + (g + 1) * 512)))
    groups.append(list(range(z0, z0 + 512)))
    groups.append(list(range(z0 + 512, z0 + 1024)))
    groups.append(list(range(k0, k0 + 128)) + list(range(dt0, dt0 + 16)) + list(range(v0, v0 + 128)) + [-1] * 240)
    for g in range(2):
        cols = []
        for j in (2 * g, 2 * g + 1):
            cols += list(range(ci0 + j * 128, ci0 + (j + 1) * 128))
            cols += list(range(ci0 + 512 + j * 128, ci0 + 512 + (j + 1) * 128))
        groups.append(cols)
    groups.append(list(range(cg0, cg0 + 512)))
    cols = []
    for g in range(4):
        cols += list(range(q0 + g * 64, q0 + (g + 1) * 64))
        cols += list(range(q0 + (4 + g) * 64, q0 + (5 + g) * 64))
    groups.append(cols)
    cols = []
    for kh in range(2):
        for gl in range(2):
            h0, h1 = kh * 4 + gl, kh * 4 + 2 + gl
            cols += list(range(ag0 + h0 * 64, ag0 + (h0 + 1) * 64))
            cols += list(range(ag0 + h1 * 64, ag0 + (h1 + 1) * 64))
    groups.append(cols)
    assert len(groups) == 11 and all(len(g) == 512 for g in groups)
    return np.array(groups, dtype=np.int64)


def _row_perm_out():
    rows = list(range(0, 1536))
    a0 = 1536
    for kh in range(2):
        for gl in range(2):
            h0, h1 = kh * 4 + gl, kh * 4 + 2 + gl
            rows += list(range(a0 + h0 * 64, a0 + (h0 + 1) * 64))
            rows += list(range(a0 + h1 * 64, a0 + (h1 + 1) * 64))
    return np.array(rows, dtype=np.int64)


def _t5_bucket(d):
    d = np.maximum(d, 0)
    dm = np.maximum(d, 1).astype(np.float32)
    large = 16 + (np.log(dm / np.float32(16)) / np.float32(math.log(128 / 16)) * np.float32(16)).astype(np.int32)
    large = np.minimum(large, 31)
    return np.where(d < 16, d, large)


def _prep(inputs):
    f32 = np.float32
    w_in = np.asarray(inputs["w_in"], f32)
    perm = _col_perm()
    w_in_pad = np.concatenate([w_in, np.zeros((DEPTH, D, 1), f32)], axis=2)
    wi = w_in_pad[:, :, perm.reshape(-1)].reshape(DEPTH, 8, 128, 11, 512)
    wi = np.ascontiguousarray(wi.transpose(0, 3, 2, 1, 4)).reshape(DEPTH, 11, 128, 4096)
    w_out = np.asarray(inputs["w_out"], f32)[:, _row_perm_out(), :]
    wo = w_out.reshape(DEPTH, 2, 8, 128, 2, 512).transpose(0, 4, 1, 3, 2, 5)
    wo = np.ascontiguousarray(wo).reshape(DEPTH, 4, 128, 4096)
    wg = np.asarray(inputs["ple_gate"], f32).reshape(DEPTH, 8, 128, 2, 512).transpose(0, 3, 2, 1, 4)
    wg = np.ascontiguousarray(wg).reshape(DEPTH, 2, 128, 4096)
    wp = np.asarray(inputs["ple_proj"], f32).reshape(DEPTH, 2, 128, 1024).transpose(0, 2, 1, 3)
    wp = np.ascontiguousarray(wp).reshape(DEPTH, 1, 128, 2048)
    prm = np.zeros((DEPTH, 128, NP), f32)
    bc = lambda v: np.broadcast_to(np.asarray(v, f32)[None, :], (128, len(v)))
    for l in range(DEPTH):
        prm[l, :, O_PREW:O_PREW + 1024] = bc(inputs["pre_norm_w"][l])
        prm[l, :, O_SSDNW:O_SSDNW + 1024] = bc(inputs["ssd_norm_w"][l])
        prm[l, :, O_POSTW:O_POSTW + 1024] = bc(inputs["post_norm_w"][l])
        prm[l, :, O_D16:O_D16 + 16] = bc(inputs["ssd_d"][l])
        prm[l, :, O_DTB:O_DTB + 16] = bc(inputs["ssd_dt_bias"][l])
        prm[l, :, O_ALOG:O_ALOG + 16] = bc(inputs["ssd_a_log"][l])
        cw = np.asarray(inputs["ssd_conv_w"][l], f32)
        prm[l, :, O_CW:O_CW + 48] = cw.reshape(4, 12, 128).transpose(2, 1, 0).reshape(128, 48)
        prm[l, :, O_CB:O_CB + 12] = np.asarray(inputs["ssd_conv_b"][l], f32).reshape(12, 128).T
        dw = np.asarray(inputs["conf_dw_w"][l], f32)
        prm[l, :, O_DWW:O_DWW + 124] = dw.reshape(31, 4, 128).transpose(2, 1, 0).reshape(128, 124)
        prm[l, :, O_DWB:O_DWB + 4] = np.asarray(inputs["conf_dw_b"][l], f32).reshape(4, 128).T
        prm[l, :, O_LNW:O_LNW + 4] = np.asarray(inputs["conf_ln_w"][l], f32).reshape(4, 128).T
        prm[l, :, O_LNB:O_LNB + 4] = np.asarray(inputs["conf_ln_b"][l], f32).reshape(4, 128).T
        prm[l, :, O_SINK:O_SINK + 8] = bc(inputs["attn_sinks"][l])
    s = np.arange(128)[:, None, None]
    kb = np.arange(2)[None, :, None]
    q = np.arange(128)[None, None, :]
    dist = q + 128 - (kb * 128 + s)
    valid = (dist >= 0) & (dist < 128)
    bucket = _t5_bucket(np.clip(dist, 0, 127))
    rel = np.asarray(inputs["rel_bias"], f32)
    abias = np.ascontiguousarray(rel[bucket].transpose(0, 1, 3, 2)).reshape(128, 2 * 8 * 128)
    amask = np.where(valid, 0.0, NEG).astype(f32).reshape(128, 256)
    ident = np.eye(128, dtype=f32)
    tri = np.triu(np.ones((128, 128), f32))
    common = {"w_in": wi, "w_out": wo, "w_gate": wg, "w_proj": wp, "prm": prm, "c_abias": abias,
              "c_amask": amask, "c_ident": ident, "c_tri": tri,
              "c_mneg": np.where(np.arange(128)[:, None] <= np.arange(128)[None, :], 0.0, NEG).astype(f32)}
    return common


def _build(layers, debug=False):
    nc = bass.Bass("TRN2", target_bir_lowering=False)
    dram = {}
    dt = lambda name, shape, dtype, kind: nc.dram_tensor(name, shape, dtype, kind=kind).ap()
    dram["x"] = dt("x", [L_SEQ, D], F32, "ExternalInput")
    dram["p"] = dt("p", [DEPTH, L_SEQ, 256], F32, "ExternalInput")
    for name, ng, w in (("w_in", 11, 4096), ("w_out", 4, 4096), ("w_gate", 2, 4096), ("w_proj", 1, 2048)):
        dram[name] = dt(name, [DEPTH, ng, 128, w], F32, "ExternalInput")
        dram[name + "_s"] = dt(name + "_s", [DEPTH, ng, 128, w], BF16, "Internal")
    dram["prm"] = dt("prm", [DEPTH, 128, NP], F32, "ExternalInput")
    dram["c_abias"] = dt("c_abias", [128, 2048], F32, "ExternalInput")
    dram["c_amask"] = dt("c_amask", [128, 256], F32, "ExternalInput")
    dram["c_ident"] = dt("c_ident", [128, 128], F32, "ExternalInput")
    dram["c_tri"] = dt("c_tri", [128, 128], F32, "ExternalInput")
    dram["c_mneg"] = dt("c_mneg", [128, 128], F32, "ExternalInput")
    dram["hscr"] = dt("hscr", [L_SEQ, D], F32, "Internal")
    dram["acs_d"] = dt("acs_d", [2, 64, 128], F32, "Internal")
    dram["out"] = dt("out", [L_SEQ, D], F32, "ExternalOutput")
    if debug:
        dram["dbg"] = dt("dbg", [128, 16, L_SEQ], F32, "ExternalOutput")
    with ExitStack() as st:
        prog = Prog(nc, st, layers, True, True)
        prog.build(dram)
    return nc


def kernel(**inputs):
    common = _prep(inputs)
    x = np.asarray(inputs["x"], np.float32)
    p = np.asarray(inputs["p"], np.float32)
    nb = x.shape[0]
    nc = _build(list(range(DEPTH)), debug=DEBUG)
    in_maps = []
    for b in range(nb):
        m = dict(common)
        m["x"] = np.ascontiguousarray(x[b])
        m["p"] = np.ascontiguousarray(p[:, b])
        in_maps.append(m)
    res = run_bass_kernel_spmd(nc, in_maps, core_ids=list(range(nb)))
    out = np.stack([np.asarray(r["out"], np.float32) for r in res.results], axis=0)
    if DEBUG:
        kernel.dbg = [np.asarray(r["dbg"]) for r in res.results]
    return out
```

```python
from contextlib import ExitStack
import math
import numpy as np
import concourse.bass as bass
import concourse.mybir as mybir
from concourse.bass_utils import run_bass_kernel_spmd

F32 = mybir.dt.float32
BF16 = mybir.dt.bfloat16
AF = mybir.ActivationFunctionType
ALU = mybir.AluOpType

L_SEQ = 2048
D = 1024
DEPTH = 2
TB = 512
NT = 4
NBLK = L_SEQ // TB
EPS = 1e-6
NEG = -30000.0

O_PREW, O_SSDNW, O_POSTW = 0, 1024, 2048
O_DTB, O_ALOG, O_D16 = 3072, 3088, 3104
O_CW, O_CB = 3120, 3168
O_DWW, O_DWB, O_LNW, O_LNB, O_SINK = 3180, 3304, 3308, 3312, 3316
NP = 3328

DEBUG = False


class Buf:
    __slots__ = ("w", "r")

    def __init__(self):
        self.w = {}
        self.r = {}


class Sched:
    ENGS = ("pe", "act", "dve", "pool", "sp")
    NSLOT = {"sp": 12, "pool": 44, "act": 8}

    def __init__(self, nc, stack):
        self.nc = nc
        self.eng = {"pe": nc.tensor, "act": nc.scalar, "dve": nc.vector,
                    "pool": nc.gpsimd, "sp": nc.sync}
        self.sems = {}
        self.cnt = {}
        self.known = {e: {} for e in self.ENGS}
        for e in self.ENGS:
            self.sems[e] = stack.enter_context(nc.semaphore("s_" + e))
            self.cnt[e] = 0
        self.slot_i = {}
        self.fence = {}
        for q, n in self.NSLOT.items():
            self.slot_i[q] = 0
            for k in range(n):
                key = "d_%s%d" % (q, k)
                self.sems[key] = stack.enter_context(nc.semaphore(key))
                self.cnt[key] = 0

    LOG = None

    def _push(self, engname, waits, fn, key, inc):
        if Sched.LOG is not None:
            Sched.LOG.append((engname, list(waits), key if fn is not None else None, inc))
        e = self.eng[engname]
        for k, v in waits:
            e.wait_ge(self.sems[k], v)
        if fn is not None:
            fn(e).then_inc(self.sems[key], inc)

    def barrier(self):
        engs = ("pe", "act", "dve", "pool")
        for e in engs:
            waits = []
            tgt = {k: self.cnt[k] for k in engs if k != e}
            tgt.update(self.fence)
            for k, v in tgt.items():
                if v > 0 and self.known[e].get(k, 0) < v:
                    self.known[e][k] = v
                    waits.append((k, v))
            self._push(e, waits, None, None, 0)

    def _deps(self, eng, reads, writes):
        need = {}

        def add(d):
            for k, v in d.items():
                if need.get(k, 0) < v:
                    need[k] = v
        for b in reads:
            add(b.w)
        for b in writes:
            add(b.w)
            add(b.r)
        out = []
        kn = self.known[eng]
        for k, v in need.items():
            if k == eng and eng == "pe":
                continue
            if kn.get(k, 0) >= v:
                continue
            kn[k] = v
            out.append((k, v))
        return out

    def op(self, eng, fn, reads=(), writes=()):
        waits = self._deps(eng, reads, writes)
        self.cnt[eng] += 1
        t = self.cnt[eng]
        for b in reads:
            if b.r.get(eng, 0) < t:
                b.r[eng] = t
        for b in writes:
            b.w = {eng: t}
            b.r = {}
        self._push(eng, waits, fn, eng, 1)

    def dma(self, q, out, in_, reads=(), writes=(), fence=None):
        if fence is None:
            fence = (q == "pool")
        waits = self._deps(q, reads, writes)
        i = self.slot_i[q]
        self.slot_i[q] += 1
        key = "d_%s%d" % (q, i % self.NSLOT[q])
        prev = self.cnt[key]
        if prev > 0 and self.known[q].get(key, 0) < prev:
            self.known[q][key] = prev
            waits.append((key, prev))
        self.cnt[key] += 16
        t = self.cnt[key]
        if fence:
            self.fence[key] = t
        for b in reads:
            if b.r.get(key, 0) < t:
                b.r[key] = t
        for b in writes:
            b.w = {key: t}
            b.r = {}
        self._push(q, waits, lambda e: e.dma_start(out=out, in_=in_), key, 16)

    def wait_all(self, eng, bufs):
        waits = self._deps(eng, bufs, ())
        self._push(eng, waits, None, None, 0)


class Tl:
    def __init__(self, t, nb=1):
        self.t = t
        self.bs = [Buf() for _ in range(nb)]

    @property
    def b(self):
        return self.bs[0]

    def __getitem__(self, k):
        return self.t[k]


class Prog:
    def __init__(self, nc, st, layers, first_in, last_out):
        self.nc = nc
        self.st = st
        self.S = Sched(nc, st)
        self.layers = layers

    _uid = [0]

    def sb(self, stack, name, shape, dt, nb=1):
        self._uid[0] += 1
        name = "%s_%d" % (name, self._uid[0])
        return Tl(stack.enter_context(self.nc.sbuf_tensor(name, shape, dt)), nb)

    def ps(self, stack, name, shape, dt):
        return Tl(stack.enter_context(self.nc.psum_tensor(name, shape, dt)))

    def mm(self, out, pairs, reads, writes, start=True, stop=True, sgc=False):
        def fn(e):
            n = len(pairs)
            ins = None
            for i, (l, r) in enumerate(pairs):
                kw = {}
                if sgc:
                    kw["skip_group_check"] = True
                ins = e.matmul(out, lhsT=l, rhs=r, start=(start and i == 0),
                               stop=(stop and i == n - 1), **kw)
            return ins
        self.S.op("pe", fn, reads, writes)

    def mms(self, items, reads, writes):
        def fn(e):
            ins = None
            for (o, l, r, s0, s1) in items:
                ins = e.matmul(o, lhsT=l, rhs=r, start=s0, stop=s1, skip_group_check=True)
            return ins
        self.S.op("pe", fn, reads, writes)

    def trs(self, items, ident, reads, writes):
        def fn(e):
            ins = None
            for (o, i) in items:
                ins = e.transpose(out=o, in_=i, identity=ident)
            return ins
        self.S.op("pe", fn, reads, writes)

    def act(self, out, in_, func, reads, writes, bias=None, scale=None, accum=None):
        kw = {}
        if bias is not None:
            kw["bias"] = bias
        if scale is not None:
            kw["scale"] = scale
        if accum is not None:
            kw["accum_out"] = accum
        self.S.op("act", lambda e: e.activation(out=out, in_=in_, func=func, **kw), reads, writes)

    def tt(self, out, in0, in1, op, reads, writes, eng="dve"):
        self.S.op(eng, lambda e: e.tensor_tensor(out=out, in0=in0, in1=in1, op=op), reads, writes)

    def ts(self, out, in0, s1, s2, op0, op1, reads, writes):
        if s2 is None:
            self.S.op("dve", lambda e: e.tensor_scalar(out=out, in0=in0, scalar1=s1, scalar2=None, op0=op0),
                      reads, writes)
        else:
            self.S.op("dve", lambda e: e.tensor_scalar(out=out, in0=in0, scalar1=s1, scalar2=s2,
                                                       op0=op0, op1=op1), reads, writes)

    def stt(self, out, in0, scalar, in1, op0, op1, reads, writes):
        self.S.op("dve", lambda e: e.scalar_tensor_tensor(out=out, in0=in0, scalar=scalar, in1=in1,
                                                          op0=op0, op1=op1), reads, writes)

    def cp(self, eng, out, in_, reads, writes):
        if eng == "act":
            self.S.op("act", lambda e: e.copy(out=out, in_=in_), reads, writes)
        else:
            self.S.op(eng, lambda e: e.tensor_copy(out=out, in_=in_), reads, writes)

    def memset(self, ap, val, writes, eng="dve"):
        self.S.op(eng, lambda e: e.memset(ap, val), (), writes)

    def rstd(self, out, ssq, inv_n, tmp, reads, writes):
        self.ts(tmp, ssq, inv_n, EPS, ALU.mult, ALU.add, reads, writes)
        self.act(tmp, tmp, AF.Ln, writes, writes)
        self.act(out, tmp, AF.Exp, writes, writes, scale=-0.5)

    def build(self, dram):
        nc, S, st = self.nc, self.S, self.st
        sb, ps = self.sb, self.ps
        mm, mms, trs, act, tt, ts, stt, cp, memset = (self.mm, self.mms, self.trs, self.act, self.tt,
                                                      self.ts, self.stt, self.cp, self.memset)
        b_scr = {}
        for l in self.layers:
            for name, ng in (("w_proj", 1), ("w_in", 11), ("w_out", 4), ("w_gate", 2)):
                for g in range(ng):
                    b_scr[(l, name, g)] = Buf()

        cast_q = []
        for l in self.layers:
            cast_q.append((l, "w_proj", 0))
            for g in range(11):
                cast_q.append((l, "w_in", g))
            for g in range(4):
                cast_q.append((l, "w_out", g))
            for g in range(2):
                cast_q.append((l, "w_gate", g))
        cast_pos = [0]

        def cast_some(n):
            for _ in range(n):
                if cast_pos[0] < len(cast_q):
                    l_, name, g = cast_q[cast_pos[0]]
                    cast_pos[0] += 1
                    S.dma("pool", dram[name + "_s"][l_, g], dram[name][l_, g], writes=[b_scr[(l_, name, g)]],
                          fence=False)

        IDF = sb(st, "IDF", [128, 128], F32)
        IDB = sb(st, "IDB", [128, 128], BF16)
        U = sb(st, "U", [128, 128], F32)
        ONEF = sb(st, "ONEF", [128, 128], F32)
        ONE512 = sb(st, "ONE512", [128, 128], F32)
        ONEB = sb(st, "ONEB", [128, 256], BF16)
        AMASK = sb(st, "AMASK", [128, 2, 128], F32)
        PRM = sb(st, "PRM", [128, NP], F32)
        DER = sb(st, "DER", [128, 64], F32)
        BT = sb(st, "BT", [128, 2, 8, 128], F32)
        WP = sb(st, "WP", [128, 2, 1024], BF16)
        WB = [sb(st, "WB%d" % i, [128, 8, 512], BF16) for i in range(3)]
        XNT = sb(st, "XNT", [128, 8, 512], BF16)
        YT = sb(st, "YT", [128, 16, 512], BF16)
        HS = sb(st, "HS", [128, 1024], F32)
        HSB = sb(st, "HSB", [128, 1024], BF16)
        HSbufs = [Buf(), Buf()]
        HSBbufs = [Buf(), Buf()]
        TAIL = sb(st, "TAIL", [128, 12, 3], F32)
        GLU = sb(st, "GLU", [128, 4, 30 + TB], BF16)
        KT = sb(st, "KT", [128, 128 + TB], BF16)
        VT = sb(st, "VT", [128, 5, 128], BF16)
        JUNK = sb(st, "JUNK", [128, 1024], BF16)
        SM = sb(st, "SM", [128, 64], F32)
        PA = ps(st, "PA", [128, 512], F32)
        PB = ps(st, "PB", [128, 512], F32)
        PT = ps(st, "PT", [128, 1024], BF16)
        PS_ = ps(st, "PS", [128, 512], F32)
        PX0 = ps(st, "PX0", [128, 512], F32)
        PX1 = ps(st, "PX1", [128, 512], F32)
        PY = ps(st, "PY", [128, 512], F32)
        PO = ps(st, "PO", [128, 512], F32)

        def view(ap, bufs):
            v = Tl.__new__(Tl)
            v.t = ap
            v.bs = list(bufs)
            return v

        XR = [sb(st, "XR%d" % i, [128, 1024], F32) for i in range(2)]
        YN = sb(st, "YN", [128, 1024], BF16)
        XN2 = sb(st, "XN2", [128, 1024], BF16)
        XNs = [YN, XN2]
        PG = sb(st, "PG", [128, 4, 8 + TB], F32, nb=4)
        PRE = [view(PG[:, i, 0:3 + TB], [PG.bs[i]]) for i in range(2)]
        PREB = [view(PG[:, i, 0:264].bitcast(BF16)[:, 0:3 + TB], [PG.bs[i]]) for i in range(2)]
        DGS = [view(PG[:, i, 264:520].bitcast(BF16).rearrange("p (a b) -> p a b", a=4), [PG.bs[i]]) for i in range(2)]
        TAILB = sb(st, "TAILB", [128, 12, 4], BF16)
        ACC = [view(PG[:, 2 + i, 0:TB], [PG.bs[2 + i]]) for i in range(2)]
        TH = [sb(st, "TH%d" % i, [128, TB], F32) for i in range(2)]
        XBC = sb(st, "XBC", [128, 12, TB], BF16, nb=12)
        ZS = sb(st, "ZS", [128, NT, 1024], BF16, nb=NT)
        DTR = sb(st, "DTR", [128, NT, 16], F32, nb=NT)
        SD2 = sb(st, "SD2", [128, 9, 64], F32, nb=9)
        XG = sb(st, "XG", [128, 6, 1024], BF16, nb=6)
        XDT = [view(XG[:, i, :], [XG.bs[i]]) for i in range(2)]
        XD = view(XG[:, 2, :], [XG.bs[2]])
        XDS = view(XG[:, 3, :], [XG.bs[3]])
        MT4 = [view(XG[:, 4 + i // 2, (i % 2) * 512:(i % 2 + 1) * 512].rearrange("p (a b) -> p a b", a=4),
                    [Buf()]) for i in range(4)]
        DG = view(PG[:].rearrange("p a b -> p (a b)").bitcast(BF16)[:, 0:31 * 128].rearrange("p (a b) -> p a b", a=31),
                  PG.bs)
        BMT = sb(st, "BMT", [128, 2, 128], BF16)
        CBM = [sb(st, "CBM%d" % i, [128, 2, 128], F32) for i in range(2)]
        MNEG = sb(st, "MNEG", [128, 4, 128], BF16)
        T1 = sb(st, "T1", [128, 512], F32)
        D1 = T1
        Y = sb(st, "Y", [128, 1024], F32)
        EB = [sb(st, "EB%d" % i, [128, 4, 128], F32) for i in range(2)]
        MU = view(XR[0][:, 0:512], [XR[0].b])
        RS = view(XR[0][:, 512:1024], [XR[0].b])
        U1 = sb(st, "U1", [128, TB], F32)
        QT = sb(st, "QT", [128, 4, TB], BF16, nb=4)
        AG = sb(st, "AG", [128, 4, TB], BF16, nb=4)
        AB = [sb(st, "AB%d" % i, [128, 4, 128], F32) for i in range(3)]
        ACST = sb(st, "ACST", [64, 128], F32)
        b_acsd = [Buf(), Buf()]
        blkctr = [0]
        UB = sb(st, "UB", [128, 128], BF16)
        DAB = sb(st, "DAB", [128, 2, 64], BF16, nb=2)
        SC = [view(XR[1][:, i * 512:(i + 1) * 512], [XR[1].b]) for i in range(2)]
        ET = [sb(st, "ET%d" % i, [128, 512], BF16) for i in range(4)]
        RD = sb(st, "RD", [128, 256], F32)
        OT = sb(st, "OT", [128, 256], F32)
        O = sb(st, "O", [128, NT, 1024], F32, nb=NT)
        HH = view(O[:, 0:2, :].rearrange("p a (b c) -> p (a b) c", b=2), [O.bs[0], O.bs[0], O.bs[1], O.bs[1]])
        HQ = [view(O[:, 2, i * TB:(i + 1) * TB], [O.bs[2]]) for i in range(2)]
        CG = view(O[:, 3, :].bitcast(BF16).rearrange("p (a b) -> p a b", a=4), [O.bs[3]] * 4)
        PR = [sb(st, "PR%d" % i, [128, 256], F32) for i in range(2)]
        SM1s = [sb(st, "SM1_%d" % i, [128, 8], F32) for i in range(2)]
        SM5s = [sb(st, "SM5_%d" % i, [128, 8], F32) for i in range(2)]
        SMEs = [sb(st, "SME_%d" % i, [128, 8], F32) for i in range(2)]
        HT = view(YT[:, 0:8, :], [Buf()])
        HBs = [view(YT[:, 8 + 6 * i:10 + 6 * i, :].rearrange("p a b -> p (a b)"), [Buf()]) for i in range(2)]
        PTT = view(YT[:, 10:12, :], [Buf()])
        TG = [view(YT[:, 12:14, :].rearrange("p a b -> p (a b)").bitcast(F32), [Buf()])] * 2
        PB16s = [sb(st, "PB16a", [128, 256], BF16), sb(st, "PB16b", [128, 256], BF16)]
        p5views = [HT, HBs[0], HBs[1], PTT, TG[0]]
        PSB = view(PS_[:, 384:512].bitcast(BF16), [PS_.b])
        mmb = [PA, PB, PX0, PX1]
        mmi = [0]

        def nextmm():
            p = mmb[mmi[0] % 4]
            mmi[0] += 1
            return p

        S.dma("sp", IDF[:], dram["c_ident"], writes=[IDF.b])
        S.dma("sp", U[:], dram["c_tri"], writes=[U.b])
        S.dma("sp", AMASK[:], dram["c_amask"], writes=[AMASK.b])
        cp("dve", IDB[:], IDF[:], [IDF.b], [IDB.b])
        cp("dve", UB[:], U[:], [U.b], [UB.b])
        S.dma("sp", EB[0][:, 0, :], dram["c_mneg"], writes=[EB[0].b])
        cp("dve", MNEG[:], EB[0][:, 0, :].unsqueeze(1).to_broadcast([128, 4, 128]), [EB[0].b], [MNEG.b])
        memset(ONEF[:], 1.0, [ONEF.b])
        memset(ONE512[:], 1.0 / 512.0, [ONE512.b])
        memset(ONEB[:], 1.0, [ONEB.b])

        wplan = []
        wloaded = []
        wstate = ["free", "free", "free"]
        cast_idx = {k: i for i, k in enumerate(cast_q)}

        def w_pump():
            while wplan and len(wloaded) < 2 and "free" in wstate:
                key = wplan.pop(0)
                bi = wstate.index("free")
                wstate[bi] = "loaded"
                l_, name, g = key
                assert cast_idx[key] < cast_pos[0], ("cast not issued", key)
                S.dma("sp", WB[bi][:].rearrange("p a b -> p (a b)"), dram[name + "_s"][l_, g],
                      reads=[b_scr[key]], writes=[WB[bi].b])
                wloaded.append((key, bi))

        def w_acquire(key):
            cast_some(1)
            if not wloaded:
                w_pump()
            k2, bi = wloaded.pop(0)
            assert k2 == key, (k2, key)
            wstate[bi] = "held"
            w_pump()
            return WB[bi]

        def w_release(buf):
            bi = WB.index(buf)
            assert wstate[bi] == "held"
            wstate[bi] = "free"
            w_pump()

        def inherit(dst, srcs):
            for sb_ in srcs:
                for d in (sb_.r, sb_.w):
                    for k, v in d.items():
                        if dst.r.get(k, 0) < v:
                            dst.r[k] = v

        def interleave(A, B):
            out = []
            na, nb = len(A), len(B)
            ia = ib = 0
            while ia < na or ib < nb:
                if ib >= nb or (ia < na and ia * nb <= ib * na):
                    out.append(A[ia]); ia += 1
                else:
                    out.append(B[ib]); ib += 1
            return out

        b_hscr = [Buf() for _ in range(16)]
        b_out = [Buf() for _ in range(16)]

        for li, l in enumerate(self.layers):
            src = dram["x"] if li == 0 else dram["hscr"]
            dst = dram["out"] if li == len(self.layers) - 1 else dram["hscr"]
            b_src = None if li == 0 else b_hscr
            b_dst = b_out if li == len(self.layers) - 1 else b_hscr
            S.dma("sp", PRM[:], dram["prm"][l], writes=[PRM.b])
            S.dma("sp", BT[:].rearrange("p a h q -> p (a h q)"), dram["c_abias"], writes=[BT.b])
            preloaded = set()
            if li == 0:
                for t in range(2):
                    S.dma("sp", XR[t][:], src[t * 128:(t + 1) * 128, :], writes=[XR[t].b])
                    preloaded.add(t)
                S.wait_all("pool", [PRM.b, BT.b, IDF.b, U.b, AMASK.b, EB[0].b, XR[0].b, XR[1].b])
                cast_some(4)
            S.dma("sp", WP[:].rearrange("p a b -> p (a b)"), dram["w_proj_s"][l, 0],
                  reads=[b_scr[(l, "w_proj", 0)]], writes=[WP.b])
            ts(PRM[:, O_CW:O_CB + 12], PRM[:, O_CW:O_CB + 12], 0.5, None, ALU.mult, None, [PRM.b], [PRM.b])
            ts(PRM[:, O_DWW:O_DWW + 124], PRM[:, O_DWW:O_DWW + 124], 0.5, None, ALU.mult, None, [PRM.b], [PRM.b])
            act(DER[:, 0:16], PRM[:, O_ALOG:O_ALOG + 16], AF.Exp, [PRM.b], [DER.b])
            ts(DER[:, 0:16], DER[:, 0:16], -1.0, None, ALU.mult, None, [DER.b], [DER.b])
            ts(DER[:, 16:24], PRM[:, O_LNW:O_LNW + 8], 0.5, None, ALU.mult, None, [PRM.b], [DER.b])
            tt(BT[:], BT[:], AMASK[:].unsqueeze(2).to_broadcast([128, 2, 8, 128]), ALU.add, [BT.b, AMASK.b], [BT.b])
            tt(BT[:], BT[:], PRM[:, O_SINK:O_SINK + 8].unsqueeze(1).unsqueeze(3).to_broadcast([128, 2, 8, 128]),
               ALU.subtract, [BT.b, PRM.b], [BT.b])
            memset(HS[:], 0.0, HSbufs)
            memset(HSB[:], 0.0, HSBbufs)
            memset(TAIL[:], 0.0, [TAIL.b])
            memset(TAILB[:], 0.0, [TAILB.b])
            memset(GLU[:], 0.0, [GLU.b])
            memset(KT[:], 0.0, [KT.b])
            memset(VT[:], 0.0, [VT.b])

            def make_p1(tb):
                tok0 = tb * TB
                ths = []

                def tileA(t):
                    xr = XR[t % 2]
                    xn = XNs[t % 2]
                    r0 = tok0 + t * 128
                    if not (tb == 0 and t in preloaded):
                        S.dma("act", xr[:], src[r0:r0 + 128, :],
                              reads=([] if b_src is None else [b_src[tb * NT + t]]), writes=[xr.b])
                    SM1 = SM1s[t % 2]
                    memset(SM1[:, 0:1], 0.0, [SM1.b])
                    act(JUNK[:], xr[:], AF.Square, [xr.b, SM1.b], [JUNK.b, SM1.b], accum=SM1[:, 0:1])
                    self.rstd(SM1[:, 2:3], SM1[:, 0:1], 1.0 / D, SM1[:, 1:2], [SM1.b], [SM1.b])
                    stt(xn[:], xr[:], SM1[:, 2:3], PRM[:, O_PREW:O_PREW + 1024], ALU.mult, ALU.mult,
                        [xr.b, SM1.b, PRM.b], [xn.b])

                def tileB(t):
                    xn = XNs[t % 2]
                    trs([(PT[:, j * 128:(j + 1) * 128], xn[:, j * 128:(j + 1) * 128]) for j in range(8)],
                        IDB[:], [xn.b, IDB.b], [PT.b])
                    cp("act", XNT[:, :, t * 128:(t + 1) * 128], PT[:].rearrange("p (a b) -> p a b", a=8),
                       [PT.b], [XNT.b])
                order = [("A", 0), ("A", 1), ("B", 0), ("A", 2), ("B", 1), ("A", 3), ("B", 2), ("B", 3)]
                for kind, t in order:
                    ths.append(((lambda t=t: tileA(t)) if kind == "A" else (lambda t=t: tileB(t)), []))
                return ths

            def make_p2a(tb):
                ths = []
                hold = {}

                def xbcA(c):
                    if c % 4 == 0:
                        hold["w"] = w_acquire((l, "w_in", c // 4))
                    wb = hold["w"]
                    c4 = c % 4
                    pm = nextmm()
                    mm(pm[:], [(wb[:, kc, c4 * 128:(c4 + 1) * 128], XNT[:, kc, :]) for kc in range(8)],
                       [wb.b, XNT.b], [pm.b])
                    if c % 4 == 3:
                        w_release(wb)
                    pre, dgs = PREB[c % 2], DGS[c % 2]
                    tt(dgs[:], IDF[:].unsqueeze(1).to_broadcast([128, 4, 128]),
                       PRM[:, O_CW + c * 4:O_CW + c * 4 + 4].unsqueeze(2).to_broadcast([128, 4, 128]), ALU.mult,
                       [IDF.b, PRM.b], [dgs.b])
                    cp("dve", pre[:, 0:3], TAILB[:, c, 0:3], [TAILB.b], [pre.b])
                    cp("act", pre[:, 3:3 + TB], pm[:], [pm.b], [pre.b])
                    cp("dve", TAILB[:, c, 0:3], pre[:, TB:TB + 3], [pre.b], [TAILB.b])

                def xbcB(c):
                    pre, acc, th, dgs = PREB[c % 2], ACC[c % 2], TH[c % 2], DGS[c % 2]
                    pc = nextmm()
                    mm(pc[:], [(dgs[:, k, :], pre[:, k:k + TB]) for k in range(4)], [dgs.b, pre.b], [pc.b])
                    act(acc[:], pc[:], AF.Identity, [pc.b, PRM.b], [acc.b], bias=PRM[:, O_CB + c:O_CB + c + 1])
                    act(th[:], acc[:], AF.Tanh, [acc.b], [th.b])
                    stt(XBC[:, c, :], th[:], 1.0, acc[:], ALU.add, ALU.mult, [th.b, acc.b], [XBC.bs[c]])

                def z_tile(half, t):
                    if t == 0:
                        hold["w"] = w_acquire((l, "w_in", 3 + half))
                    wb = hold["w"]
                    pm = nextmm()
                    mm(pm[:], [(XNT[:, kc, t * 128:(t + 1) * 128], wb[:, kc, :]) for kc in range(8)],
                       [wb.b, XNT.b], [pm.b])
                    if t == NT - 1:
                        w_release(wb)
                    th = TH[t % 2]
                    act(th[:], pm[:], AF.Tanh, [pm.b], [th.b], scale=0.5)
                    stt(ZS[:, t, half * 512:(half + 1) * 512], th[:], 1.0, pm[:], ALU.add, ALU.mult,
                        [th.b, pm.b], [ZS.bs[t]])

                def kv():
                    wb = w_acquire((l, "w_in", 5))
                    cp("dve", KT[:, 0:128], KT[:, TB:TB + 128], [KT.b], [KT.b])
                    cp("dve", VT[:, 0, :], VT[:, 4, :], [VT.b], [VT.b])
                    pm = nextmm()
                    mm(pm[:], [(wb[:, kc, 0:128], XNT[:, kc, :]) for kc in range(8)], [wb.b, XNT.b], [pm.b])
                    cp("act", KT[:, 128:128 + TB], pm[:], [pm.b], [KT.b])
                    for t in range(NT):
                        pm = nextmm()
                        mm(pm[:, 0:144], [(XNT[:, kc, t * 128:(t + 1) * 128], wb[:, kc, 128:272]) for kc in range(8)],
                           [wb.b, XNT.b], [pm.b])
                        cp("dve", DTR[:, t, :], pm[:, 0:16], [pm.b], [DTR.bs[t]])
                        cp("act", VT[:, 1 + t, :], pm[:, 16:144], [pm.b], [VT.b])
                    w_release(wb)
                for c in range(13):
                    if c < 12:
                        ths.append((lambda c=c: xbcA(c), [(l, "w_in", c // 4)] if c % 4 == 0 else []))
                    if c >= 1:
                        ths.append((lambda c=c: xbcB(c - 1), []))
                for half in range(2):
                    for t in range(NT):
                        ths.append((lambda half=half, t=t: z_tile(half, t), [(l, "w_in", 3 + half)] if t == 0 else []))
                ths.append((kv, [(l, "w_in", 5)]))
                return ths

            def make_p5b(tb):
                tok0 = tb * TB
                ths = []
                hold = {}

                def post(t):
                    r0 = tok0 + t * 128
                    xr = XR[t % 2]
                    S.dma("act", xr[:], src[r0:r0 + 128, :],
                          reads=([] if b_src is None else [b_src[tb * NT + t]]), writes=[xr.b])
                    pr = PR[t % 2]
                    S.dma("act", pr[:], dram["p"][l, r0:r0 + 128, :], writes=[pr.b])
                    SM5 = SM5s[t % 2]
                    memset(SM5[:, 0:1], 0.0, [SM5.b])
                    act(JUNK[:], O[:, t, :], AF.Square, [O.bs[t], SM5.b], [JUNK.b, SM5.b], accum=SM5[:, 0:1])
                    self.rstd(SM5[:, 2:3], SM5[:, 0:1], 1.0 / D, SM5[:, 1:2], [SM5.b], [SM5.b])
                    stt(O[:, t, :], O[:, t, :], SM5[:, 2:3], PRM[:, O_POSTW:O_POSTW + 1024], ALU.mult, ALU.mult,
                        [O.bs[t], SM5.b, PRM.b], [O.bs[t]])
                    tt(O[:, t, :], O[:, t, :], xr[:], ALU.add, [O.bs[t], xr.b], [O.bs[t]])
                    HB, PB16 = HBs[t % 2], PB16s[t % 2]
                    cp("act", HB[:], O[:, t, :], [O.bs[t]], [HB.b])
                    cp("dve", PB16[:], pr[:], [pr.b], [PB16.b])

                def postB(t):
                    HB, PB16 = HBs[t % 2], PB16s[t % 2]
                    trs([(PT[:, j * 128:(j + 1) * 128], HB[:, j * 128:(j + 1) * 128]) for j in range(8)], IDB[:],
                        [HB.b, IDB.b], [PT.b])
                    cp("act", HT[:, :, t * 128:(t + 1) * 128], PT[:].rearrange("p (a b) -> p a b", a=8),
                       [PT.b], [HT.b])
                    trs([(PSB[:, j * 128:(j + 1) * 128], PB16[:, j * 128:(j + 1) * 128]) for j in range(2)], IDB[:],
                        [PB16.b, IDB.b], [PSB.b])
                    cp("act", PTT[:, :, t * 128:(t + 1) * 128], PSB[:].rearrange("p (a b) -> p a b", a=2),
                       [PSB.b], [PTT.b])

                def ple(ch, t):
                    if t == 0:
                        hold["w"] = w_acquire((l, "w_gate", ch))
                    wg = hold["w"]
                    tcols = slice(t * 128, (t + 1) * 128)
                    pg = nextmm()
                    mm(pg[:], [(HT[:, kc, tcols], wg[:, kc, :]) for kc in range(8)], [HT.b, wg.b], [pg.b])
                    if t == NT - 1:
                        w_release(wg)
                    pp = nextmm()
                    mm(pp[:], [(PTT[:, kc, tcols], WP[:, kc, ch * 512:(ch + 1) * 512]) for kc in range(2)],
                       [PTT.b, WP.b], [pp.b])
                    tg = TG[t % 2]
                    act(tg[:], pg[:], AF.Tanh, [pg.b], [tg.b], scale=0.5)
                    stt(tg[:], tg[:], 1.0, pp[:], ALU.add, ALU.mult, [tg.b, pp.b], [tg.b])
                    stt(O[:, t, ch * 512:(ch + 1) * 512], tg[:], 0.5, O[:, t, ch * 512:(ch + 1) * 512],
                        ALU.mult, ALU.add, [tg.b, O.bs[t]], [O.bs[t]])

                def store(t):
                    r0 = tok0 + t * 128
                    S.dma("pool", dst[r0:r0 + 128, :], O[:, t, :], reads=[O.bs[t]], writes=[b_dst[tb * NT + t]])
                ths.append((lambda: post(0), []))
                ths.append((lambda: post(1), []))
                ths.append((lambda: postB(0), []))
                ths.append((lambda: post(2), []))
                ths.append((lambda: postB(1), []))
                ths.append((lambda: ple(0, 0), [(l, "w_gate", 0)]))
                ths.append((lambda: post(3), []))
                ths.append((lambda: postB(2), []))
                ths.append((lambda: ple(0, 1), []))
                ths.append((lambda: postB(3), []))
                ths.append((lambda: ple(0, 2), []))
                ths.append((lambda: ple(0, 3), []))
                for t in range(NT):
                    ths.append((lambda t=t: ple(1, t), [(l, "w_gate", 1)] if t == 0 else []))
                for t in range(NT):
                    ths.append((lambda t=t: store(t), []))
                return ths

            def run(ths):
                for fn, _ in ths:
                    fn()

            def keys_of(ths):
                return [k for _, ws in ths for k in ws]

            k_ssdx = [(l, "w_in", g) for g in range(6, 11)]
            k_p5a = [(l, "w_out", g) for g in range(4)]
            first = make_p1(0) + make_p2a(0)
            wplan.extend(keys_of(first) + k_ssdx + k_p5a)
            w_pump()
            run(first)

            for tb in range(NBLK):
                tok0 = tb * TB
                if True:
                    row = lambda i: SD2[:, i, :]
                    r3 = lambda i: SD2[:, i, :].rearrange("p (t h) -> p t h", t=4)
                    sbf = SD2.bs
                    tt(r3(0), DTR[:], PRM[:, O_DTB:O_DTB + 16].unsqueeze(1).to_broadcast([128, 4, 16]), ALU.add,
                       DTR.bs + [PRM.b], [sbf[0]])
                    act(row(1), row(0), AF.Abs, [sbf[0]], [sbf[1]])
                    act(row(1), row(1), AF.Exp, [sbf[1]], [sbf[1]], scale=-1.0)
                    act(row(1), row(1), AF.Ln, [sbf[1]], [sbf[1]], bias=1.0)
                    stt(row(2), row(0), 0.0, row(1), ALU.max, ALU.add, [sbf[0], sbf[1]], [sbf[2]])
                    tt(r3(3), r3(2), DER[:, 0:16].unsqueeze(1).to_broadcast([128, 4, 16]), ALU.mult,
                       [sbf[2], DER.b], [sbf[3]])
                    cp("dve", DAB[:, 0, :], row(3), [sbf[3]], [DAB.bs[0]])
                    cp("dve", row(4), DAB[:, 0, :], [DAB.bs[0]], [sbf[4]])
                    tt(row(4), row(3), row(4), ALU.subtract, [sbf[3], sbf[4]], [sbf[4]])
                    cp("dve", DAB[:, 1, :], row(4), [sbf[4]], [DAB.bs[1]])
                    mms([(PS_[:, 256:320], UB[:], DAB[:, 0, :], True, False),
                         (PS_[:, 256:320], UB[:], DAB[:, 1, :], False, True),
                         (PS_[:, 320:384], ONEB[:, 0:128], DAB[:, 0, :], True, False),
                         (PS_[:, 320:384], ONEB[:, 0:128], DAB[:, 1, :], False, True)],
                        [UB.b, ONEB.b] + DAB.bs, [PS_.b])
                    ts(row(5), PS_[:, 256:320], -1.0, None, ALU.mult, None, [PS_.b], [sbf[5]])
                    cp("dve", row(4), PS_[:, 256:320], [PS_.b], [sbf[4]])
                    tt(row(6), PS_[:, 320:384], row(5), ALU.add, [PS_.b, sbf[5]], [sbf[6]])
                    act(row(6), row(6), AF.Exp, [sbf[6]], [sbf[6]])
                    act(row(7), PS_[:, 256:320], AF.Exp, [PS_.b], [sbf[7]])
                    act(row(8), PS_[:, 320:384], AF.Exp, [PS_.b], [sbf[8]])
                    slot = blkctr[0] % 2
                    blkctr[0] += 1
                    self.S.op("pe", lambda e: e.transpose(out=PS_[0:64, 0:128], in_=row(4), identity=IDF[:]),
                              [sbf[4], IDF.b], [PS_.b])
                    cp("act", ACST[:], PS_[0:64, 0:128], [PS_.b], [ACST.b])
                    S.dma("pool", dram["acs_d"][slot], ACST[:], reads=[ACST.b], writes=[b_acsd[slot]])

                    def stageB1(t):
                        cols = slice(t * 128, (t + 1) * 128)
                        pcb = nextmm()
                        mms([(pcb[:, g * 128:(g + 1) * 128], XBC[:, 8 + g, cols], XBC[:, 10 + g, cols], True, True)
                             for g in range(2)], [XBC.bs[8 + i] for i in range(4)], [pcb.b])
                        tt(CBM[t % 2][:], pcb[:, 0:256].rearrange("p (a b) -> p a b", a=2),
                           U[:].unsqueeze(1).to_broadcast([128, 2, 128]), ALU.mult, [pcb.b, U.b], [CBM[t % 2].b])

                    def stageB2(t):
                        cols = slice(t * 128, (t + 1) * 128)
                        xdt = XDT[t % 2]
                        trs([(PSB[:, g * 128:(g + 1) * 128], XBC[:, 8 + g, cols]) for g in range(2)], IDB[:],
                            [XBC.bs[8], XBC.bs[9], IDB.b], [PSB.b])
                        trs([(PT[:, j * 128:(j + 1) * 128], XBC[:, j, cols]) for j in range(8)], IDB[:],
                            [XBC.bs[j] for j in range(8)] + [IDB.b], [PT.b])
                        cp("dve", BMT[:].rearrange("p a b -> p (a b)"), PSB[:], [PSB.b], [BMT.b])
                        pt3 = PT[:].rearrange("p (h d) -> p h d", h=16)
                        tt(xdt[:].rearrange("p (h d) -> p h d", h=16), pt3,
                           r3(2)[:, t, :].unsqueeze(2).to_broadcast([128, 16, 64]), ALU.mult, [PT.b, sbf[2]], [xdt.b])
                        tt(XD[:].rearrange("p (h d) -> p h d", h=16), pt3,
                           PRM[:, O_D16:O_D16 + 16].unsqueeze(2).to_broadcast([128, 16, 64]), ALU.mult,
                           [PT.b, PRM.b], [XD.b])
                        tt(XDS[:].rearrange("p (h d) -> p h d", h=16), xdt[:].rearrange("p (h d) -> p h d", h=16),
                           r3(6)[:, t, :].unsqueeze(2).to_broadcast([128, 16, 64]), ALU.mult, [xdt.b, sbf[6]], [XDS.b])

                    def kdec(k):
                        return k // 4, (k // 2) % 2, k % 2

                    def stageC1(k):
                        t, g, hf = kdec(k)
                        r0 = t * 16 + g * 8 + hf * 4
                        ab = AB[k % 3]
                        S.dma("pool", ab[:], dram["acs_d"][slot, r0:r0 + 4, :].partition_broadcast(128),
                              reads=[b_acsd[slot]], writes=[ab.b])

                    def stageC2(k):
                        t, g, hf = kdec(k)
                        h0 = g * 8 + hf * 4
                        ab, ek, mtk = AB[k % 3], EB[k % 2], MT4[k % 4]
                        for i in range(4):
                            act(ek[:, i, :], ab[:, i, :], AF.Exp, [ab.b, sbf[5]], [ek.b],
                                bias=r3(5)[:, t, h0 + i:h0 + i + 1])
                        stt(mtk[:], ek[:], 1e30, CBM[t % 2][:, g, :].unsqueeze(1).to_broadcast([128, 4, 128]),
                            ALU.min, ALU.mult, [ek.b, CBM[t % 2].b], [mtk.b])

                    def advance(k):
                        if k + 2 < 4 * NT:
                            stageC1(k + 2)
                        stageC2(k)

                    def stageD(t, g):
                        cols = slice(t * 128, (t + 1) * 128)
                        gs = slice(g * 512, (g + 1) * 512)
                        py, po, t1 = PY, PO, T1
                        xdt = XDT[t % 2]
                        k0 = 4 * t + 2 * g
                        items = [(py[:], IDB[:], XD[:, gs], True, False)]
                        for hh in range(8):
                            h = g * 8 + hh
                            items.append((py[:, hh * 64:(hh + 1) * 64], MT4[(k0 + hh // 4) % 4][:, hh % 4, :],
                                          xdt[:, h * 64:(h + 1) * 64], False, True))
                        mms(items, [IDB.b, XD.b, MT4[k0 % 4].b, MT4[(k0 + 1) % 4].b, xdt.b], [py.b])
                        mm(po[:], [(XBC[:, 10 + g, cols], HSB[:, gs])], [XBC.bs[10 + g], HSBbufs[g]], [po.b])
                        tt(t1[:].rearrange("p (h d) -> p h d", h=8), po[:].rearrange("p (h d) -> p h d", h=8),
                           r3(7)[:, t, g * 8:(g + 1) * 8].unsqueeze(2).to_broadcast([128, 8, 64]), ALU.mult,
                           [po.b, sbf[7]], [t1.b])
                        tt(Y[:, gs], t1[:], py[:], ALU.add, [t1.b, py.b], [Y.b])

                    def stageD2(t, g):
                        gs = slice(g * 512, (g + 1) * 512)
                        po = nextmm()
                        hb_, hsb_ = HSbufs[g], HSBbufs[g]
                        tt(HS[:, gs].rearrange("p (h d) -> p h d", h=8), HS[:, gs].rearrange("p (h d) -> p h d", h=8),
                           r3(8)[:, t, g * 8:(g + 1) * 8].unsqueeze(2).to_broadcast([128, 8, 64]), ALU.mult,
                           [hb_, sbf[8]], [hb_], eng="pool")
                        mm(po[:], [(BMT[:, g, :], XDS[:, gs])], [BMT.b, XDS.b], [po.b])
                        tt(HSB[:, gs], HS[:, gs], po[:], ALU.add, [hb_, po.b], [hsb_])
                        tt(HS[:, gs], HS[:, gs], po[:], ALU.add, [hb_, po.b], [hb_])

                    def stageE(t):
                        cols = slice(t * 128, (t + 1) * 128)
                        tt(Y[:], Y[:], ZS[:, t, :], ALU.mult, [Y.b, ZS.bs[t]], [Y.b])
                        SME = SMEs[t % 2]
                        memset(SME[:, 0:2], 0.0, [SME.b])
                        for g in range(2):
                            act(JUNK[:, 0:512], Y[:, g * 512:(g + 1) * 512], AF.Square, [Y.b, SME.b], [JUNK.b, SME.b],
                                accum=SME[:, g:g + 1])
                        ts(SME[:, 2:4], SME[:, 0:2], 1.0 / 512.0, 4.0 * EPS, ALU.mult, ALU.add, [SME.b], [SME.b])
                        act(SME[:, 2:4], SME[:, 2:4], AF.Ln, [SME.b], [SME.b])
                        act(SME[:, 4:6], SME[:, 2:4], AF.Exp, [SME.b], [SME.b], scale=-0.5)
                        for g in range(2):
                            gs = slice(g * 512, (g + 1) * 512)
                            stt(YN[:, gs], Y[:, gs], SME[:, 4 + g:5 + g], PRM[:, O_SSDNW + g * 512:O_SSDNW + (g + 1) * 512],
                                ALU.mult, ALU.mult, [Y.b, SME.b, PRM.b], [YN.b])

                    def stageE2(t):
                        cols = slice(t * 128, (t + 1) * 128)
                        pte = nextmm()
                        ptv = pte[:].bitcast(BF16)
                        trs([(ptv[:, j * 128:(j + 1) * 128], YN[:, j * 128:(j + 1) * 128]) for j in range(8)], IDB[:],
                            [YN.b, IDB.b], [pte.b])
                        cp("dve", YT[:, 0:8, cols], ptv.rearrange("p (a b) -> p a b", a=8), [pte.b], [YT.b])

                    extra = []
                    wbh = {}

                    def conf_pair(j):
                        if j % 2 == 0:
                            wbh["c"] = w_acquire((l, "w_in", 6 + j // 2))
                        wb = wbh["c"]
                        jj = j % 2
                        pa = nextmm()
                        mm(pa[:], [(wb[:, kc, (2 * jj) * 128:(2 * jj + 1) * 128], XNT[:, kc, :]) for kc in range(8)],
                           [wb.b, XNT.b], [pa.b])
                        pb = nextmm()
                        mm(pb[:], [(wb[:, kc, (2 * jj + 1) * 128:(2 * jj + 2) * 128], XNT[:, kc, :]) for kc in range(8)],
                           [wb.b, XNT.b], [pb.b])
                        if j % 2 == 1:
                            w_release(wb)
                        th = TH[j % 2]
                        act(th[:], pb[:], AF.Tanh, [pb.b], [th.b], scale=0.5)
                        stt(GLU[:, j, 30:30 + TB], th[:], 1.0, pa[:], ALU.add, ALU.mult, [th.b, pa.b], [GLU.b])

                    def conf_gate(j):
                        if j == 0:
                            wbh["c"] = w_acquire((l, "w_in", 8))
                        wb = wbh["c"]
                        pm = nextmm()
                        mm(pm[:], [(wb[:, kc, j * 128:(j + 1) * 128], XNT[:, kc, :]) for kc in range(8)],
                           [wb.b, XNT.b], [pm.b])
                        if j == 3:
                            w_release(wb)
                        th = TH[j % 2]
                        act(th[:], pm[:], AF.Tanh, [pm.b], [th.b], scale=0.5)
                        stt(CG[:, j, :], th[:], 1.0, pm[:], ALU.add, ALU.mult, [th.b, pm.b], [CG.bs[j]])

                    def conf_convA(j):
                        tt(DG[:], IDF[:].unsqueeze(1).to_broadcast([128, 31, 128]),
                           PRM[:, O_DWW + j * 31:O_DWW + (j + 1) * 31].unsqueeze(2).to_broadcast([128, 31, 128]),
                           ALU.mult, [IDF.b, PRM.b], DG.bs)

                    def conf_convB(j):
                        pm = nextmm()
                        mm(pm[:], [(DG[:, k, :], GLU[:, j, k:k + TB]) for k in range(31)], DG.bs + [GLU.b], [pm.b])
                        act(HH[:, j, :], pm[:], AF.Identity, [pm.b, PRM.b], [HH.bs[j]],
                            bias=PRM[:, O_DWB + j:O_DWB + j + 1])

                    def conf_stats():
                        PCA = nextmm()
                        PCB = nextmm()
                        mm(PCA[:], [(ONE512[:], HH[:, j, :]) for j in range(4)], [ONE512.b] + HH.bs, [PCA.b])
                        for j in range(4):
                            hq = HQ[j % 2]
                            act(hq[:], HH[:, j, :], AF.Square, [HH.bs[j]], [hq.b])
                            mm(PCB[:], [(ONE512[:], hq[:])], [ONE512.b, hq.b], [PCB.b], start=(j == 0), stop=(j == 3))
                        cp("act", MU[:], PCA[:], [PCA.b], [MU.b])
                        tt(D1[:], MU[:], MU[:], ALU.mult, [MU.b], [D1.b])
                        tt(RS[:], PCB[:], D1[:], ALU.subtract, [PCB.b, D1.b], [RS.b])
                        ts(RS[:], RS[:], EPS, None, ALU.add, None, [RS.b], [RS.b])
                        act(RS[:], RS[:], AF.Ln, [RS.b], [RS.b])
                        act(RS[:], RS[:], AF.Exp, [RS.b], [RS.b], scale=-0.5)

                    def conf_ln(j):
                        tt(D1[:], HH[:, j, :], MU[:], ALU.subtract, [HH.bs[j], MU.b], [D1.b])
                        tt(D1[:], D1[:], RS[:], ALU.mult, [D1.b, RS.b], [D1.b])
                        th = TH[j % 2]
                        act(th[:], D1[:], AF.Tanh, [D1.b, DER.b], [th.b], scale=DER[:, 16 + j:17 + j],
                            bias=DER[:, 20 + j:21 + j])
                        ts(U1[:], D1[:], PRM[:, O_LNW + j:O_LNW + j + 1], PRM[:, O_LNB + j:O_LNB + j + 1],
                           ALU.mult, ALU.add, [D1.b, PRM.b], [U1.b])
                        stt(U1[:], th[:], 1.0, U1[:], ALU.add, ALU.mult, [th.b, U1.b], [U1.b])
                        stt(YT[:, 8 + j, :], U1[:], 0.25, CG[:, j, :], ALU.mult, ALU.mult, [U1.b, CG.bs[j]], [YT.b])

                    def attn_q(g):
                        if g == 0:
                            wbh["a"] = w_acquire((l, "w_in", 9))
                        wb = wbh["a"]
                        pm = nextmm()
                        mm(pm[:], [(wb[:, kc, g * 128:(g + 1) * 128], XNT[:, kc, :]) for kc in range(8)],
                           [wb.b, XNT.b], [pm.b])
                        if g == 3:
                            w_release(wb)
                        cp("act", QT[:, g, :], pm[:], [pm.b], [QT.bs[g]])

                    def attn_g(c):
                        if c == 0:
                            wbh["a"] = w_acquire((l, "w_in", 10))
                        wb = wbh["a"]
                        pm = nextmm()
                        mm(pm[:], [(wb[:, kc, c * 128:(c + 1) * 128], XNT[:, kc, :]) for kc in range(8)],
                           [wb.b, XNT.b], [pm.b])
                        if c == 3:
                            w_release(wb)
                        th = TH[c % 2]
                        act(th[:], pm[:], AF.Tanh, [pm.b], [th.b], scale=0.5)
                        stt(AG[:, c, :], th[:], 1.0, pm[:], ALU.add, ALU.mult, [th.b, pm.b], [AG.bs[c]])

                    def attn_A(n, kh):
                        u = n * 2 + kh
                        nbk = tb * NT + n
                        qcols = slice(n * 128, (n + 1) * 128)
                        kbs = [1] if nbk == 0 else [0, 1]
                        prt = slice(kh * 64, (kh + 1) * 64)
                        for kb in kbs:
                            et = ET[(u % 2) * 2 + kb]
                            pm = nextmm()
                            mm(pm[:], [(KT[prt, (n + kb) * 128:(n + kb + 1) * 128], QT[prt, :, qcols])],
                               [KT.b] + QT.bs, [pm.b])
                            stt(SC[kb][:], pm[:], 0.125,
                                BT[:, kb, kh * 4:(kh + 1) * 4, :].rearrange("p h q -> p (h q)"),
                                ALU.mult, ALU.add, [pm.b, BT.b], [SC[kb].b])
                            act(et[:], SC[kb][:], AF.Exp, [SC[kb].b], [et.b])

                    def attn_B(n, kh):
                        u = n * 2 + kh
                        nbk = tb * NT + n
                        qcols = slice(n * 128, (n + 1) * 128)
                        kbs = [1] if nbk == 0 else [0, 1]
                        prt = slice(kh * 64, (kh + 1) * 64)
                        ets = {kb: ET[(u % 2) * 2 + kb] for kb in kbs}
                        pod = nextmm()
                        items = []
                        for jj in range(2):
                            for i, kb in enumerate(kbs):
                                items.append((pod[jj * 64:(jj + 1) * 64, 0:256], VT[:, n + kb, prt],
                                              ets[kb][:, jj * 256:(jj + 1) * 256], i == 0, i == len(kbs) - 1))
                        for jj in range(2):
                            for i, kb in enumerate(kbs):
                                items.append((pod[jj * 64:(jj + 1) * 64, 256:512], ONEB[:, 0:64],
                                              ets[kb][:, jj * 256:(jj + 1) * 256], i == 0, False))
                            items.append((pod[jj * 64:(jj + 1) * 64, 256:512], ONEB[0:1, 0:64], ONEB[0:1, 0:256],
                                          False, True))
                        mms(items, [VT.b, ONEB.b] + [ets[kb].b for kb in kbs], [pod.b])
                        self.S.op("dve", lambda e, o=RD[:], i=pod[:, 256:512]: e.reciprocal(out=o, in_=i),
                                  [pod.b], [RD.b])
                        tt(OT[:], pod[:, 0:256], RD[:], ALU.mult, [pod.b, RD.b], [OT.b])
                        stt(YT[:, 12 + kh * 2:14 + kh * 2, qcols], OT[:].rearrange("p (a b) -> p a b", a=2), 0.5,
                            AG[:, kh * 2:kh * 2 + 2, qcols], ALU.mult, ALU.mult,
                            [OT.b, AG.bs[kh * 2], AG.bs[kh * 2 + 1]], [YT.b])

                    extra.append(lambda: cp("dve", GLU[:, :, 0:30], GLU[:, :, TB:TB + 30], [GLU.b], [GLU.b]))
                    for j in range(4):
                        extra.append(lambda j=j: conf_pair(j))
                    for j in range(4):
                        extra.append(lambda j=j: conf_convA(j))
                        extra.append(lambda j=j: conf_gate(j))
                        extra.append(lambda j=j: conf_convB(j))
                    extra.append(conf_stats)
                    for j in range(4):
                        extra.append(lambda j=j: attn_q(j))
                        extra.append(lambda j=j: conf_ln(j))
                    for c in range(4):
                        extra.append(lambda c=c: attn_g(c))
                    units = [(n, kh) for n in range(NT) for kh in range(2)]
                    for i in range(len(units) + 1):
                        if i < len(units):
                            extra.append(lambda nk=units[i]: attn_A(*nk))
                        if i >= 1:
                            extra.append(lambda nk=units[i - 1]: attn_B(*nk))
                    nslots = 2 * NT
                    per = (len(extra) + nslots - 1) // nslots
                    epos = [0]

                    def run_extra(n):
                        for _ in range(n):
                            if epos[0] < len(extra):
                                extra[epos[0]]()
                                epos[0] += 1

                    stageB1(0)
                    stageB2(0)
                    stageC1(0)
                    stageC1(1)
                    run_extra(6)
                    advance(0)
                    advance(1)
                    run_extra(3)
                    advance(2)
                    advance(3)
                    stageB1(1)
                    run_extra(2)
                    reserve = 4
                    per = max(1, (len(extra) - epos[0] - reserve + nslots - 1) // nslots)
                    for t in range(NT):
                        for g in range(2):
                            stageD(t, g)
                            if t + 1 < NT:
                                advance(4 * (t + 1) + 2 * g)
                                advance(4 * (t + 1) + 2 * g + 1)
                            run_extra(min(per, max(0, len(extra) - epos[0] - reserve)))
                            stageD2(t, g)
                        if t >= 1:
                            stageE2(t - 1)
                        if t + 1 < NT:
                            stageB2(t + 1)
                        stageE(t)
                        if t + 2 < NT:
                            stageB1(t + 2)
                    run_extra(2)
                    stageE2(NT - 1)
                    run_extra(len(extra))

                p5a = []
                ohold = {}

                def outproj(ch, t):
                    if t == 0:
                        ohold["w0"] = w_acquire((l, "w_out", ch * 2))
                        ohold["w1"] = w_acquire((l, "w_out", ch * 2 + 1))
                    w0, w1 = ohold["w0"], ohold["w1"]
                    pm = nextmm()
                    pairs = [(YT[:, kc, t * 128:(t + 1) * 128], (w0 if kc < 8 else w1)[:, kc % 8, :])
                             for kc in range(16)]
                    mm(pm[:], pairs, [YT.b, w0.b, w1.b], [pm.b])
                    cp("act", O[:, t, ch * 512:(ch + 1) * 512], pm[:], [pm.b], [O.bs[t]])
                    if t == NT - 1:
                        w_release(w0)
                        w_release(w1)
                for ch in range(2):
                    for t in range(NT):
                        p5a.append((lambda ch=ch, t=t: outproj(ch, t), []))
                if tb + 1 < NBLK:
                    run(interleave(p5a, make_p1(tb + 1)))
                else:
                    run(p5a)
                for v in p5views:
                    v.b.w, v.b.r = {}, {}
                    inherit(v.b, [YT.b])
                A = make_p5b(tb)
                if tb + 1 < NBLK:
                    Bn = make_p2a(tb + 1)
                    merged = interleave(A, Bn)
                    wplan.extend(keys_of(merged) + k_ssdx + k_p5a)
                else:
                    merged = A
                    wplan.extend(keys_of(merged))
                w_pump()
                run(merged)
                inherit(YT.b, [v.b for v in p5views])
        S.wait_all("sp", b_out)
        S.wait_all("pool", b_out)


def _col_perm():
    z0, xbc0, dt0, ci0, cg0, q0, k0, v0, ag0 = 0, 1024, 2560, 2576, 3600, 4112, 4624, 4752, 4880
    groups = []
    for g in range(3):
        groups.append(list(range(xbc0 + g * 512, xbc0 + (g + 1) * 512)))
    groups.append(list(range(z0, z0 + 512)))
    groups.append(list(range(z0 + 512, z0 + 1024)))
    groups.append(list(range(k0, k0 + 128)) + list(range(dt0, dt0 + 16)) + list(range(v0, v0 + 128)) + [-1] * 240)
    for g in range(2):
        cols = []
        for j in (2 * g, 2 * g + 1):
            cols += list(range(ci0 + j * 128, ci0 + (j + 1) * 128))
            cols += list(range(ci0 + 512 + j * 128, ci0 + 512 + (j + 1) * 128))
        groups.append(cols)
    groups.append(list(range(cg0, cg0 + 512)))
    cols = []
    for g in range(4):
        cols += list(range(q0 + g * 64, q0 + (g + 1) * 64))
        cols += list(range(q0 + (4 + g) * 64, q0 + (5 + g) * 64))
    groups.append(cols)
    cols = []
    for kh in range(2):
        for gl in range(2):
            h0, h1 = kh * 4 + gl, kh * 4 + 2 + gl
            cols += list(range(ag0 + h0 * 64, ag0 + (h0 + 1) * 64))
            cols += list(range(ag0 + h1 * 64, ag0 + (h1 + 1) * 64))
    groups.append(cols)
    assert len(groups) == 11 and all(len(g) == 512 for g in groups)
    return np.array(groups, dtype=np.int64)


def _row_perm_out():
    rows = list(range(0, 1536))
    a0 = 1536
    for kh in range(2):
        for gl in range(2):
            h0, h1 = kh * 4 + gl, kh * 4 + 2 + gl
            rows += list(range(a0 + h0 * 64, a0 + (h0 + 1) * 64))
            rows += list(range(a0 + h1 * 64, a0 + (h1 + 1) * 64))
    return np.array(rows, dtype=np.int64)


def _t5_bucket(d):
    d = np.maximum(d, 0)
    dm = np.maximum(d, 1).astype(np.float32)
    large = 16 + (np.log(dm / np.float32(16)) / np.float32(math.log(128 / 16)) * np.float32(16)).astype(np.int32)
    large = np.minimum(large, 31)
    return np.where(d < 16, d, large)


def _prep(inputs):
    f32 = np.float32
    w_in = np.asarray(inputs["w_in"], f32)
    perm = _col_perm()
    w_in_pad = np.concatenate([w_in, np.zeros((DEPTH, D, 1), f32)], axis=2)
    wi = w_in_pad[:, :, perm.reshape(-1)].reshape(DEPTH, 8, 128, 11, 512)
    wi = np.ascontiguousarray(wi.transpose(0, 3, 2, 1, 4)).reshape(DEPTH, 11, 128, 4096)
    w_out = np.asarray(inputs["w_out"], f32)[:, _row_perm_out(), :]
    wo = w_out.reshape(DEPTH, 2, 8, 128, 2, 512).transpose(0, 4, 1, 3, 2, 5)
    wo = np.ascontiguousarray(wo).reshape(DEPTH, 4, 128, 4096)
    wg = np.asarray(inputs["ple_gate"], f32).reshape(DEPTH, 8, 128, 2, 512).transpose(0, 3, 2, 1, 4)
    wg = np.ascontiguousarray(wg).reshape(DEPTH, 2, 128, 4096)
    wp = np.asarray(inputs["ple_proj"], f32).reshape(DEPTH, 2, 128, 1024).transpose(0, 2, 1, 3)
    wp = np.ascontiguousarray(wp).reshape(DEPTH, 1, 128, 2048)
    prm = np.zeros((DEPTH, 128, NP), f32)
    bc = lambda v: np.broadcast_to(np.asarray(v, f32)[None, :], (128, len(v)))
    for l in range(DEPTH):
        prm[l, :, O_PREW:O_PREW + 1024] = bc(inputs["pre_norm_w"][l])
        prm[l, :, O_SSDNW:O_SSDNW + 1024] = bc(inputs["ssd_norm_w"][l])
        prm[l, :, O_POSTW:O_POSTW + 1024] = bc(inputs["post_norm_w"][l])
        prm[l, :, O_D16:O_D16 + 16] = bc(inputs["ssd_d"][l])
        prm[l, :, O_DTB:O_DTB + 16] = bc(inputs["ssd_dt_bias"][l])
        prm[l, :, O_ALOG:O_ALOG + 16] = bc(inputs["ssd_a_log"][l])
        cw = np.asarray(inputs["ssd_conv_w"][l], f32)
        prm[l, :, O_CW:O_CW + 48] = cw.reshape(4, 12, 128).transpose(2, 1, 0).reshape(128, 48)
        prm[l, :, O_CB:O_CB + 12] = np.asarray(inputs["ssd_conv_b"][l], f32).reshape(12, 128).T
        dw = np.asarray(inputs["conf_dw_w"][l], f32)
        prm[l, :, O_DWW:O_DWW + 124] = dw.reshape(31, 4, 128).transpose(2, 1, 0).reshape(128, 124)
        prm[l, :, O_DWB:O_DWB + 4] = np.asarray(inputs["conf_dw_b"][l], f32).reshape(4, 128).T
        prm[l, :, O_LNW:O_LNW + 4] = np.asarray(inputs["conf_ln_w"][l], f32).reshape(4, 128).T
        prm[l, :, O_LNB:O_LNB + 4] = np.asarray(inputs["conf_ln_b"][l], f32).reshape(4, 128).T
        prm[l, :, O_SINK:O_SINK + 8] = bc(inputs["attn_sinks"][l])
    s = np.arange(128)[:, None, None]
    kb = np.arange(2)[None, :, None]
    q = np.arange(128)[None, None, :]
    dist = q + 128 - (kb * 128 + s)
    valid = (dist >= 0) & (dist < 128)
    bucket = _t5_bucket(np.clip(dist, 0, 127))
    rel = np.asarray(inputs["rel_bias"], f32)
    abias = np.ascontiguousarray(rel[bucket].transpose(0, 1, 3, 2)).reshape(128, 2 * 8 * 128)
    amask = np.where(valid, 0.0, NEG).astype(f32).reshape(128, 256)
    ident = np.eye(128, dtype=f32)
    tri = np.triu(np.ones((128, 128), f32))
    common = {"w_in": wi, "w_out": wo, "w_gate": wg, "w_proj": wp, "prm": prm, "c_abias": abias,
              "c_amask": amask, "c_ident": ident, "c_tri": tri,
              "c_mneg": np.where(np.arange(128)[:, None] <= np.arange(128)[None, :], 0.0, NEG).astype(f32)}
    return common


def _build(layers, debug=False):
    nc = bass.Bass("TRN2", target_bir_lowering=False)
    dram = {}
    dt = lambda name, shape, dtype, kind: nc.dram_tensor(name, shape, dtype, kind=kind).ap()
    dram["x"] = dt("x", [L_SEQ, D], F32, "ExternalInput")
    dram["p"] = dt("p", [DEPTH, L_SEQ, 256], F32, "ExternalInput")
    for name, ng, w in (("w_in", 11, 4096), ("w_out", 4, 4096), ("w_gate", 2, 4096), ("w_proj", 1, 2048)):
        dram[name] = dt(name, [DEPTH, ng, 128, w], F32, "ExternalInput")
        dram[name + "_s"] = dt(name + "_s", [DEPTH, ng, 128, w], BF16, "Internal")
    dram["prm"] = dt("prm", [DEPTH, 128, NP], F32, "ExternalInput")
    dram["c_abias"] = dt("c_abias", [128, 2048], F32, "ExternalInput")
    dram["c_amask"] = dt("c_amask", [128, 256], F32, "ExternalInput")
    dram["c_ident"] = dt("c_ident", [128, 128], F32, "ExternalInput")
    dram["c_tri"] = dt("c_tri", [128, 128], F32, "ExternalInput")
    dram["c_mneg"] = dt("c_mneg", [128, 128], F32, "ExternalInput")
    dram["hscr"] = dt("hscr", [L_SEQ, D], F32, "Internal")
    dram["acs_d"] = dt("acs_d", [2, 64, 128], F32, "Internal")
    dram["out"] = dt("out", [L_SEQ, D], F32, "ExternalOutput")
    if debug:
        dram["dbg"] = dt("dbg", [128, 16, L_SEQ], F32, "ExternalOutput")
    with ExitStack() as st:
        prog = Prog(nc, st, layers, True, True)
        prog.build(dram)
    return nc


def kernel(**inputs):
    common = _prep(inputs)
    x = np.asarray(inputs["x"], np.float32)
    p = np.asarray(inputs["p"], np.float32)
    nb = x.shape[0]
    nc = _build(list(range(DEPTH)), debug=DEBUG)
    in_maps = []
    for b in range(nb):
        m = dict(common)
        m["x"] = np.ascontiguousarray(x[b])
        m["p"] = np.ascontiguousarray(p[:, b])
        in_maps.append(m)
    res = run_bass_kernel_spmd(nc, in_maps, core_ids=list(range(nb)))
    out = np.stack([np.asarray(r["out"], np.float32) for r in res.results], axis=0)
    if DEBUG:
        kernel.dbg = [np.asarray(r["dbg"]) for r in res.results]
    return out
```
